# Optimizing a Trainium2 kernel written in Bass

```python
import math
import jax, jax.numpy as jnp
from jax import lax
import numpy as np

D_MODEL = 1024
BATCH = 16
SEQ = 256
DEPTH = 2
DEC_BATCH = 4
DEC_SEQ = 2048
PAST_LEN = 512

GRID_W = 64
N_EVEN = (DEPTH + 1) // 2
N_ODD = DEPTH // 2
MLSTM_HEADS = 4
MLSTM_WIDTH = D_MODEL // 2
MLSTM_HD = MLSTM_WIDTH // MLSTM_HEADS
MLSTM_CHUNK = 64
HYENA_WIDTH = D_MODEL // 2
N_BANDS = 16
FILTER_EMB = 2 * N_BANDS + 1
FILTER_HIDDEN = 64
DECAY_FAST = 0.3
DECAY_SLOW = 1.5
DECAY_TARGET = 1e-2
NA_HEADS = 16
NA_HD = D_MODEL // NA_HEADS
NA_KH = 8
NA_KW = 16
Q_BLOCK = 128
FF = 2816
CONV_W = 3
EPS = 1e-6
IN_AB = 4 * MLSTM_WIDTH + 4 * MLSTM_HEADS + 3 * HYENA_WIDTH

kernel_name = 'hybrid_mlstm_hyena_natten_dit_step'

F32 = jnp.float32


def rmsnorm(x, g):
    x32 = x.astype(F32)
    y = x32 * lax.rsqrt(jnp.mean(x32 * x32, axis=-1, keepdims=True) + EPS)
    return (y * g.astype(F32)).astype(x.dtype)


def dwconv3(x, w):
    xp = jnp.pad(x, ((0, 0), (1, 1), (0, 0)))
    return xp[:, :-2] * w[0] + xp[:, 1:-1] * w[1] + xp[:, 2:] * w[2]


def adaln(cvec, w, b):
    mod = jax.nn.silu(cvec) @ w + b
    return jnp.split(mod[:, None, :], 6, axis=-1)


def mlstm_chunked(q, k, v, ig, lf, C0, n0, m0):
    B, H, L, d = q.shape
    nc = L // MLSTM_CHUNK

    def to_chunks(t):
        return jnp.moveaxis(t.reshape((B, H, nc, MLSTM_CHUNK) + t.shape[3:]), 2, 0)

    lower = jnp.tril(jnp.ones((MLSTM_CHUNK, MLSTM_CHUNK), dtype=bool))

    def step(carry, xs):
        C, n, m = carry
        qc, kc, vc, ic, fc = xs
        b = jnp.cumsum(fc, axis=-1)
        dmat = jnp.where(lower, b[..., :, None] - b[..., None, :] + ic[..., None, :], -jnp.inf)
        m_inter = b + m[..., None]
        m_t = jnp.maximum(m_inter, jnp.max(dmat, axis=-1))
        s = jnp.exp(dmat - m_t[..., None]) * jnp.einsum('bhtd,bhsd->bhts', qc, kc)
        sc = jnp.exp(m_inter - m_t)
        num = sc[..., None] * jnp.einsum('bhtd,bhde->bhte', qc, C) + jnp.einsum('bhts,bhse->bhte', s, vc)
        den = sc * jnp.einsum('bhtd,bhd->bht', qc, n) + jnp.sum(s, axis=-1)
        hc = num / jnp.maximum(jnp.abs(den), jnp.exp(-m_t))[..., None]
        b_end = b[..., -1]
        to_end = b_end[..., None] - b + ic
        m_new = jnp.maximum(b_end + m, jnp.max(to_end, axis=-1))
        w = jnp.exp(to_end - m_new[..., None])
        decay = jnp.exp(b_end + m - m_new)
        C_new = decay[..., None, None] * C + jnp.einsum('bhs,bhsd,bhse->bhde', w, kc, vc)
        n_new = decay[..., None] * n + jnp.einsum('bhs,bhsd->bhd', w, kc)
        return (C_new, n_new, m_new), hc

    xs = (to_chunks(q), to_chunks(k), to_chunks(v), to_chunks(ig), to_chunks(lf))
    (C, n, m), hs = lax.scan(step, (C0, n0, m0), xs)
    return jnp.moveaxis(hs, 0, 2).reshape(B, H, L, d), (C, n, m)


def mlstm_bidir(q, k, v, gates, state0):
    B, L, _ = q.shape

    def heads(t):
        return t.astype(F32).reshape(B, L, MLSTM_HEADS, MLSTM_HD).transpose(0, 2, 1, 3)

    qh, kh, vh = heads(q), heads(k) * (MLSTM_HD ** -0.5), heads(v)
    g = gates.astype(F32).transpose(0, 2, 1)
    i_f, f_f, i_b, f_b = jnp.split(g, 4, axis=1)
    C0, n0, m0 = state0[0].astype(F32), state0[1].astype(F32), state0[2].astype(F32)
    h_f, (Cf, nf, mf) = mlstm_chunked(qh, kh, vh, i_f, jax.nn.log_sigmoid(f_f), C0[:, 0], n0[:, 0], m0[:, 0])

    def flip(t):
        return jnp.flip(t, axis=2)

    h_b, (Cb, nb, mb) = mlstm_chunked(flip(qh), flip(kh), flip(vh), flip(i_b), flip(jax.nn.log_sigmoid(f_b)),
                                      C0[:, 1], n0[:, 1], m0[:, 1])
    h = h_f + flip(h_b)
    state = (jnp.stack([Cf, Cb], axis=1), jnp.stack([nf, nb], axis=1), jnp.stack([mf, mb], axis=1))
    return h, state


def hyena_filters(L, w1, b1, w2, b2, w3, freq):
    t = jnp.linspace(0.0, 1.0, L, dtype=F32)[:, None]
    wpos = 2.0 * math.pi * jnp.arange(L, dtype=F32)[:, None] / L
    bands = jnp.linspace(1e-4, N_BANDS - 1, N_BANDS, dtype=F32)[None, :]
    z = jnp.concatenate([t, jnp.cos(bands * wpos), -jnp.sin(bands * wpos)], axis=-1)
    fr = freq.astype(F32)
    hdn = jnp.sin(fr * (z @ w1.astype(F32) + b1.astype(F32)))
    hdn = jnp.sin(fr * (hdn @ w2.astype(F32) + b2.astype(F32)))
    filt = hdn @ w3.astype(F32)
    max_decay = math.log(DECAY_TARGET) / DECAY_FAST
    min_decay = math.log(DECAY_TARGET) / DECAY_SLOW
    deltas = jnp.abs(jnp.linspace(min_decay, max_decay, HYENA_WIDTH, dtype=F32))
    decay = jnp.exp(-t * jnp.concatenate([deltas, deltas])[None, :])
    filt = filt * decay
    return filt[:, :HYENA_WIDTH], filt[:, HYENA_WIDTH:]


def long_conv_bidir(u, h_past, h_future):
    B, L, C = u.shape
    two_sided = jnp.concatenate([h_past, jnp.zeros((1, C), F32), h_future[1:][::-1]], axis=0)
    U = jnp.fft.rfft(u, n=2 * L, axis=1)
    K = jnp.fft.rfft(two_sided, axis=0)
    return jnp.fft.irfft(U * K[None], n=2 * L, axis=1)[:, :L]


def mixer_ab(h, state0, w_in, b_gates, w_conv_qk, g_head, w_conv_hy, w_f1, b_f1, w_f2, b_f2, w_f3, freq,
             hy_bias, w_out):
    B, L, _ = h.shape
    MW = MLSTM_WIDTH
    proj = h @ w_in
    qk_raw = proj[..., :2 * MW]
    v_m = proj[..., 2 * MW:3 * MW]
    o_pre = proj[..., 3 * MW:4 * MW]
    gates = proj[..., 4 * MW:4 * MW + 4 * MLSTM_HEADS] + b_gates
    hy = proj[..., 4 * MW + 4 * MLSTM_HEADS:]
    qk = jax.nn.silu(dwconv3(qk_raw, w_conv_qk))
    hm, state = mlstm_bidir(qk[..., :MW], qk[..., MW:], v_m, gates, state0)
    hm = hm * lax.rsqrt(jnp.mean(hm * hm, axis=-1, keepdims=True) + EPS)
    hm = hm.transpose(0, 2, 1, 3).reshape(B, L, MW) * g_head.astype(F32)
    y_m = (hm * jax.nn.sigmoid(o_pre.astype(F32))).astype(h.dtype)
    hy = dwconv3(hy, w_conv_hy)
    v_h, x1, x2 = jnp.split(hy, 3, axis=-1)
    u = (x1 * v_h).astype(F32)
    h_past, h_future = hyena_filters(L, w_f1, b_f1, w_f2, b_f2, w_f3, freq)
    y_h = x2.astype(F32) * (long_conv_bidir(u, h_past, h_future) + hy_bias.astype(F32) * u)
    out = jnp.concatenate([y_m, y_h.astype(h.dtype)], axis=-1) @ w_out
    return out, (state[0].astype(h.dtype), state[1].astype(h.dtype), state[2].astype(h.dtype))


def na_heads(t):
    B, L, _ = t.shape
    return t.reshape(B, L, NA_HEADS, NA_HD).transpose(0, 2, 1, 3)


def dense_attention(q, k, v):
    B, H, L, d = q.shape
    nb = L // Q_BLOCK
    qb = jnp.moveaxis(q.reshape(B, H, nb, Q_BLOCK, d), 2, 0)

    def block(qi):
        s = jnp.einsum('bhqd,bhkd->bhqk', qi, k).astype(F32) * (d ** -0.5)
        p = jax.nn.softmax(s, axis=-1)
        return jnp.einsum('bhqk,bhkd->bhqd', p.astype(v.dtype), v)

    o = lax.map(block, qb)
    return jnp.moveaxis(o, 0, 2).reshape(B, H, L, d)


def mixer_c_context(h, w_in, w_out):
    B, L, _ = h.shape
    q, k, v = jnp.split(h @ w_in, 3, axis=-1)
    q, k, v = na_heads(q), na_heads(k), na_heads(v)
    o = dense_attention(q, k, v)
    return o.transpose(0, 2, 1, 3).reshape(B, L, D_MODEL) @ w_out, k, v


def na_attention(q, k, v, k_ctx, v_ctx, rpb):
    B, H, L, d = q.shape
    R = L // GRID_W
    kh = min(NA_KH, R)
    rows = jnp.arange(R)
    cols = jnp.arange(GRID_W)
    r_start = jnp.clip(rows - kh // 2, 0, R - kh)
    key_rows = r_start[:, None] + jnp.arange(kh)[None, :]
    c_start = jnp.clip(cols - NA_KW // 2, 0, GRID_W - NA_KW)
    col_mask = (cols[None, :] >= c_start[:, None]) & (cols[None, :] < c_start[:, None] + NA_KW)
    qg = q.reshape(B, H, R, GRID_W, d) * (d ** -0.5)
    kg = k.reshape(B, H, R, GRID_W, d)[:, :, key_rows]
    vg = v.reshape(B, H, R, GRID_W, d)[:, :, key_rows]
    s_loc = jnp.einsum('bhrqd,bhrjwd->bhrqjw', qg, kg).astype(F32)
    idx_r = key_rows - rows[:, None] + (NA_KH - 1)
    idx_c = jnp.clip(cols[None, :] - cols[:, None] + (NA_KW - 1), 0, 2 * NA_KW - 2)
    bias = rpb[:, idx_r[:, None, :, None], idx_c[None, :, None, :]]
    s_loc = jnp.where(col_mask[:, None, :], s_loc + bias.astype(F32)[None], -jnp.inf)
    s_ctx = jnp.einsum('bhrqd,bhnd->bhrqn', qg, k_ctx).astype(F32)
    n_loc = kh * GRID_W
    s = jnp.concatenate([s_loc.reshape(B, H, R, GRID_W, n_loc), s_ctx], axis=-1)
    p = jax.nn.softmax(s, axis=-1).astype(v.dtype)
    p_loc = p[..., :n_loc].reshape(B, H, R, GRID_W, kh, GRID_W)
    p_ctx = p[..., n_loc:]
    o = jnp.einsum('bhrqjw,bhrjwd->bhrqd', p_loc, vg) + jnp.einsum('bhrqn,bhnd->bhrqd', p_ctx, v_ctx)
    return o.reshape(B, H, L, d)


def mixer_c_latent(h, k_ctx, v_ctx, w_in, rpb, w_out):
    B, L, _ = h.shape
    q, k, v = jnp.split(h @ w_in, 3, axis=-1)
    o = na_attention(na_heads(q), na_heads(k), na_heads(v), k_ctx, v_ctx, rpb)
    return o.transpose(0, 2, 1, 3).reshape(B, L, D_MODEL) @ w_out


def conv_ffn(h, w_up, w_conv, w_down):
    a, g = jnp.split(h @ w_up, 2, axis=-1)
    return (jax.nn.gelu(dwconv3(a, w_conv)) * g) @ w_down


def setup_inputs(seed: int = 0) -> dict:
    key = jax.random.key(seed)
    ks = jax.random.split(key, 40)

    def nrm(i, shape, scale):
        return jax.random.normal(ks[i], shape, F32) * scale

    D = D_MODEL
    H = MLSTM_HEADS
    gate_noise = nrm(10, (N_EVEN, 4, H), 0.1)
    f_bias = jnp.linspace(3.0, 6.0, H, dtype=F32)[None, :]
    b_gates = jnp.concatenate([gate_noise[:, 0], gate_noise[:, 1] + f_bias,
                               gate_noise[:, 2], gate_noise[:, 3] + f_bias], axis=-1)
    return {
        'x_prompt': nrm(0, (BATCH, SEQ, D), 1.0),
        'x_sample': nrm(1, (DEC_BATCH, DEC_SEQ, D), 1.0),
        'state_mlstm_C': nrm(2, (DEC_BATCH, N_EVEN, 2, H, MLSTM_HD, MLSTM_HD), 0.1),
        'state_mlstm_n': nrm(3, (DEC_BATCH, N_EVEN, 2, H, MLSTM_HD), 0.1),
        'state_mlstm_m': nrm(4, (DEC_BATCH, N_EVEN, 2, H), 0.5),
        'cache_na_k': nrm(5, (DEC_BATCH, N_ODD, NA_HEADS, PAST_LEN, NA_HD), 1.0),
        'cache_na_v': nrm(6, (DEC_BATCH, N_ODD, NA_HEADS, PAST_LEN, NA_HD), 1.0),
        'c': nrm(7, (DEC_BATCH, D), 1.0),
        'c_ctx': nrm(8, (D,), 1.0),
        'w_ada': nrm(9, (DEPTH, D, 6 * D), D ** -0.5),
        'b_ada': nrm(11, (DEPTH, 6 * D), 0.02),
        'g_mix': 1.0 + nrm(12, (DEPTH, D), 0.05),
        'g_ffn': 1.0 + nrm(13, (DEPTH, D), 0.05),
        'g_final': 1.0 + nrm(14, (D,), 0.05),
        'w_in_ab': nrm(15, (N_EVEN, D, IN_AB), D ** -0.5),
        'b_gates': b_gates,
        'w_conv_qk': nrm(16, (N_EVEN, CONV_W, 2 * MLSTM_WIDTH), 0.5),
        'g_mlstm': 1.0 + nrm(17, (N_EVEN, MLSTM_WIDTH), 0.05),
        'w_conv_hy': nrm(18, (N_EVEN, CONV_W, 3 * HYENA_WIDTH), 0.5),
        'w_filt1': nrm(19, (N_EVEN, FILTER_EMB, FILTER_HIDDEN), FILTER_EMB ** -0.5),
        'b_filt1': nrm(20, (N_EVEN, FILTER_HIDDEN), 0.1),
        'w_filt2': nrm(21, (N_EVEN, FILTER_HIDDEN, FILTER_HIDDEN), FILTER_HIDDEN ** -0.5),
        'b_filt2': nrm(22, (N_EVEN, FILTER_HIDDEN), 0.1),
        'w_filt3': nrm(23, (N_EVEN, FILTER_HIDDEN, 2 * HYENA_WIDTH), 0.01),
        'filt_freq': 1.0 + nrm(24, (N_EVEN, FILTER_HIDDEN), 0.1),
        'hyena_bias': nrm(25, (N_EVEN, HYENA_WIDTH), 0.5),
        'w_out_ab': nrm(26, (N_EVEN, D, D), D ** -0.5),
        'w_in_c': nrm(27, (N_ODD, D, 3 * D), D ** -0.5),
        'rpb_c': nrm(28, (N_ODD, NA_HEADS, 2 * NA_KH - 1, 2 * NA_KW - 1), 0.1),
        'w_out_c': nrm(29, (N_ODD, D, D), D ** -0.5),
        'w_up': nrm(30, (DEPTH, D, 2 * FF), D ** -0.5),
        'w_conv_ffn': nrm(31, (DEPTH, CONV_W, FF), 0.5),
        'w_down': nrm(32, (DEPTH, FF, D), FF ** -0.5),
    }


def reference(x_prompt, x_sample, state_mlstm_C, state_mlstm_n, state_mlstm_m, cache_na_k, cache_na_v, c, c_ctx,
              w_ada, b_ada, g_mix, g_ffn, g_final, w_in_ab, b_gates, w_conv_qk, g_mlstm, w_conv_hy,
              w_filt1, b_filt1, w_filt2, b_filt2, w_filt3, filt_freq, hyena_bias, w_out_ab,
              w_in_c, rpb_c, w_out_c, w_up, w_conv_ffn, w_down):
    xp, xs = x_prompt, x_sample
    Bp = xp.shape[0]
    zero_state = (jnp.zeros((Bp, 2, MLSTM_HEADS, MLSTM_HD, MLSTM_HD), xp.dtype),
                  jnp.zeros((Bp, 2, MLSTM_HEADS, MLSTM_HD), xp.dtype),
                  jnp.zeros((Bp, 2, MLSTM_HEADS), xp.dtype))
    new_C, new_n, new_m, new_k, new_v = [], [], [], [], []
    for l in range(DEPTH):
        sh_p1, sc_p1, gt_p1, sh_p2, sc_p2, gt_p2 = adaln(c_ctx[None, :], w_ada[l], b_ada[l])
        sh_s1, sc_s1, gt_s1, sh_s2, sc_s2, gt_s2 = adaln(c, w_ada[l], b_ada[l])
        hp = rmsnorm(xp, g_mix[l]) * (1.0 + sc_p1) + sh_p1
        hs = rmsnorm(xs, g_mix[l]) * (1.0 + sc_s1) + sh_s1
        e = l // 2
        if l % 2 == 0:
            ab = (w_in_ab[e], b_gates[e], w_conv_qk[e], g_mlstm[e], w_conv_hy[e], w_filt1[e], b_filt1[e],
                  w_filt2[e], b_filt2[e], w_filt3[e], filt_freq[e], hyena_bias[e], w_out_ab[e])
            op, (Cp, np_, mp) = mixer_ab(hp, zero_state, *ab)
            os_, _ = mixer_ab(hs, (state_mlstm_C[:, e], state_mlstm_n[:, e], state_mlstm_m[:, e]), *ab)
            new_C.append(Cp)
            new_n.append(np_)
            new_m.append(mp)
        else:
            op, kp, vp = mixer_c_context(hp, w_in_c[e], w_out_c[e])
            os_ = mixer_c_latent(hs, cache_na_k[:, e], cache_na_v[:, e], w_in_c[e], rpb_c[e], w_out_c[e])
            new_k.append(kp)
            new_v.append(vp)
        xp = xp + gt_p1 * op
        xs = xs + gt_s1 * os_
        xp = xp + gt_p2 * conv_ffn(rmsnorm(xp, g_ffn[l]) * (1.0 + sc_p2) + sh_p2, w_up[l], w_conv_ffn[l], w_down[l])
        xs = xs + gt_s2 * conv_ffn(rmsnorm(xs, g_ffn[l]) * (1.0 + sc_s2) + sh_s2, w_up[l], w_conv_ffn[l], w_down[l])
    y_prompt = rmsnorm(xp, g_final)
    y_sample = rmsnorm(xs, g_final)
    new_state_mlstm_C = jnp.stack(new_C, axis=1)
    new_state_mlstm_n = jnp.stack(new_n, axis=1)
    new_state_mlstm_m = jnp.stack(new_m, axis=1)
    new_cache_na_k = jnp.stack(new_k, axis=1)
    new_cache_na_v = jnp.stack(new_v, axis=1)
    return (y_prompt, y_sample, new_state_mlstm_C, new_state_mlstm_n, new_state_mlstm_m, new_cache_na_k, new_cache_na_v)
```

```python
import math
import os
import numpy as np
from contextlib import ExitStack
import concourse.bass as bass
import concourse.mybir as mybir
from concourse.bass_utils import run_bass_kernel_spmd

F32 = mybir.dt.float32
BF16 = mybir.dt.bfloat16
AF = mybir.ActivationFunctionType
ALU = mybir.AluOpType
AX = mybir.AxisListType

SAME_ENGINE_SYNC = True
N_DMA_SEMS = 20
_READ_KW = ("in_", "in0", "in1", "lhsT", "rhs", "scalar1", "scalar2", "scalar", "bias", "scale",
            "data0", "data1", "initial", "identity")
_WRITE_KW = ("out", "accum_out", "ap")


class Op:
    __slots__ = ("eng", "meth", "kw", "reads", "writes", "deps", "needs_inc", "sem", "val", "is_dma", "done")

    def __init__(self, eng, meth, kw, reads, writes, is_dma):
        self.eng, self.meth, self.kw = eng, meth, kw
        self.reads, self.writes = reads, writes
        self.deps = []
        self.needs_inc = False
        self.sem = None
        self.val = None
        self.is_dma = is_dma
        self.done = False


class Prog:
    def __init__(self, nc):
        self.nc = nc
        self.es = ExitStack()
        self.phase_es = None
        self.ops = []
        self.state = {}
        self.engs = {"pe": nc.tensor, "act": nc.scalar, "dve": nc.vector, "pool": nc.gpsimd, "sp": nc.sync}
        self.esem = {}
        for e in ("pe", "act", "dve", "pool"):
            self.esem[e] = self.es.enter_context(nc.semaphore("es_" + e))
        self.dsems = {}
        for q in ("sp", "pool"):
            self.dsems[q] = [self.es.enter_context(nc.semaphore("ds_%s_%d" % (q, i))) for i in range(N_DMA_SEMS)]
        self.dcount = {"sp": 0, "pool": 0}
        self.ecount = {e: 0 for e in self.esem}
        self.seen = {e: {} for e in self.engs}
        self.emitted = 0
        self.uid = 0
        self.phase_counts = []

    def sb(self, name, shape, dtype, persist=False):
        self.uid += 1
        st = self.es if (persist or self.phase_es is None) else self.phase_es
        return st.enter_context(self.nc.sbuf_tensor("%s_%d" % (name, self.uid), list(shape), dtype))

    def ps(self, name, shape=(128, 512), dtype=F32):
        self.uid += 1
        return self.phase_es.enter_context(self.nc.psum_tensor("%s_%d" % (name, self.uid), list(shape), dtype))

    def dram(self, name, shape, dtype, kind="Internal"):
        return self.nc.dram_tensor(name, list(shape), dtype, kind=kind).ap()

    @staticmethod
    def _key(x):
        if isinstance(x, tuple):
            return (x[0].name if not isinstance(x[0], str) else x[0]), x[1]
        if isinstance(x, str):
            return x, None
        return x.name, None

    def _access(self, op, key, write):
        name, tag = key
        st = self.state.setdefault(name, {})
        tags = list(st.keys()) if tag is None else [t for t in (tag, None) if t in st]
        for t in tags:
            lw, rd = st[t]
            if lw is not None:
                op.deps.append(lw)
            if write:
                op.deps.extend(rd)
        if write:
            if tag is None:
                st.clear()
            st[tag] = [op, []]
        else:
            if tag not in st:
                st[tag] = [None, []]
            st[tag][1].append(op)

    def add(self, eng, meth, kw, r=None, w=None, rt=None, wt=None, is_dma=False):
        reads = list(r) if r is not None else []
        writes = list(w) if w is not None else []
        if r is None:
            for k in _READ_KW:
                v = kw.get(k, None)
                if v is not None and hasattr(v, "name") and hasattr(v, "ap"):
                    reads.append((v, rt) if rt is not None else v)
        if w is None:
            for k in _WRITE_KW:
                v = kw.get(k, None)
                if v is not None and hasattr(v, "name"):
                    writes.append((v, wt) if wt is not None else v)
        op = Op(eng, meth, kw, [self._key(x) for x in reads], [self._key(x) for x in writes], is_dma)
        for k in op.reads:
            self._access(op, k, False)
        for k in op.writes:
            self._access(op, k, True)
        seen = set()
        dd = []
        for d in op.deps:
            if d is op or id(d) in seen or d.done:
                continue
            seen.add(id(d))
            dd.append(d)
        op.deps = dd
        for d in dd:
            if d.is_dma:
                continue
            if d.eng == op.eng and not op.is_dma and (d.eng == "pe" or not SAME_ENGINE_SYNC):
                continue
            d.needs_inc = True
        self.ops.append(op)
        return op

    def dma(self, out, in_, q="sp", slow=False, **k):
        kw = dict(out=out, in_=in_)
        if slow:
            kw["allow_slow_non_contiguous"] = True
        return self.add(q, "dma_start", kw, is_dma=True, **k)

    def mm(self, out, lhsT, rhs, start=True, stop=True, lt=None, rtg=None, **k):
        if lt is not None or rtg is not None:
            k["r"] = [(lhsT, lt) if lt is not None else lhsT, (rhs, rtg) if rtg is not None else rhs]
        return self.add("pe", "matmul", dict(out=out, lhsT=lhsT, rhs=rhs, start=start, stop=stop), **k)

    def tr(self, out, in_, identity, **k):
        return self.add("pe", "transpose", dict(out=out, in_=in_, identity=identity), **k)

    def act(self, out, in_, func, rt=None, wt=None, **kw):
        kw.update(out=out, in_=in_, func=func)
        return self.add("act", "activation", kw, rt=rt, wt=wt)

    def v(self, meth, eng="dve", rt=None, wt=None, r=None, w=None, **kw):
        return self.add(eng, meth, kw, rt=rt, wt=wt, r=r, w=w)

    def _wait(self, eng, sem, val):
        s = self.seen[eng]
        k = id(sem)
        if s.get(k, 0) >= val:
            return
        s[k] = val
        self.engs[eng].wait_ge(sem, val)

    def flush(self):
        for op in self.ops[self.emitted:]:
            e = op.eng
            for d in op.deps:
                if d.is_dma:
                    self._wait(e, d.sem, d.val)
                else:
                    if d.eng == e and not op.is_dma and (e == "pe" or not SAME_ENGINE_SYNC):
                        continue
                    self._wait(e, d.sem, d.val)
            if op.is_dma:
                q = e
                i = self.dcount[q]
                self.dcount[q] += 1
                sem = self.dsems[q][i % N_DMA_SEMS]
                val = 16 * (i // N_DMA_SEMS + 1)
                if val > 16:
                    self._wait(q, sem, val - 16)
                op.sem, op.val = sem, val
                self.engs[q].dma_start(**op.kw).then_inc(sem, 16)
            else:
                ins = getattr(self.engs[e], op.meth)(**op.kw)
                if op.needs_inc:
                    self.ecount[e] += 1
                    op.sem, op.val = self.esem[e], self.ecount[e]
                    ins.then_inc(op.sem, 1)
        self.emitted = len(self.ops)

    def barrier(self):
        for e in ("pe", "act", "dve", "pool"):
            for op in reversed(self.ops[self.emitted:]):
                if op.eng == e and not op.is_dma:
                    op.needs_inc = True
                    break
        self.flush()
        cnt = {}
        for op in self.ops:
            k = op.eng + ("_dma" if op.is_dma else "")
            cnt[k] = cnt.get(k, 0) + 1
        self.phase_counts.append(cnt)
        for eng in ("sp", "pe", "act", "dve", "pool"):
            for q in ("sp", "pool"):
                n = self.dcount[q]
                for j in range(min(n, N_DMA_SEMS)):
                    cnt = (n - 1 - j) // N_DMA_SEMS + 1
                    self._wait(eng, self.dsems[q][j], 16 * cnt)
            for e in ("pe", "act", "dve", "pool"):
                if e != eng and self.ecount[e] > 0:
                    self._wait(eng, self.esem[e], self.ecount[e])
        for op in self.ops:
            op.done = True
        self.ops = []
        self.emitted = 0
        self.state = {}

    def begin_phase(self):
        self.phase_es = ExitStack()

    def end_phase(self):
        self.barrier()
        self.phase_es.close()
        self.phase_es = None

    def close(self):
        self.es.close()


D = 1024
T = 2560
LS, LP = 2048, 256
SEQS = [(0, 2048), (2048, 2304), (2304, 2560)]
NTB = 5
NTC = 20
FF = 2816
IN_AB = 3600
EPS = 1e-6
MAGIC = 12582912.0
TWO_PI = 2.0 * math.pi
NEG = -30000.0


def hy_cfg(L):
    N = 2 * L
    na = L // 128
    nb = (L + 1 + 127) // 128
    return dict(L=L, N=N, na=na, nb=nb)


def build(debug=None):
    nc = bass.Bass("TRN2", target_bir_lowering=False)
    P = Prog(nc)

    def din(name, shape, dt=F32):
        return nc.dram_tensor(name, list(shape), dt, kind="ExternalInput").ap()

    def dout(name, shape, dt=F32):
        return nc.dram_tensor(name, list(shape), dt, kind="ExternalOutput").ap()

    def dbgdump(name, ap, dt=F32):
        if debug is None:
            return
        o = dout("dbgm_" + name, list(ap.shape), dt)
        P.dma(o, ap)

    xT = din("xT", [8, 128, T])
    cv = din("cv", [128, 8, 2])
    st_C = din("st_C", [8, 128, 128])
    st_n = din("st_n", [128, 8])
    st_m = din("st_m", [128, 8])
    kcT = din("kcT", [8, 128, 512])
    vc = din("vc", [16, 512, 64])
    w_ada = din("w_ada", [2, D, 6 * D])
    b_adaP = din("b_adaP", [128, 2, 48])
    gP = din("gP", [128, 5, 8])
    w_in_ab = din("w_in_ab", [D, IN_AB])
    b_gates = din("b_gates", [1, 16])
    wcqk = din("wcqk", [128, 8, 3])
    g_ml = din("g_ml", [128, 4])
    wchy = din("wchy", [128, 12, 3])
    w_f1 = din("w_f1", [33, 64])
    w_f2 = din("w_f2", [64, 64])
    w_f3 = din("w_f3", [64, 1024])
    fvec = din("fvec", [64, 3])
    hyb = din("hyb", [128, 4])
    w_out_ab = din("w_out_ab", [D, D])
    w_in_c = din("w_in_c", [D, 3 * D])
    TmE = din("TmE", [16, 128, 15, 64])
    w_out_c = din("w_out_c", [D, D])
    w_up = din("w_up", [2, D, 2 * FF])
    wcffn = din("wcffn", [128, 2, 22, 3])
    w_down = din("w_down", [2, FF, D])
    cst = din("cst", [128, 6, 128])
    deltas = din("deltas", [1, 1024])
    hc = {}
    for L in (LS, LP):
        c = hy_cfg(L)
        c["TF"] = din("TF%d" % L, [c["nb"], 128, c["na"], 2, 128])
        c["TI"] = din("TI%d" % L, [128, c["nb"], 2, L])
        c["zT"] = din("zT%d" % L, [33, L])
        c["tneg"] = din("tneg%d" % L, [128, c["na"]])
        c["wt"] = din("wt%d" % L, [128, c["nb"], 2])
        hc[L] = c

    yT = dout("yT", [8, 128, T])
    o_C = dout("o_C", [2, 2, 4, 128, 128])
    o_n = dout("o_n", [2, 2, 4, 128])
    o_m = dout("o_m", [2, 2, 4])
    o_k = dout("o_k", [2, 16, 256, 64])
    o_v = dout("o_v", [2, 16, 256, 64])

    X = P.dram("X", [8, 128, T], F32)
    HN = P.dram("HN", [8, 128, T], BF16)
    QK = P.dram("QK", [16, 128, T], BF16)
    V = P.dram("V", [NTC, 128, 1024], BF16)
    V64 = P.dram("V64", [16, 128, 1024], BF16)
    SO = P.dram("SO", [4, 128, T], BF16)
    G = P.dram("G", [128, NTC, 16], F32)
    HY = P.dram("HY", [12, 128, T], F32)
    U = P.dram("U", [4, 128, T], F32)
    HF = P.dram("HF", [4, 128, T], F32)
    HB = P.dram("HB", [4, 128, T], F32)
    Y = P.dram("Y", [8, 128, T], BF16)
    ACTS = P.dram("ACTS", [22, 128, T], BF16)
    for L in (LS, LP):
        hc[L]["FL"] = P.dram("FL%d" % L, [hc[L]["na"], 128, 1024], BF16)

    cst_t = P.sb("cst", [128, 6, 128], F32, persist=True)
    ident = cst_t[:, 0, :]
    tri = [cst_t[:, 1, :], cst_t[:, 2, :]]
    maskb = [cst_t[:, 3, :], cst_t[:, 4, :]]
    colmask = cst_t[:, 5, 0:64]
    ident_bf = P.sb("identbf", [128, 128], BF16, persist=True)
    ones_f = P.sb("onesf", [128, 128], F32, persist=True)
    ones_bf = P.sb("onesbf", [128, 128], BF16, persist=True)
    epsc = P.sb("epsc", [128, 1], F32, persist=True)
    modt = P.sb("modt", [128, 2, 48, 2], F32, persist=True)
    Atab = P.sb("Atab", [128, 2, 2, 8, 2], F32, persist=True)
    gPt = P.sb("gPt", [128, 5, 8], F32, persist=True)
    wst = [P.sb("wst%d" % i, [128, 4096], F32, persist=True) for i in range(2)]
    wbf = [P.sb("wbf%d" % i, [128, 4096], BF16, persist=True) for i in range(2)]
    wctr = [0]
    scv_p = P.sb("scv", [128, 8, 2], F32, persist=True)
    bt_p = P.sb("bada", [128, 2, 48], F32, persist=True)

    P.begin_phase()
    P.dma(cst_t[:], cst[:, :, :])
    P.dma(gPt[:], gP[:, :, :])
    P.v("tensor_copy", out=ident_bf[:], in_=ident)
    P.v("memset", ap=ones_f[:], constant=1.0)
    P.v("memset", ap=ones_bf[:], constant=1.0)
    P.v("memset", ap=epsc[:], constant=EPS)

    wplan = {"reqs": None}

    def w_plan(reqs):
        mx_e = max(r[1] * r[3] for r in reqs)
        wplan.update(reqs=reqs, cur=0, dma=0, cast=0, nslots=(2 if mx_e > 1024 else 8),
                     se=(4096 if mx_e > 1024 else 1024), base=wctr[0])
        wctr[0] += 1

    def _w_views(k):
        Wap, nk, c0, ncols = wplan["reqs"][k]
        slot = k % wplan["nslots"]
        off = slot * wplan["se"]
        ti, o = off // 4096, off % 4096
        tag = "w%d_%d" % (wplan["base"], slot)
        sv = wst[ti][:, o:o + nk * ncols].rearrange("p (k n) -> p k n", k=nk)
        bv = wbf[ti][:, o:o + nk * ncols].rearrange("p (k n) -> p k n", k=nk)
        return Wap, nk, c0, ncols, sv, bv, tag

    def _w_dma(k):
        Wap, nk, c0, ncols, sv, bv, tag = _w_views(k)
        P.dma(sv, Wap[0:nk * 128, c0:c0 + ncols].rearrange("(k p) n -> p k n", p=128), q="sp", wt=tag)

    def _w_cast(k):
        Wap, nk, c0, ncols, sv, bv, tag = _w_views(k)
        P.act(bv, sv, AF.Identity, rt=tag, wt=tag)

    def w_tick():
        if wplan["reqs"] is None:
            return
        k = wplan["cur"]
        if k < len(wplan["reqs"]) and wplan["cast"] <= k and wplan["dma"] > k:
            _w_cast(k)
            wplan["cast"] = k + 1

    def w_end():
        wplan["reqs"] = None

    def load_w(Wap, k0, nk, c0, ncols, cast=True):
        if wplan["reqs"] is None or not cast:
            i = wctr[0]
            wctr[0] += 1
            sv = wst[i % 2][:, 0:nk * ncols].rearrange("p (k n) -> p k n", k=nk)
            P.dma(sv, Wap[k0 * 128:(k0 + nk) * 128, c0:c0 + ncols].rearrange("(k p) n -> p k n", p=128), q="sp")
            if not cast:
                return sv
            bv = wbf[i % 2][:, 0:nk * ncols].rearrange("p (k n) -> p k n", k=nk)
            P.act(bv, sv, AF.Identity)
            return bv, None
        k = wplan["cur"]
        rq = wplan["reqs"][k]
        assert rq[1] == nk and rq[2] == c0 and rq[3] == ncols, (rq[1:], nk, c0, ncols)
        while wplan["dma"] <= k:
            _w_dma(wplan["dma"])
            wplan["dma"] += 1
        while wplan["cast"] <= k:
            _w_cast(wplan["cast"])
            wplan["cast"] += 1
        wplan["cur"] = k + 1
        depth = 1 if wplan["nslots"] == 2 else 3
        while wplan["dma"] < min(len(wplan["reqs"]), k + 1 + depth):
            _w_dma(wplan["dma"])
            wplan["dma"] += 1
        return _w_views(k)[5], _w_views(k)[6]

    def linear_fm(Wap, nk, c0, ncols_total, src, psums, epi, tbs=range(NTB), chunk0=0, ensure=None):
        pi = [0]
        done = 0
        pending = []
        while done < ncols_total:
            nc_ = min(512, ncols_total - done)
            bv, wtag = load_w(Wap, 0, nk, c0 + done, nc_)
            nm = nc_ // 128
            for m in range(nm):
                new_pending = []
                for tb in tbs:
                    if ensure is not None:
                        ensure(tb)
                    ps = psums[pi[0] % len(psums)]
                    pi[0] += 1
                    for kc in range(nk):
                        P.mm(ps[:, :], bv[:, kc, m * 128:(m + 1) * 128], src[:, kc, tb * 512:(tb + 1) * 512],
                             start=(kc == 0), stop=(kc == nk - 1), lt=wtag, rtg=tb)
                    t_ = epi(chunk0 + (done // 128) + m, tb, ps)
                    if t_ is not None:
                        new_pending.append(t_)
                    if m == (nm - 1) // 2 and tb == 2:
                        w_tick()
                for t_ in pending:
                    t_()
                pending = new_pending
            done += nc_
        for t_ in pending:
            t_()

    def linear_tm(Wap, nk, c0, ncols, src, psums, epi, tok_chunks, tok_off=0):
        bv, wtag = load_w(Wap, 0, nk, c0, ncols)
        tok_chunks = list(tok_chunks)
        for j, tc in enumerate(tok_chunks):
            ps = psums[j % len(psums)]
            t0 = tok_off + tc * 128
            for kc in range(nk):
                P.mm(ps[:, 0:ncols], src[:, kc, t0:t0 + 128], bv[:, kc, :], start=(kc == 0), stop=(kc == nk - 1),
                     lt=(t0 // 512 if tok_off == 0 else None), rtg=wtag)
            epi(tc, ps)
            if j == len(tok_chunks) // 2:
                w_tick()

    def conv3(acc, raw, taps):
        P.v("tensor_scalar", out=acc[:, :], in0=raw[:, :], scalar1=taps[:, 1:2], scalar2=None, op0=ALU.mult)
        for (s0, s1) in SEQS:
            P.v("scalar_tensor_tensor", out=acc[:, s0 + 1:s1], in0=raw[:, s0:s1 - 1], scalar=taps[:, 0:1],
                in1=acc[:, s0 + 1:s1], op0=ALU.mult, op1=ALU.add)
            P.v("scalar_tensor_tensor", out=acc[:, s0:s1 - 1], in0=raw[:, s0 + 1:s1], scalar=taps[:, 2:3],
                in1=acc[:, s0:s1 - 1], op0=ALU.mult, op1=ALU.add)

    def adaln_layer(l, bufs, pa, pre_hook=None):
        nb_ = len(bufs)

        def dma_(s_):
            sv_ = bufs[s_ % nb_][:, :].rearrange("p (k n) -> p k n", k=8)
            P.dma(sv_, w_ada[l][:, s_ * 512:(s_ + 1) * 512].rearrange("(k p) n -> p k n", p=128), q="sp")
        for s_ in range(min(12, nb_)):
            dma_(s_)
        if pre_hook is not None:
            pre_hook()
        for s_ in range(12):
            sv = bufs[s_ % nb_][:, :].rearrange("p (k n) -> p k n", k=8)
            if s_ >= nb_:
                dma_(s_)
            ps = pa[s_ % len(pa)]
            for m in range(4):
                for kc in range(8):
                    P.mm(ps[:, 2 * m:2 * m + 2], sv[:, kc, m * 128:(m + 1) * 128], scv_p[:, kc, :],
                         start=(kc == 0), stop=(kc == 7))
            for m in range(4):
                n = s_ * 4 + m
                P.v("tensor_scalar", out=modt[:, l, n, :], in0=ps[:, 2 * m:2 * m + 2], scalar1=bt_p[:, l, n:n + 1],
                    scalar2=None, op0=ALU.add)
        for sub in range(2):
            sc0 = 8 + 24 * sub
            P.v("tensor_scalar", out=Atab[:, l, sub, :, :], in0=modt[:, l, sc0:sc0 + 8, :], scalar1=1.0,
                scalar2=None, op0=ALU.add)
            for j in range(2):
                P.v("tensor_tensor", out=Atab[:, l, sub, :, j], in0=Atab[:, l, sub, :, j],
                    in1=gPt[:, 2 * sub + l, :], op=ALU.mult)

    def phase_adaln():
        pa = [P.ps("pa%d" % i) for i in range(2)]
        P.dma(scv_p[:], cv[:, :, :])
        P.dma(bt_p[:], b_adaP[:, :, :])
        P.act(scv_p[:], scv_p[:], AF.Silu)
        extra = [P.sb("adx%d" % i, [128, 4096], F32) for i in range(4)]
        adaln_layer(0, wst + extra, pa)

    def phase_norm(l, sub, src, final=False):
        xb = [P.sb("xb%d" % i, [128, 8, 512], F32) for i in range(2)]
        sq = [P.sb("sq%d" % i, [128, 8, 512], BF16) for i in range(2)]
        rs = [P.sb("rs%d" % i, [128, 512], F32) for i in range(2)]
        tmp = [P.sb("tmp%d" % i, [128, 8, 512], F32) for i in range(2)]
        hb = [P.sb("hb%d" % i, [128, 8, 512], (F32 if final else BF16)) for i in range(2)]
        pn = [P.ps("pn%d" % i) for i in range(2)]
        for tb in range(NTB):
            j = 0 if tb < 4 else 1
            i = tb % 2
            tsl = slice(tb * 512, (tb + 1) * 512)
            P.dma(xb[i][:], src[:, :, tsl].rearrange("c p t -> p c t"))
            P.act(sq[i][:], xb[i][:], AF.Square)
            for kc in range(8):
                P.mm(pn[i][:, :], ones_bf[:], sq[i][:, kc, :], start=(kc == 0), stop=(kc == 7))
            P.act(rs[i][:], pn[i][:, :], AF.Sqrt, scale=1.0 / D, bias=epsc[:, 0:1])
            P.v("reciprocal", out=rs[i][:], in_=rs[i][:])
            for kc in range(8):
                if final:
                    P.v("scalar_tensor_tensor", out=hb[i][:, kc, :], in0=xb[i][:, kc, :], scalar=gPt[:, 4, kc:kc + 1],
                        in1=rs[i][:], op0=ALU.mult, op1=ALU.mult, rt=kc, wt=kc)
                else:
                    P.v("scalar_tensor_tensor", out=tmp[i][:, kc, :], in0=xb[i][:, kc, :],
                        scalar=Atab[:, l, sub, kc, j:j + 1], in1=rs[i][:], op0=ALU.mult, op1=ALU.mult, rt=kc, wt=kc)
                    P.act(hb[i][:, kc, :], tmp[i][:, kc, :], AF.Identity, bias=modt[:, l, 24 * sub + kc, j:j + 1],
                          rt=kc, wt=kc)
            dst = yT if final else HN
            P.dma(dst[:, :, tsl].rearrange("c p t -> p c t"), hb[i][:], q="pool")
            if tb == 0 and l == 0 and sub == 0 and not final:
                dbgdump("rs", rs[i][:])
                dbgdump("xb", xb[i][:, 0, :])
                dbgdump("tmp", tmp[i][:, 0, :])
                dbgdump("atab", Atab[:].rearrange("p a b c d -> p (a b c d)"))
                dbgdump("modt", modt[:].rearrange("p a b c -> p (a b c)"))

    def norm_hn(l, sub, src):
        hn = P.sb("hn", [128, 8, T], BF16)
        xb = [P.sb("nxb%d" % i, [128, 8, 512], F32) for i in range(2)]
        rs = [P.sb("nrs%d" % i, [128, 512], F32) for i in range(2)]
        pn = [P.ps("npn%d" % i) for i in range(2)]
        st_ = {"dma": 0, "done": 0}

        def dma(tb):
            tsl = slice(tb * 512, (tb + 1) * 512)
            P.dma(xb[tb % 2][:], src[:, :, tsl].rearrange("c p t -> p c t"), q="pool")

        def ensure(tb):
            while st_["done"] <= tb:
                t = st_["done"]
                while st_["dma"] < min(NTB, t + 2):
                    dma(st_["dma"])
                    st_["dma"] += 1
                i = t % 2
                j = 0 if t < 4 else 1
                tsl = slice(t * 512, (t + 1) * 512)
                P.act(hn[:, :, tsl], xb[i][:], AF.Square, wt=t)
                for kc in range(8):
                    P.mm(pn[i][:, :], ones_bf[:], hn[:, kc, tsl], start=(kc == 0), stop=(kc == 7), rtg=t)
                P.act(rs[i][:], pn[i][:, :], AF.Sqrt, scale=1.0 / D, bias=epsc[:, 0:1])
                P.v("reciprocal", out=rs[i][:], in_=rs[i][:])
                for kc in range(8):
                    P.v("scalar_tensor_tensor", out=xb[i][:, kc, :], in0=xb[i][:, kc, :],
                        scalar=Atab[:, l, sub, kc, j:j + 1], in1=rs[i][:], op0=ALU.mult, op1=ALU.mult, rt=kc, wt=kc)
                for kc in range(8):
                    P.add("act", "activation", dict(out=hn[:, kc, tsl], in_=xb[i][:, kc, :], func=AF.Identity,
                                                    bias=modt[:, l, 24 * sub + kc, j:j + 1]),
                          r=[(xb[i], kc), modt], w=[(hn, t)])
                st_["done"] += 1
        return hn, ensure

    def load_hn():
        hn = P.sb("hn", [128, 8, T], BF16)
        for tb in range(NTB):
            tsl = slice(tb * 512, (tb + 1) * 512)
            P.dma(hn[:, :, tsl], HN[:, :, tsl].rearrange("c p t -> p c t"), q="pool", wt=tb)
        return hn

    def phase_proj_ab():
        w_plan([(w_in_ab, 8, c_, n_) for (c_, n_) in [(0, 512), (512, 512), (1024, 512), (1536, 512), (2048, 16),
                                                      (2064, 512), (2576, 512), (3088, 512)]])
        hn, ens = norm_hn(0, 0, xT)
        raw = [P.sb("raw%d" % i, [128, T], F32) for i in range(2)]
        acc = [P.sb("acc%d" % i, [128, T], F32) for i in range(2)]
        obf = [P.sb("obf%d" % i, [128, T], BF16) for i in range(2)]
        tq = P.sb("tq", [128, 8, 3], F32)
        th = P.sb("th", [128, 12, 3], F32)
        bg = P.sb("bg", [128, 16], F32)
        gt = P.sb("gt", [128, NTC, 16], F32)
        gtmp = P.sb("gtmp", [128, NTC, 8], F32)
        vb = [P.sb("vb%d" % i, [128, 512], BF16) for i in range(2)]
        sob = [P.sb("sob%d" % i, [128, 512], BF16) for i in range(2)]
        pp = [P.ps("pp%d" % i) for i in range(4)]
        P.dma(tq[:], wcqk[:, :, :])
        P.dma(th[:], wchy[:, :, :])
        P.dma(bg[:], b_gates[0:1, :].broadcast_to([128, 16]))

        def epi_qk(n, tb, ps):
            i = n % 2
            P.act(raw[i][:, tb * 512:(tb + 1) * 512], ps[:, :], AF.Identity)
            if tb == NTB - 1:
                def tail():
                    conv3(acc[i], raw[i], tq[:, n, :])
                    P.act(acc[i][:], acc[i][:], AF.Silu)
                    P.v("tensor_scalar", out=obf[i][:], in0=acc[i][:], scalar1=(1.0 if n < 4 else 128.0 ** -0.5),
                        scalar2=None, op0=ALU.mult)
                    P.dma(QK[n], obf[i][:], q="pool")
                return tail
        linear_fm(w_in_ab, 8, 0, 1024, hn, pp, epi_qk, ensure=ens)

        def epi_v(tc, ps):
            i = tc % 2
            P.act(vb[i][:], ps[:, :], AF.Identity)
            P.dma(V[tc, :, 0:512], vb[i][:], q="pool")
        linear_tm(w_in_ab, 8, 1024, 512, hn, pp, epi_v, range(NTC))

        def epi_o(n, tb, ps):
            i = (n * NTB + tb) % 2
            P.act(sob[i][:], ps[:, :], AF.Sigmoid)
            P.dma(SO[n, :, tb * 512:(tb + 1) * 512], sob[i][:], q="pool")
        linear_fm(w_in_ab, 8, 1536, 512, hn, pp, epi_o)

        def epi_g(tc, ps):
            P.v("tensor_tensor", out=gt[:, tc, :], in0=ps[:, 0:16], in1=bg[:], op=ALU.add)
        linear_tm(w_in_ab, 8, 2048, 16, hn, pp, epi_g, range(NTC))
        for d in range(2):
            fs = gt[:, :, 4 + 8 * d:8 + 8 * d]
            ts = gtmp[:, :, 4 * d:4 * d + 4]
            P.act(ts, fs, AF.Exp, scale=-1.0)
            P.act(ts, ts, AF.Ln, bias=ones_f[:, 0:1])
            P.v("tensor_scalar", out=fs, in0=ts, scalar1=-1.0, scalar2=None, op0=ALU.mult)
        P.dma(G[:, :, :], gt[:], q="pool")

        def epi_hy(n, tb, ps):
            i = n % 2
            P.act(raw[i][:, tb * 512:(tb + 1) * 512], ps[:, :], AF.Identity)
            if tb == NTB - 1:
                def tail():
                    conv3(acc[i], raw[i], th[:, n, :])
                    P.dma(HY[n], acc[i][:], q="pool")
                return tail
        linear_fm(w_in_ab, 8, 2064, 1536, hn, pp, epi_hy)
        w_end()

    def phase_mlstm():
        gt = P.sb("gt", [128, NTC, 16], F32)
        P.dma(gt[:], G[:, :, :])
        stn = P.sb("stn", [128, 8], F32)
        stm = P.sb("stm", [128, 8], F32)
        P.dma(stn[:], st_n[:, :])
        P.dma(stm[:], st_m[:, :])
        P.act(stm[:], stm[:], AF.Exp)
        Cst = P.sb("Cst", [128, 4, 128], F32)
        Cbf = P.sb("Cbf", [128, 4, 128], BF16)
        nst = P.sb("nst", [128, 4], F32)
        nrep = P.sb("nrep", [128, 4, 128], BF16)
        mrun = P.sb("mrun", [128, 4], F32)
        ktm_all = P.sb("ktmall", [128, NTC, 512], BF16)
        H4 = range(4)
        NB = 2

        def mk(name, shape, dt):
            return [[P.sb("%s%d_%d" % (name, b, h), shape, dt) for h in H4] for b in range(NB)]
        qTt = [P.sb("qTt%d" % i, [128, 4, 128], BF16) for i in range(3)]
        kTt = [P.sb("kTt%d" % i, [128, 4, 128], BF16) for i in range(3)]
        vch = [P.sb("vch%d" % i, [128, 512], BF16) for i in range(3)]
        acol = [P.sb("acol%d" % i, [128, 4], F32) for i in range(NB)]
        hfo = [P.sb("hfo%d" % i, [128, 4, 128], F32) for i in range(NB)]
        lfrep, brow, arg, eb = mk("lfrep", [128, 128], F32), mk("brow", [128, 128], F32), mk("arg", [128, 128], F32), mk("eb", [128, 128], F32)
        PT, qt, kw = mk("PT", [128, 128], BF16), mk("qt", [128, 128], BF16), mk("kw", [128, 128], BF16)
        wcol, tmx = mk("wcol", [128, 2], F32), mk("tmx", [128, 1], F32)
        irep, te = mk("irep", [128, 128], F32), mk("te", [128, 128], F32)
        dd = [P.sb("dd%d" % h, [128, 128], F32) for h in H4]
        c0t = [P.sb("c0t%d" % i, [128, 128], F32) for i in H4]
        cout = P.sb("cout", [128, 4, 130], F32)
        pA = [P.ps("pmA%d" % i) for i in H4]
        pND = [P.ps("pmN%d" % i) for i in range(2)]
        pCU = [P.ps("pmCU%d" % i) for i in range(NB)]

        for tc in range(NTC):
            i = tc % 3
            P.dma(kTt[i][:], QK[4:8, :, tc * 128:(tc + 1) * 128].rearrange("h p t -> p h t"))
            for h in H4:
                P.tr(pND[tc % 2][:, :].bitcast(BF16)[:, h * 128:(h + 1) * 128], kTt[i][:, h, :], ident_bf[:])
            P.act(ktm_all[:, tc, :], pND[tc % 2][:, :].bitcast(BF16)[:, 0:512], AF.Identity)

        steps = []
        for si, (s0, s1) in enumerate(SEQS):
            nch = (s1 - s0) // 128
            for d in range(2):
                order = list(range(nch)) if d == 0 else list(range(nch - 1, -1, -1))
                for j, c in enumerate(order):
                    steps.append(dict(si=si, d=d, c=c, first=(j == 0), last=(j == nch - 1), t0=s0 + c * 128))
        ld = [0]

        def front(k):
            st = steps[k]
            b = k % NB
            d, t0 = st["d"], st["t0"]
            prompt = st["si"] > 0
            tc = t0 // 128
            li = ld[0] % 3
            ld[0] += 1
            st["li"] = li
            bend_c = 127 if d == 0 else 0
            P.dma(qTt[li][:], QK[0:4, :, t0:t0 + 128].rearrange("h p t -> p h t"))
            P.dma(kTt[li][:], QK[4:8, :, t0:t0 + 128].rearrange("h p t -> p h t"))
            P.dma(vch[li][:], V[tc, :, 0:512])
            g = gt[:, tc, :]
            P.mm(pA[0][:, 392 + 4 * b:396 + 4 * b], tri[d], g[:, 4 + 8 * d:8 + 8 * d])
            P.v("tensor_tensor", out=acol[b][:], in0=g[:, 8 * d:8 * d + 4], in1=pA[0][:, 392 + 4 * b:396 + 4 * b], op=ALU.subtract)
            for h in H4:
                P.act(lfrep[b][h][:], ones_f[:], AF.Identity, scale=g[:, 4 + 8 * d + h:5 + 8 * d + h])
                if prompt:
                    P.act(irep[b][h][:], ones_f[:], AF.Identity, scale=g[:, 8 * d + h:8 * d + h + 1])
            for h in H4:
                P.mm(pA[h][:, 0:128], lfrep[b][h][:], tri[d])
                P.mm(pA[h][:, 256:384], kTt[li][:, h, :], qTt[li][:, h, :])
                if prompt:
                    P.mm(pA[h][:, 128:256], irep[b][h][:], ident)
            for h in H4:
                P.act(brow[b][h][:], pA[h][:, 0:128], AF.Identity)
            for h in H4:
                P.v("scalar_tensor_tensor", out=arg[b][h][:], in0=brow[b][h][:], scalar=acol[b][:, h:h + 1],
                    in1=maskb[d], op0=ALU.add, op1=ALU.add)
                if prompt:
                    bend = brow[b][h][:, bend_c:bend_c + 1]
                    P.v("scalar_tensor_tensor", out=te[b][h][:], in0=pA[h][:, 128:256], scalar=bend, in1=brow[b][h][:],
                        op0=ALU.add, op1=ALU.subtract)
                    P.v("tensor_reduce", out=tmx[b][h][:], in_=te[b][h][:], axis=AX.X, op=ALU.max)
            for h in H4:
                bend = brow[b][h][:, bend_c:bend_c + 1]
                P.act(arg[b][h][:], arg[b][h][:], AF.Exp)
                P.act(eb[b][h][:], brow[b][h][:], AF.Exp)
                P.act(wcol[b][h][:, 0:1], acol[b][:, h:h + 1], AF.Exp, bias=bend)
                P.act(wcol[b][h][:, 1:2], bend, AF.Exp)
            for h in H4:
                P.v("tensor_tensor", out=PT[b][h][:], in0=pA[h][:, 256:384], in1=arg[b][h][:], op=ALU.mult)
                P.v("tensor_tensor", out=qt[b][h][:], in0=qTt[li][:, h, :], in1=eb[b][h][:], op=ALU.mult)
                P.v("tensor_scalar", out=kw[b][h][:], in0=ktm_all[:, tc, h * 128:(h + 1) * 128], scalar1=wcol[b][h][:, 0:1],
                    scalar2=None, op0=ALU.mult)
            for h in H4:
                P.mm(pCU[b][:, h * 128:(h + 1) * 128], kw[b][h][:], vch[li][:, h * 128:(h + 1) * 128])
                P.mm(pA[h][:, 384 + b:385 + b], kw[b][h][:], ones_bf[:, 0:1])

        def back(k):
            st = steps[k]
            b = k % NB
            d, t0, li = st["d"], st["t0"], st["li"]
            prompt = st["si"] > 0
            bend_c = 127 if d == 0 else 0
            HOUT = HF if d == 0 else HB
            if st["first"]:
                for h in H4:
                    if prompt:
                        P.v("memset", ap=Cst[:, h, :], constant=0.0)
                    else:
                        P.dma(c0t[h][:], st_C[d * 4 + h], q="pool")
                        P.v("tensor_scalar", out=Cst[:, h, :], in0=c0t[h][:], scalar1=stm[:, d * 4 + h:d * 4 + h + 1],
                            scalar2=None, op0=ALU.mult)
                if prompt:
                    P.v("memset", ap=nst[:], constant=0.0)
                    P.v("memset", ap=mrun[:], constant=0.0)
                else:
                    P.v("tensor_tensor", out=nst[:], in0=stn[:, d * 4:d * 4 + 4], in1=stm[:, d * 4:d * 4 + 4], op=ALU.mult)
                for h in H4:
                    P.act(Cbf[:, h, :], Cst[:, h, :], AF.Identity, wt=h)
                    P.act(nrep[:, h, :], ones_f[:], AF.Identity, scale=nst[:, h:h + 1], wt=h)
            for h in H4:
                nd = pND[h // 2]
                o0 = (h % 2) * 256
                P.mm(nd[:, o0:o0 + 128], vch[li][:, h * 128:(h + 1) * 128], PT[b][h][:], start=True, stop=False)
                P.mm(nd[:, o0:o0 + 128], Cbf[:, h, :], qt[b][h][:], start=False, stop=True, lt=h)
                P.mm(nd[:, o0 + 128:o0 + 256], ones_bf[:], PT[b][h][:], start=True, stop=False)
                P.mm(nd[:, o0 + 128:o0 + 256], nrep[:, h, :], qt[b][h][:], start=False, stop=True, lt=h)
            for h in H4:
                nd = pND[h // 2]
                o0 = (h % 2) * 256
                P.act(dd[h][:], nd[:, o0 + 128:o0 + 256], AF.Abs)
            for h in H4:
                nd = pND[h // 2]
                o0 = (h % 2) * 256
                bend = brow[b][h][:, bend_c:bend_c + 1]
                P.v("tensor_scalar", out=dd[h][:], in0=dd[h][:], scalar1=1.0, scalar2=None, op0=ALU.max)
                P.v("reciprocal", out=dd[h][:], in_=dd[h][:])
                P.v("tensor_tensor", out=hfo[b][:, h, :], in0=nd[:, o0:o0 + 128], in1=dd[h][:], op=ALU.mult)
                if prompt:
                    P.v("scalar_tensor_tensor", out=mrun[:, h:h + 1], in0=mrun[:, h:h + 1], scalar=bend,
                        in1=tmx[b][h][:], op0=ALU.add, op1=ALU.max)
                P.v("scalar_tensor_tensor", out=Cst[:, h, :], in0=Cst[:, h, :], scalar=wcol[b][h][:, 1:2],
                    in1=pCU[b][:, h * 128:(h + 1) * 128], op0=ALU.mult, op1=ALU.add)
                P.v("scalar_tensor_tensor", out=nst[:, h:h + 1], in0=nst[:, h:h + 1], scalar=wcol[b][h][:, 1:2],
                    in1=pA[h][:, 384 + b:385 + b], op0=ALU.mult, op1=ALU.add)
            for h in H4:
                P.act(Cbf[:, h, :], Cst[:, h, :], AF.Identity, wt=h)
                P.act(nrep[:, h, :], ones_f[:], AF.Identity, scale=nst[:, h:h + 1], wt=h)
            P.dma(HOUT[:, :, t0:t0 + 128].rearrange("h p t -> p h t"), hfo[b][:], q="pool")
            if st["last"] and prompt:
                pi = st["si"] - 1
                P.act(cout[:, :, 129:130], mrun[:].rearrange("p (h o) -> p h o", o=1), AF.Exp, scale=-1.0)
                for h in H4:
                    P.v("tensor_scalar", out=cout[:, h, 0:128], in0=Cst[:, h, :], scalar1=cout[:, h, 129:130],
                        scalar2=None, op0=ALU.mult)
                    P.v("tensor_scalar", out=cout[:, h, 128:129], in0=nst[:, h:h + 1], scalar1=cout[:, h, 129:130],
                        scalar2=None, op0=ALU.mult)
                P.dma(o_C[pi, d].rearrange("h p e -> p h e"), cout[:, :, 0:128], q="pool")
                P.dma(o_n[pi, d].rearrange("h (p o) -> p h o", o=1), cout[:, :, 128:129], q="pool", slow=True)
                P.dma(o_m[pi, d:d + 1, :], mrun[0:1, :], q="pool")

        front(0)
        for k in range(len(steps)):
            if k + 1 < len(steps):
                front(k + 1)
            back(k)

    def phase_mlstm_fin():
        gml = P.sb("gml", [128, 4], F32)
        P.dma(gml[:], g_ml[:, :])
        hf = [P.sb("fhf%d" % i, [128, T], F32) for i in range(2)]
        hb_ = [P.sb("fhb%d" % i, [128, T], F32) for i in range(2)]
        so = [P.sb("fso%d" % i, [128, T], BF16) for i in range(2)]
        sq = [P.sb("fsq%d" % i, [128, T], BF16) for i in range(2)]
        rs = [P.sb("frs%d" % i, [128, T], F32) for i in range(2)]
        ym = [P.sb("fym%d" % i, [128, T], BF16) for i in range(2)]
        pn = [P.ps("fpn%d" % i) for i in range(5)]
        for h in range(4):
            i = h % 2
            P.dma(hf[i][:], HF[h])
            P.dma(hb_[i][:], HB[h])
            P.dma(so[i][:], SO[h])
            P.v("tensor_tensor", out=hf[i][:], in0=hf[i][:], in1=hb_[i][:], op=ALU.add)
            P.act(sq[i][:], hf[i][:], AF.Square)
            for tb in range(NTB):
                P.mm(pn[tb][:, :], ones_bf[:], sq[i][:, tb * 512:(tb + 1) * 512])
                P.act(rs[i][:, tb * 512:(tb + 1) * 512], pn[tb][:, :], AF.Sqrt, scale=1.0 / 128, bias=epsc[:, 0:1])
            P.v("reciprocal", out=rs[i][:], in_=rs[i][:])
            P.v("scalar_tensor_tensor", out=hf[i][:], in0=hf[i][:], scalar=gml[:, h:h + 1], in1=rs[i][:], op0=ALU.mult,
                op1=ALU.mult)
            P.v("tensor_tensor", out=ym[i][:], in0=hf[i][:], in1=so[i][:], op=ALU.mult)
            P.dma(Y[h], ym[i][:], q="pool")

    def phase_filters():
        w1 = P.sb("w1", [33, 64], F32)
        w2 = P.sb("w2", [64, 64], F32)
        w3 = P.sb("w3", [64, 1024], F32)
        fv = P.sb("fv", [64, 4], F32)
        dl = P.sb("dl", [128, 512], F32)
        P.dma(w1[:], w_f1[:, :], q="pool")
        P.dma(w2[:], w_f2[:, :], q="pool")
        P.dma(w3[:], w_f3[:, :], q="pool")
        P.dma(fv[:, 0:3], fvec[:, :], q="pool")
        P.dma(dl[:], deltas[0:1, 0:512].broadcast_to([128, 512]), q="pool")
        pg = P.ps("pg")
        pf = [P.ps("pff%d" % i) for i in range(2)]
        z = P.sb("z", [33, LS], F32)
        h1 = P.sb("h1", [64, LS], F32)
        h2 = P.sb("h2", [64, LS], F32)
        a1 = P.sb("a1", [64, 512], F32)
        a2 = P.sb("a2", [64, 512], F32)
        dec = P.sb("dec", [128, 512], F32)
        tng = P.sb("tng", [128, 16], F32)
        fo = [P.sb("fo%d" % i, [128, 1024], BF16) for i in range(2)]
        frb2 = P.sb("frb2", [64, 1], F32)
        P.v("tensor_scalar", out=fv[:, 3:4], in0=fv[:, 0:1], scalar1=fv[:, 2:3], scalar2=None, op0=ALU.mult)
        P.v("tensor_scalar", out=frb2[:], in0=fv[:, 1:2], scalar1=fv[:, 2:3], scalar2=None, op0=ALU.mult)

        def sin_layer(dst, n, bias_ap):
            P.act(a1[:, 0:n], pg[0:64, 0:n], AF.Identity, scale=fv[:, 2:3], bias=bias_ap)
            P.v("tensor_scalar", out=a2[:, 0:n], in0=a1[:, 0:n], scalar1=1.0 / TWO_PI, scalar2=MAGIC, op0=ALU.mult,
                op1=ALU.add)
            P.v("tensor_scalar", out=a2[:, 0:n], in0=a2[:, 0:n], scalar1=MAGIC, scalar2=None, op0=ALU.subtract)
            P.v("scalar_tensor_tensor", out=a1[:, 0:n], in0=a2[:, 0:n], scalar=-TWO_PI, in1=a1[:, 0:n], op0=ALU.mult,
                op1=ALU.add)
            P.v("tensor_scalar", out=a1[:, 0:n], in0=a1[:, 0:n], scalar1=-3.141592, scalar2=3.141592, op0=ALU.max,
                op1=ALU.min)
            P.act(dst, a1[:, 0:n], AF.Sin)

        for L in (LS, LP):
            c = hc[L]
            na = c["na"]
            P.dma(z[:, 0:L], c["zT"][:, :], q="pool")
            P.dma(tng[:, 0:na], c["tneg"][:, :], q="pool")
            nblk = max(1, L // 512)
            n = min(L, 512)
            for b in range(nblk):
                sl = slice(b * n, (b + 1) * n)
                P.mm(pg[0:64, 0:n], w1[:, :], z[:, sl])
                sin_layer(h1[:, sl], n, fv[:, 3:4])
            for b in range(nblk):
                sl = slice(b * n, (b + 1) * n)
                P.mm(pg[0:64, 0:n], w2[:, :], h1[:, sl])
                sin_layer(h2[:, sl], n, frb2[:, 0:1])
            for a in range(na):
                i = a % 2
                P.act(dec[:], dl[:], AF.Exp, scale=tng[:, a:a + 1])
                for half in range(2):
                    ps = pf[half]
                    P.mm(ps[:, :], h2[:, a * 128:(a + 1) * 128], w3[:, half * 512:(half + 1) * 512])
                    dst = fo[i][:, half * 512:(half + 1) * 512]
                    P.v("tensor_tensor", out=dst, in0=ps[:, :], in1=dec[:], op=ALU.mult)
                    if half == 1 and a == 0:
                        P.v("tensor_scalar", out=dst, in0=dst, scalar1=cst_t[:, 5, 64:65], scalar2=None, op0=ALU.mult)
                P.dma(c["FL"][a], fo[i][:], q="pool")

    def phase_filters_adaln1():
        extra = [P.sb("adx%d" % i, [128, 4096], F32) for i in range(6)]
        pa = [P.ps("pa%d" % i) for i in range(2)]
        adaln_layer(1, wst + extra, pa, pre_hook=phase_filters)

    def phase_hyena():
        hb_t = P.sb("hybt", [128, 4], F32)
        P.dma(hb_t[:], hyb[:, :])
        pf = [P.ps("pf%d" % i) for i in range(6)]
        p_tr = P.ps("ptrh", [128, 1024], BF16)
        rhs_all = P.sb("rhsall", [128, 16, 1536], BF16)
        GH = P.sb("GH", [128, 17, 2, 512], BF16)
        tis = [P.sb("tis%d" % i, [128, 2, 512], F32) for i in range(3)]
        tib = [P.sb("tib%d" % i, [128, 2, 512], BF16) for i in range(3)]
        wtt = P.sb("wtt", [128, 17, 2], F32)
        ua = [P.sb("ua%d" % i, [128, 512], F32) for i in range(4)]
        ub = [P.sb("ub%d" % i, [128, 512], BF16) for i in range(2)]
        e1 = [P.sb("e1_%d" % i, [128, 512], F32) for i in range(12)]
        m1 = [P.sb("m1_%d" % i, [128, 512], F32) for i in range(2)]
        uc = [P.sb("uc%d" % i, [128, 512], F32) for i in range(2)]
        x2c = [P.sb("x2c%d" % i, [128, 512], F32) for i in range(2)]
        yh = [P.sb("yh%d" % i, [128, 512], BF16) for i in range(2)]
        tctr = [0]

        def hyena_seq(c, s0, load_filt):
            L, na, nb = c["L"], c["na"], c["nb"]
            n = min(L, 512)
            nblk = max(1, L // 512)
            if load_filt:
                P.dma(wtt[:, 0:nb, :], c["wt"][:, :, :])
                P.dma(rhs_all[:, 0:na, 512:1536], c["FL"][:, :, :].rearrange("a p n -> p a n"))
            for cc in range(4):
                for tb in range(nblk):
                    i = (cc * nblk + tb) % 2
                    tsl = slice(s0 + tb * n, s0 + (tb + 1) * n)
                    P.dma(ua[i][:, 0:n], HY[cc, :, tsl])
                    P.dma(ua[2 + i][:, 0:n], HY[4 + cc, :, tsl])
                    P.v("tensor_tensor", out=ua[i][:, 0:n], in0=ua[i][:, 0:n], in1=ua[2 + i][:, 0:n], op=ALU.mult)
                    P.dma(U[cc, :, tsl], ua[i][:, 0:n])
                    P.v("tensor_copy", out=ub[i][:, 0:n], in_=ua[i][:, 0:n])
                    nn = n // 128
                    for a in range(nn):
                        P.tr(p_tr[:, a * 128:(a + 1) * 128], ub[i][:, a * 128:(a + 1) * 128], ident_bf[:])
                    a0 = tb * 4
                    P.act(rhs_all[:, a0:a0 + nn, cc * 128:(cc + 1) * 128],
                          p_tr[:, 0:nn * 128].rearrange("p (a t) -> p a t", a=nn), AF.Identity)
            def load_tf(b):
                i = wctr[0] % 2
                wctr[0] += 1
                sv = wst[i][:, 0:na * 256].rearrange("p (a s f) -> p a s f", a=na, s=2)
                bv = wbf[i][:, 0:na * 256].rearrange("p (a s f) -> p a s f", a=na, s=2)
                P.dma(sv, c["TF"][b], q="pool")
                P.act(bv, sv, AF.Identity)
                return bv
            nxt = load_tf(0)
            for b in range(nb):
                bv = nxt
                if b + 1 < nb:
                    nxt = load_tf(b + 1)
                for a in range(na):
                    for j in range(3):
                        P.mm(pf[j][:, :], bv[:, a, 0, :], rhs_all[:, a, j * 512:(j + 1) * 512], start=(a == 0),
                             stop=(a == na - 1))
                        P.mm(pf[3 + j][:, :], bv[:, a, 1, :], rhs_all[:, a, j * 512:(j + 1) * 512], start=(a == 0),
                             stop=(a == na - 1))
                wtc = wtt[:, b, 0:1]
                nwtc = wtt[:, b, 1:2]
                ee = e1[(b % 2) * 6:(b % 2) * 6 + 6]
                au, ap_, aq, bu, bp, bq = ee
                P.act(au[:], pf[0][:, :], AF.Identity)
                P.act(ap_[:], pf[1][:, :], AF.Identity, scale=wtc)
                P.act(aq[:], pf[2][:, :], AF.Identity, scale=wtc)
                P.act(bu[:], pf[3][:, :], AF.Identity)
                P.act(bp[:], pf[4][:, :], AF.Identity, scale=wtc)
                P.act(bq[:], pf[5][:, :], AF.Identity, scale=nwtc)
                Kr, Ki = ap_, bp
                P.v("tensor_tensor", out=Kr[:], in0=ap_[:], in1=aq[:], op=ALU.add)
                P.v("tensor_tensor", out=Ki[:], in0=bp[:], in1=bq[:], op=ALU.add)
                P.v("tensor_tensor", out=m1[0][:], in0=au[:], in1=Kr[:], op=ALU.mult)
                P.v("tensor_tensor", out=m1[1][:], in0=bu[:], in1=Ki[:], op=ALU.mult)
                P.v("tensor_tensor", out=GH[:, b, 0, :], in0=m1[0][:], in1=m1[1][:], op=ALU.subtract)
                P.v("tensor_tensor", out=m1[0][:], in0=au[:], in1=Ki[:], op=ALU.mult)
                P.v("tensor_tensor", out=m1[1][:], in0=bu[:], in1=Kr[:], op=ALU.mult)
                P.v("tensor_tensor", out=GH[:, b, 1, :], in0=m1[0][:], in1=m1[1][:], op=ALU.add)
            seq = [(tb, b) for tb in range(nblk) for b in range(nb)]
            st_ = {"dma": 0, "cast": 0}

            def ti_dma(k):
                tb_k, b_k = seq[k]
                i = (tctr[0] + k) % 3
                P.dma(tis[i][:, :, 0:n], c["TI"][:, b_k, :, tb_k * n:(tb_k + 1) * n], q=("pool" if k % 2 == 0 else "sp"))

            def ti_cast(k):
                i = (tctr[0] + k) % 3
                P.act(tib[i][:, :, 0:n], tis[i][:, :, 0:n], AF.Identity)

            def ti_get(k):
                while st_["dma"] < min(len(seq), k + 3):
                    ti_dma(st_["dma"])
                    st_["dma"] += 1
                while st_["cast"] < min(len(seq), k + 2):
                    ti_cast(st_["cast"])
                    st_["cast"] += 1
                return tib[(tctr[0] + k) % 3]
            kk = 0
            for tb in range(nblk):
                for b in range(nb):
                    tb_ = ti_get(kk)
                    kk += 1
                    for cc in range(4):
                        P.mm(pf[cc][:, 0:n], GH[:, b, 0, cc * 128:(cc + 1) * 128], tb_[:, 0, 0:n], start=(b == 0),
                             stop=False)
                        P.mm(pf[cc][:, 0:n], GH[:, b, 1, cc * 128:(cc + 1) * 128], tb_[:, 1, 0:n], start=False,
                             stop=(b == nb - 1))
                for cc in range(4):
                    ps = pf[cc]
                    i = cc % 2
                    tsl = slice(s0 + tb * n, s0 + (tb + 1) * n)
                    P.dma(uc[i][:, 0:n], U[cc, :, tsl])
                    P.dma(x2c[i][:, 0:n], HY[8 + cc, :, tsl])
                    P.v("scalar_tensor_tensor", out=uc[i][:, 0:n], in0=uc[i][:, 0:n], scalar=hb_t[:, cc:cc + 1],
                        in1=ps[:, 0:n], op0=ALU.mult, op1=ALU.add)
                    P.v("tensor_tensor", out=yh[i][:, 0:n], in0=uc[i][:, 0:n], in1=x2c[i][:, 0:n], op=ALU.mult)
                    P.dma(Y[4 + cc, :, tsl], yh[i][:, 0:n])
            tctr[0] += len(seq)

        hyena_seq(hc[LS], 0, True)
        hyena_seq(hc[LP], 2048, True)
        hyena_seq(hc[LP], 2304, False)

    def phase_resproj(Wap, nk, srcD, l, sub, xsrc):
        src = P.sb("rsrc", [128, nk, T], BF16)
        for tb in range(NTB):
            tsl_ = slice(tb * 512, (tb + 1) * 512)
            P.dma(src[:, :, tsl_], srcD[:, :, tsl_].rearrange("c p t -> p c t"), q=("pool" if tb % 2 == 0 else "sp"), wt=tb)
        pp = [P.ps("pr%d" % i) for i in range(4)]
        xr = [P.sb("xr%d" % i, [128, T], F32) for i in range(2)]

        def epi(n, tb, ps):
            i = n % 2
            j = 0 if tb < 4 else 1
            tsl = slice(tb * 512, (tb + 1) * 512)
            if tb == 0:
                P.dma(xr[i][:], xsrc[n], q="pool")
            P.v("scalar_tensor_tensor", out=xr[i][:, tsl], in0=ps[:, :], scalar=modt[:, l, 16 + 24 * sub + n, j:j + 1],
                in1=xr[i][:, tsl], op0=ALU.mult, op1=ALU.add)
            if tb == NTB - 1:
                P.dma(X[n], xr[i][:], q="pool")
        ncols = max(128, (4096 // nk) // 128 * 128)
        ncols = min(ncols, 512)
        w_plan([(Wap, nk, c_, ncols) for c_ in range(0, D, ncols)])
        done = 0
        pi = [0]
        while done < D:
            nc_ = min(ncols, D - done)
            bv, wtag = load_w(Wap, 0, nk, done, nc_)
            nm = nc_ // 128
            for m in range(nm):
                for tb in range(NTB):
                    ps = pp[pi[0] % 4]
                    pi[0] += 1
                    for kc in range(nk):
                        P.mm(ps[:, :], bv[:, kc, m * 128:(m + 1) * 128], src[:, kc, tb * 512:(tb + 1) * 512],
                             start=(kc == 0), stop=(kc == nk - 1), lt=wtag, rtg=tb)
                    epi(done // 128 + m, tb, ps)
                    if m == (nm - 1) // 2 and tb == 2:
                        w_tick()
            done += nc_
        w_end()

    def phase_ffn_up(l):
        w_plan([(w_up[l], 8, br * FF + n * 128, 128) for n in range(22) for br in range(2)])
        hn, ens = norm_hn(l, 1, X)
        raw = [P.sb("raw%d" % i, [128, T], F32) for i in range(2)]
        acc = [P.sb("acc%d" % i, [128, T], F32) for i in range(2)]
        gbuf = [P.sb("gbuf%d" % i, [128, T], F32) for i in range(2)]
        obf = [P.sb("obf%d" % i, [128, T], BF16) for i in range(2)]
        tw = P.sb("tw", [128, 22, 3], F32)
        P.dma(tw[:], wcffn[:, l, :, :])
        pp = [P.ps("pu%d" % i) for i in range(4)]
        W = w_up[l]
        pend = [None]
        for n in range(22):
            i = n % 2
            for br in range(2):
                bv, wtag = load_w(W, 0, 8, br * FF + n * 128, 128)
                for tb in range(NTB):
                    ens(tb)
                    ps = pp[(br * NTB + tb) % 4]
                    for kc in range(8):
                        P.mm(ps[:, :], bv[:, kc, :], hn[:, kc, tb * 512:(tb + 1) * 512], start=(kc == 0), stop=(kc == 7),
                             lt=wtag, rtg=tb)
                    dstt = raw[i] if br == 0 else gbuf[i]
                    P.act(dstt[:, tb * 512:(tb + 1) * 512], ps[:, :], AF.Identity)
                    if tb == 2:
                        w_tick()
            def tail(i=i, n=n):
                conv3(acc[i], raw[i], tw[:, n, :])
                P.act(acc[i][:], acc[i][:], AF.Gelu_apprx_tanh)
                P.v("tensor_tensor", out=obf[i][:], in0=acc[i][:], in1=gbuf[i][:], op=ALU.mult)
                P.dma(ACTS[n], obf[i][:], q="pool")
            if pend[0] is not None:
                pend[0]()
            pend[0] = tail
        pend[0]()
        w_end()

    def phase_proj_c():
        rq = [(w_in_c, 8, c_, 512) for c_ in (0, 512, 1024, 1536)]
        for half_ in range(2):
            rq += [(w_in_c, 8, 2048 + half_ * 512, 512), (w_in_c, 8, 2048 + half_ * 512, 512),
                   (w_in_c, 8, 1024 + half_ * 512, 512)]
        w_plan(rq)
        hn, ens = norm_hn(1, 0, X)
        pp = [P.ps("pc%d" % i) for i in range(4)]
        qb = [P.sb("qb%d" % i, [128, 512], BF16) for i in range(3)]
        vb = [P.sb("vb%d" % i, [128, 512], BF16) for i in range(3)]
        kf = [P.sb("kf%d" % i, [128, 512], F32) for i in range(3)]
        ctr = [0]

        def epi_qk(n, tb, ps):
            i = ctr[0] % 3
            ctr[0] += 1
            if n < 8:
                P.act(qb[i][:], ps[:, :], AF.Identity, scale=0.125)
            else:
                P.act(qb[i][:], ps[:, :], AF.Identity)
            P.dma(QK[n, :, tb * 512:(tb + 1) * 512], qb[i][:], q="pool")
        linear_fm(w_in_c, 8, 0, 2048, hn, pp, epi_qk, ensure=ens)
        for half in range(2):
            def epi_v(tc, ps, half=half):
                i = ctr[0] % 3
                ctr[0] += 1
                P.act(vb[i][:], ps[:, :], AF.Identity)
                P.dma(V[tc, :, half * 512:(half + 1) * 512], vb[i][:], q="pool")
                if tc >= 16:
                    pi_, t0 = (tc - 16) // 2, ((tc - 16) % 2) * 128
                    P.v("tensor_copy", out=kf[i][:], in_=ps[:, :])
                    P.dma(o_v[pi_, half * 8:(half + 1) * 8, t0:t0 + 128, :].rearrange("h t d -> t h d"),
                          kf[i][:].rearrange("p (h d) -> p h d", d=64), q="pool")
            linear_tm(w_in_c, 8, 2048 + half * 512, 512, hn, pp, epi_v, range(NTC))

            def epi_v64(tc, ps, half=half):
                i = ctr[0] % 3
                ctr[0] += 1
                P.act(vb[i][:], ps[:, :], AF.Identity)
                P.dma(V64[tc, :, half * 512:(half + 1) * 512], vb[i][:], q="pool")
            linear_tm(w_in_c, 8, 2048 + half * 512, 512, hn, pp, epi_v64, range(15), tok_off=64)

            def epi_k(tc, ps, half=half):
                i = ctr[0] % 3
                ctr[0] += 1
                pi_, t0 = (tc - 16) // 2, ((tc - 16) % 2) * 128
                P.act(kf[i][:], ps[:, :], AF.Identity)
                P.dma(o_k[pi_, half * 8:(half + 1) * 8, t0:t0 + 128, :].rearrange("h t d -> t h d"),
                      kf[i][:].rearrange("p (h d) -> p h d", d=64), q="pool")
            linear_tm(w_in_c, 8, 1024 + half * 512, 512, hn, pp, epi_k, range(16, 20))
        w_end()

    def phase_attn():
        def mkset(i):
            d_ = {}
            d_["qbd"] = P.sb("aqbd%d" % i, [128, 2, T], BF16)
            d_["kT"] = P.sb("akT%d" % i, [128, T], BF16)
            d_["kcx"] = P.sb("akc%d" % i, [128, 512], F32)
            d_["kcb"] = P.sb("akcb%d" % i, [128, 512], BF16)
            d_["vcx"] = P.sb("avc%d" % i, [128, 4, 2, 64], F32)
            d_["vcb"] = P.sb("avcb%d" % i, [128, 4, 128], BF16)
            d_["Vs"] = P.sb("aVs%d" % i, [128, 16, 128], BF16)
            d_["Vs64"] = P.sb("aVs64%d" % i, [128, 15, 128], BF16)
            d_["Vp"] = P.sb("aVp%d" % i, [128, 4, 128], BF16)
            d_["Tmf"] = [P.sb("aTmf%d_%d" % (i, j), [128, 15, 64], F32) for j in range(2)]
            d_["Tmb2"] = P.sb("aTmb2%d" % i, [128, 15, 2, 64], BF16)
            d_["mx"] = P.sb("amx%d" % i, [128, 32], F32)
            d_["negC"] = P.sb("anegC%d" % i, [128, 1], F32)
            d_["ysb"] = P.sb("aysb%d" % i, [128, T], BF16)
            P.v("memset", ap=d_["qbd"][:], constant=0.0)
            return d_
        sets = [mkset(0), mkset(1)]
        sqq = P.sb("asqq", [128, 2, T], BF16)
        sqk = P.sb("asqk", [128, T + 512], BF16)
        selM = P.sb("aselM", [128, 2, 128], BF16)
        P.v("memset", ap=selM[:], constant=0.0)
        P.v("memset", ap=selM[0:64, 0, :], constant=1.0)
        P.v("memset", ap=selM[64:128, 1, :], constant=1.0)
        NW = 2
        PTt = [P.sb("aPT%d" % i, [128, 1024], BF16) for i in range(NW)]
        pts = [P.sb("apts%d" % i, [128, 256], F32) for i in range(NW)]
        rd = [P.sb("ard%d" % i, [128, 256], F32) for i in range(NW)]
        pstA = [P.ps("apsa%d" % i) for i in range(NW)]
        pstB = [P.ps("apsb%d" % i) for i in range(NW)]
        po = [P.ps("apo%d" % i) for i in range(NW)]
        pn_ = P.ps("apn")

        def setup_dma(n):
            S_ = sets[n % 2]
            q_ = "pool"
            P.dma(S_["qbd"][0:64, 0, :], QK[n, 0:64, :], q=q_)
            P.dma(S_["qbd"][64:128, 1, :], QK[n, 64:128, :], q=q_)
            P.dma(S_["kT"][:], QK[8 + n], q=q_)
            P.dma(S_["kcx"][:], kcT[n], q=q_)
            for hh in range(2):
                P.dma(S_["vcx"][:, :, hh, :], vc[2 * n + hh].rearrange("(c p) d -> p c d", p=128), q=q_)
                P.dma(S_["Tmf"][hh][:], TmE[2 * n + hh], q=q_)
            P.dma(S_["Vs"][:], V[0:16, :, n * 128:(n + 1) * 128].rearrange("c p d -> p c d"), q=q_)
            P.dma(S_["Vs64"][:], V64[0:15, :, n * 128:(n + 1) * 128].rearrange("c p d -> p c d"), q=q_)
            P.dma(S_["Vp"][:], V[16:20, :, n * 128:(n + 1) * 128].rearrange("c p d -> p c d"), q=q_)

        def setup_compute(n):
            S_ = sets[n % 2]
            mx = S_["mx"]
            P.v("tensor_copy", eng="pool", out=S_["kcb"][:], in_=S_["kcx"][:])
            P.v("tensor_copy", eng="pool", out=S_["vcb"][:].rearrange("p c (h d) -> p c h d", h=2), in_=S_["vcx"][:])
            for hh in range(2):
                P.v("tensor_tensor", eng="pool", out=S_["Tmf"][hh][:], in0=S_["Tmf"][hh][:],
                    in1=colmask[:, None, :].broadcast_to([128, 15, 64]), op=ALU.add)
                P.v("tensor_copy", eng="pool", out=S_["Tmb2"][:, :, hh, :], in_=S_["Tmf"][hh][:])
            P.act(sqq[:], S_["qbd"][:], AF.Square)
            P.act(sqk[:, 0:T], S_["kT"][:], AF.Square)
            P.act(sqk[:, T:T + 512], S_["kcb"][:], AF.Square)
            for hh in range(2):
                for tb in range(5):
                    P.mm(pn_[:, 0:512], ones_bf[:], sqq[:, hh, tb * 512:(tb + 1) * 512])
                    P.v("tensor_reduce", out=mx[:, hh * 16 + tb:hh * 16 + tb + 1], in_=pn_[:, 0:512], axis=AX.X, op=ALU.max)
                for tb in range(6):
                    P.mm(pn_[:, 0:512], selM[:, hh, :], sqk[:, tb * 512:(tb + 1) * 512])
                    P.v("tensor_reduce", out=mx[:, hh * 16 + 5 + tb:hh * 16 + 6 + tb], in_=pn_[:, 0:512], axis=AX.X,
                        op=ALU.max)
                P.v("tensor_reduce", out=mx[:, hh * 16 + 12:hh * 16 + 13], in_=mx[:, hh * 16:hh * 16 + 5], axis=AX.X, op=ALU.max)
                P.v("tensor_reduce", out=mx[:, hh * 16 + 13:hh * 16 + 14], in_=mx[:, hh * 16 + 5:hh * 16 + 11], axis=AX.X,
                    op=ALU.max)
                P.v("tensor_tensor", out=mx[:, hh * 16 + 14:hh * 16 + 15], in0=mx[:, hh * 16 + 12:hh * 16 + 13],
                    in1=mx[:, hh * 16 + 13:hh * 16 + 14], op=ALU.mult)
            P.v("tensor_tensor", out=mx[:, 15:16], in0=mx[:, 14:15], in1=mx[:, 30:31], op=ALU.max)
            P.act(mx[:, 31:32], mx[:, 15:16], AF.Sqrt)
            P.v("tensor_scalar", out=S_["negC"][:], in0=mx[:, 31:32], scalar1=-1.0, scalar2=None, op0=ALU.mult)

        def stage_s(w, S_, qt0, nq, chunks, x0):
            n2 = 2 * nq
            qsl = S_["qbd"][:, :, qt0:qt0 + nq]
            for ch, (kap, vap) in enumerate(chunks):
                pt_, c_ = (pstA[w], ch) if ch * n2 < 512 else (pstB[w], ch - 512 // n2)
                dst = pt_[:, c_ * n2:(c_ + 1) * n2]
                local = (x0 is not None and ch < 4)
                P.mm(dst, kap, qsl, start=True, stop=(not local))
                if local:
                    P.mm(dst, ident_bf[:], S_["Tmb2"][:, x0 + 2 * ch, :, :], start=False, stop=True)
            P.act(PTt[w][:, 0:512], pstA[w][:, :], AF.Exp, bias=S_["negC"][:, 0:1])
            if len(chunks) * n2 > 512:
                P.act(PTt[w][:, 512:1024], pstB[w][:, :], AF.Exp, bias=S_["negC"][:, 0:1])

        def stage_o(w, S_, qt0, nq, chunks, x0):
            n2 = 2 * nq
            nch = len(chunks)
            for ch, (kap, vap) in enumerate(chunks):
                P.mm(po[w][:, 0:n2], vap, PTt[w][:, ch * n2:(ch + 1) * n2], start=(ch == 0), stop=(ch == nch - 1))
            for ch in range(nch):
                P.mm(po[w][:, 256:256 + n2], ones_bf[:], PTt[w][:, ch * n2:(ch + 1) * n2], start=(ch == 0),
                     stop=(ch == nch - 1))
            P.act(rd[w][:, 0:n2], po[w][:, 256:256 + n2], AF.Ln)
            P.act(rd[w][:, 0:n2], rd[w][:, 0:n2], AF.Exp, scale=-1.0)
            for hh in range(2):
                ps_ = slice(hh * 64, hh * 64 + 64)
                cs_ = slice(hh * nq, (hh + 1) * nq)
                P.v("tensor_tensor", out=S_["ysb"][ps_, qt0:qt0 + nq], in0=po[w][ps_, cs_], in1=rd[w][ps_, cs_], op=ALU.mult,
                    wt=(qt0 * 2 + hh))

        blk = [0]
        setup_dma(0)
        setup_compute(0)
        for n in range(8):
            S_ = sets[n % 2]
            if n + 1 < 8:
                setup_dma(n + 1)
            kT, kcb, vcb, Vs, Vs64, Vp = S_["kT"], S_["kcb"], S_["vcb"], S_["Vs"], S_["Vs64"], S_["Vp"]
            blocks = []
            for r in range(32):
                j0 = min(max(r - 4, 0), 24)
                k0 = j0 * 64
                chunks = []
                for i in range(4):
                    vap = Vs[:, j0 // 2 + i, :] if j0 % 2 == 0 else Vs64[:, j0 // 2 + i, :]
                    chunks.append((kT[:, k0 + i * 128:k0 + (i + 1) * 128], vap))
                for i in range(4):
                    chunks.append((kcb[:, i * 128:(i + 1) * 128], vcb[:, i, :]))
                blocks.append((S_, r * 64, 64, chunks, j0 - r + 7))
            for pi_ in range(2):
                s0 = 2048 + pi_ * 256
                chunks = [(kT[:, s0 + i * 128:s0 + (i + 1) * 128], Vp[:, pi_ * 2 + i, :]) for i in range(2)]
                for hf in range(2):
                    blocks.append((S_, s0 + hf * 128, 128, chunks, None))
            prev = None
            for bi, bk in enumerate(blocks):
                w = blk[0] % NW
                blk[0] += 1
                stage_s(w, *bk)
                if prev is not None:
                    stage_o(*prev)
                prev = (w,) + bk
                if bi == 12 and n + 1 < 8:
                    setup_compute(n + 1)
            stage_o(*prev)
            P.dma(Y[n], S_["ysb"][:])

    P.end_phase()

    stages = [
        ("adaln", phase_adaln),
        ("copyx", None),
        ("projab", phase_proj_ab),
        ("mlstm0", phase_mlstm),
        ("mlstm", phase_mlstm_fin),
        ("filters", phase_filters_adaln1),
        ("hyena", phase_hyena),
        ("outab", lambda: phase_resproj(w_out_ab, 8, Y, 0, 0, xT)),
        ("ffnup0", lambda: phase_ffn_up(0)),
        ("down0", lambda: phase_resproj(w_down[0], 22, ACTS, 0, 1, X)),
        ("projc", phase_proj_c),
        ("attn", phase_attn),
        ("outc", lambda: phase_resproj(w_out_c, 8, Y, 1, 0, X)),
        ("ffnup1", lambda: phase_ffn_up(1)),
        ("down1", lambda: phase_resproj(w_down[1], 22, ACTS, 1, 1, X)),
        ("final", lambda: phase_norm(0, 0, X, final=True)),
    ]
    dbg_outs = {}
    for name, fn in stages:
        if fn is None:
            continue
        P.begin_phase()
        fn()
        P.end_phase()
        if debug is not None and name == debug[0]:
            P.begin_phase()
            for (tn, shape, dt) in debug[1]:
                srcT = {"X": X, "HN": HN, "QK": QK, "V": V, "SO": SO, "G": G, "HY": HY, "U": U, "HF": HF, "Y": Y,
                        "ACTS": ACTS, "V64": V64}[tn]
                o = dout("dbg_" + tn, shape, dt)
                dbg_outs[tn] = o
                isc = (len(shape) == 3 and shape[1] == 128)
                buf = P.sb("dbgbuf", [128, int(np.prod(shape[2:])) if isc else int(np.prod(shape[1:]))], dt)
                if isc:
                    for ci in range(shape[0]):
                        P.dma(buf[:], srcT[ci])
                        P.dma(o[ci], buf[:])
                else:
                    P.dma(buf[:], srcT.rearrange("p a b -> p (a b)"))
                    P.dma(o.rearrange("p a b -> p (a b)"), buf[:])
            P.end_phase()
            break
    P.close()
    nc._phase_counts = P.phase_counts
    return nc


def _consts():
    ar = np.arange(128)
    ident = np.eye(128, dtype=np.float32)
    tri_f = (ar[:, None] <= ar[None, :]).astype(np.float32)
    tri_b = (ar[:, None] >= ar[None, :]).astype(np.float32)
    maskb_f = np.where(ar[:, None] <= ar[None, :], 0.0, NEG).astype(np.float32)
    maskb_b = np.where(ar[:, None] >= ar[None, :], 0.0, NEG).astype(np.float32)
    cols = np.arange(64)
    c_start = np.clip(cols - 8, 0, 48)
    valid = (cols[:, None] >= c_start[None, :]) & (cols[:, None] < c_start[None, :] + 16)
    cm = np.where(valid, 0.0, NEG).astype(np.float32)
    last = np.zeros((128, 128), np.float32)
    last[0:64, 0:64] = cm
    last[64:128, 0:64] = cm
    last[:, 64] = 1.0
    last[0, 64] = 0.0
    cst = np.stack([ident, tri_f, tri_b, maskb_f, maskb_b, last], axis=1)
    out = {"cst": np.ascontiguousarray(cst)}
    deltas = np.abs(np.linspace(math.log(1e-2) / 1.5, math.log(1e-2) / 0.3, 512, dtype=np.float32))
    out["deltas"] = np.concatenate([deltas, deltas])[None, :].astype(np.float32)
    for L in (LS, LP):
        c = hy_cfg(L)
        N, na, nb = c["N"], c["na"], c["nb"]
        t = np.arange(na * 128, dtype=np.int64)
        f = np.arange(nb * 128, dtype=np.int64)
        ang = 2.0 * np.pi * ((t[:, None] * f[None, :]) % N).astype(np.float64) / N
        Cm, Sm = np.cos(ang), np.sin(ang)
        TF = np.stack([Cm, Sm], axis=0).reshape(2, na, 128, nb, 128).transpose(3, 2, 1, 0, 4)
        out["TF%d" % L] = np.ascontiguousarray(TF).astype(np.float32)
        tt = np.arange(L, dtype=np.int64)
        ang2 = 2.0 * np.pi * ((f[:, None] * tt[None, :]) % N).astype(np.float64) / N
        TI = np.stack([np.cos(ang2), np.sin(ang2)], axis=0).reshape(2, nb, 128, L).transpose(2, 1, 0, 3)
        out["TI%d" % L] = np.ascontiguousarray(TI).astype(np.float32)
        tl = np.linspace(0.0, 1.0, L, dtype=np.float32)
        wpos = (2.0 * np.pi * np.arange(L, dtype=np.float32) / L).astype(np.float32)
        bands = np.linspace(1e-4, 15, 16, dtype=np.float32)
        z = np.concatenate([tl[:, None], np.cos(bands[None, :] * wpos[:, None]), -np.sin(bands[None, :] * wpos[:, None])],
                           axis=-1).astype(np.float32)
        out["zT%d" % L] = np.ascontiguousarray(z.T)
        out["tneg%d" % L] = np.ascontiguousarray((-tl).reshape(na, 128).T)
        wt = np.zeros(nb * 128, np.float64)
        wt[0] = 1.0 / N
        wt[1:L] = 2.0 / N
        wt[L] = 1.0 / N
        wt2 = np.stack([wt, -wt], axis=-1).reshape(nb, 128, 2).transpose(1, 0, 2)
        out["wt%d" % L] = np.ascontiguousarray(wt2).astype(np.float32)
    return out


_CACHE = {}


def _prep_inputs(inp):
    f = lambda a: np.ascontiguousarray(np.asarray(a, dtype=np.float32))
    shared = dict(_consts())
    shared["w_ada"] = f(inp["w_ada"])
    shared["b_adaP"] = f(np.asarray(inp["b_ada"]).reshape(2, 48, 128).transpose(2, 0, 1))
    gs = np.stack([inp["g_mix"][0], inp["g_mix"][1], inp["g_ffn"][0], inp["g_ffn"][1], inp["g_final"]], axis=0)
    shared["gP"] = f(gs.reshape(5, 8, 128).transpose(2, 0, 1))
    shared["w_in_ab"] = f(inp["w_in_ab"][0])
    shared["b_gates"] = f(inp["b_gates"][0][None, :])
    shared["wcqk"] = f(np.asarray(inp["w_conv_qk"][0]).reshape(3, 8, 128).transpose(2, 1, 0))
    shared["g_ml"] = f(np.asarray(inp["g_mlstm"][0]).reshape(4, 128).T)
    shared["wchy"] = f(np.asarray(inp["w_conv_hy"][0]).reshape(3, 12, 128).transpose(2, 1, 0))
    shared["w_f1"] = f(inp["w_filt1"][0])
    shared["w_f2"] = f(inp["w_filt2"][0])
    shared["w_f3"] = f(inp["w_filt3"][0])
    shared["fvec"] = f(np.stack([inp["b_filt1"][0], inp["b_filt2"][0], inp["filt_freq"][0]], axis=-1))
    shared["hyb"] = f(np.asarray(inp["hyena_bias"][0]).reshape(4, 128).T)
    shared["w_out_ab"] = f(inp["w_out_ab"][0])
    shared["w_in_c"] = f(inp["w_in_c"][0])
    rp = np.asarray(inp["rpb_c"][0], dtype=np.float32)
    pidx = np.arange(128)
    wk = (pidx % 64)[:, None, None]
    up = (pidx >= 64).astype(np.int64)[:, None, None]
    xx = np.arange(15)[None, :, None]
    wq = np.arange(64)[None, None, :]
    ci = wk - wq + 15
    ri = xx + up
    ok = (ci >= 0) & (ci <= 30) & (ri <= 14)
    gat = rp[:, np.clip(ri, 0, 14), np.clip(ci, 0, 30)]
    shared["TmE"] = np.ascontiguousarray(np.where(ok[None], gat, 0.0).astype(np.float32))
    shared["w_out_c"] = f(inp["w_out_c"][0])
    shared["w_up"] = f(inp["w_up"])
    shared["wcffn"] = f(np.asarray(inp["w_conv_ffn"]).reshape(2, 3, 22, 128).transpose(3, 0, 2, 1))
    shared["w_down"] = f(inp["w_down"])
    maps = []
    for c in range(8):
        b = c % 4
        m = dict(shared)
        toks = np.concatenate([inp["x_sample"][b], inp["x_prompt"][2 * c], inp["x_prompt"][2 * c + 1]], axis=0)
        m["xT"] = f(np.asarray(toks).T.reshape(8, 128, T))
        cvv = np.stack([inp["c"][b], inp["c_ctx"]], axis=-1)
        m["cv"] = f(cvv.reshape(8, 128, 2).transpose(1, 0, 2))
        m["st_C"] = f(np.asarray(inp["state_mlstm_C"][b, 0]).reshape(8, 128, 128))
        m["st_n"] = f(np.asarray(inp["state_mlstm_n"][b, 0]).reshape(8, 128).T)
        m["st_m"] = f(np.broadcast_to(np.asarray(inp["state_mlstm_m"][b, 0]).reshape(1, 8), (128, 8)))
        m["kcT"] = f(np.asarray(inp["cache_na_k"][b, 0]).transpose(0, 2, 1).reshape(8, 128, 512))
        m["vc"] = f(inp["cache_na_v"][b, 0])
        maps.append(m)
    return maps


def kernel(**inputs):
    inp = {k: np.asarray(v) for k, v in inputs.items()}
    if "nc" not in _CACHE:
        _CACHE["nc"] = build()
    nc = _CACHE["nc"]
    maps = _prep_inputs(inp)
    res = run_bass_kernel_spmd(nc, maps, core_ids=list(range(8)))
    R = res.results
    y_prompt = np.zeros((16, 256, D), np.float32)
    y_sample = np.zeros((4, 2048, D), np.float32)
    nC = np.zeros((16, 1, 2, 4, 128, 128), np.float32)
    nn = np.zeros((16, 1, 2, 4, 128), np.float32)
    nm = np.zeros((16, 1, 2, 4), np.float32)
    nk = np.zeros((16, 1, 16, 256, 64), np.float32)
    nv = np.zeros((16, 1, 16, 256, 64), np.float32)
    for c in range(8):
        yt = np.asarray(R[c]["yT"]).reshape(D, T).T
        if c < 4:
            y_sample[c] = yt[0:2048]
        y_prompt[2 * c] = yt[2048:2304]
        y_prompt[2 * c + 1] = yt[2304:2560]
        for pi in range(2):
            nC[2 * c + pi, 0] = np.asarray(R[c]["o_C"])[pi]
            nn[2 * c + pi, 0] = np.asarray(R[c]["o_n"])[pi]
            nm[2 * c + pi, 0] = np.asarray(R[c]["o_m"])[pi]
            nk[2 * c + pi, 0] = np.asarray(R[c]["o_k"])[pi]
            nv[2 * c + pi, 0] = np.asarray(R[c]["o_v"])[pi]
    return (y_prompt, y_sample, nC, nn, nm, nk, nv)
```

```python
import math
import os
import numpy as np
from contextlib import ExitStack
import concourse.bass as bass
import concourse.mybir as mybir
from concourse.bass_utils import run_bass_kernel_spmd

F32 = mybir.dt.float32
BF16 = mybir.dt.bfloat16
AF = mybir.ActivationFunctionType
ALU = mybir.AluOpType
AX = mybir.AxisListType

SAME_ENGINE_SYNC = True
N_DMA_SEMS = 20
_READ_KW = ("in_", "in0", "in1", "lhsT", "rhs", "scalar1", "scalar2", "scalar", "bias", "scale",
            "data0", "data1", "initial", "identity")
_WRITE_KW = ("out", "accum_out", "ap")


class Op:
    __slots__ = ("eng", "meth", "kw", "reads", "writes", "deps", "needs_inc", "sem", "val", "is_dma", "done")

    def __init__(self, eng, meth, kw, reads, writes, is_dma):
        self.eng, self.meth, self.kw = eng, meth, kw
        self.reads, self.writes = reads, writes
        self.deps = []
        self.needs_inc = False
        self.sem = None
        self.val = None
        self.is_dma = is_dma
        self.done = False


class Prog:
    def __init__(self, nc):
        self.nc = nc
        self.es = ExitStack()
        self.phase_es = None
        self.ops = []
        self.state = {}
        self.engs = {"pe": nc.tensor, "act": nc.scalar, "dve": nc.vector, "pool": nc.gpsimd, "sp": nc.sync}
        self.esem = {}
        for e in ("pe", "act", "dve", "pool"):
            self.esem[e] = self.es.enter_context(nc.semaphore("es_" + e))
        self.dsems = {}
        for q in ("sp", "pool"):
            self.dsems[q] = [self.es.enter_context(nc.semaphore("ds_%s_%d" % (q, i))) for i in range(N_DMA_SEMS)]
        self.dcount = {"sp": 0, "pool": 0}
        self.ecount = {e: 0 for e in self.esem}
        self.seen = {e: {} for e in self.engs}
        self.emitted = 0
        self.uid = 0
        self.phase_counts = []

    def sb(self, name, shape, dtype, persist=False):
        self.uid += 1
        st = self.es if (persist or self.phase_es is None) else self.phase_es
        return st.enter_context(self.nc.sbuf_tensor("%s_%d" % (name, self.uid), list(shape), dtype))

    def ps(self, name, shape=(128, 512), dtype=F32):
        self.uid += 1
        return self.phase_es.enter_context(self.nc.psum_tensor("%s_%d" % (name, self.uid), list(shape), dtype))

    def dram(self, name, shape, dtype, kind="Internal"):
        return self.nc.dram_tensor(name, list(shape), dtype, kind=kind).ap()

    @staticmethod
    def _key(x):
        if isinstance(x, tuple):
            return (x[0].name if not isinstance(x[0], str) else x[0]), x[1]
        if isinstance(x, str):
            return x, None
        return x.name, None

    def _access(self, op, key, write):
        name, tag = key
        st = self.state.setdefault(name, {})
        tags = list(st.keys()) if tag is None else [t for t in (tag, None) if t in st]
        for t in tags:
            lw, rd = st[t]
            if lw is not None:
                op.deps.append(lw)
            if write:
                op.deps.extend(rd)
        if write:
            if tag is None:
                st.clear()
            st[tag] = [op, []]
        else:
            if tag not in st:
                st[tag] = [None, []]
            st[tag][1].append(op)

    def add(self, eng, meth, kw, r=None, w=None, rt=None, wt=None, is_dma=False):
        reads = list(r) if r is not None else []
        writes = list(w) if w is not None else []
        if r is None:
            for k in _READ_KW:
                v = kw.get(k, None)
                if v is not None and hasattr(v, "name") and hasattr(v, "ap"):
                    reads.append((v, rt) if rt is not None else v)
        if w is None:
            for k in _WRITE_KW:
                v = kw.get(k, None)
                if v is not None and hasattr(v, "name"):
                    writes.append((v, wt) if wt is not None else v)
        op = Op(eng, meth, kw, [self._key(x) for x in reads], [self._key(x) for x in writes], is_dma)
        for k in op.reads:
            self._access(op, k, False)
        for k in op.writes:
            self._access(op, k, True)
        seen = set()
        dd = []
        for d in op.deps:
            if d is op or id(d) in seen or d.done:
                continue
            seen.add(id(d))
            dd.append(d)
        op.deps = dd
        for d in dd:
            if d.is_dma:
                continue
            if d.eng == op.eng and not op.is_dma and (d.eng == "pe" or not SAME_ENGINE_SYNC):
                continue
            d.needs_inc = True
        self.ops.append(op)
        return op

    def dma(self, out, in_, q="sp", slow=False, **k):
        kw = dict(out=out, in_=in_)
        if slow:
            kw["allow_slow_non_contiguous"] = True
        return self.add(q, "dma_start", kw, is_dma=True, **k)

    def mm(self, out, lhsT, rhs, start=True, stop=True, lt=None, rtg=None, **k):
        if lt is not None or rtg is not None:
            k["r"] = [(lhsT, lt) if lt is not None else lhsT, (rhs, rtg) if rtg is not None else rhs]
        return self.add("pe", "matmul", dict(out=out, lhsT=lhsT, rhs=rhs, start=start, stop=stop), **k)

    def tr(self, out, in_, identity, **k):
        return self.add("pe", "transpose", dict(out=out, in_=in_, identity=identity), **k)

    def act(self, out, in_, func, rt=None, wt=None, **kw):
        kw.update(out=out, in_=in_, func=func)
        return self.add("act", "activation", kw, rt=rt, wt=wt)

    def v(self, meth, eng="dve", rt=None, wt=None, r=None, w=None, **kw):
        return self.add(eng, meth, kw, rt=rt, wt=wt, r=r, w=w)

    def _wait(self, eng, sem, val):
        s = self.seen[eng]
        k = id(sem)
        if s.get(k, 0) >= val:
            return
        s[k] = val
        self.engs[eng].wait_ge(sem, val)

    def flush(self):
        for op in self.ops[self.emitted:]:
            e = op.eng
            for d in op.deps:
                if d.is_dma:
                    self._wait(e, d.sem, d.val)
                else:
                    if d.eng == e and not op.is_dma and (e == "pe" or not SAME_ENGINE_SYNC):
                        continue
                    self._wait(e, d.sem, d.val)
            if op.is_dma:
                q = e
                i = self.dcount[q]
                self.dcount[q] += 1
                sem = self.dsems[q][i % N_DMA_SEMS]
                val = 16 * (i // N_DMA_SEMS + 1)
                if val > 16:
                    self._wait(q, sem, val - 16)
                op.sem, op.val = sem, val
                self.engs[q].dma_start(**op.kw).then_inc(sem, 16)
            else:
                ins = getattr(self.engs[e], op.meth)(**op.kw)
                if op.needs_inc:
                    self.ecount[e] += 1
                    op.sem, op.val = self.esem[e], self.ecount[e]
                    ins.then_inc(op.sem, 1)
        self.emitted = len(self.ops)

    def barrier(self):
        for e in ("pe", "act", "dve", "pool"):
            for op in reversed(self.ops[self.emitted:]):
                if op.eng == e and not op.is_dma:
                    op.needs_inc = True
                    break
        self.flush()
        cnt = {}
        for op in self.ops:
            k = op.eng + ("_dma" if op.is_dma else "")
            cnt[k] = cnt.get(k, 0) + 1
        self.phase_counts.append(cnt)
        for eng in ("sp", "pe", "act", "dve", "pool"):
            for q in ("sp", "pool"):
                n = self.dcount[q]
                for j in range(min(n, N_DMA_SEMS)):
                    cnt = (n - 1 - j) // N_DMA_SEMS + 1
                    self._wait(eng, self.dsems[q][j], 16 * cnt)
            for e in ("pe", "act", "dve", "pool"):
                if e != eng and self.ecount[e] > 0:
                    self._wait(eng, self.esem[e], self.ecount[e])
        for op in self.ops:
            op.done = True
        self.ops = []
        self.emitted = 0
        self.state = {}

    def begin_phase(self):
        self.phase_es = ExitStack()

    def end_phase(self):
        self.barrier()
        self.phase_es.close()
        self.phase_es = None

    def close(self):
        self.es.close()


D = 1024
T = 2560
LS, LP = 2048, 256
SEQS = [(0, 2048), (2048, 2304), (2304, 2560)]
NTB = 5
NTC = 20
FF = 2816
IN_AB = 3600
EPS = 1e-6
MAGIC = 12582912.0
TWO_PI = 2.0 * math.pi
NEG = -30000.0


def hy_cfg(L):
    N = 2 * L
    na = L // 128
    nb = (L + 1 + 127) // 128
    return dict(L=L, N=N, na=na, nb=nb)


def build(debug=None):
    nc = bass.Bass("TRN2", target_bir_lowering=False)
    P = Prog(nc)

    def din(name, shape, dt=F32):
        return nc.dram_tensor(name, list(shape), dt, kind="ExternalInput").ap()

    def dout(name, shape, dt=F32):
        return nc.dram_tensor(name, list(shape), dt, kind="ExternalOutput").ap()

    def dbgdump(name, ap, dt=F32):
        if debug is None:
            return
        o = dout("dbgm_" + name, list(ap.shape), dt)
        P.dma(o, ap)

    xT = din("xT", [8, 128, T])
    cv = din("cv", [128, 8, 2])
    st_C = din("st_C", [8, 128, 128])
    st_n = din("st_n", [128, 8])
    st_m = din("st_m", [128, 8])
    kcT = din("kcT", [8, 128, 512])
    vc = din("vc", [16, 512, 64])
    w_ada = din("w_ada", [2, D, 6 * D])
    b_adaP = din("b_adaP", [128, 2, 48])
    gP = din("gP", [128, 5, 8])
    w_in_ab = din("w_in_ab", [D, IN_AB])
    b_gates = din("b_gates", [1, 16])
    wcqk = din("wcqk", [128, 8, 3])
    g_ml = din("g_ml", [128, 4])
    wchy = din("wchy", [128, 12, 3])
    w_f1 = din("w_f1", [33, 64])
    w_f2 = din("w_f2", [64, 64])
    w_f3 = din("w_f3", [64, 1024])
    fvec = din("fvec", [64, 3])
    hyb = din("hyb", [128, 4])
    w_out_ab = din("w_out_ab", [D, D])
    w_in_c = din("w_in_c", [D, 3 * D])
    TmE = din("TmE", [16, 128, 15, 64])
    w_out_c = din("w_out_c", [D, D])
    w_up = din("w_up", [2, D, 2 * FF])
    wcffn = din("wcffn", [128, 2, 22, 3])
    w_down = din("w_down", [2, FF, D])
    cst = din("cst", [128, 6, 128])
    deltas = din("deltas", [1, 1024])
    hc = {}
    for L in (LS, LP):
        c = hy_cfg(L)
        c["TF"] = din("TF%d" % L, [c["nb"], 128, c["na"], 2, 128])
        c["TI"] = din("TI%d" % L, [128, c["nb"], 2, L])
        c["zT"] = din("zT%d" % L, [33, L])
        c["tneg"] = din("tneg%d" % L, [128, c["na"]])
        c["wt"] = din("wt%d" % L, [128, c["nb"], 2])
        hc[L] = c

    yT = dout("yT", [8, 128, T])
    o_C = dout("o_C", [2, 2, 4, 128, 128])
    o_n = dout("o_n", [2, 2, 4, 128])
    o_m = dout("o_m", [2, 2, 4])
    o_k = dout("o_k", [2, 16, 256, 64])
    o_v = dout("o_v", [2, 16, 256, 64])

    X = P.dram("X", [8, 128, T], F32)
    HN = P.dram("HN", [8, 128, T], BF16)
    QK = P.dram("QK", [16, 128, T], BF16)
    V = P.dram("V", [NTC, 128, 1024], BF16)
    V64 = P.dram("V64", [16, 128, 1024], BF16)
    SO = P.dram("SO", [4, 128, T], BF16)
    G = P.dram("G", [128, NTC, 16], F32)
    HY = P.dram("HY", [12, 128, T], F32)
    U = P.dram("U", [4, 128, T], F32)
    HF = P.dram("HF", [4, 128, T], F32)
    HB = P.dram("HB", [4, 128, T], F32)
    Y = P.dram("Y", [8, 128, T], BF16)
    ACTS = P.dram("ACTS", [22, 128, T], BF16)
    for L in (LS, LP):
        hc[L]["FL"] = P.dram("FL%d" % L, [hc[L]["na"], 128, 1024], BF16)

    cst_t = P.sb("cst", [128, 6, 128], F32, persist=True)
    ident = cst_t[:, 0, :]
    tri = [cst_t[:, 1, :], cst_t[:, 2, :]]
    maskb = [cst_t[:, 3, :], cst_t[:, 4, :]]
    colmask = cst_t[:, 5, 0:64]
    ident_bf = P.sb("identbf", [128, 128], BF16, persist=True)
    ones_f = P.sb("onesf", [128, 128], F32, persist=True)
    ones_bf = P.sb("onesbf", [128, 128], BF16, persist=True)
    epsc = P.sb("epsc", [128, 1], F32, persist=True)
    modt = P.sb("modt", [128, 2, 48, 2], F32, persist=True)
    Atab = P.sb("Atab", [128, 2, 2, 8, 2], F32, persist=True)
    gPt = P.sb("gPt", [128, 5, 8], F32, persist=True)
    wst = [P.sb("wst%d" % i, [128, 4096], F32, persist=True) for i in range(2)]
    wbf = [P.sb("wbf%d" % i, [128, 4096], BF16, persist=True) for i in range(2)]
    wctr = [0]
    scv_p = P.sb("scv", [128, 8, 2], F32, persist=True)
    bt_p = P.sb("bada", [128, 2, 48], F32, persist=True)

    P.begin_phase()
    P.dma(cst_t[:], cst[:, :, :])
    P.dma(gPt[:], gP[:, :, :])
    P.v("tensor_copy", out=ident_bf[:], in_=ident)
    P.v("memset", ap=ones_f[:], constant=1.0)
    P.v("memset", ap=ones_bf[:], constant=1.0)
    P.v("memset", ap=epsc[:], constant=EPS)

    wplan = {"reqs": None}

    def w_plan(reqs):
        mx_e = max(r[1] * r[3] for r in reqs)
        wplan.update(reqs=reqs, cur=0, dma=0, cast=0, nslots=(2 if mx_e > 1024 else 8),
                     se=(4096 if mx_e > 1024 else 1024), base=wctr[0])
        wctr[0] += 1

    def _w_views(k):
        Wap, nk, c0, ncols = wplan["reqs"][k]
        slot = k % wplan["nslots"]
        off = slot * wplan["se"]
        ti, o = off // 4096, off % 4096
        tag = "w%d_%d" % (wplan["base"], slot)
        sv = wst[ti][:, o:o + nk * ncols].rearrange("p (k n) -> p k n", k=nk)
        bv = wbf[ti][:, o:o + nk * ncols].rearrange("p (k n) -> p k n", k=nk)
        return Wap, nk, c0, ncols, sv, bv, tag

    def _w_dma(k):
        Wap, nk, c0, ncols, sv, bv, tag = _w_views(k)
        P.dma(sv, Wap[0:nk * 128, c0:c0 + ncols].rearrange("(k p) n -> p k n", p=128), q="sp", wt=tag)

    def _w_cast(k):
        Wap, nk, c0, ncols, sv, bv, tag = _w_views(k)
        P.act(bv, sv, AF.Identity, rt=tag, wt=tag)

    def w_tick():
        if wplan["reqs"] is None:
            return
        k = wplan["cur"]
        if k < len(wplan["reqs"]) and wplan["cast"] <= k and wplan["dma"] > k:
            _w_cast(k)
            wplan["cast"] = k + 1

    def w_end():
        wplan["reqs"] = None

    def load_w(Wap, k0, nk, c0, ncols, cast=True):
        if wplan["reqs"] is None or not cast:
            i = wctr[0]
            wctr[0] += 1
            sv = wst[i % 2][:, 0:nk * ncols].rearrange("p (k n) -> p k n", k=nk)
            P.dma(sv, Wap[k0 * 128:(k0 + nk) * 128, c0:c0 + ncols].rearrange("(k p) n -> p k n", p=128), q="sp")
            if not cast:
                return sv
            bv = wbf[i % 2][:, 0:nk * ncols].rearrange("p (k n) -> p k n", k=nk)
            P.act(bv, sv, AF.Identity)
            return bv, None
        k = wplan["cur"]
        rq = wplan["reqs"][k]
        assert rq[1] == nk and rq[2] == c0 and rq[3] == ncols, (rq[1:], nk, c0, ncols)
        while wplan["dma"] <= k:
            _w_dma(wplan["dma"])
            wplan["dma"] += 1
        while wplan["cast"] <= k:
            _w_cast(wplan["cast"])
            wplan["cast"] += 1
        wplan["cur"] = k + 1
        depth = 1 if wplan["nslots"] == 2 else 3
        while wplan["dma"] < min(len(wplan["reqs"]), k + 1 + depth):
            _w_dma(wplan["dma"])
            wplan["dma"] += 1
        return _w_views(k)[5], _w_views(k)[6]

    def linear_fm(Wap, nk, c0, ncols_total, src, psums, epi, tbs=range(NTB), chunk0=0, ensure=None):
        pi = [0]
        done = 0
        pending = []
        while done < ncols_total:
            nc_ = min(512, ncols_total - done)
            bv, wtag = load_w(Wap, 0, nk, c0 + done, nc_)
            nm = nc_ // 128
            for m in range(nm):
                new_pending = []
                for tb in tbs:
                    if ensure is not None:
                        ensure(tb)
                    ps = psums[pi[0] % len(psums)]
                    pi[0] += 1
                    for kc in range(nk):
                        P.mm(ps[:, :], bv[:, kc, m * 128:(m + 1) * 128], src[:, kc, tb * 512:(tb + 1) * 512],
                             start=(kc == 0), stop=(kc == nk - 1), lt=wtag, rtg=tb)
                    t_ = epi(chunk0 + (done // 128) + m, tb, ps)
                    if t_ is not None:
                        new_pending.append(t_)
                    if m == (nm - 1) // 2 and tb == 2:
                        w_tick()
                for t_ in pending:
                    t_()
                pending = new_pending
            done += nc_
        for t_ in pending:
            t_()

    def linear_tm(Wap, nk, c0, ncols, src, psums, epi, tok_chunks, tok_off=0):
        bv, wtag = load_w(Wap, 0, nk, c0, ncols)
        tok_chunks = list(tok_chunks)
        for j, tc in enumerate(tok_chunks):
            ps = psums[j % len(psums)]
            t0 = tok_off + tc * 128
            for kc in range(nk):
                P.mm(ps[:, 0:ncols], src[:, kc, t0:t0 + 128], bv[:, kc, :], start=(kc == 0), stop=(kc == nk - 1),
                     lt=(t0 // 512 if tok_off == 0 else None), rtg=wtag)
            epi(tc, ps)
            if j == len(tok_chunks) // 2:
                w_tick()

    def conv3(acc, raw, taps):
        P.v("tensor_scalar", out=acc[:, :], in0=raw[:, :], scalar1=taps[:, 1:2], scalar2=None, op0=ALU.mult)
        for (s0, s1) in SEQS:
            P.v("scalar_tensor_tensor", out=acc[:, s0 + 1:s1], in0=raw[:, s0:s1 - 1], scalar=taps[:, 0:1],
                in1=acc[:, s0 + 1:s1], op0=ALU.mult, op1=ALU.add)
            P.v("scalar_tensor_tensor", out=acc[:, s0:s1 - 1], in0=raw[:, s0 + 1:s1], scalar=taps[:, 2:3],
                in1=acc[:, s0:s1 - 1], op0=ALU.mult, op1=ALU.add)

    def adaln_layer(l, bufs, pa, pre_hook=None):
        nb_ = len(bufs)

        def dma_(s_):
            sv_ = bufs[s_ % nb_][:, :].rearrange("p (k n) -> p k n", k=8)
            P.dma(sv_, w_ada[l][:, s_ * 512:(s_ + 1) * 512].rearrange("(k p) n -> p k n", p=128), q="sp")
        for s_ in range(min(12, nb_)):
            dma_(s_)
        if pre_hook is not None:
            pre_hook()
        for s_ in range(12):
            sv = bufs[s_ % nb_][:, :].rearrange("p (k n) -> p k n", k=8)
            if s_ >= nb_:
                dma_(s_)
            ps = pa[s_ % len(pa)]
            for m in range(4):
                for kc in range(8):
                    P.mm(ps[:, 2 * m:2 * m + 2], sv[:, kc, m * 128:(m + 1) * 128], scv_p[:, kc, :],
                         start=(kc == 0), stop=(kc == 7))
            for m in range(4):
                n = s_ * 4 + m
                P.v("tensor_scalar", out=modt[:, l, n, :], in0=ps[:, 2 * m:2 * m + 2], scalar1=bt_p[:, l, n:n + 1],
                    scalar2=None, op0=ALU.add)
        for sub in range(2):
            sc0 = 8 + 24 * sub
            P.v("tensor_scalar", out=Atab[:, l, sub, :, :], in0=modt[:, l, sc0:sc0 + 8, :], scalar1=1.0,
                scalar2=None, op0=ALU.add)
            for j in range(2):
                P.v("tensor_tensor", out=Atab[:, l, sub, :, j], in0=Atab[:, l, sub, :, j],
                    in1=gPt[:, 2 * sub + l, :], op=ALU.mult)

    def phase_adaln():
        pa = [P.ps("pa%d" % i) for i in range(2)]
        P.dma(scv_p[:], cv[:, :, :])
        P.dma(bt_p[:], b_adaP[:, :, :])
        P.act(scv_p[:], scv_p[:], AF.Silu)
        extra = [P.sb("adx%d" % i, [128, 4096], F32) for i in range(4)]
        adaln_layer(0, wst + extra, pa)

    def phase_norm(l, sub, src, final=False):
        xb = [P.sb("xb%d" % i, [128, 8, 512], F32) for i in range(2)]
        sq = [P.sb("sq%d" % i, [128, 8, 512], BF16) for i in range(2)]
        rs = [P.sb("rs%d" % i, [128, 512], F32) for i in range(2)]
        tmp = [P.sb("tmp%d" % i, [128, 8, 512], F32) for i in range(2)]
        hb = [P.sb("hb%d" % i, [128, 8, 512], (F32 if final else BF16)) for i in range(2)]
        pn = [P.ps("pn%d" % i) for i in range(2)]
        for tb in range(NTB):
            j = 0 if tb < 4 else 1
            i = tb % 2
            tsl = slice(tb * 512, (tb + 1) * 512)
            P.dma(xb[i][:], src[:, :, tsl].rearrange("c p t -> p c t"))
            P.act(sq[i][:], xb[i][:], AF.Square)
            for kc in range(8):
                P.mm(pn[i][:, :], ones_bf[:], sq[i][:, kc, :], start=(kc == 0), stop=(kc == 7))
            P.act(rs[i][:], pn[i][:, :], AF.Sqrt, scale=1.0 / D, bias=epsc[:, 0:1])
            P.v("reciprocal", out=rs[i][:], in_=rs[i][:])
            for kc in range(8):
                if final:
                    P.v("scalar_tensor_tensor", out=hb[i][:, kc, :], in0=xb[i][:, kc, :], scalar=gPt[:, 4, kc:kc + 1],
                        in1=rs[i][:], op0=ALU.mult, op1=ALU.mult, rt=kc, wt=kc)
                else:
                    P.v("scalar_tensor_tensor", out=tmp[i][:, kc, :], in0=xb[i][:, kc, :],
                        scalar=Atab[:, l, sub, kc, j:j + 1], in1=rs[i][:], op0=ALU.mult, op1=ALU.mult, rt=kc, wt=kc)
                    P.act(hb[i][:, kc, :], tmp[i][:, kc, :], AF.Identity, bias=modt[:, l, 24 * sub + kc, j:j + 1],
                          rt=kc, wt=kc)
            dst = yT if final else HN
            P.dma(dst[:, :, tsl].rearrange("c p t -> p c t"), hb[i][:], q="pool")
            if tb == 0 and l == 0 and sub == 0 and not final:
                dbgdump("rs", rs[i][:])
                dbgdump("xb", xb[i][:, 0, :])
                dbgdump("tmp", tmp[i][:, 0, :])
                dbgdump("atab", Atab[:].rearrange("p a b c d -> p (a b c d)"))
                dbgdump("modt", modt[:].rearrange("p a b c -> p (a b c)"))

    def norm_hn(l, sub, src):
        hn = P.sb("hn", [128, 8, T], BF16)
        xb = [P.sb("nxb%d" % i, [128, 8, 512], F32) for i in range(2)]
        rs = [P.sb("nrs%d" % i, [128, 512], F32) for i in range(2)]
        pn = [P.ps("npn%d" % i) for i in range(2)]
        st_ = {"dma": 0, "done": 0}

        def dma(tb):
            tsl = slice(tb * 512, (tb + 1) * 512)
            P.dma(xb[tb % 2][:], src[:, :, tsl].rearrange("c p t -> p c t"), q="pool")

        def ensure(tb):
            while st_["done"] <= tb:
                t = st_["done"]
                while st_["dma"] < min(NTB, t + 2):
                    dma(st_["dma"])
                    st_["dma"] += 1
                i = t % 2
                j = 0 if t < 4 else 1
                tsl = slice(t * 512, (t + 1) * 512)
                P.act(hn[:, :, tsl], xb[i][:], AF.Square, wt=t)
                for kc in range(8):
                    P.mm(pn[i][:, :], ones_bf[:], hn[:, kc, tsl], start=(kc == 0), stop=(kc == 7), rtg=t)
                P.act(rs[i][:], pn[i][:, :], AF.Sqrt, scale=1.0 / D, bias=epsc[:, 0:1])
                P.v("reciprocal", out=rs[i][:], in_=rs[i][:])
                for kc in range(8):
                    P.v("scalar_tensor_tensor", out=xb[i][:, kc, :], in0=xb[i][:, kc, :],
                        scalar=Atab[:, l, sub, kc, j:j + 1], in1=rs[i][:], op0=ALU.mult, op1=ALU.mult, rt=kc, wt=kc)
                for kc in range(8):
                    P.add("act", "activation", dict(out=hn[:, kc, tsl], in_=xb[i][:, kc, :], func=AF.Identity,
                                                    bias=modt[:, l, 24 * sub + kc, j:j + 1]),
                          r=[(xb[i], kc), modt], w=[(hn, t)])
                st_["done"] += 1
        return hn, ensure

    def load_hn():
        hn = P.sb("hn", [128, 8, T], BF16)
        for tb in range(NTB):
            tsl = slice(tb * 512, (tb + 1) * 512)
            P.dma(hn[:, :, tsl], HN[:, :, tsl].rearrange("c p t -> p c t"), q="pool", wt=tb)
        return hn

    def phase_proj_ab():
        w_plan([(w_in_ab, 8, c_, n_) for (c_, n_) in [(0, 512), (512, 512), (1024, 512), (1536, 512), (2048, 16),
                                                      (2064, 512), (2576, 512), (3088, 512)]])
        hn, ens = norm_hn(0, 0, xT)
        raw = [P.sb("raw%d" % i, [128, T], F32) for i in range(2)]
        acc = [P.sb("acc%d" % i, [128, T], F32) for i in range(2)]
        obf = [P.sb("obf%d" % i, [128, T], BF16) for i in range(2)]
        tq = P.sb("tq", [128, 8, 3], F32)
        th = P.sb("th", [128, 12, 3], F32)
        bg = P.sb("bg", [128, 16], F32)
        gt = P.sb("gt", [128, NTC, 16], F32)
        gtmp = P.sb("gtmp", [128, NTC, 8], F32)
        vb = [P.sb("vb%d" % i, [128, 512], BF16) for i in range(2)]
        sob = [P.sb("sob%d" % i, [128, 512], BF16) for i in range(2)]
        pp = [P.ps("pp%d" % i) for i in range(4)]
        P.dma(tq[:], wcqk[:, :, :])
        P.dma(th[:], wchy[:, :, :])
        P.dma(bg[:], b_gates[0:1, :].broadcast_to([128, 16]))

        def epi_qk(n, tb, ps):
            i = n % 2
            P.act(raw[i][:, tb * 512:(tb + 1) * 512], ps[:, :], AF.Identity)
            if tb == NTB - 1:
                def tail():
                    conv3(acc[i], raw[i], tq[:, n, :])
                    P.act(acc[i][:], acc[i][:], AF.Silu)
                    P.v("tensor_scalar", out=obf[i][:], in0=acc[i][:], scalar1=(1.0 if n < 4 else 128.0 ** -0.5),
                        scalar2=None, op0=ALU.mult)
                    P.dma(QK[n], obf[i][:], q="pool")
                return tail
        linear_fm(w_in_ab, 8, 0, 1024, hn, pp, epi_qk, ensure=ens)

        def epi_v(tc, ps):
            i = tc % 2
            P.act(vb[i][:], ps[:, :], AF.Identity)
            P.dma(V[tc, :, 0:512], vb[i][:], q="pool")
        linear_tm(w_in_ab, 8, 1024, 512, hn, pp, epi_v, range(NTC))

        def epi_o(n, tb, ps):
            i = (n * NTB + tb) % 2
            P.act(sob[i][:], ps[:, :], AF.Sigmoid)
            P.dma(SO[n, :, tb * 512:(tb + 1) * 512], sob[i][:], q="pool")
        linear_fm(w_in_ab, 8, 1536, 512, hn, pp, epi_o)

        def epi_g(tc, ps):
            P.v("tensor_tensor", out=gt[:, tc, :], in0=ps[:, 0:16], in1=bg[:], op=ALU.add)
        linear_tm(w_in_ab, 8, 2048, 16, hn, pp, epi_g, range(NTC))
        for d in range(2):
            fs = gt[:, :, 4 + 8 * d:8 + 8 * d]
            ts = gtmp[:, :, 4 * d:4 * d + 4]
            P.act(ts, fs, AF.Exp, scale=-1.0)
            P.act(ts, ts, AF.Ln, bias=ones_f[:, 0:1])
            P.v("tensor_scalar", out=fs, in0=ts, scalar1=-1.0, scalar2=None, op0=ALU.mult)
        P.dma(G[:, :, :], gt[:], q="pool")

        def epi_hy(n, tb, ps):
            i = n % 2
            P.act(raw[i][:, tb * 512:(tb + 1) * 512], ps[:, :], AF.Identity)
            if tb == NTB - 1:
                def tail():
                    conv3(acc[i], raw[i], th[:, n, :])
                    P.dma(HY[n], acc[i][:], q="pool")
                return tail
        linear_fm(w_in_ab, 8, 2064, 1536, hn, pp, epi_hy)
        w_end()

    def phase_mlstm():
        gt = P.sb("gt", [128, NTC, 16], F32)
        P.dma(gt[:], G[:, :, :])
        stn = P.sb("stn", [128, 8], F32)
        stm = P.sb("stm", [128, 8], F32)
        P.dma(stn[:], st_n[:, :])
        P.dma(stm[:], st_m[:, :])
        P.act(stm[:], stm[:], AF.Exp)
        Cst = P.sb("Cst", [128, 4, 128], F32)
        Cbf = P.sb("Cbf", [128, 4, 128], BF16)
        nst = P.sb("nst", [128, 4], F32)
        nrep = P.sb("nrep", [128, 4, 128], BF16)
        mrun = P.sb("mrun", [128, 4], F32)
        ktm_all = P.sb("ktmall", [128, NTC, 512], BF16)
        H4 = range(4)
        NB = 2

        def mk(name, shape, dt):
            return [[P.sb("%s%d_%d" % (name, b, h), shape, dt) for h in H4] for b in range(NB)]
        qTt = [P.sb("qTt%d" % i, [128, 4, 128], BF16) for i in range(3)]
        kTt = [P.sb("kTt%d" % i, [128, 4, 128], BF16) for i in range(3)]
        vch = [P.sb("vch%d" % i, [128, 512], BF16) for i in range(3)]
        acol = [P.sb("acol%d" % i, [128, 4], F32) for i in range(NB)]
        hfo = [P.sb("hfo%d" % i, [128, 4, 128], F32) for i in range(NB)]
        lfrep, brow, arg, eb = mk("lfrep", [128, 128], F32), mk("brow", [128, 128], F32), mk("arg", [128, 128], F32), mk("eb", [128, 128], F32)
        PT, qt, kw = mk("PT", [128, 128], BF16), mk("qt", [128, 128], BF16), mk("kw", [128, 128], BF16)
        wcol, tmx = mk("wcol", [128, 2], F32), mk("tmx", [128, 1], F32)
        irep, te = mk("irep", [128, 128], F32), mk("te", [128, 128], F32)
        dd = [P.sb("dd%d" % h, [128, 128], F32) for h in H4]
        c0t = [P.sb("c0t%d" % i, [128, 128], F32) for i in H4]
        cout = P.sb("cout", [128, 4, 130], F32)
        pA = [P.ps("pmA%d" % i) for i in H4]
        pND = [P.ps("pmN%d" % i) for i in range(2)]
        pCU = [P.ps("pmCU%d" % i) for i in range(NB)]

        for tc in range(NTC):
            i = tc % 3
            P.dma(kTt[i][:], QK[4:8, :, tc * 128:(tc + 1) * 128].rearrange("h p t -> p h t"))
            for h in H4:
                P.tr(pND[tc % 2][:, :].bitcast(BF16)[:, h * 128:(h + 1) * 128], kTt[i][:, h, :], ident_bf[:])
            P.act(ktm_all[:, tc, :], pND[tc % 2][:, :].bitcast(BF16)[:, 0:512], AF.Identity)

        steps = []
        for si, (s0, s1) in enumerate(SEQS):
            nch = (s1 - s0) // 128
            for d in range(2):
                order = list(range(nch)) if d == 0 else list(range(nch - 1, -1, -1))
                for j, c in enumerate(order):
                    steps.append(dict(si=si, d=d, c=c, first=(j == 0), last=(j == nch - 1), t0=s0 + c * 128))
        ld = [0]

        def front(k):
            st = steps[k]
            b = k % NB
            d, t0 = st["d"], st["t0"]
            prompt = st["si"] > 0
            tc = t0 // 128
            li = ld[0] % 3
            ld[0] += 1
            st["li"] = li
            bend_c = 127 if d == 0 else 0
            g = gt[:, tc, :]

            def f0():
                P.dma(qTt[li][:], QK[0:4, :, t0:t0 + 128].rearrange("h p t -> p h t"))
                P.dma(kTt[li][:], QK[4:8, :, t0:t0 + 128].rearrange("h p t -> p h t"))
                P.dma(vch[li][:], V[tc, :, 0:512])
                P.mm(pA[0][:, 392 + 4 * b:396 + 4 * b], tri[d], g[:, 4 + 8 * d:8 + 8 * d])
                P.v("tensor_tensor", out=acol[b][:], in0=g[:, 8 * d:8 * d + 4], in1=pA[0][:, 392 + 4 * b:396 + 4 * b],
                    op=ALU.subtract)

            def f1():
                for h in H4:
                    P.act(lfrep[b][h][:], ones_f[:], AF.Identity, scale=g[:, 4 + 8 * d + h:5 + 8 * d + h])
                    if prompt:
                        P.act(irep[b][h][:], ones_f[:], AF.Identity, scale=g[:, 8 * d + h:8 * d + h + 1])

            def f2():
                for h in H4:
                    P.mm(pA[h][:, 0:128], lfrep[b][h][:], tri[d])
                    P.mm(pA[h][:, 256:384], kTt[li][:, h, :], qTt[li][:, h, :])
                    if prompt:
                        P.mm(pA[h][:, 128:256], irep[b][h][:], ident)

            def f3():
                for h in H4:
                    P.act(brow[b][h][:], pA[h][:, 0:128], AF.Identity)

            def f4():
                for h in H4:
                    P.v("scalar_tensor_tensor", out=arg[b][h][:], in0=brow[b][h][:], scalar=acol[b][:, h:h + 1],
                        in1=maskb[d], op0=ALU.add, op1=ALU.add)
                    if prompt:
                        bend = brow[b][h][:, bend_c:bend_c + 1]
                        P.v("scalar_tensor_tensor", out=te[b][h][:], in0=pA[h][:, 128:256], scalar=bend,
                            in1=brow[b][h][:], op0=ALU.add, op1=ALU.subtract)
                        P.v("tensor_reduce", out=tmx[b][h][:], in_=te[b][h][:], axis=AX.X, op=ALU.max)

            def f5():
                for h in H4:
                    bend = brow[b][h][:, bend_c:bend_c + 1]
                    P.act(arg[b][h][:], arg[b][h][:], AF.Exp)
                    P.act(eb[b][h][:], brow[b][h][:], AF.Exp)
                    P.act(wcol[b][h][:, 0:1], acol[b][:, h:h + 1], AF.Exp, bias=bend)
                    P.act(wcol[b][h][:, 1:2], bend, AF.Exp)

            def f6():
                for h in H4:
                    P.v("tensor_tensor", out=PT[b][h][:], in0=pA[h][:, 256:384], in1=arg[b][h][:], op=ALU.mult)
                    P.v("tensor_tensor", out=qt[b][h][:], in0=qTt[li][:, h, :], in1=eb[b][h][:], op=ALU.mult)
                    P.v("tensor_scalar", out=kw[b][h][:], in0=ktm_all[:, tc, h * 128:(h + 1) * 128],
                        scalar1=wcol[b][h][:, 0:1], scalar2=None, op0=ALU.mult)

            def f7():
                for h in H4:
                    P.mm(pCU[b][:, h * 128:(h + 1) * 128], kw[b][h][:], vch[li][:, h * 128:(h + 1) * 128])
                    P.mm(pA[h][:, 384 + b:385 + b], kw[b][h][:], ones_bf[:, 0:1])
            return [f0, f1, f2, f3, f4, f5, f6, f7]

        def back(k):
            st = steps[k]
            b = k % NB
            d, t0, li = st["d"], st["t0"], st["li"]
            prompt = st["si"] > 0
            bend_c = 127 if d == 0 else 0
            HOUT = HF if d == 0 else HB

            def b0():
                if st["first"]:
                    for h in H4:
                        if prompt:
                            P.v("memset", ap=Cst[:, h, :], constant=0.0)
                        else:
                            P.dma(c0t[h][:], st_C[d * 4 + h], q="pool")
                            P.v("tensor_scalar", out=Cst[:, h, :], in0=c0t[h][:], scalar1=stm[:, d * 4 + h:d * 4 + h + 1],
                                scalar2=None, op0=ALU.mult)
                    if prompt:
                        P.v("memset", ap=nst[:], constant=0.0)
                        P.v("memset", ap=mrun[:], constant=0.0)
                    else:
                        P.v("tensor_tensor", out=nst[:], in0=stn[:, d * 4:d * 4 + 4], in1=stm[:, d * 4:d * 4 + 4],
                            op=ALU.mult)
                    for h in H4:
                        P.act(Cbf[:, h, :], Cst[:, h, :], AF.Identity, wt=h)
                        P.act(nrep[:, h, :], ones_f[:], AF.Identity, scale=nst[:, h:h + 1], wt=h)
                for h in H4:
                    nd = pND[h // 2]
                    o0 = (h % 2) * 256
                    P.mm(nd[:, o0:o0 + 128], vch[li][:, h * 128:(h + 1) * 128], PT[b][h][:], start=True, stop=False)
                    P.mm(nd[:, o0:o0 + 128], Cbf[:, h, :], qt[b][h][:], start=False, stop=True, lt=h)
                    P.mm(nd[:, o0 + 128:o0 + 256], ones_bf[:], PT[b][h][:], start=True, stop=False)
                    P.mm(nd[:, o0 + 128:o0 + 256], nrep[:, h, :], qt[b][h][:], start=False, stop=True, lt=h)

            def b1():
                for h in H4:
                    nd = pND[h // 2]
                    o0 = (h % 2) * 256
                    P.act(dd[h][:], nd[:, o0 + 128:o0 + 256], AF.Abs)

            def b2():
                for h in H4:
                    nd = pND[h // 2]
                    o0 = (h % 2) * 256
                    bend = brow[b][h][:, bend_c:bend_c + 1]
                    P.v("tensor_scalar", out=dd[h][:], in0=dd[h][:], scalar1=1.0, scalar2=None, op0=ALU.max)
                    P.v("reciprocal", out=dd[h][:], in_=dd[h][:])
                    P.v("tensor_tensor", out=hfo[b][:, h, :], in0=nd[:, o0:o0 + 128], in1=dd[h][:], op=ALU.mult)
                    if prompt:
                        P.v("scalar_tensor_tensor", out=mrun[:, h:h + 1], in0=mrun[:, h:h + 1], scalar=bend,
                            in1=tmx[b][h][:], op0=ALU.add, op1=ALU.max)
                    P.v("scalar_tensor_tensor", out=Cst[:, h, :], in0=Cst[:, h, :], scalar=wcol[b][h][:, 1:2],
                        in1=pCU[b][:, h * 128:(h + 1) * 128], op0=ALU.mult, op1=ALU.add)
                    P.v("scalar_tensor_tensor", out=nst[:, h:h + 1], in0=nst[:, h:h + 1], scalar=wcol[b][h][:, 1:2],
                        in1=pA[h][:, 384 + b:385 + b], op0=ALU.mult, op1=ALU.add)

            def b3():
                for h in H4:
                    P.act(Cbf[:, h, :], Cst[:, h, :], AF.Identity, wt=h)
                    P.act(nrep[:, h, :], ones_f[:], AF.Identity, scale=nst[:, h:h + 1], wt=h)
                P.dma(HOUT[:, :, t0:t0 + 128].rearrange("h p t -> p h t"), hfo[b][:], q="pool")
                if st["last"] and prompt:
                    pi = st["si"] - 1
                    P.act(cout[:, :, 129:130], mrun[:].rearrange("p (h o) -> p h o", o=1), AF.Exp, scale=-1.0)
                    for h in H4:
                        P.v("tensor_scalar", out=cout[:, h, 0:128], in0=Cst[:, h, :], scalar1=cout[:, h, 129:130],
                            scalar2=None, op0=ALU.mult)
                        P.v("tensor_scalar", out=cout[:, h, 128:129], in0=nst[:, h:h + 1], scalar1=cout[:, h, 129:130],
                            scalar2=None, op0=ALU.mult)
                    P.dma(o_C[pi, d].rearrange("h p e -> p h e"), cout[:, :, 0:128], q="pool")
                    P.dma(o_n[pi, d].rearrange("h (p o) -> p h o", o=1), cout[:, :, 128:129], q="pool", slow=True)
                    P.dma(o_m[pi, d:d + 1, :], mrun[0:1, :], q="pool")
            return [b0, b1, b2, b3]

        for f in front(0):
            f()
        for k in range(len(steps)):
            B = back(k)
            if k + 1 < len(steps):
                F = front(k + 1)
                for fn in (F[0], F[1], B[0], F[2], B[1], F[3], B[2], F[4], B[3], F[5], F[6], F[7]):
                    fn()
            else:
                for fn in B:
                    fn()

    def phase_mlstm_fin():
        gml = P.sb("gml", [128, 4], F32)
        P.dma(gml[:], g_ml[:, :])
        hf = [P.sb("fhf%d" % i, [128, T], F32) for i in range(2)]
        hb_ = [P.sb("fhb%d" % i, [128, T], F32) for i in range(2)]
        so = [P.sb("fso%d" % i, [128, T], BF16) for i in range(2)]
        sq = [P.sb("fsq%d" % i, [128, T], BF16) for i in range(2)]
        rs = [P.sb("frs%d" % i, [128, T], F32) for i in range(2)]
        ym = [P.sb("fym%d" % i, [128, T], BF16) for i in range(2)]
        pn = [P.ps("fpn%d" % i) for i in range(5)]
        for h in range(4):
            i = h % 2
            P.dma(hf[i][:], HF[h])
            P.dma(hb_[i][:], HB[h])
            P.dma(so[i][:], SO[h])
            P.v("tensor_tensor", out=hf[i][:], in0=hf[i][:], in1=hb_[i][:], op=ALU.add)
            P.act(sq[i][:], hf[i][:], AF.Square)
            for tb in range(NTB):
                P.mm(pn[tb][:, :], ones_bf[:], sq[i][:, tb * 512:(tb + 1) * 512])
                P.act(rs[i][:, tb * 512:(tb + 1) * 512], pn[tb][:, :], AF.Sqrt, scale=1.0 / 128, bias=epsc[:, 0:1])
            P.v("reciprocal", out=rs[i][:], in_=rs[i][:])
            P.v("scalar_tensor_tensor", out=hf[i][:], in0=hf[i][:], scalar=gml[:, h:h + 1], in1=rs[i][:], op0=ALU.mult,
                op1=ALU.mult)
            P.v("tensor_tensor", out=ym[i][:], in0=hf[i][:], in1=so[i][:], op=ALU.mult)
            P.dma(Y[h], ym[i][:], q="pool")

    def phase_filters():
        w1 = P.sb("w1", [33, 64], F32)
        w2 = P.sb("w2", [64, 64], F32)
        w3 = P.sb("w3", [64, 1024], F32)
        fv = P.sb("fv", [64, 4], F32)
        dl = P.sb("dl", [128, 512], F32)
        P.dma(w1[:], w_f1[:, :], q="pool")
        P.dma(w2[:], w_f2[:, :], q="pool")
        P.dma(w3[:], w_f3[:, :], q="pool")
        P.dma(fv[:, 0:3], fvec[:, :], q="pool")
        P.dma(dl[:], deltas[0:1, 0:512].broadcast_to([128, 512]), q="pool")
        pg = P.ps("pg")
        pf = [P.ps("pff%d" % i) for i in range(2)]
        z = P.sb("z", [33, LS], F32)
        h1 = P.sb("h1", [64, LS], F32)
        h2 = P.sb("h2", [64, LS], F32)
        a1 = P.sb("a1", [64, 512], F32)
        a2 = P.sb("a2", [64, 512], F32)
        dec = P.sb("dec", [128, 512], F32)
        tng = P.sb("tng", [128, 16], F32)
        fo = [P.sb("fo%d" % i, [128, 1024], BF16) for i in range(2)]
        frb2 = P.sb("frb2", [64, 1], F32)
        P.v("tensor_scalar", out=fv[:, 3:4], in0=fv[:, 0:1], scalar1=fv[:, 2:3], scalar2=None, op0=ALU.mult)
        P.v("tensor_scalar", out=frb2[:], in0=fv[:, 1:2], scalar1=fv[:, 2:3], scalar2=None, op0=ALU.mult)

        def sin_layer(dst, n, bias_ap):
            P.act(a1[:, 0:n], pg[0:64, 0:n], AF.Identity, scale=fv[:, 2:3], bias=bias_ap)
            P.v("tensor_scalar", out=a2[:, 0:n], in0=a1[:, 0:n], scalar1=1.0 / TWO_PI, scalar2=MAGIC, op0=ALU.mult,
                op1=ALU.add)
            P.v("tensor_scalar", out=a2[:, 0:n], in0=a2[:, 0:n], scalar1=MAGIC, scalar2=None, op0=ALU.subtract)
            P.v("scalar_tensor_tensor", out=a1[:, 0:n], in0=a2[:, 0:n], scalar=-TWO_PI, in1=a1[:, 0:n], op0=ALU.mult,
                op1=ALU.add)
            P.v("tensor_scalar", out=a1[:, 0:n], in0=a1[:, 0:n], scalar1=-3.141592, scalar2=3.141592, op0=ALU.max,
                op1=ALU.min)
            P.act(dst, a1[:, 0:n], AF.Sin)

        for L in (LS, LP):
            c = hc[L]
            na = c["na"]
            P.dma(z[:, 0:L], c["zT"][:, :], q="pool")
            P.dma(tng[:, 0:na], c["tneg"][:, :], q="pool")
            nblk = max(1, L // 512)
            n = min(L, 512)
            for b in range(nblk):
                sl = slice(b * n, (b + 1) * n)
                P.mm(pg[0:64, 0:n], w1[:, :], z[:, sl])
                sin_layer(h1[:, sl], n, fv[:, 3:4])
            for b in range(nblk):
                sl = slice(b * n, (b + 1) * n)
                P.mm(pg[0:64, 0:n], w2[:, :], h1[:, sl])
                sin_layer(h2[:, sl], n, frb2[:, 0:1])
            for a in range(na):
                i = a % 2
                P.act(dec[:], dl[:], AF.Exp, scale=tng[:, a:a + 1])
                for half in range(2):
                    ps = pf[half]
                    P.mm(ps[:, :], h2[:, a * 128:(a + 1) * 128], w3[:, half * 512:(half + 1) * 512])
                    dst = fo[i][:, half * 512:(half + 1) * 512]
                    P.v("tensor_tensor", out=dst, in0=ps[:, :], in1=dec[:], op=ALU.mult)
                    if half == 1 and a == 0:
                        P.v("tensor_scalar", out=dst, in0=dst, scalar1=cst_t[:, 5, 64:65], scalar2=None, op0=ALU.mult)
                P.dma(c["FL"][a], fo[i][:], q="pool")

    def phase_filters_adaln1():
        extra = [P.sb("adx%d" % i, [128, 4096], F32) for i in range(6)]
        pa = [P.ps("pa%d" % i) for i in range(2)]
        adaln_layer(1, wst + extra, pa, pre_hook=phase_filters)

    def phase_hyena():
        hb_t = P.sb("hybt", [128, 4], F32)
        P.dma(hb_t[:], hyb[:, :])
        pf = [P.ps("pf%d" % i) for i in range(6)]
        p_tr = P.ps("ptrh", [128, 1024], BF16)
        rhs_all = P.sb("rhsall", [128, 16, 1536], BF16)
        GH = P.sb("GH", [128, 17, 2, 512], BF16)
        tis = [P.sb("tis%d" % i, [128, 2, 512], F32) for i in range(3)]
        tib = [P.sb("tib%d" % i, [128, 2, 512], BF16) for i in range(3)]
        wtt = P.sb("wtt", [128, 17, 2], F32)
        ua = [P.sb("ua%d" % i, [128, 512], F32) for i in range(4)]
        ub = [P.sb("ub%d" % i, [128, 512], BF16) for i in range(2)]
        e1 = [P.sb("e1_%d" % i, [128, 512], F32) for i in range(12)]
        m1 = [P.sb("m1_%d" % i, [128, 512], F32) for i in range(2)]
        uc = [P.sb("uc%d" % i, [128, 512], F32) for i in range(2)]
        x2c = [P.sb("x2c%d" % i, [128, 512], F32) for i in range(2)]
        yh = [P.sb("yh%d" % i, [128, 512], BF16) for i in range(2)]
        tctr = [0]

        def hyena_seq(c, s0, load_filt):
            L, na, nb = c["L"], c["na"], c["nb"]
            n = min(L, 512)
            nblk = max(1, L // 512)
            if load_filt:
                P.dma(wtt[:, 0:nb, :], c["wt"][:, :, :])
                P.dma(rhs_all[:, 0:na, 512:1536], c["FL"][:, :, :].rearrange("a p n -> p a n"))
            for cc in range(4):
                for tb in range(nblk):
                    i = (cc * nblk + tb) % 2
                    tsl = slice(s0 + tb * n, s0 + (tb + 1) * n)
                    P.dma(ua[i][:, 0:n], HY[cc, :, tsl])
                    P.dma(ua[2 + i][:, 0:n], HY[4 + cc, :, tsl])
                    P.v("tensor_tensor", out=ua[i][:, 0:n], in0=ua[i][:, 0:n], in1=ua[2 + i][:, 0:n], op=ALU.mult)
                    P.dma(U[cc, :, tsl], ua[i][:, 0:n])
                    P.v("tensor_copy", out=ub[i][:, 0:n], in_=ua[i][:, 0:n])
                    nn = n // 128
                    for a in range(nn):
                        P.tr(p_tr[:, a * 128:(a + 1) * 128], ub[i][:, a * 128:(a + 1) * 128], ident_bf[:])
                    a0 = tb * 4
                    P.act(rhs_all[:, a0:a0 + nn, cc * 128:(cc + 1) * 128],
                          p_tr[:, 0:nn * 128].rearrange("p (a t) -> p a t", a=nn), AF.Identity)
            def load_tf(b):
                i = wctr[0] % 2
                wctr[0] += 1
                sv = wst[i][:, 0:na * 256].rearrange("p (a s f) -> p a s f", a=na, s=2)
                bv = wbf[i][:, 0:na * 256].rearrange("p (a s f) -> p a s f", a=na, s=2)
                P.dma(sv, c["TF"][b], q="pool")
                P.act(bv, sv, AF.Identity)
                return bv
            nxt = load_tf(0)
            for b in range(nb):
                bv = nxt
                if b + 1 < nb:
                    nxt = load_tf(b + 1)
                for a in range(na):
                    for j in range(3):
                        P.mm(pf[j][:, :], bv[:, a, 0, :], rhs_all[:, a, j * 512:(j + 1) * 512], start=(a == 0),
                             stop=(a == na - 1))
                        P.mm(pf[3 + j][:, :], bv[:, a, 1, :], rhs_all[:, a, j * 512:(j + 1) * 512], start=(a == 0),
                             stop=(a == na - 1))
                wtc = wtt[:, b, 0:1]
                nwtc = wtt[:, b, 1:2]
                ee = e1[(b % 2) * 6:(b % 2) * 6 + 6]
                au, ap_, aq, bu, bp, bq = ee
                P.act(au[:], pf[0][:, :], AF.Identity)
                P.act(ap_[:], pf[1][:, :], AF.Identity, scale=wtc)
                P.act(aq[:], pf[2][:, :], AF.Identity, scale=wtc)
                P.act(bu[:], pf[3][:, :], AF.Identity)
                P.act(bp[:], pf[4][:, :], AF.Identity, scale=wtc)
                P.act(bq[:], pf[5][:, :], AF.Identity, scale=nwtc)
                Kr, Ki = ap_, bp
                P.v("tensor_tensor", out=Kr[:], in0=ap_[:], in1=aq[:], op=ALU.add)
                P.v("tensor_tensor", out=Ki[:], in0=bp[:], in1=bq[:], op=ALU.add)
                P.v("tensor_tensor", out=m1[0][:], in0=au[:], in1=Kr[:], op=ALU.mult)
                P.v("tensor_tensor", out=m1[1][:], in0=bu[:], in1=Ki[:], op=ALU.mult)
                P.v("tensor_tensor", out=GH[:, b, 0, :], in0=m1[0][:], in1=m1[1][:], op=ALU.subtract)
                P.v("tensor_tensor", out=m1[0][:], in0=au[:], in1=Ki[:], op=ALU.mult)
                P.v("tensor_tensor", out=m1[1][:], in0=bu[:], in1=Kr[:], op=ALU.mult)
                P.v("tensor_tensor", out=GH[:, b, 1, :], in0=m1[0][:], in1=m1[1][:], op=ALU.add)
            seq = [(tb, b) for tb in range(nblk) for b in range(nb)]
            st_ = {"dma": 0, "cast": 0}

            def ti_dma(k):
                tb_k, b_k = seq[k]
                i = (tctr[0] + k) % 3
                P.dma(tis[i][:, :, 0:n], c["TI"][:, b_k, :, tb_k * n:(tb_k + 1) * n], q=("pool" if k % 2 == 0 else "sp"))

            def ti_cast(k):
                i = (tctr[0] + k) % 3
                P.act(tib[i][:, :, 0:n], tis[i][:, :, 0:n], AF.Identity)

            def ti_get(k):
                while st_["dma"] < min(len(seq), k + 3):
                    ti_dma(st_["dma"])
                    st_["dma"] += 1
                while st_["cast"] < min(len(seq), k + 2):
                    ti_cast(st_["cast"])
                    st_["cast"] += 1
                return tib[(tctr[0] + k) % 3]
            kk = 0
            for tb in range(nblk):
                for b in range(nb):
                    tb_ = ti_get(kk)
                    kk += 1
                    for cc in range(4):
                        P.mm(pf[cc][:, 0:n], GH[:, b, 0, cc * 128:(cc + 1) * 128], tb_[:, 0, 0:n], start=(b == 0),
                             stop=False)
                        P.mm(pf[cc][:, 0:n], GH[:, b, 1, cc * 128:(cc + 1) * 128], tb_[:, 1, 0:n], start=False,
                             stop=(b == nb - 1))
                for cc in range(4):
                    ps = pf[cc]
                    i = cc % 2
                    tsl = slice(s0 + tb * n, s0 + (tb + 1) * n)
                    P.dma(uc[i][:, 0:n], U[cc, :, tsl])
                    P.dma(x2c[i][:, 0:n], HY[8 + cc, :, tsl])
                    P.v("scalar_tensor_tensor", out=uc[i][:, 0:n], in0=uc[i][:, 0:n], scalar=hb_t[:, cc:cc + 1],
                        in1=ps[:, 0:n], op0=ALU.mult, op1=ALU.add)
                    P.v("tensor_tensor", out=yh[i][:, 0:n], in0=uc[i][:, 0:n], in1=x2c[i][:, 0:n], op=ALU.mult)
                    P.dma(Y[4 + cc, :, tsl], yh[i][:, 0:n])
            tctr[0] += len(seq)

        hyena_seq(hc[LS], 0, True)
        hyena_seq(hc[LP], 2048, True)
        hyena_seq(hc[LP], 2304, False)

    def phase_resproj(Wap, nk, srcD, l, sub, xsrc):
        src = P.sb("rsrc", [128, nk, T], BF16)
        for tb in range(NTB):
            tsl_ = slice(tb * 512, (tb + 1) * 512)
            P.dma(src[:, :, tsl_], srcD[:, :, tsl_].rearrange("c p t -> p c t"), q=("pool" if tb % 2 == 0 else "sp"), wt=tb)
        pp = [P.ps("pr%d" % i) for i in range(4)]
        xr = [P.sb("xr%d" % i, [128, T], F32) for i in range(2)]

        def epi(n, tb, ps):
            i = n % 2
            j = 0 if tb < 4 else 1
            tsl = slice(tb * 512, (tb + 1) * 512)
            if tb == 0:
                P.dma(xr[i][:], xsrc[n], q="pool")
            P.v("scalar_tensor_tensor", out=xr[i][:, tsl], in0=ps[:, :], scalar=modt[:, l, 16 + 24 * sub + n, j:j + 1],
                in1=xr[i][:, tsl], op0=ALU.mult, op1=ALU.add)
            if tb == NTB - 1:
                P.dma(X[n], xr[i][:], q="pool")
        ncols = max(128, (4096 // nk) // 128 * 128)
        ncols = min(ncols, 512)
        w_plan([(Wap, nk, c_, ncols) for c_ in range(0, D, ncols)])
        done = 0
        pi = [0]
        while done < D:
            nc_ = min(ncols, D - done)
            bv, wtag = load_w(Wap, 0, nk, done, nc_)
            nm = nc_ // 128
            for m in range(nm):
                for tb in range(NTB):
                    ps = pp[pi[0] % 4]
                    pi[0] += 1
                    for kc in range(nk):
                        P.mm(ps[:, :], bv[:, kc, m * 128:(m + 1) * 128], src[:, kc, tb * 512:(tb + 1) * 512],
                             start=(kc == 0), stop=(kc == nk - 1), lt=wtag, rtg=tb)
                    epi(done // 128 + m, tb, ps)
                    if m == (nm - 1) // 2 and tb == 2:
                        w_tick()
            done += nc_
        w_end()

    def phase_ffn_up(l):
        w_plan([(w_up[l], 8, br * FF + n * 128, 128) for n in range(22) for br in range(2)])
        hn, ens = norm_hn(l, 1, X)
        raw = [P.sb("raw%d" % i, [128, T], F32) for i in range(2)]
        acc = [P.sb("acc%d" % i, [128, T], F32) for i in range(2)]
        gbuf = [P.sb("gbuf%d" % i, [128, T], F32) for i in range(2)]
        obf = [P.sb("obf%d" % i, [128, T], BF16) for i in range(2)]
        tw = P.sb("tw", [128, 22, 3], F32)
        P.dma(tw[:], wcffn[:, l, :, :])
        pp = [P.ps("pu%d" % i) for i in range(4)]
        W = w_up[l]
        pend = [None]
        for n in range(22):
            i = n % 2
            for br in range(2):
                bv, wtag = load_w(W, 0, 8, br * FF + n * 128, 128)
                for tb in range(NTB):
                    ens(tb)
                    ps = pp[(br * NTB + tb) % 4]
                    for kc in range(8):
                        P.mm(ps[:, :], bv[:, kc, :], hn[:, kc, tb * 512:(tb + 1) * 512], start=(kc == 0), stop=(kc == 7),
                             lt=wtag, rtg=tb)
                    dstt = raw[i] if br == 0 else gbuf[i]
                    P.act(dstt[:, tb * 512:(tb + 1) * 512], ps[:, :], AF.Identity)
                    if tb == 2:
                        w_tick()
            def tail(i=i, n=n):
                conv3(acc[i], raw[i], tw[:, n, :])
                P.act(acc[i][:], acc[i][:], AF.Gelu_apprx_tanh)
                P.v("tensor_tensor", out=obf[i][:], in0=acc[i][:], in1=gbuf[i][:], op=ALU.mult)
                P.dma(ACTS[n], obf[i][:], q="pool")
            if pend[0] is not None:
                pend[0]()
            pend[0] = tail
        pend[0]()
        w_end()

    def phase_proj_c():
        rq = [(w_in_c, 8, c_, 512) for c_ in (0, 512, 1024, 1536)]
        for half_ in range(2):
            rq += [(w_in_c, 8, 2048 + half_ * 512, 512), (w_in_c, 8, 2048 + half_ * 512, 512),
                   (w_in_c, 8, 1024 + half_ * 512, 512)]
        w_plan(rq)
        hn, ens = norm_hn(1, 0, X)
        pp = [P.ps("pc%d" % i) for i in range(4)]
        qb = [P.sb("qb%d" % i, [128, 512], BF16) for i in range(3)]
        vb = [P.sb("vb%d" % i, [128, 512], BF16) for i in range(3)]
        kf = [P.sb("kf%d" % i, [128, 512], F32) for i in range(3)]
        ctr = [0]

        def epi_qk(n, tb, ps):
            i = ctr[0] % 3
            ctr[0] += 1
            if n < 8:
                P.act(qb[i][:], ps[:, :], AF.Identity, scale=0.125)
            else:
                P.act(qb[i][:], ps[:, :], AF.Identity)
            P.dma(QK[n, :, tb * 512:(tb + 1) * 512], qb[i][:], q="pool")
        linear_fm(w_in_c, 8, 0, 2048, hn, pp, epi_qk, ensure=ens)
        for half in range(2):
            def epi_v(tc, ps, half=half):
                i = ctr[0] % 3
                ctr[0] += 1
                P.act(vb[i][:], ps[:, :], AF.Identity)
                P.dma(V[tc, :, half * 512:(half + 1) * 512], vb[i][:], q="pool")
                if tc >= 16:
                    pi_, t0 = (tc - 16) // 2, ((tc - 16) % 2) * 128
                    P.v("tensor_copy", out=kf[i][:], in_=ps[:, :])
                    P.dma(o_v[pi_, half * 8:(half + 1) * 8, t0:t0 + 128, :].rearrange("h t d -> t h d"),
                          kf[i][:].rearrange("p (h d) -> p h d", d=64), q="pool")
            linear_tm(w_in_c, 8, 2048 + half * 512, 512, hn, pp, epi_v, range(NTC))

            def epi_v64(tc, ps, half=half):
                i = ctr[0] % 3
                ctr[0] += 1
                P.act(vb[i][:], ps[:, :], AF.Identity)
                P.dma(V64[tc, :, half * 512:(half + 1) * 512], vb[i][:], q="pool")
            linear_tm(w_in_c, 8, 2048 + half * 512, 512, hn, pp, epi_v64, range(15), tok_off=64)

            def epi_k(tc, ps, half=half):
                i = ctr[0] % 3
                ctr[0] += 1
                pi_, t0 = (tc - 16) // 2, ((tc - 16) % 2) * 128
                P.act(kf[i][:], ps[:, :], AF.Identity)
                P.dma(o_k[pi_, half * 8:(half + 1) * 8, t0:t0 + 128, :].rearrange("h t d -> t h d"),
                      kf[i][:].rearrange("p (h d) -> p h d", d=64), q="pool")
            linear_tm(w_in_c, 8, 1024 + half * 512, 512, hn, pp, epi_k, range(16, 20))
        w_end()

    def phase_attn():
        def mkset(i):
            d_ = {}
            d_["qbd"] = P.sb("aqbd%d" % i, [128, 2, T], BF16)
            d_["kT"] = P.sb("akT%d" % i, [128, T], BF16)
            d_["kcx"] = P.sb("akc%d" % i, [128, 512], F32)
            d_["kcb"] = P.sb("akcb%d" % i, [128, 512], BF16)
            d_["vcx"] = P.sb("avc%d" % i, [128, 4, 2, 64], F32)
            d_["vcb"] = P.sb("avcb%d" % i, [128, 4, 128], BF16)
            d_["Vs"] = P.sb("aVs%d" % i, [128, 16, 128], BF16)
            d_["Vs64"] = P.sb("aVs64%d" % i, [128, 15, 128], BF16)
            d_["Vp"] = P.sb("aVp%d" % i, [128, 4, 128], BF16)
            d_["Tmf"] = [P.sb("aTmf%d_%d" % (i, j), [128, 15, 64], F32) for j in range(2)]
            d_["Tmb2"] = P.sb("aTmb2%d" % i, [128, 15, 2, 64], BF16)
            d_["mx"] = P.sb("amx%d" % i, [128, 32], F32)
            d_["negC"] = P.sb("anegC%d" % i, [128, 1], F32)
            d_["ysb"] = P.sb("aysb%d" % i, [128, T], BF16)
            P.v("memset", ap=d_["qbd"][:], constant=0.0)
            return d_
        sets = [mkset(0), mkset(1)]
        sqq = P.sb("asqq", [128, 2, T], BF16)
        sqk = P.sb("asqk", [128, T + 512], BF16)
        selM = P.sb("aselM", [128, 2, 128], BF16)
        P.v("memset", ap=selM[:], constant=0.0)
        P.v("memset", ap=selM[0:64, 0, :], constant=1.0)
        P.v("memset", ap=selM[64:128, 1, :], constant=1.0)
        NW = 2
        PTt = [P.sb("aPT%d" % i, [128, 1024], BF16) for i in range(NW)]
        pts = [P.sb("apts%d" % i, [128, 256], F32) for i in range(NW)]
        rd = [P.sb("ard%d" % i, [128, 256], F32) for i in range(NW)]
        pstA = [P.ps("apsa%d" % i) for i in range(NW)]
        pstB = [P.ps("apsb%d" % i) for i in range(NW)]
        po = [P.ps("apo%d" % i) for i in range(NW)]
        pn_ = P.ps("apn")

        def setup_dma(n):
            S_ = sets[n % 2]
            q_ = "pool"
            P.dma(S_["qbd"][0:64, 0, :], QK[n, 0:64, :], q=q_)
            P.dma(S_["qbd"][64:128, 1, :], QK[n, 64:128, :], q=q_)
            P.dma(S_["kT"][:], QK[8 + n], q=q_)
            P.dma(S_["kcx"][:], kcT[n], q=q_)
            for hh in range(2):
                P.dma(S_["vcx"][:, :, hh, :], vc[2 * n + hh].rearrange("(c p) d -> p c d", p=128), q=q_)
                P.dma(S_["Tmf"][hh][:], TmE[2 * n + hh], q=q_)
            P.dma(S_["Vs"][:], V[0:16, :, n * 128:(n + 1) * 128].rearrange("c p d -> p c d"), q=q_)
            P.dma(S_["Vs64"][:], V64[0:15, :, n * 128:(n + 1) * 128].rearrange("c p d -> p c d"), q=q_)
            P.dma(S_["Vp"][:], V[16:20, :, n * 128:(n + 1) * 128].rearrange("c p d -> p c d"), q=q_)

        def setup_compute(n):
            S_ = sets[n % 2]
            mx = S_["mx"]
            P.v("tensor_copy", eng="pool", out=S_["kcb"][:], in_=S_["kcx"][:])
            P.v("tensor_copy", eng="pool", out=S_["vcb"][:].rearrange("p c (h d) -> p c h d", h=2), in_=S_["vcx"][:])
            for hh in range(2):
                P.v("tensor_tensor", eng="pool", out=S_["Tmf"][hh][:], in0=S_["Tmf"][hh][:],
                    in1=colmask[:, None, :].broadcast_to([128, 15, 64]), op=ALU.add)
                P.v("tensor_copy", eng="pool", out=S_["Tmb2"][:, :, hh, :], in_=S_["Tmf"][hh][:])
            P.act(sqq[:], S_["qbd"][:], AF.Square)
            P.act(sqk[:, 0:T], S_["kT"][:], AF.Square)
            P.act(sqk[:, T:T + 512], S_["kcb"][:], AF.Square)
            for hh in range(2):
                for tb in range(5):
                    P.mm(pn_[:, 0:512], ones_bf[:], sqq[:, hh, tb * 512:(tb + 1) * 512])
                    P.v("tensor_reduce", out=mx[:, hh * 16 + tb:hh * 16 + tb + 1], in_=pn_[:, 0:512], axis=AX.X, op=ALU.max)
                for tb in range(6):
                    P.mm(pn_[:, 0:512], selM[:, hh, :], sqk[:, tb * 512:(tb + 1) * 512])
                    P.v("tensor_reduce", out=mx[:, hh * 16 + 5 + tb:hh * 16 + 6 + tb], in_=pn_[:, 0:512], axis=AX.X,
                        op=ALU.max)
                P.v("tensor_reduce", out=mx[:, hh * 16 + 12:hh * 16 + 13], in_=mx[:, hh * 16:hh * 16 + 5], axis=AX.X, op=ALU.max)
                P.v("tensor_reduce", out=mx[:, hh * 16 + 13:hh * 16 + 14], in_=mx[:, hh * 16 + 5:hh * 16 + 11], axis=AX.X,
                    op=ALU.max)
                P.v("tensor_tensor", out=mx[:, hh * 16 + 14:hh * 16 + 15], in0=mx[:, hh * 16 + 12:hh * 16 + 13],
                    in1=mx[:, hh * 16 + 13:hh * 16 + 14], op=ALU.mult)
            P.v("tensor_tensor", out=mx[:, 15:16], in0=mx[:, 14:15], in1=mx[:, 30:31], op=ALU.max)
            P.act(mx[:, 31:32], mx[:, 15:16], AF.Sqrt)
            P.v("tensor_scalar", out=S_["negC"][:], in0=mx[:, 31:32], scalar1=-1.0, scalar2=None, op0=ALU.mult)

        def stage_s(w, S_, qt0, nq, chunks, x0):
            n2 = 2 * nq
            qsl = S_["qbd"][:, :, qt0:qt0 + nq]
            for ch, (kap, vap) in enumerate(chunks):
                pt_, c_ = (pstA[w], ch) if ch * n2 < 512 else (pstB[w], ch - 512 // n2)
                dst = pt_[:, c_ * n2:(c_ + 1) * n2]
                local = (x0 is not None and ch < 4)
                P.mm(dst, kap, qsl, start=True, stop=(not local))
                if local:
                    P.mm(dst, ident_bf[:], S_["Tmb2"][:, x0 + 2 * ch, :, :], start=False, stop=True)
            P.act(PTt[w][:, 0:512], pstA[w][:, :], AF.Exp, bias=S_["negC"][:, 0:1])
            if len(chunks) * n2 > 512:
                P.act(PTt[w][:, 512:1024], pstB[w][:, :], AF.Exp, bias=S_["negC"][:, 0:1])

        def stage_o(w, S_, qt0, nq, chunks, x0):
            n2 = 2 * nq
            nch = len(chunks)
            for ch, (kap, vap) in enumerate(chunks):
                P.mm(po[w][:, 0:n2], vap, PTt[w][:, ch * n2:(ch + 1) * n2], start=(ch == 0), stop=(ch == nch - 1))
            for ch in range(nch):
                P.mm(po[w][:, 256:256 + n2], ones_bf[:], PTt[w][:, ch * n2:(ch + 1) * n2], start=(ch == 0),
                     stop=(ch == nch - 1))
            P.act(rd[w][:, 0:n2], po[w][:, 256:256 + n2], AF.Ln)
            P.act(rd[w][:, 0:n2], rd[w][:, 0:n2], AF.Exp, scale=-1.0)
            for hh in range(2):
                ps_ = slice(hh * 64, hh * 64 + 64)
                cs_ = slice(hh * nq, (hh + 1) * nq)
                P.v("tensor_tensor", out=S_["ysb"][ps_, qt0:qt0 + nq], in0=po[w][ps_, cs_], in1=rd[w][ps_, cs_], op=ALU.mult,
                    wt=(qt0 * 2 + hh))

        blk = [0]
        setup_dma(0)
        setup_compute(0)
        for n in range(8):
            S_ = sets[n % 2]
            if n + 1 < 8:
                setup_dma(n + 1)
            kT, kcb, vcb, Vs, Vs64, Vp = S_["kT"], S_["kcb"], S_["vcb"], S_["Vs"], S_["Vs64"], S_["Vp"]
            blocks = []
            for r in range(32):
                j0 = min(max(r - 4, 0), 24)
                k0 = j0 * 64
                chunks = []
                for i in range(4):
                    vap = Vs[:, j0 // 2 + i, :] if j0 % 2 == 0 else Vs64[:, j0 // 2 + i, :]
                    chunks.append((kT[:, k0 + i * 128:k0 + (i + 1) * 128], vap))
                for i in range(4):
                    chunks.append((kcb[:, i * 128:(i + 1) * 128], vcb[:, i, :]))
                blocks.append((S_, r * 64, 64, chunks, j0 - r + 7))
            for pi_ in range(2):
                s0 = 2048 + pi_ * 256
                chunks = [(kT[:, s0 + i * 128:s0 + (i + 1) * 128], Vp[:, pi_ * 2 + i, :]) for i in range(2)]
                for hf in range(2):
                    blocks.append((S_, s0 + hf * 128, 128, chunks, None))
            prev = None
            for bi, bk in enumerate(blocks):
                w = blk[0] % NW
                blk[0] += 1
                stage_s(w, *bk)
                if prev is not None:
                    stage_o(*prev)
                prev = (w,) + bk
                if bi == 12 and n + 1 < 8:
                    setup_compute(n + 1)
            stage_o(*prev)
            P.dma(Y[n], S_["ysb"][:])

    P.end_phase()

    stages = [
        ("adaln", phase_adaln),
        ("copyx", None),
        ("projab", phase_proj_ab),
        ("mlstm0", phase_mlstm),
        ("mlstm", phase_mlstm_fin),
        ("filters", phase_filters_adaln1),
        ("hyena", phase_hyena),
        ("outab", lambda: phase_resproj(w_out_ab, 8, Y, 0, 0, xT)),
        ("ffnup0", lambda: phase_ffn_up(0)),
        ("down0", lambda: phase_resproj(w_down[0], 22, ACTS, 0, 1, X)),
        ("projc", phase_proj_c),
        ("attn", phase_attn),
        ("outc", lambda: phase_resproj(w_out_c, 8, Y, 1, 0, X)),
        ("ffnup1", lambda: phase_ffn_up(1)),
        ("down1", lambda: phase_resproj(w_down[1], 22, ACTS, 1, 1, X)),
        ("final", lambda: phase_norm(0, 0, X, final=True)),
    ]
    dbg_outs = {}
    for name, fn in stages:
        if fn is None:
            continue
        P.begin_phase()
        fn()
        P.end_phase()
        if debug is not None and name == debug[0]:
            P.begin_phase()
            for (tn, shape, dt) in debug[1]:
                srcT = {"X": X, "HN": HN, "QK": QK, "V": V, "SO": SO, "G": G, "HY": HY, "U": U, "HF": HF, "Y": Y,
                        "ACTS": ACTS, "V64": V64}[tn]
                o = dout("dbg_" + tn, shape, dt)
                dbg_outs[tn] = o
                isc = (len(shape) == 3 and shape[1] == 128)
                buf = P.sb("dbgbuf", [128, int(np.prod(shape[2:])) if isc else int(np.prod(shape[1:]))], dt)
                if isc:
                    for ci in range(shape[0]):
                        P.dma(buf[:], srcT[ci])
                        P.dma(o[ci], buf[:])
                else:
                    P.dma(buf[:], srcT.rearrange("p a b -> p (a b)"))
                    P.dma(o.rearrange("p a b -> p (a b)"), buf[:])
            P.end_phase()
            break
    P.close()
    nc._phase_counts = P.phase_counts
    return nc


def _consts():
    ar = np.arange(128)
    ident = np.eye(128, dtype=np.float32)
    tri_f = (ar[:, None] <= ar[None, :]).astype(np.float32)
    tri_b = (ar[:, None] >= ar[None, :]).astype(np.float32)
    maskb_f = np.where(ar[:, None] <= ar[None, :], 0.0, NEG).astype(np.float32)
    maskb_b = np.where(ar[:, None] >= ar[None, :], 0.0, NEG).astype(np.float32)
    cols = np.arange(64)
    c_start = np.clip(cols - 8, 0, 48)
    valid = (cols[:, None] >= c_start[None, :]) & (cols[:, None] < c_start[None, :] + 16)
    cm = np.where(valid, 0.0, NEG).astype(np.float32)
    last = np.zeros((128, 128), np.float32)
    last[0:64, 0:64] = cm
    last[64:128, 0:64] = cm
    last[:, 64] = 1.0
    last[0, 64] = 0.0
    cst = np.stack([ident, tri_f, tri_b, maskb_f, maskb_b, last], axis=1)
    out = {"cst": np.ascontiguousarray(cst)}
    deltas = np.abs(np.linspace(math.log(1e-2) / 1.5, math.log(1e-2) / 0.3, 512, dtype=np.float32))
    out["deltas"] = np.concatenate([deltas, deltas])[None, :].astype(np.float32)
    for L in (LS, LP):
        c = hy_cfg(L)
        N, na, nb = c["N"], c["na"], c["nb"]
        t = np.arange(na * 128, dtype=np.int64)
        f = np.arange(nb * 128, dtype=np.int64)
        ang = 2.0 * np.pi * ((t[:, None] * f[None, :]) % N).astype(np.float64) / N
        Cm, Sm = np.cos(ang), np.sin(ang)
        TF = np.stack([Cm, Sm], axis=0).reshape(2, na, 128, nb, 128).transpose(3, 2, 1, 0, 4)
        out["TF%d" % L] = np.ascontiguousarray(TF).astype(np.float32)
        tt = np.arange(L, dtype=np.int64)
        ang2 = 2.0 * np.pi * ((f[:, None] * tt[None, :]) % N).astype(np.float64) / N
        TI = np.stack([np.cos(ang2), np.sin(ang2)], axis=0).reshape(2, nb, 128, L).transpose(2, 1, 0, 3)
        out["TI%d" % L] = np.ascontiguousarray(TI).astype(np.float32)
        tl = np.linspace(0.0, 1.0, L, dtype=np.float32)
        wpos = (2.0 * np.pi * np.arange(L, dtype=np.float32) / L).astype(np.float32)
        bands = np.linspace(1e-4, 15, 16, dtype=np.float32)
        z = np.concatenate([tl[:, None], np.cos(bands[None, :] * wpos[:, None]), -np.sin(bands[None, :] * wpos[:, None])],
                           axis=-1).astype(np.float32)
        out["zT%d" % L] = np.ascontiguousarray(z.T)
        out["tneg%d" % L] = np.ascontiguousarray((-tl).reshape(na, 128).T)
        wt = np.zeros(nb * 128, np.float64)
        wt[0] = 1.0 / N
        wt[1:L] = 2.0 / N
        wt[L] = 1.0 / N
        wt2 = np.stack([wt, -wt], axis=-1).reshape(nb, 128, 2).transpose(1, 0, 2)
        out["wt%d" % L] = np.ascontiguousarray(wt2).astype(np.float32)
    return out


_CACHE = {}


def _prep_inputs(inp):
    f = lambda a: np.ascontiguousarray(np.asarray(a, dtype=np.float32))
    shared = dict(_consts())
    shared["w_ada"] = f(inp["w_ada"])
    shared["b_adaP"] = f(np.asarray(inp["b_ada"]).reshape(2, 48, 128).transpose(2, 0, 1))
    gs = np.stack([inp["g_mix"][0], inp["g_mix"][1], inp["g_ffn"][0], inp["g_ffn"][1], inp["g_final"]], axis=0)
    shared["gP"] = f(gs.reshape(5, 8, 128).transpose(2, 0, 1))
    shared["w_in_ab"] = f(inp["w_in_ab"][0])
    shared["b_gates"] = f(inp["b_gates"][0][None, :])
    shared["wcqk"] = f(np.asarray(inp["w_conv_qk"][0]).reshape(3, 8, 128).transpose(2, 1, 0))
    shared["g_ml"] = f(np.asarray(inp["g_mlstm"][0]).reshape(4, 128).T)
    shared["wchy"] = f(np.asarray(inp["w_conv_hy"][0]).reshape(3, 12, 128).transpose(2, 1, 0))
    shared["w_f1"] = f(inp["w_filt1"][0])
    shared["w_f2"] = f(inp["w_filt2"][0])
    shared["w_f3"] = f(inp["w_filt3"][0])
    shared["fvec"] = f(np.stack([inp["b_filt1"][0], inp["b_filt2"][0], inp["filt_freq"][0]], axis=-1))
    shared["hyb"] = f(np.asarray(inp["hyena_bias"][0]).reshape(4, 128).T)
    shared["w_out_ab"] = f(inp["w_out_ab"][0])
    shared["w_in_c"] = f(inp["w_in_c"][0])
    rp = np.asarray(inp["rpb_c"][0], dtype=np.float32)
    pidx = np.arange(128)
    wk = (pidx % 64)[:, None, None]
    up = (pidx >= 64).astype(np.int64)[:, None, None]
    xx = np.arange(15)[None, :, None]
    wq = np.arange(64)[None, None, :]
    ci = wk - wq + 15
    ri = xx + up
    ok = (ci >= 0) & (ci <= 30) & (ri <= 14)
    gat = rp[:, np.clip(ri, 0, 14), np.clip(ci, 0, 30)]
    shared["TmE"] = np.ascontiguousarray(np.where(ok[None], gat, 0.0).astype(np.float32))
    shared["w_out_c"] = f(inp["w_out_c"][0])
    shared["w_up"] = f(inp["w_up"])
    shared["wcffn"] = f(np.asarray(inp["w_conv_ffn"]).reshape(2, 3, 22, 128).transpose(3, 0, 2, 1))
    shared["w_down"] = f(inp["w_down"])
    maps = []
    for c in range(8):
        b = c % 4
        m = dict(shared)
        toks = np.concatenate([inp["x_sample"][b], inp["x_prompt"][2 * c], inp["x_prompt"][2 * c + 1]], axis=0)
        m["xT"] = f(np.asarray(toks).T.reshape(8, 128, T))
        cvv = np.stack([inp["c"][b], inp["c_ctx"]], axis=-1)
        m["cv"] = f(cvv.reshape(8, 128, 2).transpose(1, 0, 2))
        m["st_C"] = f(np.asarray(inp["state_mlstm_C"][b, 0]).reshape(8, 128, 128))
        m["st_n"] = f(np.asarray(inp["state_mlstm_n"][b, 0]).reshape(8, 128).T)
        m["st_m"] = f(np.broadcast_to(np.asarray(inp["state_mlstm_m"][b, 0]).reshape(1, 8), (128, 8)))
        m["kcT"] = f(np.asarray(inp["cache_na_k"][b, 0]).transpose(0, 2, 1).reshape(8, 128, 512))
        m["vc"] = f(inp["cache_na_v"][b, 0])
        maps.append(m)
    return maps


def kernel(**inputs):
    inp = {k: np.asarray(v) for k, v in inputs.items()}
    if "nc" not in _CACHE:
        _CACHE["nc"] = build()
    nc = _CACHE["nc"]
    maps = _prep_inputs(inp)
    res = run_bass_kernel_spmd(nc, maps, core_ids=list(range(8)))
    R = res.results
    y_prompt = np.zeros((16, 256, D), np.float32)
    y_sample = np.zeros((4, 2048, D), np.float32)
    nC = np.zeros((16, 1, 2, 4, 128, 128), np.float32)
    nn = np.zeros((16, 1, 2, 4, 128), np.float32)
    nm = np.zeros((16, 1, 2, 4), np.float32)
    nk = np.zeros((16, 1, 16, 256, 64), np.float32)
    nv = np.zeros((16, 1, 16, 256, 64), np.float32)
    for c in range(8):
        yt = np.asarray(R[c]["yT"]).reshape(D, T).T
        if c < 4:
            y_sample[c] = yt[0:2048]
        y_prompt[2 * c] = yt[2048:2304]
        y_prompt[2 * c + 1] = yt[2304:2560]
        for pi in range(2):
            nC[2 * c + pi, 0] = np.asarray(R[c]["o_C"])[pi]
            nn[2 * c + pi, 0] = np.asarray(R[c]["o_n"])[pi]
            nm[2 * c + pi, 0] = np.asarray(R[c]["o_m"])[pi]
            nk[2 * c + pi, 0] = np.asarray(R[c]["o_k"])[pi]
            nv[2 * c + pi, 0] = np.asarray(R[c]["o_v"])[pi]
    return (y_prompt, y_sample, nC, nn, nm, nk, nv)
```

```python
import math
import os
import numpy as np
from contextlib import ExitStack
import concourse.bass as bass
import concourse.mybir as mybir
from concourse.bass_utils import run_bass_kernel_spmd

F32 = mybir.dt.float32
BF16 = mybir.dt.bfloat16
AF = mybir.ActivationFunctionType
ALU = mybir.AluOpType
AX = mybir.AxisListType

SAME_ENGINE_SYNC = True
N_DMA_SEMS = 20
_READ_KW = ("in_", "in0", "in1", "lhsT", "rhs", "scalar1", "scalar2", "scalar", "bias", "scale",
            "data0", "data1", "initial", "identity")
_WRITE_KW = ("out", "accum_out", "ap")


class Op:
    __slots__ = ("eng", "meth", "kw", "reads", "writes", "deps", "needs_inc", "sem", "val", "is_dma", "done")

    def __init__(self, eng, meth, kw, reads, writes, is_dma):
        self.eng, self.meth, self.kw = eng, meth, kw
        self.reads, self.writes = reads, writes
        self.deps = []
        self.needs_inc = False
        self.sem = None
        self.val = None
        self.is_dma = is_dma
        self.done = False


class Prog:
    def __init__(self, nc):
        self.nc = nc
        self.es = ExitStack()
        self.phase_es = None
        self.ops = []
        self.state = {}
        self.engs = {"pe": nc.tensor, "act": nc.scalar, "dve": nc.vector, "pool": nc.gpsimd, "sp": nc.sync}
        self.esem = {}
        for e in ("pe", "act", "dve", "pool"):
            self.esem[e] = self.es.enter_context(nc.semaphore("es_" + e))
        self.dsems = {}
        for q in ("sp", "pool"):
            self.dsems[q] = [self.es.enter_context(nc.semaphore("ds_%s_%d" % (q, i))) for i in range(N_DMA_SEMS)]
        self.dcount = {"sp": 0, "pool": 0}
        self.ecount = {e: 0 for e in self.esem}
        self.seen = {e: {} for e in self.engs}
        self.emitted = 0
        self.uid = 0
        self.phase_counts = []

    def sb(self, name, shape, dtype, persist=False):
        self.uid += 1
        st = self.es if (persist or self.phase_es is None) else self.phase_es
        return st.enter_context(self.nc.sbuf_tensor("%s_%d" % (name, self.uid), list(shape), dtype))

    def ps(self, name, shape=(128, 512), dtype=F32):
        self.uid += 1
        return self.phase_es.enter_context(self.nc.psum_tensor("%s_%d" % (name, self.uid), list(shape), dtype))

    def dram(self, name, shape, dtype, kind="Internal"):
        return self.nc.dram_tensor(name, list(shape), dtype, kind=kind).ap()

    @staticmethod
    def _key(x):
        if isinstance(x, tuple):
            return (x[0].name if not isinstance(x[0], str) else x[0]), x[1]
        if isinstance(x, str):
            return x, None
        return x.name, None

    def _access(self, op, key, write):
        name, tag = key
        st = self.state.setdefault(name, {})
        tags = list(st.keys()) if tag is None else [t for t in (tag, None) if t in st]
        for t in tags:
            lw, rd = st[t]
            if lw is not None:
                op.deps.append(lw)
            if write:
                op.deps.extend(rd)
        if write:
            if tag is None:
                st.clear()
            st[tag] = [op, []]
        else:
            if tag not in st:
                st[tag] = [None, []]
            st[tag][1].append(op)

    def add(self, eng, meth, kw, r=None, w=None, rt=None, wt=None, is_dma=False):
        reads = list(r) if r is not None else []
        writes = list(w) if w is not None else []
        if r is None:
            for k in _READ_KW:
                v = kw.get(k, None)
                if v is not None and hasattr(v, "name") and hasattr(v, "ap"):
                    reads.append((v, rt) if rt is not None else v)
        if w is None:
            for k in _WRITE_KW:
                v = kw.get(k, None)
                if v is not None and hasattr(v, "name"):
                    writes.append((v, wt) if wt is not None else v)
        op = Op(eng, meth, kw, [self._key(x) for x in reads], [self._key(x) for x in writes], is_dma)
        for k in op.reads:
            self._access(op, k, False)
        for k in op.writes:
            self._access(op, k, True)
        seen = set()
        dd = []
        for d in op.deps:
            if d is op or id(d) in seen or d.done:
                continue
            seen.add(id(d))
            dd.append(d)
        op.deps = dd
        for d in dd:
            if d.is_dma:
                continue
            if d.eng == op.eng and not op.is_dma and (d.eng == "pe" or not SAME_ENGINE_SYNC):
                continue
            d.needs_inc = True
        self.ops.append(op)
        return op

    def dma(self, out, in_, q="sp", slow=False, **k):
        kw = dict(out=out, in_=in_)
        if slow:
            kw["allow_slow_non_contiguous"] = True
        return self.add(q, "dma_start", kw, is_dma=True, **k)

    def mm(self, out, lhsT, rhs, start=True, stop=True, lt=None, rtg=None, **k):
        if lt is not None or rtg is not None:
            k["r"] = [(lhsT, lt) if lt is not None else lhsT, (rhs, rtg) if rtg is not None else rhs]
        return self.add("pe", "matmul", dict(out=out, lhsT=lhsT, rhs=rhs, start=start, stop=stop), **k)

    def tr(self, out, in_, identity, **k):
        return self.add("pe", "transpose", dict(out=out, in_=in_, identity=identity), **k)

    def act(self, out, in_, func, rt=None, wt=None, **kw):
        kw.update(out=out, in_=in_, func=func)
        return self.add("act", "activation", kw, rt=rt, wt=wt)

    def v(self, meth, eng="dve", rt=None, wt=None, r=None, w=None, **kw):
        return self.add(eng, meth, kw, rt=rt, wt=wt, r=r, w=w)

    def _wait(self, eng, sem, val):
        s = self.seen[eng]
        k = id(sem)
        if s.get(k, 0) >= val:
            return
        s[k] = val
        self.engs[eng].wait_ge(sem, val)

    def flush(self):
        for op in self.ops[self.emitted:]:
            e = op.eng
            for d in op.deps:
                if d.is_dma:
                    self._wait(e, d.sem, d.val)
                else:
                    if d.eng == e and not op.is_dma and (e == "pe" or not SAME_ENGINE_SYNC):
                        continue
                    self._wait(e, d.sem, d.val)
            if op.is_dma:
                q = e
                i = self.dcount[q]
                self.dcount[q] += 1
                sem = self.dsems[q][i % N_DMA_SEMS]
                val = 16 * (i // N_DMA_SEMS + 1)
                if val > 16:
                    self._wait(q, sem, val - 16)
                op.sem, op.val = sem, val
                self.engs[q].dma_start(**op.kw).then_inc(sem, 16)
            else:
                ins = getattr(self.engs[e], op.meth)(**op.kw)
                if op.needs_inc:
                    self.ecount[e] += 1
                    op.sem, op.val = self.esem[e], self.ecount[e]
                    ins.then_inc(op.sem, 1)
        self.emitted = len(self.ops)

    def barrier(self):
        for e in ("pe", "act", "dve", "pool"):
            for op in reversed(self.ops[self.emitted:]):
                if op.eng == e and not op.is_dma:
                    op.needs_inc = True
                    break
        self.flush()
        cnt = {}
        for op in self.ops:
            k = op.eng + ("_dma" if op.is_dma else "")
            cnt[k] = cnt.get(k, 0) + 1
        self.phase_counts.append(cnt)
        for eng in ("sp", "pe", "act", "dve", "pool"):
            for q in ("sp", "pool"):
                n = self.dcount[q]
                for j in range(min(n, N_DMA_SEMS)):
                    cnt = (n - 1 - j) // N_DMA_SEMS + 1
                    self._wait(eng, self.dsems[q][j], 16 * cnt)
            for e in ("pe", "act", "dve", "pool"):
                if e != eng and self.ecount[e] > 0:
                    self._wait(eng, self.esem[e], self.ecount[e])
        for op in self.ops:
            op.done = True
        self.ops = []
        self.emitted = 0
        self.state = {}

    def begin_phase(self):
        self.phase_es = ExitStack()

    def end_phase(self):
        self.barrier()
        self.phase_es.close()
        self.phase_es = None

    def close(self):
        self.es.close()


D = 1024
T = 2560
LS, LP = 2048, 256
SEQS = [(0, 2048), (2048, 2304), (2304, 2560)]
NTB = 5
NTC = 20
FF = 2816
IN_AB = 3600
EPS = 1e-6
MAGIC = 12582912.0
TWO_PI = 2.0 * math.pi
NEG = -30000.0


def hy_cfg(L):
    N = 2 * L
    na = L // 128
    nb = (L + 1 + 127) // 128
    return dict(L=L, N=N, na=na, nb=nb)


def build(debug=None):
    nc = bass.Bass("TRN2", target_bir_lowering=False)
    P = Prog(nc)

    def din(name, shape, dt=F32):
        return nc.dram_tensor(name, list(shape), dt, kind="ExternalInput").ap()

    def dout(name, shape, dt=F32):
        return nc.dram_tensor(name, list(shape), dt, kind="ExternalOutput").ap()

    def dbgdump(name, ap, dt=F32):
        if debug is None:
            return
        o = dout("dbgm_" + name, list(ap.shape), dt)
        P.dma(o, ap)

    xT = din("xT", [8, 128, T])
    cv = din("cv", [128, 8, 2])
    st_C = din("st_C", [8, 128, 128])
    st_n = din("st_n", [128, 8])
    st_m = din("st_m", [128, 8])
    kcT = din("kcT", [8, 128, 512])
    vc = din("vc", [16, 512, 64])
    w_ada = din("w_ada", [2, D, 6 * D])
    b_adaP = din("b_adaP", [128, 2, 48])
    gP = din("gP", [128, 5, 8])
    w_in_ab = din("w_in_ab", [D, IN_AB])
    b_gates = din("b_gates", [1, 16])
    wcqk = din("wcqk", [128, 8, 3])
    g_ml = din("g_ml", [128, 4])
    wchy = din("wchy", [128, 12, 3])
    w_f1 = din("w_f1", [33, 64])
    w_f2 = din("w_f2", [64, 64])
    w_f3 = din("w_f3", [64, 1024])
    fvec = din("fvec", [64, 3])
    hyb = din("hyb", [128, 4])
    w_out_ab = din("w_out_ab", [D, D])
    w_in_c = din("w_in_c", [D, 3 * D])
    TmE = din("TmE", [16, 128, 15, 64])
    w_out_c = din("w_out_c", [D, D])
    w_up = din("w_up", [2, D, 2 * FF])
    wcffn = din("wcffn", [128, 2, 22, 3])
    w_down = din("w_down", [2, FF, D])
    cst = din("cst", [128, 6, 128])
    deltas = din("deltas", [1, 1024])
    hc = {}
    for L in (LS, LP):
        c = hy_cfg(L)
        c["TF"] = din("TF%d" % L, [c["nb"], 128, c["na"], 2, 128])
        c["TI"] = din("TI%d" % L, [128, c["nb"], 2, L])
        c["zT"] = din("zT%d" % L, [33, L])
        c["tneg"] = din("tneg%d" % L, [128, c["na"]])
        c["wt"] = din("wt%d" % L, [128, c["nb"], 2])
        hc[L] = c

    yT = dout("yT", [8, 128, T])
    o_C = dout("o_C", [2, 2, 4, 128, 128])
    o_n = dout("o_n", [2, 2, 4, 128])
    o_m = dout("o_m", [2, 2, 4])
    o_k = dout("o_k", [2, 16, 256, 64])
    o_v = dout("o_v", [2, 16, 256, 64])

    X = P.dram("X", [8, 128, T], F32)
    HN = P.dram("HN", [8, 128, T], BF16)
    QK = P.dram("QK", [16, 128, T], BF16)
    V = P.dram("V", [NTC, 128, 1024], BF16)
    V64 = P.dram("V64", [16, 128, 1024], BF16)
    SO = P.dram("SO", [4, 128, T], BF16)
    G = P.dram("G", [128, NTC, 16], F32)
    HY = P.dram("HY", [12, 128, T], F32)
    U = P.dram("U", [4, 128, T], F32)
    HF = P.dram("HF", [4, 128, T], F32)
    HB = P.dram("HB", [4, 128, T], F32)
    Y = P.dram("Y", [8, 128, T], BF16)
    ACTS = P.dram("ACTS", [22, 128, T], BF16)
    for L in (LS, LP):
        hc[L]["FL"] = P.dram("FL%d" % L, [hc[L]["na"], 128, 1024], BF16)

    cst_t = P.sb("cst", [128, 6, 128], F32, persist=True)
    ident = cst_t[:, 0, :]
    tri = [cst_t[:, 1, :], cst_t[:, 2, :]]
    maskb = [cst_t[:, 3, :], cst_t[:, 4, :]]
    colmask = cst_t[:, 5, 0:64]
    ident_bf = P.sb("identbf", [128, 128], BF16, persist=True)
    ones_f = P.sb("onesf", [128, 128], F32, persist=True)
    ones_bf = P.sb("onesbf", [128, 128], BF16, persist=True)
    epsc = P.sb("epsc", [128, 1], F32, persist=True)
    modt = P.sb("modt", [128, 2, 48, 2], F32, persist=True)
    Atab = P.sb("Atab", [128, 2, 2, 8, 2], F32, persist=True)
    gPt = P.sb("gPt", [128, 5, 8], F32, persist=True)
    wst = [P.sb("wst%d" % i, [128, 4096], F32, persist=True) for i in range(2)]
    wbf = [P.sb("wbf%d" % i, [128, 4096], BF16, persist=True) for i in range(2)]
    wctr = [0]
    scv_p = P.sb("scv", [128, 8, 2], F32, persist=True)
    bt_p = P.sb("bada", [128, 2, 48], F32, persist=True)

    P.begin_phase()
    P.dma(cst_t[:], cst[:, :, :])
    P.dma(gPt[:], gP[:, :, :])
    P.v("tensor_copy", out=ident_bf[:], in_=ident)
    P.v("memset", ap=ones_f[:], constant=1.0)
    P.v("memset", ap=ones_bf[:], constant=1.0)
    P.v("memset", ap=epsc[:], constant=EPS)

    wplan = {"reqs": None}

    def w_plan(reqs):
        mx_e = max(r[1] * r[3] for r in reqs)
        wplan.update(reqs=reqs, cur=0, dma=0, cast=0, nslots=(2 if mx_e > 1024 else 8),
                     se=(4096 if mx_e > 1024 else 1024), base=wctr[0])
        wctr[0] += 1

    def _w_views(k):
        Wap, nk, c0, ncols = wplan["reqs"][k]
        slot = k % wplan["nslots"]
        off = slot * wplan["se"]
        ti, o = off // 4096, off % 4096
        tag = "w%d_%d" % (wplan["base"], slot)
        sv = wst[ti][:, o:o + nk * ncols].rearrange("p (k n) -> p k n", k=nk)
        bv = wbf[ti][:, o:o + nk * ncols].rearrange("p (k n) -> p k n", k=nk)
        return Wap, nk, c0, ncols, sv, bv, tag

    def _w_dma(k):
        Wap, nk, c0, ncols, sv, bv, tag = _w_views(k)
        P.dma(sv, Wap[0:nk * 128, c0:c0 + ncols].rearrange("(k p) n -> p k n", p=128), q="sp", wt=tag)

    def _w_cast(k):
        Wap, nk, c0, ncols, sv, bv, tag = _w_views(k)
        P.act(bv, sv, AF.Identity, rt=tag, wt=tag)

    def w_tick():
        if wplan["reqs"] is None:
            return
        k = wplan["cur"]
        if k < len(wplan["reqs"]) and wplan["cast"] <= k and wplan["dma"] > k:
            _w_cast(k)
            wplan["cast"] = k + 1

    def w_end():
        wplan["reqs"] = None

    def load_w(Wap, k0, nk, c0, ncols, cast=True):
        if wplan["reqs"] is None or not cast:
            i = wctr[0]
            wctr[0] += 1
            sv = wst[i % 2][:, 0:nk * ncols].rearrange("p (k n) -> p k n", k=nk)
            P.dma(sv, Wap[k0 * 128:(k0 + nk) * 128, c0:c0 + ncols].rearrange("(k p) n -> p k n", p=128), q="sp")
            if not cast:
                return sv
            bv = wbf[i % 2][:, 0:nk * ncols].rearrange("p (k n) -> p k n", k=nk)
            P.act(bv, sv, AF.Identity)
            return bv, None
        k = wplan["cur"]
        rq = wplan["reqs"][k]
        assert rq[1] == nk and rq[2] == c0 and rq[3] == ncols, (rq[1:], nk, c0, ncols)
        while wplan["dma"] <= k:
            _w_dma(wplan["dma"])
            wplan["dma"] += 1
        while wplan["cast"] <= k:
            _w_cast(wplan["cast"])
            wplan["cast"] += 1
        wplan["cur"] = k + 1
        depth = 1 if wplan["nslots"] == 2 else 3
        while wplan["dma"] < min(len(wplan["reqs"]), k + 1 + depth):
            _w_dma(wplan["dma"])
            wplan["dma"] += 1
        return _w_views(k)[5], _w_views(k)[6]

    def linear_fm(Wap, nk, c0, ncols_total, src, psums, epi, tbs=range(NTB), chunk0=0, ensure=None):
        pi = [0]
        done = 0
        pending = []
        while done < ncols_total:
            nc_ = min(512, ncols_total - done)
            bv, wtag = load_w(Wap, 0, nk, c0 + done, nc_)
            nm = nc_ // 128
            for m in range(nm):
                new_pending = []
                for tb in tbs:
                    if ensure is not None:
                        ensure(tb)
                    ps = psums[pi[0] % len(psums)]
                    pi[0] += 1
                    for kc in range(nk):
                        P.mm(ps[:, :], bv[:, kc, m * 128:(m + 1) * 128], src[:, kc, tb * 512:(tb + 1) * 512],
                             start=(kc == 0), stop=(kc == nk - 1), lt=wtag, rtg=tb)
                    t_ = epi(chunk0 + (done // 128) + m, tb, ps)
                    if t_ is not None:
                        new_pending.append(t_)
                    if m == (nm - 1) // 2 and tb == 2:
                        w_tick()
                for t_ in pending:
                    t_()
                pending = new_pending
            done += nc_
        for t_ in pending:
            t_()

    def linear_tm(Wap, nk, c0, ncols, src, psums, epi, tok_chunks, tok_off=0):
        bv, wtag = load_w(Wap, 0, nk, c0, ncols)
        tok_chunks = list(tok_chunks)
        for j, tc in enumerate(tok_chunks):
            ps = psums[j % len(psums)]
            t0 = tok_off + tc * 128
            for kc in range(nk):
                P.mm(ps[:, 0:ncols], src[:, kc, t0:t0 + 128], bv[:, kc, :], start=(kc == 0), stop=(kc == nk - 1),
                     lt=(t0 // 512 if tok_off == 0 else None), rtg=wtag)
            epi(tc, ps)
            if j == len(tok_chunks) // 2:
                w_tick()

    def conv3(acc, raw, taps):
        P.v("tensor_scalar", out=acc[:, :], in0=raw[:, :], scalar1=taps[:, 1:2], scalar2=None, op0=ALU.mult)
        for (s0, s1) in SEQS:
            P.v("scalar_tensor_tensor", out=acc[:, s0 + 1:s1], in0=raw[:, s0:s1 - 1], scalar=taps[:, 0:1],
                in1=acc[:, s0 + 1:s1], op0=ALU.mult, op1=ALU.add)
            P.v("scalar_tensor_tensor", out=acc[:, s0:s1 - 1], in0=raw[:, s0 + 1:s1], scalar=taps[:, 2:3],
                in1=acc[:, s0:s1 - 1], op0=ALU.mult, op1=ALU.add)

    def adaln_layer(l, bufs, pa, pre_hook=None):
        nb_ = len(bufs)

        def dma_(s_):
            sv_ = bufs[s_ % nb_][:, :].rearrange("p (k n) -> p k n", k=8)
            P.dma(sv_, w_ada[l][:, s_ * 512:(s_ + 1) * 512].rearrange("(k p) n -> p k n", p=128), q="sp")
        for s_ in range(min(12, nb_)):
            dma_(s_)
        if pre_hook is not None:
            pre_hook()
        for s_ in range(12):
            sv = bufs[s_ % nb_][:, :].rearrange("p (k n) -> p k n", k=8)
            if s_ >= nb_:
                dma_(s_)
            ps = pa[s_ % len(pa)]
            for m in range(4):
                for kc in range(8):
                    P.mm(ps[:, 2 * m:2 * m + 2], sv[:, kc, m * 128:(m + 1) * 128], scv_p[:, kc, :],
                         start=(kc == 0), stop=(kc == 7))
            for m in range(4):
                n = s_ * 4 + m
                P.v("tensor_scalar", out=modt[:, l, n, :], in0=ps[:, 2 * m:2 * m + 2], scalar1=bt_p[:, l, n:n + 1],
                    scalar2=None, op0=ALU.add)
        for sub in range(2):
            sc0 = 8 + 24 * sub
            P.v("tensor_scalar", out=Atab[:, l, sub, :, :], in0=modt[:, l, sc0:sc0 + 8, :], scalar1=1.0,
                scalar2=None, op0=ALU.add)
            for j in range(2):
                P.v("tensor_tensor", out=Atab[:, l, sub, :, j], in0=Atab[:, l, sub, :, j],
                    in1=gPt[:, 2 * sub + l, :], op=ALU.mult)

    def phase_adaln():
        pa = [P.ps("pa%d" % i) for i in range(2)]
        P.dma(scv_p[:], cv[:, :, :])
        P.dma(bt_p[:], b_adaP[:, :, :])
        P.act(scv_p[:], scv_p[:], AF.Silu)
        extra = [P.sb("adx%d" % i, [128, 4096], F32) for i in range(4)]
        adaln_layer(0, wst + extra, pa)

    def phase_norm(l, sub, src, final=False):
        xb = [P.sb("xb%d" % i, [128, 8, 512], F32) for i in range(2)]
        sq = [P.sb("sq%d" % i, [128, 8, 512], BF16) for i in range(2)]
        rs = [P.sb("rs%d" % i, [128, 512], F32) for i in range(2)]
        tmp = [P.sb("tmp%d" % i, [128, 8, 512], F32) for i in range(2)]
        hb = [P.sb("hb%d" % i, [128, 8, 512], (F32 if final else BF16)) for i in range(2)]
        pn = [P.ps("pn%d" % i) for i in range(2)]
        for tb in range(NTB):
            j = 0 if tb < 4 else 1
            i = tb % 2
            tsl = slice(tb * 512, (tb + 1) * 512)
            P.dma(xb[i][:], src[:, :, tsl].rearrange("c p t -> p c t"))
            P.act(sq[i][:], xb[i][:], AF.Square)
            for kc in range(8):
                P.mm(pn[i][:, :], ones_bf[:], sq[i][:, kc, :], start=(kc == 0), stop=(kc == 7))
            P.act(rs[i][:], pn[i][:, :], AF.Sqrt, scale=1.0 / D, bias=epsc[:, 0:1])
            P.v("reciprocal", out=rs[i][:], in_=rs[i][:])
            for kc in range(8):
                if final:
                    P.v("scalar_tensor_tensor", out=hb[i][:, kc, :], in0=xb[i][:, kc, :], scalar=gPt[:, 4, kc:kc + 1],
                        in1=rs[i][:], op0=ALU.mult, op1=ALU.mult, rt=kc, wt=kc)
                else:
                    P.v("scalar_tensor_tensor", out=tmp[i][:, kc, :], in0=xb[i][:, kc, :],
                        scalar=Atab[:, l, sub, kc, j:j + 1], in1=rs[i][:], op0=ALU.mult, op1=ALU.mult, rt=kc, wt=kc)
                    P.act(hb[i][:, kc, :], tmp[i][:, kc, :], AF.Identity, bias=modt[:, l, 24 * sub + kc, j:j + 1],
                          rt=kc, wt=kc)
            dst = yT if final else HN
            P.dma(dst[:, :, tsl].rearrange("c p t -> p c t"), hb[i][:], q="pool")
            if tb == 0 and l == 0 and sub == 0 and not final:
                dbgdump("rs", rs[i][:])
                dbgdump("xb", xb[i][:, 0, :])
                dbgdump("tmp", tmp[i][:, 0, :])
                dbgdump("atab", Atab[:].rearrange("p a b c d -> p (a b c d)"))
                dbgdump("modt", modt[:].rearrange("p a b c -> p (a b c)"))

    def norm_hn(l, sub, src):
        hn = P.sb("hn", [128, 8, T], BF16)
        xb = [P.sb("nxb%d" % i, [128, 8, 512], F32) for i in range(2)]
        rs = [P.sb("nrs%d" % i, [128, 512], F32) for i in range(2)]
        pn = [P.ps("npn%d" % i) for i in range(2)]
        st_ = {"dma": 0, "done": 0}

        def dma(tb):
            tsl = slice(tb * 512, (tb + 1) * 512)
            P.dma(xb[tb % 2][:], src[:, :, tsl].rearrange("c p t -> p c t"), q=("pool" if tb % 2 == 0 else "sp"))

        def ensure(tb):
            while st_["done"] <= tb:
                t = st_["done"]
                while st_["dma"] < min(NTB, t + 2):
                    dma(st_["dma"])
                    st_["dma"] += 1
                i = t % 2
                j = 0 if t < 4 else 1
                tsl = slice(t * 512, (t + 1) * 512)
                P.act(hn[:, :, tsl], xb[i][:], AF.Square, wt=t)
                for kc in range(8):
                    P.mm(pn[i][:, :], ones_bf[:], hn[:, kc, tsl], start=(kc == 0), stop=(kc == 7), rtg=t)
                P.act(rs[i][:], pn[i][:, :], AF.Sqrt, scale=1.0 / D, bias=epsc[:, 0:1])
                P.v("reciprocal", out=rs[i][:], in_=rs[i][:])
                for kc in range(8):
                    P.v("scalar_tensor_tensor", out=xb[i][:, kc, :], in0=xb[i][:, kc, :],
                        scalar=Atab[:, l, sub, kc, j:j + 1], in1=rs[i][:], op0=ALU.mult, op1=ALU.mult, rt=kc, wt=kc)
                for kc in range(8):
                    P.add("act", "activation", dict(out=hn[:, kc, tsl], in_=xb[i][:, kc, :], func=AF.Identity,
                                                    bias=modt[:, l, 24 * sub + kc, j:j + 1]),
                          r=[(xb[i], kc), modt], w=[(hn, t)])
                st_["done"] += 1
        return hn, ensure

    def load_hn():
        hn = P.sb("hn", [128, 8, T], BF16)
        for tb in range(NTB):
            tsl = slice(tb * 512, (tb + 1) * 512)
            P.dma(hn[:, :, tsl], HN[:, :, tsl].rearrange("c p t -> p c t"), q="pool", wt=tb)
        return hn

    def phase_proj_ab():
        w_plan([(w_in_ab, 8, c_, n_) for (c_, n_) in [(0, 512), (512, 512), (1024, 512), (1536, 512), (2048, 16),
                                                      (2064, 512), (2576, 512), (3088, 512)]])
        hn, ens = norm_hn(0, 0, xT)
        raw = [P.sb("raw%d" % i, [128, T], F32) for i in range(2)]
        acc = [P.sb("acc%d" % i, [128, T], F32) for i in range(2)]
        obf = [P.sb("obf%d" % i, [128, T], BF16) for i in range(2)]
        tq = P.sb("tq", [128, 8, 3], F32)
        th = P.sb("th", [128, 12, 3], F32)
        bg = P.sb("bg", [128, 16], F32)
        gt = P.sb("gt", [128, NTC, 16], F32)
        gtmp = P.sb("gtmp", [128, NTC, 8], F32)
        vb = [P.sb("vb%d" % i, [128, 512], BF16) for i in range(2)]
        sob = [P.sb("sob%d" % i, [128, 512], BF16) for i in range(2)]
        pp = [P.ps("pp%d" % i) for i in range(4)]
        P.dma(tq[:], wcqk[:, :, :])
        P.dma(th[:], wchy[:, :, :])
        P.dma(bg[:], b_gates[0:1, :].broadcast_to([128, 16]))

        def epi_qk(n, tb, ps):
            i = n % 2
            P.act(raw[i][:, tb * 512:(tb + 1) * 512], ps[:, :], AF.Identity)
            if tb == NTB - 1:
                def tail():
                    conv3(acc[i], raw[i], tq[:, n, :])
                    P.act(acc[i][:], acc[i][:], AF.Silu)
                    P.v("tensor_scalar", out=obf[i][:], in0=acc[i][:], scalar1=(1.0 if n < 4 else 128.0 ** -0.5),
                        scalar2=None, op0=ALU.mult)
                    P.dma(QK[n], obf[i][:], q="pool")
                return tail
        linear_fm(w_in_ab, 8, 0, 1024, hn, pp, epi_qk, ensure=ens)

        def epi_v(tc, ps):
            i = tc % 2
            P.act(vb[i][:], ps[:, :], AF.Identity)
            P.dma(V[tc, :, 0:512], vb[i][:], q="pool")
        linear_tm(w_in_ab, 8, 1024, 512, hn, pp, epi_v, range(NTC))

        def epi_o(n, tb, ps):
            i = (n * NTB + tb) % 2
            P.act(sob[i][:], ps[:, :], AF.Sigmoid)
            P.dma(SO[n, :, tb * 512:(tb + 1) * 512], sob[i][:], q="pool")
        linear_fm(w_in_ab, 8, 1536, 512, hn, pp, epi_o)

        def epi_g(tc, ps):
            P.v("tensor_tensor", out=gt[:, tc, :], in0=ps[:, 0:16], in1=bg[:], op=ALU.add)
        linear_tm(w_in_ab, 8, 2048, 16, hn, pp, epi_g, range(NTC))
        for d in range(2):
            fs = gt[:, :, 4 + 8 * d:8 + 8 * d]
            ts = gtmp[:, :, 4 * d:4 * d + 4]
            P.act(ts, fs, AF.Exp, scale=-1.0)
            P.act(ts, ts, AF.Ln, bias=ones_f[:, 0:1])
            P.v("tensor_scalar", out=fs, in0=ts, scalar1=-1.0, scalar2=None, op0=ALU.mult)
        P.dma(G[:, :, :], gt[:], q="pool")

        def epi_hy(n, tb, ps):
            i = n % 2
            P.act(raw[i][:, tb * 512:(tb + 1) * 512], ps[:, :], AF.Identity)
            if tb == NTB - 1:
                def tail():
                    conv3(acc[i], raw[i], th[:, n, :])
                    P.dma(HY[n], acc[i][:], q="pool")
                return tail
        linear_fm(w_in_ab, 8, 2064, 1536, hn, pp, epi_hy)
        w_end()

    def phase_mlstm():
        gt = P.sb("gt", [128, NTC, 16], F32)
        P.dma(gt[:], G[:, :, :])
        stn = P.sb("stn", [128, 8], F32)
        stm = P.sb("stm", [128, 8], F32)
        P.dma(stn[:], st_n[:, :])
        P.dma(stm[:], st_m[:, :])
        P.act(stm[:], stm[:], AF.Exp)
        Cst = P.sb("Cst", [128, 4, 128], F32)
        Cbf = P.sb("Cbf", [128, 4, 128], BF16)
        nst = P.sb("nst", [128, 4], F32)
        nrep = P.sb("nrep", [128, 4, 128], BF16)
        mrun = P.sb("mrun", [128, 4], F32)
        ktm_all = P.sb("ktmall", [128, NTC, 512], BF16)
        H4 = range(4)
        NB = 2

        def mk(name, shape, dt):
            return [[P.sb("%s%d_%d" % (name, b, h), shape, dt) for h in H4] for b in range(NB)]
        qTt = [P.sb("qTt%d" % i, [128, 4, 128], BF16) for i in range(3)]
        kTt = [P.sb("kTt%d" % i, [128, 4, 128], BF16) for i in range(3)]
        vch = [P.sb("vch%d" % i, [128, 512], BF16) for i in range(3)]
        acol = [P.sb("acol%d" % i, [128, 4], F32) for i in range(NB)]
        hfo = [P.sb("hfo%d" % i, [128, 4, 128], F32) for i in range(NB)]
        lfrep, brow, arg, eb = mk("lfrep", [128, 128], F32), mk("brow", [128, 128], F32), mk("arg", [128, 128], F32), mk("eb", [128, 128], F32)
        PT, qt, kw = mk("PT", [128, 128], BF16), mk("qt", [128, 128], BF16), mk("kw", [128, 128], BF16)
        wcol, tmx = mk("wcol", [128, 2], F32), mk("tmx", [128, 1], F32)
        irep, te = mk("irep", [128, 128], F32), mk("te", [128, 128], F32)
        dd = [P.sb("dd%d" % h, [128, 128], F32) for h in H4]
        c0t = [P.sb("c0t%d" % i, [128, 128], F32) for i in H4]
        cout = P.sb("cout", [128, 4, 130], F32)
        pA = [P.ps("pmA%d" % i) for i in H4]
        pND = [P.ps("pmN%d" % i) for i in range(2)]
        pCU = [P.ps("pmCU%d" % i) for i in range(NB)]

        for tc in range(NTC):
            i = tc % 3
            P.dma(kTt[i][:], QK[4:8, :, tc * 128:(tc + 1) * 128].rearrange("h p t -> p h t"))
            for h in H4:
                P.tr(pND[tc % 2][:, :].bitcast(BF16)[:, h * 128:(h + 1) * 128], kTt[i][:, h, :], ident_bf[:])
            P.act(ktm_all[:, tc, :], pND[tc % 2][:, :].bitcast(BF16)[:, 0:512], AF.Identity)

        steps = []
        for si, (s0, s1) in enumerate(SEQS):
            nch = (s1 - s0) // 128
            for d in range(2):
                order = list(range(nch)) if d == 0 else list(range(nch - 1, -1, -1))
                for j, c in enumerate(order):
                    steps.append(dict(si=si, d=d, c=c, first=(j == 0), last=(j == nch - 1), t0=s0 + c * 128))
        ld = [0]

        def front(k):
            st = steps[k]
            b = k % NB
            d, t0 = st["d"], st["t0"]
            prompt = st["si"] > 0
            tc = t0 // 128
            li = ld[0] % 3
            ld[0] += 1
            st["li"] = li
            bend_c = 127 if d == 0 else 0
            g = gt[:, tc, :]

            def f0():
                P.dma(qTt[li][:], QK[0:4, :, t0:t0 + 128].rearrange("h p t -> p h t"))
                P.dma(kTt[li][:], QK[4:8, :, t0:t0 + 128].rearrange("h p t -> p h t"))
                P.dma(vch[li][:], V[tc, :, 0:512])
                P.mm(pA[0][:, 392 + 4 * b:396 + 4 * b], tri[d], g[:, 4 + 8 * d:8 + 8 * d])
                P.v("tensor_tensor", out=acol[b][:], in0=g[:, 8 * d:8 * d + 4], in1=pA[0][:, 392 + 4 * b:396 + 4 * b],
                    op=ALU.subtract)

            def f1():
                for h in H4:
                    P.act(lfrep[b][h][:], ones_f[:], AF.Identity, scale=g[:, 4 + 8 * d + h:5 + 8 * d + h])
                    if prompt:
                        P.act(irep[b][h][:], ones_f[:], AF.Identity, scale=g[:, 8 * d + h:8 * d + h + 1])

            def f2():
                for h in H4:
                    P.mm(pA[h][:, 0:128], lfrep[b][h][:], tri[d])
                    P.mm(pA[h][:, 256:384], kTt[li][:, h, :], qTt[li][:, h, :])
                    if prompt:
                        P.mm(pA[h][:, 128:256], irep[b][h][:], ident)

            def f3():
                for h in H4:
                    P.act(brow[b][h][:], pA[h][:, 0:128], AF.Identity)

            def f4():
                for h in H4:
                    P.v("scalar_tensor_tensor", out=arg[b][h][:], in0=brow[b][h][:], scalar=acol[b][:, h:h + 1],
                        in1=maskb[d], op0=ALU.add, op1=ALU.add)
                    if prompt:
                        bend = brow[b][h][:, bend_c:bend_c + 1]
                        P.v("scalar_tensor_tensor", out=te[b][h][:], in0=pA[h][:, 128:256], scalar=bend,
                            in1=brow[b][h][:], op0=ALU.add, op1=ALU.subtract)
                        P.v("tensor_reduce", out=tmx[b][h][:], in_=te[b][h][:], axis=AX.X, op=ALU.max)

            def f5():
                for h in H4:
                    bend = brow[b][h][:, bend_c:bend_c + 1]
                    P.act(arg[b][h][:], arg[b][h][:], AF.Exp)
                    P.act(eb[b][h][:], brow[b][h][:], AF.Exp)
                    P.act(wcol[b][h][:, 0:1], acol[b][:, h:h + 1], AF.Exp, bias=bend)
                    P.act(wcol[b][h][:, 1:2], bend, AF.Exp)

            def f6():
                for h in H4:
                    P.v("tensor_tensor", out=PT[b][h][:], in0=pA[h][:, 256:384], in1=arg[b][h][:], op=ALU.mult)
                    P.v("tensor_tensor", out=qt[b][h][:], in0=qTt[li][:, h, :], in1=eb[b][h][:], op=ALU.mult)
                    P.v("tensor_scalar", out=kw[b][h][:], in0=ktm_all[:, tc, h * 128:(h + 1) * 128],
                        scalar1=wcol[b][h][:, 0:1], scalar2=None, op0=ALU.mult)

            def f7():
                for h in H4:
                    P.mm(pCU[b][:, h * 128:(h + 1) * 128], kw[b][h][:], vch[li][:, h * 128:(h + 1) * 128])
                    P.mm(pA[h][:, 384 + b:385 + b], kw[b][h][:], ones_bf[:, 0:1])
            return [f0, f1, f2, f3, f4, f5, f6, f7]

        def back(k):
            st = steps[k]
            b = k % NB
            d, t0, li = st["d"], st["t0"], st["li"]
            prompt = st["si"] > 0
            bend_c = 127 if d == 0 else 0
            HOUT = HF if d == 0 else HB

            def b0():
                if st["first"]:
                    for h in H4:
                        if prompt:
                            P.v("memset", ap=Cst[:, h, :], constant=0.0)
                        else:
                            P.dma(c0t[h][:], st_C[d * 4 + h], q="pool")
                            P.v("tensor_scalar", out=Cst[:, h, :], in0=c0t[h][:], scalar1=stm[:, d * 4 + h:d * 4 + h + 1],
                                scalar2=None, op0=ALU.mult)
                    if prompt:
                        P.v("memset", ap=nst[:], constant=0.0)
                        P.v("memset", ap=mrun[:], constant=0.0)
                    else:
                        P.v("tensor_tensor", out=nst[:], in0=stn[:, d * 4:d * 4 + 4], in1=stm[:, d * 4:d * 4 + 4],
                            op=ALU.mult)
                    for h in H4:
                        P.act(Cbf[:, h, :], Cst[:, h, :], AF.Identity, wt=h)
                        P.act(nrep[:, h, :], ones_f[:], AF.Identity, scale=nst[:, h:h + 1], wt=h)
                for h in H4:
                    nd = pND[h // 2]
                    o0 = (h % 2) * 256
                    P.mm(nd[:, o0:o0 + 128], vch[li][:, h * 128:(h + 1) * 128], PT[b][h][:], start=True, stop=False)
                    P.mm(nd[:, o0:o0 + 128], Cbf[:, h, :], qt[b][h][:], start=False, stop=True, lt=h)
                    P.mm(nd[:, o0 + 128:o0 + 256], ones_bf[:], PT[b][h][:], start=True, stop=False)
                    P.mm(nd[:, o0 + 128:o0 + 256], nrep[:, h, :], qt[b][h][:], start=False, stop=True, lt=h)

            def b1():
                for h in H4:
                    nd = pND[h // 2]
                    o0 = (h % 2) * 256
                    P.act(dd[h][:], nd[:, o0 + 128:o0 + 256], AF.Abs)

            def b2():
                for h in H4:
                    nd = pND[h // 2]
                    o0 = (h % 2) * 256
                    bend = brow[b][h][:, bend_c:bend_c + 1]
                    P.v("tensor_scalar", out=dd[h][:], in0=dd[h][:], scalar1=1.0, scalar2=None, op0=ALU.max)
                    P.v("reciprocal", out=dd[h][:], in_=dd[h][:])
                    P.v("tensor_tensor", out=hfo[b][:, h, :], in0=nd[:, o0:o0 + 128], in1=dd[h][:], op=ALU.mult)
                    if prompt:
                        P.v("scalar_tensor_tensor", out=mrun[:, h:h + 1], in0=mrun[:, h:h + 1], scalar=bend,
                            in1=tmx[b][h][:], op0=ALU.add, op1=ALU.max)
                    P.v("scalar_tensor_tensor", out=Cst[:, h, :], in0=Cst[:, h, :], scalar=wcol[b][h][:, 1:2],
                        in1=pCU[b][:, h * 128:(h + 1) * 128], op0=ALU.mult, op1=ALU.add)
                    P.v("scalar_tensor_tensor", out=nst[:, h:h + 1], in0=nst[:, h:h + 1], scalar=wcol[b][h][:, 1:2],
                        in1=pA[h][:, 384 + b:385 + b], op0=ALU.mult, op1=ALU.add)

            def b3():
                for h in H4:
                    P.act(Cbf[:, h, :], Cst[:, h, :], AF.Identity, wt=h)
                    P.act(nrep[:, h, :], ones_f[:], AF.Identity, scale=nst[:, h:h + 1], wt=h)
                P.dma(HOUT[:, :, t0:t0 + 128].rearrange("h p t -> p h t"), hfo[b][:], q="pool")
                if st["last"] and prompt:
                    pi = st["si"] - 1
                    P.act(cout[:, :, 129:130], mrun[:].rearrange("p (h o) -> p h o", o=1), AF.Exp, scale=-1.0)
                    for h in H4:
                        P.v("tensor_scalar", out=cout[:, h, 0:128], in0=Cst[:, h, :], scalar1=cout[:, h, 129:130],
                            scalar2=None, op0=ALU.mult)
                        P.v("tensor_scalar", out=cout[:, h, 128:129], in0=nst[:, h:h + 1], scalar1=cout[:, h, 129:130],
                            scalar2=None, op0=ALU.mult)
                    P.dma(o_C[pi, d].rearrange("h p e -> p h e"), cout[:, :, 0:128], q="pool")
                    P.dma(o_n[pi, d].rearrange("h (p o) -> p h o", o=1), cout[:, :, 128:129], q="pool", slow=True)
                    P.dma(o_m[pi, d:d + 1, :], mrun[0:1, :], q="pool")
            return [b0, b1, b2, b3]

        for f in front(0):
            f()
        for k in range(len(steps)):
            B = back(k)
            if k + 1 < len(steps):
                F = front(k + 1)
                for fn in (F[0], F[1], B[0], F[2], B[1], F[3], B[2], F[4], B[3], F[5], F[6], F[7]):
                    fn()
            else:
                for fn in B:
                    fn()

    def phase_mlstm_fin():
        gml = P.sb("gml", [128, 4], F32)
        P.dma(gml[:], g_ml[:, :])
        hf = [P.sb("fhf%d" % i, [128, T], F32) for i in range(2)]
        hb_ = [P.sb("fhb%d" % i, [128, T], F32) for i in range(2)]
        so = [P.sb("fso%d" % i, [128, T], BF16) for i in range(2)]
        sq = [P.sb("fsq%d" % i, [128, T], BF16) for i in range(2)]
        rs = [P.sb("frs%d" % i, [128, T], F32) for i in range(2)]
        ym = [P.sb("fym%d" % i, [128, T], BF16) for i in range(2)]
        pn = [P.ps("fpn%d" % i) for i in range(5)]
        for h in range(4):
            i = h % 2
            P.dma(hf[i][:], HF[h])
            P.dma(hb_[i][:], HB[h])
            P.dma(so[i][:], SO[h])
            P.v("tensor_tensor", out=hf[i][:], in0=hf[i][:], in1=hb_[i][:], op=ALU.add)
            P.act(sq[i][:], hf[i][:], AF.Square)
            for tb in range(NTB):
                P.mm(pn[tb][:, :], ones_bf[:], sq[i][:, tb * 512:(tb + 1) * 512])
                P.act(rs[i][:, tb * 512:(tb + 1) * 512], pn[tb][:, :], AF.Ln, scale=1.0 / 128, bias=epsc[:, 0:1])
                P.act(rs[i][:, tb * 512:(tb + 1) * 512], rs[i][:, tb * 512:(tb + 1) * 512], AF.Exp, scale=-0.5)
            P.v("scalar_tensor_tensor", out=hf[i][:], in0=hf[i][:], scalar=gml[:, h:h + 1], in1=rs[i][:], op0=ALU.mult,
                op1=ALU.mult)
            P.v("tensor_tensor", out=ym[i][:], in0=hf[i][:], in1=so[i][:], op=ALU.mult)
            P.dma(Y[h], ym[i][:], q="pool")

    def phase_filters():
        w1 = P.sb("w1", [33, 64], F32)
        w2 = P.sb("w2", [64, 64], F32)
        w3 = P.sb("w3", [64, 1024], F32)
        fv = P.sb("fv", [64, 4], F32)
        dl = P.sb("dl", [128, 512], F32)
        P.dma(w1[:], w_f1[:, :], q="pool")
        P.dma(w2[:], w_f2[:, :], q="pool")
        P.dma(w3[:], w_f3[:, :], q="pool")
        P.dma(fv[:, 0:3], fvec[:, :], q="pool")
        P.dma(dl[:], deltas[0:1, 0:512].broadcast_to([128, 512]), q="pool")
        pg = P.ps("pg")
        pf = [P.ps("pff%d" % i) for i in range(2)]
        z = P.sb("z", [33, LS], F32)
        h1 = P.sb("h1", [64, LS], F32)
        h2 = P.sb("h2", [64, LS], F32)
        a1 = P.sb("a1", [64, 512], F32)
        a2 = P.sb("a2", [64, 512], F32)
        dec = P.sb("dec", [128, 512], F32)
        tng = P.sb("tng", [128, 16], F32)
        fo = [P.sb("fo%d" % i, [128, 1024], BF16) for i in range(2)]
        frb2 = P.sb("frb2", [64, 1], F32)
        P.v("tensor_scalar", out=fv[:, 3:4], in0=fv[:, 0:1], scalar1=fv[:, 2:3], scalar2=None, op0=ALU.mult)
        P.v("tensor_scalar", out=frb2[:], in0=fv[:, 1:2], scalar1=fv[:, 2:3], scalar2=None, op0=ALU.mult)

        def sin_layer(dst, n, bias_ap):
            P.act(a1[:, 0:n], pg[0:64, 0:n], AF.Identity, scale=fv[:, 2:3], bias=bias_ap)
            P.v("tensor_scalar", out=a2[:, 0:n], in0=a1[:, 0:n], scalar1=1.0 / TWO_PI, scalar2=MAGIC, op0=ALU.mult,
                op1=ALU.add)
            P.v("tensor_scalar", out=a2[:, 0:n], in0=a2[:, 0:n], scalar1=MAGIC, scalar2=None, op0=ALU.subtract)
            P.v("scalar_tensor_tensor", out=a1[:, 0:n], in0=a2[:, 0:n], scalar=-TWO_PI, in1=a1[:, 0:n], op0=ALU.mult,
                op1=ALU.add)
            P.v("tensor_scalar", out=a1[:, 0:n], in0=a1[:, 0:n], scalar1=-3.141592, scalar2=3.141592, op0=ALU.max,
                op1=ALU.min)
            P.act(dst, a1[:, 0:n], AF.Sin)

        for L in (LS, LP):
            c = hc[L]
            na = c["na"]
            P.dma(z[:, 0:L], c["zT"][:, :], q="pool")
            P.dma(tng[:, 0:na], c["tneg"][:, :], q="pool")
            nblk = max(1, L // 512)
            n = min(L, 512)
            for b in range(nblk):
                sl = slice(b * n, (b + 1) * n)
                P.mm(pg[0:64, 0:n], w1[:, :], z[:, sl])
                sin_layer(h1[:, sl], n, fv[:, 3:4])
            for b in range(nblk):
                sl = slice(b * n, (b + 1) * n)
                P.mm(pg[0:64, 0:n], w2[:, :], h1[:, sl])
                sin_layer(h2[:, sl], n, frb2[:, 0:1])
            for a in range(na):
                i = a % 2
                P.act(dec[:], dl[:], AF.Exp, scale=tng[:, a:a + 1])
                for half in range(2):
                    ps = pf[half]
                    P.mm(ps[:, :], h2[:, a * 128:(a + 1) * 128], w3[:, half * 512:(half + 1) * 512])
                    dst = fo[i][:, half * 512:(half + 1) * 512]
                    P.v("tensor_tensor", out=dst, in0=ps[:, :], in1=dec[:], op=ALU.mult)
                    if half == 1 and a == 0:
                        P.v("tensor_scalar", out=dst, in0=dst, scalar1=cst_t[:, 5, 64:65], scalar2=None, op0=ALU.mult)
                P.dma(c["FL"][a], fo[i][:], q="pool")

    def phase_filters_adaln1():
        extra = [P.sb("adx%d" % i, [128, 4096], F32) for i in range(6)]
        pa = [P.ps("pa%d" % i) for i in range(2)]
        adaln_layer(1, wst + extra, pa, pre_hook=phase_filters)

    def phase_hyena():
        hb_t = P.sb("hybt", [128, 4], F32)
        P.dma(hb_t[:], hyb[:, :])
        pf = [P.ps("pf%d" % i) for i in range(6)]
        p_tr = P.ps("ptrh", [128, 1024], BF16)
        rhs_all = P.sb("rhsall", [128, 16, 1536], BF16)
        GH = P.sb("GH", [128, 17, 2, 512], BF16)
        tis = [P.sb("tis%d" % i, [128, 2, 512], F32) for i in range(3)]
        tib = [P.sb("tib%d" % i, [128, 2, 512], BF16) for i in range(3)]
        wtt = P.sb("wtt", [128, 17, 2], F32)
        ua = [P.sb("ua%d" % i, [128, 512], F32) for i in range(4)]
        ub = [P.sb("ub%d" % i, [128, 512], BF16) for i in range(2)]
        e1 = [P.sb("e1_%d" % i, [128, 512], F32) for i in range(12)]
        m1 = [P.sb("m1_%d" % i, [128, 512], F32) for i in range(2)]
        uc = [P.sb("uc%d" % i, [128, 512], F32) for i in range(2)]
        x2c = [P.sb("x2c%d" % i, [128, 512], F32) for i in range(2)]
        yh = [P.sb("yh%d" % i, [128, 512], BF16) for i in range(2)]
        tctr = [0]

        def hyena_seq(c, s0, load_filt):
            L, na, nb = c["L"], c["na"], c["nb"]
            n = min(L, 512)
            nblk = max(1, L // 512)
            if load_filt:
                P.dma(wtt[:, 0:nb, :], c["wt"][:, :, :])
                P.dma(rhs_all[:, 0:na, 512:1536], c["FL"][:, :, :].rearrange("a p n -> p a n"))
            for cc in range(4):
                for tb in range(nblk):
                    i = (cc * nblk + tb) % 2
                    tsl = slice(s0 + tb * n, s0 + (tb + 1) * n)
                    P.dma(ua[i][:, 0:n], HY[cc, :, tsl])
                    P.dma(ua[2 + i][:, 0:n], HY[4 + cc, :, tsl])
                    P.v("tensor_tensor", out=ua[i][:, 0:n], in0=ua[i][:, 0:n], in1=ua[2 + i][:, 0:n], op=ALU.mult)
                    P.dma(U[cc, :, tsl], ua[i][:, 0:n])
                    P.v("tensor_copy", out=ub[i][:, 0:n], in_=ua[i][:, 0:n])
                    nn = n // 128
                    for a in range(nn):
                        P.tr(p_tr[:, a * 128:(a + 1) * 128], ub[i][:, a * 128:(a + 1) * 128], ident_bf[:])
                    a0 = tb * 4
                    P.act(rhs_all[:, a0:a0 + nn, cc * 128:(cc + 1) * 128],
                          p_tr[:, 0:nn * 128].rearrange("p (a t) -> p a t", a=nn), AF.Identity)
            def load_tf(b):
                i = wctr[0] % 2
                wctr[0] += 1
                sv = wst[i][:, 0:na * 256].rearrange("p (a s f) -> p a s f", a=na, s=2)
                bv = wbf[i][:, 0:na * 256].rearrange("p (a s f) -> p a s f", a=na, s=2)
                P.dma(sv, c["TF"][b], q="pool")
                P.act(bv, sv, AF.Identity)
                return bv
            nxt = load_tf(0)
            for b in range(nb):
                bv = nxt
                if b + 1 < nb:
                    nxt = load_tf(b + 1)
                for a in range(na):
                    for j in range(3):
                        P.mm(pf[j][:, :], bv[:, a, 0, :], rhs_all[:, a, j * 512:(j + 1) * 512], start=(a == 0),
                             stop=(a == na - 1))
                        P.mm(pf[3 + j][:, :], bv[:, a, 1, :], rhs_all[:, a, j * 512:(j + 1) * 512], start=(a == 0),
                             stop=(a == na - 1))
                wtc = wtt[:, b, 0:1]
                nwtc = wtt[:, b, 1:2]
                ee = e1[(b % 2) * 6:(b % 2) * 6 + 6]
                au, ap_, aq, bu, bp, bq = ee
                P.act(au[:], pf[0][:, :], AF.Identity)
                P.act(ap_[:], pf[1][:, :], AF.Identity, scale=wtc)
                P.act(aq[:], pf[2][:, :], AF.Identity, scale=wtc)
                P.act(bu[:], pf[3][:, :], AF.Identity)
                P.act(bp[:], pf[4][:, :], AF.Identity, scale=wtc)
                P.act(bq[:], pf[5][:, :], AF.Identity, scale=nwtc)
                Kr, Ki = ap_, bp
                P.v("tensor_tensor", out=Kr[:], in0=ap_[:], in1=aq[:], op=ALU.add)
                P.v("tensor_tensor", out=Ki[:], in0=bp[:], in1=bq[:], op=ALU.add)
                P.v("tensor_tensor", out=m1[0][:], in0=au[:], in1=Kr[:], op=ALU.mult)
                P.v("tensor_tensor", out=m1[1][:], in0=bu[:], in1=Ki[:], op=ALU.mult)
                P.v("tensor_tensor", out=GH[:, b, 0, :], in0=m1[0][:], in1=m1[1][:], op=ALU.subtract)
                P.v("tensor_tensor", out=m1[0][:], in0=au[:], in1=Ki[:], op=ALU.mult)
                P.v("tensor_tensor", out=m1[1][:], in0=bu[:], in1=Kr[:], op=ALU.mult)
                P.v("tensor_tensor", out=GH[:, b, 1, :], in0=m1[0][:], in1=m1[1][:], op=ALU.add)
            seq = [(tb, b) for tb in range(nblk) for b in range(nb)]
            st_ = {"dma": 0, "cast": 0}

            def ti_dma(k):
                tb_k, b_k = seq[k]
                i = (tctr[0] + k) % 3
                P.dma(tis[i][:, :, 0:n], c["TI"][:, b_k, :, tb_k * n:(tb_k + 1) * n], q=("pool" if k % 2 == 0 else "sp"))

            def ti_cast(k):
                i = (tctr[0] + k) % 3
                P.act(tib[i][:, :, 0:n], tis[i][:, :, 0:n], AF.Identity)

            def ti_get(k):
                while st_["dma"] < min(len(seq), k + 3):
                    ti_dma(st_["dma"])
                    st_["dma"] += 1
                while st_["cast"] < min(len(seq), k + 2):
                    ti_cast(st_["cast"])
                    st_["cast"] += 1
                return tib[(tctr[0] + k) % 3]
            kk = 0
            for tb in range(nblk):
                for b in range(nb):
                    tb_ = ti_get(kk)
                    kk += 1
                    for cc in range(4):
                        P.mm(pf[cc][:, 0:n], GH[:, b, 0, cc * 128:(cc + 1) * 128], tb_[:, 0, 0:n], start=(b == 0),
                             stop=False)
                        P.mm(pf[cc][:, 0:n], GH[:, b, 1, cc * 128:(cc + 1) * 128], tb_[:, 1, 0:n], start=False,
                             stop=(b == nb - 1))
                for cc in range(4):
                    ps = pf[cc]
                    i = cc % 2
                    tsl = slice(s0 + tb * n, s0 + (tb + 1) * n)
                    P.dma(uc[i][:, 0:n], U[cc, :, tsl])
                    P.dma(x2c[i][:, 0:n], HY[8 + cc, :, tsl])
                    P.v("scalar_tensor_tensor", out=uc[i][:, 0:n], in0=uc[i][:, 0:n], scalar=hb_t[:, cc:cc + 1],
                        in1=ps[:, 0:n], op0=ALU.mult, op1=ALU.add)
                    P.v("tensor_tensor", out=yh[i][:, 0:n], in0=uc[i][:, 0:n], in1=x2c[i][:, 0:n], op=ALU.mult)
                    P.dma(Y[4 + cc, :, tsl], yh[i][:, 0:n])
            tctr[0] += len(seq)

        hyena_seq(hc[LS], 0, True)
        hyena_seq(hc[LP], 2048, True)
        hyena_seq(hc[LP], 2304, False)

    def phase_resproj(Wap, nk, srcD, l, sub, xsrc):
        src = P.sb("rsrc", [128, nk, T], BF16)
        for tb in range(NTB):
            tsl_ = slice(tb * 512, (tb + 1) * 512)
            P.dma(src[:, :, tsl_], srcD[:, :, tsl_].rearrange("c p t -> p c t"), q=("pool" if tb % 2 == 0 else "sp"), wt=tb)
        pp = [P.ps("pr%d" % i) for i in range(4)]
        xr = [P.sb("xr%d" % i, [128, T], F32) for i in range(2)]

        def epi(n, tb, ps):
            i = n % 2
            j = 0 if tb < 4 else 1
            tsl = slice(tb * 512, (tb + 1) * 512)
            if tb == 0:
                P.dma(xr[i][:], xsrc[n], q="pool")
            P.v("scalar_tensor_tensor", out=xr[i][:, tsl], in0=ps[:, :], scalar=modt[:, l, 16 + 24 * sub + n, j:j + 1],
                in1=xr[i][:, tsl], op0=ALU.mult, op1=ALU.add)
            if tb == NTB - 1:
                P.dma(X[n], xr[i][:], q="pool")
        ncols = max(128, (4096 // nk) // 128 * 128)
        ncols = min(ncols, 512)
        w_plan([(Wap, nk, c_, ncols) for c_ in range(0, D, ncols)])
        done = 0
        pi = [0]
        while done < D:
            nc_ = min(ncols, D - done)
            bv, wtag = load_w(Wap, 0, nk, done, nc_)
            nm = nc_ // 128
            for m in range(nm):
                for tb in range(NTB):
                    ps = pp[pi[0] % 4]
                    pi[0] += 1
                    for kc in range(nk):
                        P.mm(ps[:, :], bv[:, kc, m * 128:(m + 1) * 128], src[:, kc, tb * 512:(tb + 1) * 512],
                             start=(kc == 0), stop=(kc == nk - 1), lt=wtag, rtg=tb)
                    epi(done // 128 + m, tb, ps)
                    if m == (nm - 1) // 2 and tb == 2:
                        w_tick()
            done += nc_
        w_end()

    def phase_ffn_up(l):
        w_plan([(w_up[l], 8, br * FF + n * 128, 128) for n in range(22) for br in range(2)])
        hn, ens = norm_hn(l, 1, X)
        raw = [P.sb("raw%d" % i, [128, T], F32) for i in range(2)]
        acc = [P.sb("acc%d" % i, [128, T], F32) for i in range(2)]
        gbuf = [P.sb("gbuf%d" % i, [128, T], F32) for i in range(2)]
        obf = [P.sb("obf%d" % i, [128, T], BF16) for i in range(2)]
        tw = P.sb("tw", [128, 22, 3], F32)
        P.dma(tw[:], wcffn[:, l, :, :])
        pp = [P.ps("pu%d" % i) for i in range(4)]
        W = w_up[l]
        pend = [None]
        for n in range(22):
            i = n % 2
            for br in range(2):
                bv, wtag = load_w(W, 0, 8, br * FF + n * 128, 128)
                for tb in range(NTB):
                    ens(tb)
                    ps = pp[(br * NTB + tb) % 4]
                    for kc in range(8):
                        P.mm(ps[:, :], bv[:, kc, :], hn[:, kc, tb * 512:(tb + 1) * 512], start=(kc == 0), stop=(kc == 7),
                             lt=wtag, rtg=tb)
                    dstt = raw[i] if br == 0 else gbuf[i]
                    P.act(dstt[:, tb * 512:(tb + 1) * 512], ps[:, :], AF.Identity)
                    if tb == 2:
                        w_tick()
            def tail(i=i, n=n):
                conv3(acc[i], raw[i], tw[:, n, :])
                P.act(acc[i][:], acc[i][:], AF.Gelu_apprx_tanh)
                P.v("tensor_tensor", out=obf[i][:], in0=acc[i][:], in1=gbuf[i][:], op=ALU.mult)
                P.dma(ACTS[n], obf[i][:], q="pool")
            if pend[0] is not None:
                pend[0]()
            pend[0] = tail
        pend[0]()
        w_end()

    def phase_proj_c():
        rq = [(w_in_c, 8, c_, 512) for c_ in (0, 512, 1024, 1536)]
        for half_ in range(2):
            rq += [(w_in_c, 8, 2048 + half_ * 512, 512), (w_in_c, 8, 1024 + half_ * 512, 512)]
        w_plan(rq)
        hn, ens = norm_hn(1, 0, X)
        pp = [P.ps("pc%d" % i) for i in range(4)]
        qb = [P.sb("qb%d" % i, [128, 512], BF16) for i in range(3)]
        vb = [P.sb("vb%d" % i, [128, 512], BF16) for i in range(3)]
        kf = [P.sb("kf%d" % i, [128, 512], F32) for i in range(3)]
        ctr = [0]

        def epi_qk(n, tb, ps):
            i = ctr[0] % 3
            ctr[0] += 1
            if n < 8:
                P.act(qb[i][:], ps[:, :], AF.Identity, scale=0.125)
            else:
                P.act(qb[i][:], ps[:, :], AF.Identity)
            P.dma(QK[n, :, tb * 512:(tb + 1) * 512], qb[i][:], q="pool")
        linear_fm(w_in_c, 8, 0, 2048, hn, pp, epi_qk, ensure=ens)
        for half in range(2):
            def epi_v(tc, ps, half=half):
                i = ctr[0] % 3
                ctr[0] += 1
                P.act(vb[i][:], ps[:, :], AF.Identity)
                P.dma(V[tc, :, half * 512:(half + 1) * 512], vb[i][:], q="pool")
                if tc >= 16:
                    pi_, t0 = (tc - 16) // 2, ((tc - 16) % 2) * 128
                    P.v("tensor_copy", out=kf[i][:], in_=ps[:, :])
                    P.dma(o_v[pi_, half * 8:(half + 1) * 8, t0:t0 + 128, :].rearrange("h t d -> t h d"),
                          kf[i][:].rearrange("p (h d) -> p h d", d=64), q="pool")
            linear_tm(w_in_c, 8, 2048 + half * 512, 512, hn, pp, epi_v, range(NTC))

            def epi_k(tc, ps, half=half):
                i = ctr[0] % 3
                ctr[0] += 1
                pi_, t0 = (tc - 16) // 2, ((tc - 16) % 2) * 128
                P.act(kf[i][:], ps[:, :], AF.Identity)
                P.dma(o_k[pi_, half * 8:(half + 1) * 8, t0:t0 + 128, :].rearrange("h t d -> t h d"),
                      kf[i][:].rearrange("p (h d) -> p h d", d=64), q="pool")
            linear_tm(w_in_c, 8, 1024 + half * 512, 512, hn, pp, epi_k, range(16, 20))
        w_end()

    def phase_attn():
        def mkset(i):
            d_ = {}
            d_["qbd"] = P.sb("aqbd%d" % i, [128, 2, T], BF16)
            d_["kT"] = P.sb("akT%d" % i, [128, T], BF16)
            d_["kcx"] = P.sb("akc%d" % i, [128, 512], F32)
            d_["kcb"] = P.sb("akcb%d" % i, [128, 512], BF16)
            d_["vcx"] = P.sb("avc%d" % i, [128, 4, 2, 64], F32)
            d_["vcb"] = P.sb("avcb%d" % i, [128, 4, 128], BF16)
            d_["Vs"] = P.sb("aVs%d" % i, [128, 16, 128], BF16)
            d_["Vs64"] = P.sb("aVs64%d" % i, [128, 15, 128], BF16)
            d_["Vp"] = P.sb("aVp%d" % i, [128, 4, 128], BF16)
            d_["Tmf"] = [P.sb("aTmf%d_%d" % (i, j), [128, 15, 64], F32) for j in range(2)]
            d_["Tmb2"] = P.sb("aTmb2%d" % i, [128, 15, 2, 64], BF16)
            d_["mx"] = P.sb("amx%d" % i, [128, 32], F32)
            d_["negC"] = P.sb("anegC%d" % i, [128, 1], F32)
            d_["ysb"] = P.sb("aysb%d" % i, [128, T], BF16)
            P.v("memset", ap=d_["qbd"][:], constant=0.0)
            return d_
        sets = [mkset(0), mkset(1)]
        sqq = P.sb("asqq", [128, 2, T], BF16)
        sqk = P.sb("asqk", [128, T + 512], BF16)
        selM = P.sb("aselM", [128, 2, 128], BF16)
        P.v("memset", ap=selM[:], constant=0.0)
        P.v("memset", ap=selM[0:64, 0, :], constant=1.0)
        P.v("memset", ap=selM[64:128, 1, :], constant=1.0)
        NW = 2
        PTt = [P.sb("aPT%d" % i, [128, 1024], BF16) for i in range(NW)]
        pts = [P.sb("apts%d" % i, [128, 256], F32) for i in range(NW)]
        rd = [P.sb("ard%d" % i, [128, 256], F32) for i in range(NW)]
        pstA = [P.ps("apsa%d" % i) for i in range(NW)]
        pstB = [P.ps("apsb%d" % i) for i in range(NW)]
        po = [P.ps("apo%d" % i) for i in range(NW)]
        pn_ = P.ps("apn")

        def setup_dma(n):
            S_ = sets[n % 2]
            q_ = "pool"
            P.dma(S_["qbd"][0:64, 0, :], QK[n, 0:64, :], q=q_)
            P.dma(S_["qbd"][64:128, 1, :], QK[n, 64:128, :], q=q_)
            P.dma(S_["kT"][:], QK[8 + n], q=q_)
            P.dma(S_["kcx"][:], kcT[n], q=q_)
            for hh in range(2):
                P.dma(S_["vcx"][:, :, hh, :], vc[2 * n + hh].rearrange("(c p) d -> p c d", p=128), q=q_)
                P.dma(S_["Tmf"][hh][:], TmE[2 * n + hh], q=q_)
            P.dma(S_["Vs"][:], V[0:16, :, n * 128:(n + 1) * 128].rearrange("c p d -> p c d"), q=q_)
            P.dma(S_["Vs64"][0:64, :, :], V[0:15, 64:128, n * 128:(n + 1) * 128].rearrange("c p d -> p c d"), q=q_)
            P.dma(S_["Vs64"][64:128, :, :], V[1:16, 0:64, n * 128:(n + 1) * 128].rearrange("c p d -> p c d"), q=q_)
            P.dma(S_["Vp"][:], V[16:20, :, n * 128:(n + 1) * 128].rearrange("c p d -> p c d"), q=q_)

        def setup_compute(n):
            S_ = sets[n % 2]
            mx = S_["mx"]
            P.v("tensor_copy", eng="pool", out=S_["kcb"][:], in_=S_["kcx"][:])
            P.v("tensor_copy", eng="pool", out=S_["vcb"][:].rearrange("p c (h d) -> p c h d", h=2), in_=S_["vcx"][:])
            for hh in range(2):
                P.v("tensor_tensor", eng="pool", out=S_["Tmf"][hh][:], in0=S_["Tmf"][hh][:],
                    in1=colmask[:, None, :].broadcast_to([128, 15, 64]), op=ALU.add)
                P.v("tensor_copy", eng="pool", out=S_["Tmb2"][:, :, hh, :], in_=S_["Tmf"][hh][:])
            P.act(sqq[:], S_["qbd"][:], AF.Square)
            P.act(sqk[:, 0:T], S_["kT"][:], AF.Square)
            P.act(sqk[:, T:T + 512], S_["kcb"][:], AF.Square)
            for hh in range(2):
                for tb in range(5):
                    P.mm(pn_[:, 0:512], ones_bf[:], sqq[:, hh, tb * 512:(tb + 1) * 512])
                    P.v("tensor_reduce", out=mx[:, hh * 16 + tb:hh * 16 + tb + 1], in_=pn_[:, 0:512], axis=AX.X, op=ALU.max)
                for tb in range(6):
                    P.mm(pn_[:, 0:512], selM[:, hh, :], sqk[:, tb * 512:(tb + 1) * 512])
                    P.v("tensor_reduce", out=mx[:, hh * 16 + 5 + tb:hh * 16 + 6 + tb], in_=pn_[:, 0:512], axis=AX.X,
                        op=ALU.max)
                P.v("tensor_reduce", out=mx[:, hh * 16 + 12:hh * 16 + 13], in_=mx[:, hh * 16:hh * 16 + 5], axis=AX.X, op=ALU.max)
                P.v("tensor_reduce", out=mx[:, hh * 16 + 13:hh * 16 + 14], in_=mx[:, hh * 16 + 5:hh * 16 + 11], axis=AX.X,
                    op=ALU.max)
                P.v("tensor_tensor", out=mx[:, hh * 16 + 14:hh * 16 + 15], in0=mx[:, hh * 16 + 12:hh * 16 + 13],
                    in1=mx[:, hh * 16 + 13:hh * 16 + 14], op=ALU.mult)
            P.v("tensor_tensor", out=mx[:, 15:16], in0=mx[:, 14:15], in1=mx[:, 30:31], op=ALU.max)
            P.act(mx[:, 31:32], mx[:, 15:16], AF.Sqrt)
            P.v("tensor_scalar", out=S_["negC"][:], in0=mx[:, 31:32], scalar1=-1.0, scalar2=None, op0=ALU.mult)

        def stage_s(w, S_, qt0, nq, chunks, x0):
            n2 = 2 * nq
            qsl = S_["qbd"][:, :, qt0:qt0 + nq]
            for ch, (kap, vap) in enumerate(chunks):
                pt_, c_ = (pstA[w], ch) if ch * n2 < 512 else (pstB[w], ch - 512 // n2)
                dst = pt_[:, c_ * n2:(c_ + 1) * n2]
                local = (x0 is not None and ch < 4)
                P.mm(dst, kap, qsl, start=True, stop=(not local))
                if local:
                    P.mm(dst, ident_bf[:], S_["Tmb2"][:, x0 + 2 * ch, :, :], start=False, stop=True)
            P.act(PTt[w][:, 0:512], pstA[w][:, :], AF.Exp, bias=S_["negC"][:, 0:1])
            if len(chunks) * n2 > 512:
                P.act(PTt[w][:, 512:1024], pstB[w][:, :], AF.Exp, bias=S_["negC"][:, 0:1])

        def stage_o(w, S_, qt0, nq, chunks, x0):
            n2 = 2 * nq
            nch = len(chunks)
            for ch, (kap, vap) in enumerate(chunks):
                P.mm(po[w][:, 0:n2], vap, PTt[w][:, ch * n2:(ch + 1) * n2], start=(ch == 0), stop=(ch == nch - 1))
            for ch in range(nch):
                P.mm(po[w][:, 256:256 + n2], ones_bf[:], PTt[w][:, ch * n2:(ch + 1) * n2], start=(ch == 0),
                     stop=(ch == nch - 1))
            P.act(rd[w][:, 0:n2], po[w][:, 256:256 + n2], AF.Ln)
            P.act(rd[w][:, 0:n2], rd[w][:, 0:n2], AF.Exp, scale=-1.0)
            for hh in range(2):
                ps_ = slice(hh * 64, hh * 64 + 64)
                cs_ = slice(hh * nq, (hh + 1) * nq)
                P.v("tensor_tensor", out=S_["ysb"][ps_, qt0:qt0 + nq], in0=po[w][ps_, cs_], in1=rd[w][ps_, cs_], op=ALU.mult,
                    wt=(qt0 * 2 + hh))

        blk = [0]
        setup_dma(0)
        setup_compute(0)
        for n in range(8):
            S_ = sets[n % 2]
            if n + 1 < 8:
                setup_dma(n + 1)
            kT, kcb, vcb, Vs, Vs64, Vp = S_["kT"], S_["kcb"], S_["vcb"], S_["Vs"], S_["Vs64"], S_["Vp"]
            blocks = []
            for r in range(32):
                j0 = min(max(r - 4, 0), 24)
                k0 = j0 * 64
                chunks = []
                for i in range(4):
                    vap = Vs[:, j0 // 2 + i, :] if j0 % 2 == 0 else Vs64[:, j0 // 2 + i, :]
                    chunks.append((kT[:, k0 + i * 128:k0 + (i + 1) * 128], vap))
                for i in range(4):
                    chunks.append((kcb[:, i * 128:(i + 1) * 128], vcb[:, i, :]))
                blocks.append((S_, r * 64, 64, chunks, j0 - r + 7))
            for pi_ in range(2):
                s0 = 2048 + pi_ * 256
                chunks = [(kT[:, s0 + i * 128:s0 + (i + 1) * 128], Vp[:, pi_ * 2 + i, :]) for i in range(2)]
                for hf in range(2):
                    blocks.append((S_, s0 + hf * 128, 128, chunks, None))
            prev = None
            for bi, bk in enumerate(blocks):
                w = blk[0] % NW
                blk[0] += 1
                stage_s(w, *bk)
                if prev is not None:
                    stage_o(*prev)
                prev = (w,) + bk
                if bi == 12 and n + 1 < 8:
                    setup_compute(n + 1)
            stage_o(*prev)
            P.dma(Y[n], S_["ysb"][:])

    P.end_phase()

    stages = [
        ("adaln", phase_adaln),
        ("copyx", None),
        ("projab", phase_proj_ab),
        ("mlstm0", phase_mlstm),
        ("mlstm", phase_mlstm_fin),
        ("filters", phase_filters_adaln1),
        ("hyena", phase_hyena),
        ("outab", lambda: phase_resproj(w_out_ab, 8, Y, 0, 0, xT)),
        ("ffnup0", lambda: phase_ffn_up(0)),
        ("down0", lambda: phase_resproj(w_down[0], 22, ACTS, 0, 1, X)),
        ("projc", phase_proj_c),
        ("attn", phase_attn),
        ("outc", lambda: phase_resproj(w_out_c, 8, Y, 1, 0, X)),
        ("ffnup1", lambda: phase_ffn_up(1)),
        ("down1", lambda: phase_resproj(w_down[1], 22, ACTS, 1, 1, X)),
        ("final", lambda: phase_norm(0, 0, X, final=True)),
    ]
    dbg_outs = {}
    for name, fn in stages:
        if fn is None:
            continue
        P.begin_phase()
        fn()
        P.end_phase()
        if debug is not None and name == debug[0]:
            P.begin_phase()
            for (tn, shape, dt) in debug[1]:
                srcT = {"X": X, "HN": HN, "QK": QK, "V": V, "SO": SO, "G": G, "HY": HY, "U": U, "HF": HF, "Y": Y,
                        "ACTS": ACTS, "V64": V64}[tn]
                o = dout("dbg_" + tn, shape, dt)
                dbg_outs[tn] = o
                isc = (len(shape) == 3 and shape[1] == 128)
                buf = P.sb("dbgbuf", [128, int(np.prod(shape[2:])) if isc else int(np.prod(shape[1:]))], dt)
                if isc:
                    for ci in range(shape[0]):
                        P.dma(buf[:], srcT[ci])
                        P.dma(o[ci], buf[:])
                else:
                    P.dma(buf[:], srcT.rearrange("p a b -> p (a b)"))
                    P.dma(o.rearrange("p a b -> p (a b)"), buf[:])
            P.end_phase()
            break
    P.close()
    nc._phase_counts = P.phase_counts
    return nc


def _consts():
    ar = np.arange(128)
    ident = np.eye(128, dtype=np.float32)
    tri_f = (ar[:, None] <= ar[None, :]).astype(np.float32)
    tri_b = (ar[:, None] >= ar[None, :]).astype(np.float32)
    maskb_f = np.where(ar[:, None] <= ar[None, :], 0.0, NEG).astype(np.float32)
    maskb_b = np.where(ar[:, None] >= ar[None, :], 0.0, NEG).astype(np.float32)
    cols = np.arange(64)
    c_start = np.clip(cols - 8, 0, 48)
    valid = (cols[:, None] >= c_start[None, :]) & (cols[:, None] < c_start[None, :] + 16)
    cm = np.where(valid, 0.0, NEG).astype(np.float32)
    last = np.zeros((128, 128), np.float32)
    last[0:64, 0:64] = cm
    last[64:128, 0:64] = cm
    last[:, 64] = 1.0
    last[0, 64] = 0.0
    cst = np.stack([ident, tri_f, tri_b, maskb_f, maskb_b, last], axis=1)
    out = {"cst": np.ascontiguousarray(cst)}
    deltas = np.abs(np.linspace(math.log(1e-2) / 1.5, math.log(1e-2) / 0.3, 512, dtype=np.float32))
    out["deltas"] = np.concatenate([deltas, deltas])[None, :].astype(np.float32)
    for L in (LS, LP):
        c = hy_cfg(L)
        N, na, nb = c["N"], c["na"], c["nb"]
        t = np.arange(na * 128, dtype=np.int64)
        f = np.arange(nb * 128, dtype=np.int64)
        ang = 2.0 * np.pi * ((t[:, None] * f[None, :]) % N).astype(np.float64) / N
        Cm, Sm = np.cos(ang), np.sin(ang)
        TF = np.stack([Cm, Sm], axis=0).reshape(2, na, 128, nb, 128).transpose(3, 2, 1, 0, 4)
        out["TF%d" % L] = np.ascontiguousarray(TF).astype(np.float32)
        tt = np.arange(L, dtype=np.int64)
        ang2 = 2.0 * np.pi * ((f[:, None] * tt[None, :]) % N).astype(np.float64) / N
        TI = np.stack([np.cos(ang2), np.sin(ang2)], axis=0).reshape(2, nb, 128, L).transpose(2, 1, 0, 3)
        out["TI%d" % L] = np.ascontiguousarray(TI).astype(np.float32)
        tl = np.linspace(0.0, 1.0, L, dtype=np.float32)
        wpos = (2.0 * np.pi * np.arange(L, dtype=np.float32) / L).astype(np.float32)
        bands = np.linspace(1e-4, 15, 16, dtype=np.float32)
        z = np.concatenate([tl[:, None], np.cos(bands[None, :] * wpos[:, None]), -np.sin(bands[None, :] * wpos[:, None])],
                           axis=-1).astype(np.float32)
        out["zT%d" % L] = np.ascontiguousarray(z.T)
        out["tneg%d" % L] = np.ascontiguousarray((-tl).reshape(na, 128).T)
        wt = np.zeros(nb * 128, np.float64)
        wt[0] = 1.0 / N
        wt[1:L] = 2.0 / N
        wt[L] = 1.0 / N
        wt2 = np.stack([wt, -wt], axis=-1).reshape(nb, 128, 2).transpose(1, 0, 2)
        out["wt%d" % L] = np.ascontiguousarray(wt2).astype(np.float32)
    return out


_CACHE = {}


def _prep_inputs(inp):
    f = lambda a: np.ascontiguousarray(np.asarray(a, dtype=np.float32))
    shared = dict(_consts())
    shared["w_ada"] = f(inp["w_ada"])
    shared["b_adaP"] = f(np.asarray(inp["b_ada"]).reshape(2, 48, 128).transpose(2, 0, 1))
    gs = np.stack([inp["g_mix"][0], inp["g_mix"][1], inp["g_ffn"][0], inp["g_ffn"][1], inp["g_final"]], axis=0)
    shared["gP"] = f(gs.reshape(5, 8, 128).transpose(2, 0, 1))
    shared["w_in_ab"] = f(inp["w_in_ab"][0])
    shared["b_gates"] = f(inp["b_gates"][0][None, :])
    shared["wcqk"] = f(np.asarray(inp["w_conv_qk"][0]).reshape(3, 8, 128).transpose(2, 1, 0))
    shared["g_ml"] = f(np.asarray(inp["g_mlstm"][0]).reshape(4, 128).T)
    shared["wchy"] = f(np.asarray(inp["w_conv_hy"][0]).reshape(3, 12, 128).transpose(2, 1, 0))
    shared["w_f1"] = f(inp["w_filt1"][0])
    shared["w_f2"] = f(inp["w_filt2"][0])
    shared["w_f3"] = f(inp["w_filt3"][0])
    shared["fvec"] = f(np.stack([inp["b_filt1"][0], inp["b_filt2"][0], inp["filt_freq"][0]], axis=-1))
    shared["hyb"] = f(np.asarray(inp["hyena_bias"][0]).reshape(4, 128).T)
    shared["w_out_ab"] = f(inp["w_out_ab"][0])
    shared["w_in_c"] = f(inp["w_in_c"][0])
    rp = np.asarray(inp["rpb_c"][0], dtype=np.float32)
    pidx = np.arange(128)
    wk = (pidx % 64)[:, None, None]
    up = (pidx >= 64).astype(np.int64)[:, None, None]
    xx = np.arange(15)[None, :, None]
    wq = np.arange(64)[None, None, :]
    ci = wk - wq + 15
    ri = xx + up
    ok = (ci >= 0) & (ci <= 30) & (ri <= 14)
    gat = rp[:, np.clip(ri, 0, 14), np.clip(ci, 0, 30)]
    shared["TmE"] = np.ascontiguousarray(np.where(ok[None], gat, 0.0).astype(np.float32))
    shared["w_out_c"] = f(inp["w_out_c"][0])
    shared["w_up"] = f(inp["w_up"])
    shared["wcffn"] = f(np.asarray(inp["w_conv_ffn"]).reshape(2, 3, 22, 128).transpose(3, 0, 2, 1))
    shared["w_down"] = f(inp["w_down"])
    maps = []
    for c in range(8):
        b = c % 4
        m = dict(shared)
        toks = np.concatenate([inp["x_sample"][b], inp["x_prompt"][2 * c], inp["x_prompt"][2 * c + 1]], axis=0)
        m["xT"] = f(np.asarray(toks).T.reshape(8, 128, T))
        cvv = np.stack([inp["c"][b], inp["c_ctx"]], axis=-1)
        m["cv"] = f(cvv.reshape(8, 128, 2).transpose(1, 0, 2))
        m["st_C"] = f(np.asarray(inp["state_mlstm_C"][b, 0]).reshape(8, 128, 128))
        m["st_n"] = f(np.asarray(inp["state_mlstm_n"][b, 0]).reshape(8, 128).T)
        m["st_m"] = f(np.broadcast_to(np.asarray(inp["state_mlstm_m"][b, 0]).reshape(1, 8), (128, 8)))
        m["kcT"] = f(np.asarray(inp["cache_na_k"][b, 0]).transpose(0, 2, 1).reshape(8, 128, 512))
        m["vc"] = f(inp["cache_na_v"][b, 0])
        maps.append(m)
    return maps


def kernel(**inputs):
    inp = {k: np.asarray(v) for k, v in inputs.items()}
    if "nc" not in _CACHE:
        _CACHE["nc"] = build()
    nc = _CACHE["nc"]
    maps = _prep_inputs(inp)
    res = run_bass_kernel_spmd(nc, maps, core_ids=list(range(8)))
    R = res.results
    y_prompt = np.zeros((16, 256, D), np.float32)
    y_sample = np.zeros((4, 2048, D), np.float32)
    nC = np.zeros((16, 1, 2, 4, 128, 128), np.float32)
    nn = np.zeros((16, 1, 2, 4, 128), np.float32)
    nm = np.zeros((16, 1, 2, 4), np.float32)
    nk = np.zeros((16, 1, 16, 256, 64), np.float32)
    nv = np.zeros((16, 1, 16, 256, 64), np.float32)
    for c in range(8):
        yt = np.asarray(R[c]["yT"]).reshape(D, T).T
        if c < 4:
            y_sample[c] = yt[0:2048]
        y_prompt[2 * c] = yt[2048:2304]
        y_prompt[2 * c + 1] = yt[2304:2560]
        for pi in range(2):
            nC[2 * c + pi, 0] = np.asarray(R[c]["o_C"])[pi]
            nn[2 * c + pi, 0] = np.asarray(R[c]["o_n"])[pi]
            nm[2 * c + pi, 0] = np.asarray(R[c]["o_m"])[pi]
            nk[2 * c + pi, 0] = np.asarray(R[c]["o_k"])[pi]
            nv[2 * c + pi, 0] = np.asarray(R[c]["o_v"])[pi]
    return (y_prompt, y_sample, nC, nn, nm, nk, nv)
```

```python
import math
import os
import numpy as np
from contextlib import ExitStack
import concourse.bass as bass
import concourse.mybir as mybir
from concourse.bass_utils import run_bass_kernel_spmd

F32 = mybir.dt.float32
BF16 = mybir.dt.bfloat16
AF = mybir.ActivationFunctionType
ALU = mybir.AluOpType
AX = mybir.AxisListType

SAME_ENGINE_SYNC = True
N_DMA_SEMS = 20
_READ_KW = ("in_", "in0", "in1", "lhsT", "rhs", "scalar1", "scalar2", "scalar", "bias", "scale",
            "data0", "data1", "initial", "identity")
_WRITE_KW = ("out", "accum_out", "ap")


class Op:
    __slots__ = ("eng", "meth", "kw", "reads", "writes", "deps", "needs_inc", "sem", "val", "is_dma", "done")

    def __init__(self, eng, meth, kw, reads, writes, is_dma):
        self.eng, self.meth, self.kw = eng, meth, kw
        self.reads, self.writes = reads, writes
        self.deps = []
        self.needs_inc = False
        self.sem = None
        self.val = None
        self.is_dma = is_dma
        self.done = False


class Prog:
    def __init__(self, nc):
        self.nc = nc
        self.es = ExitStack()
        self.phase_es = None
        self.ops = []
        self.state = {}
        self.engs = {"pe": nc.tensor, "act": nc.scalar, "dve": nc.vector, "pool": nc.gpsimd, "sp": nc.sync}
        self.esem = {}
        for e in ("pe", "act", "dve", "pool"):
            self.esem[e] = self.es.enter_context(nc.semaphore("es_" + e))
        self.dsems = {}
        for q in ("sp", "pool"):
            self.dsems[q] = [self.es.enter_context(nc.semaphore("ds_%s_%d" % (q, i))) for i in range(N_DMA_SEMS)]
        self.dcount = {"sp": 0, "pool": 0}
        self.ecount = {e: 0 for e in self.esem}
        self.seen = {e: {} for e in self.engs}
        self.emitted = 0
        self.uid = 0
        self.phase_counts = []

    def sb(self, name, shape, dtype, persist=False):
        self.uid += 1
        st = self.es if (persist or self.phase_es is None) else self.phase_es
        return st.enter_context(self.nc.sbuf_tensor("%s_%d" % (name, self.uid), list(shape), dtype))

    def ps(self, name, shape=(128, 512), dtype=F32):
        self.uid += 1
        return self.phase_es.enter_context(self.nc.psum_tensor("%s_%d" % (name, self.uid), list(shape), dtype))

    def dram(self, name, shape, dtype, kind="Internal"):
        return self.nc.dram_tensor(name, list(shape), dtype, kind=kind).ap()

    @staticmethod
    def _key(x):
        if isinstance(x, tuple):
            return (x[0].name if not isinstance(x[0], str) else x[0]), x[1]
        if isinstance(x, str):
            return x, None
        return x.name, None

    def _access(self, op, key, write):
        name, tag = key
        st = self.state.setdefault(name, {})
        tags = list(st.keys()) if tag is None else [t for t in (tag, None) if t in st]
        for t in tags:
            lw, rd = st[t]
            if lw is not None:
                op.deps.append(lw)
            if write:
                op.deps.extend(rd)
        if write:
            if tag is None:
                st.clear()
            st[tag] = [op, []]
        else:
            if tag not in st:
                st[tag] = [None, []]
            st[tag][1].append(op)

    def add(self, eng, meth, kw, r=None, w=None, rt=None, wt=None, is_dma=False):
        reads = list(r) if r is not None else []
        writes = list(w) if w is not None else []
        if r is None:
            for k in _READ_KW:
                v = kw.get(k, None)
                if v is not None and hasattr(v, "name") and hasattr(v, "ap"):
                    reads.append((v, rt) if rt is not None else v)
        if w is None:
            for k in _WRITE_KW:
                v = kw.get(k, None)
                if v is not None and hasattr(v, "name"):
                    writes.append((v, wt) if wt is not None else v)
        op = Op(eng, meth, kw, [self._key(x) for x in reads], [self._key(x) for x in writes], is_dma)
        for k in op.reads:
            self._access(op, k, False)
        for k in op.writes:
            self._access(op, k, True)
        seen = set()
        dd = []
        for d in op.deps:
            if d is op or id(d) in seen or d.done:
                continue
            seen.add(id(d))
            dd.append(d)
        op.deps = dd
        for d in dd:
            if d.is_dma:
                continue
            if d.eng == op.eng and not op.is_dma and (d.eng == "pe" or not SAME_ENGINE_SYNC):
                continue
            d.needs_inc = True
        self.ops.append(op)
        return op

    def dma(self, out, in_, q="sp", slow=False, **k):
        kw = dict(out=out, in_=in_)
        if slow:
            kw["allow_slow_non_contiguous"] = True
        return self.add(q, "dma_start", kw, is_dma=True, **k)

    def mm(self, out, lhsT, rhs, start=True, stop=True, lt=None, rtg=None, **k):
        if lt is not None or rtg is not None:
            k["r"] = [(lhsT, lt) if lt is not None else lhsT, (rhs, rtg) if rtg is not None else rhs]
        return self.add("pe", "matmul", dict(out=out, lhsT=lhsT, rhs=rhs, start=start, stop=stop), **k)

    def tr(self, out, in_, identity, **k):
        return self.add("pe", "transpose", dict(out=out, in_=in_, identity=identity), **k)

    def act(self, out, in_, func, rt=None, wt=None, **kw):
        kw.update(out=out, in_=in_, func=func)
        return self.add("act", "activation", kw, rt=rt, wt=wt)

    def v(self, meth, eng="dve", rt=None, wt=None, r=None, w=None, **kw):
        return self.add(eng, meth, kw, rt=rt, wt=wt, r=r, w=w)

    def _wait(self, eng, sem, val):
        s = self.seen[eng]
        k = id(sem)
        if s.get(k, 0) >= val:
            return
        s[k] = val
        self.engs[eng].wait_ge(sem, val)

    def flush(self):
        for op in self.ops[self.emitted:]:
            e = op.eng
            for d in op.deps:
                if d.is_dma:
                    self._wait(e, d.sem, d.val)
                else:
                    if d.eng == e and not op.is_dma and (e == "pe" or not SAME_ENGINE_SYNC):
                        continue
                    self._wait(e, d.sem, d.val)
            if op.is_dma:
                q = e
                i = self.dcount[q]
                self.dcount[q] += 1
                sem = self.dsems[q][i % N_DMA_SEMS]
                val = 16 * (i // N_DMA_SEMS + 1)
                if val > 16:
                    self._wait(q, sem, val - 16)
                op.sem, op.val = sem, val
                self.engs[q].dma_start(**op.kw).then_inc(sem, 16)
            else:
                ins = getattr(self.engs[e], op.meth)(**op.kw)
                if op.needs_inc:
                    self.ecount[e] += 1
                    op.sem, op.val = self.esem[e], self.ecount[e]
                    ins.then_inc(op.sem, 1)
        self.emitted = len(self.ops)

    def barrier(self):
        for e in ("pe", "act", "dve", "pool"):
            for op in reversed(self.ops[self.emitted:]):
                if op.eng == e and not op.is_dma:
                    op.needs_inc = True
                    break
        self.flush()
        cnt = {}
        for op in self.ops:
            k = op.eng + ("_dma" if op.is_dma else "")
            cnt[k] = cnt.get(k, 0) + 1
        self.phase_counts.append(cnt)
        for eng in ("sp", "pe", "act", "dve", "pool"):
            for q in ("sp", "pool"):
                n = self.dcount[q]
                for j in range(min(n, N_DMA_SEMS)):
                    cnt = (n - 1 - j) // N_DMA_SEMS + 1
                    self._wait(eng, self.dsems[q][j], 16 * cnt)
            for e in ("pe", "act", "dve", "pool"):
                if e != eng and self.ecount[e] > 0:
                    self._wait(eng, self.esem[e], self.ecount[e])
        for op in self.ops:
            op.done = True
        self.ops = []
        self.emitted = 0
        self.state = {}

    def begin_phase(self):
        self.phase_es = ExitStack()

    def end_phase(self):
        self.barrier()
        self.phase_es.close()
        self.phase_es = None

    def close(self):
        self.es.close()


D = 1024
T = 2560
LS, LP = 2048, 256
SEQS = [(0, 2048), (2048, 2304), (2304, 2560)]
NTB = 5
NTC = 20
FF = 2816
IN_AB = 3600
EPS = 1e-6
MAGIC = 12582912.0
TWO_PI = 2.0 * math.pi
NEG = -30000.0


def hy_cfg(L):
    N = 2 * L
    na = L // 128
    nb = (L + 1 + 127) // 128
    return dict(L=L, N=N, na=na, nb=nb)


def build(debug=None):
    nc = bass.Bass("TRN2", target_bir_lowering=False)
    P = Prog(nc)

    def din(name, shape, dt=F32):
        return nc.dram_tensor(name, list(shape), dt, kind="ExternalInput").ap()

    def dout(name, shape, dt=F32):
        return nc.dram_tensor(name, list(shape), dt, kind="ExternalOutput").ap()

    def dbgdump(name, ap, dt=F32):
        if debug is None:
            return
        o = dout("dbgm_" + name, list(ap.shape), dt)
        P.dma(o, ap)

    xT = din("xT", [8, 128, T])
    cv = din("cv", [128, 8, 2])
    st_C = din("st_C", [8, 128, 128])
    st_n = din("st_n", [128, 8])
    st_m = din("st_m", [128, 8])
    kcT = din("kcT", [8, 128, 512])
    vc = din("vc", [16, 512, 64])
    w_ada = din("w_ada", [2, D, 6 * D])
    b_adaP = din("b_adaP", [128, 2, 48])
    gP = din("gP", [128, 5, 8])
    w_in_ab = din("w_in_ab", [D, IN_AB])
    b_gates = din("b_gates", [1, 16])
    wcqk = din("wcqk", [128, 8, 3])
    g_ml = din("g_ml", [128, 4])
    wchy = din("wchy", [128, 12, 3])
    w_f1 = din("w_f1", [33, 64])
    w_f2 = din("w_f2", [64, 64])
    w_f3 = din("w_f3", [64, 1024])
    fvec = din("fvec", [64, 3])
    hyb = din("hyb", [128, 4])
    w_out_ab = din("w_out_ab", [D, D])
    w_in_c = din("w_in_c", [D, 3 * D])
    TmE = din("TmE", [16, 128, 15, 64])
    w_out_c = din("w_out_c", [D, D])
    w_up = din("w_up", [2, D, 2 * FF])
    wcffn = din("wcffn", [128, 2, 22, 3])
    w_down = din("w_down", [2, FF, D])
    cst = din("cst", [128, 6, 128])
    deltas = din("deltas", [1, 1024])
    hc = {}
    for L in (LS, LP):
        c = hy_cfg(L)
        c["TF"] = din("TF%d" % L, [c["nb"], 128, c["na"], 2, 128])
        c["TI"] = din("TI%d" % L, [128, c["nb"], 2, L])
        c["zT"] = din("zT%d" % L, [33, L])
        c["tneg"] = din("tneg%d" % L, [128, c["na"]])
        c["wt"] = din("wt%d" % L, [128, c["nb"], 2])
        hc[L] = c

    yT = dout("yT", [8, 128, T])
    o_C = dout("o_C", [2, 2, 4, 128, 128])
    o_n = dout("o_n", [2, 2, 4, 128])
    o_m = dout("o_m", [2, 2, 4])
    o_k = dout("o_k", [2, 16, 256, 64])
    o_v = dout("o_v", [2, 16, 256, 64])

    X = P.dram("X", [8, 128, T], F32)
    HN = P.dram("HN", [8, 128, T], BF16)
    QK = P.dram("QK", [16, 128, T], BF16)
    V = P.dram("V", [NTC, 128, 1024], BF16)
    V64 = P.dram("V64", [16, 128, 1024], BF16)
    SO = P.dram("SO", [4, 128, T], BF16)
    G = P.dram("G", [128, NTC, 16], F32)
    HY = P.dram("HY", [12, 128, T], F32)
    U = P.dram("U", [4, 128, T], F32)
    HF = P.dram("HF", [4, 128, T], F32)
    HB = P.dram("HB", [4, 128, T], F32)
    Y = P.dram("Y", [8, 128, T], BF16)
    ACTS = P.dram("ACTS", [22, 128, T], BF16)
    for L in (LS, LP):
        hc[L]["FL"] = P.dram("FL%d" % L, [hc[L]["na"], 128, 1024], BF16)

    cst_t = P.sb("cst", [128, 6, 128], F32, persist=True)
    ident = cst_t[:, 0, :]
    tri = [cst_t[:, 1, :], cst_t[:, 2, :]]
    maskb = [cst_t[:, 3, :], cst_t[:, 4, :]]
    colmask = cst_t[:, 5, 0:64]
    ident_bf = P.sb("identbf", [128, 128], BF16, persist=True)
    ones_f = P.sb("onesf", [128, 128], F32, persist=True)
    ones_bf = P.sb("onesbf", [128, 128], BF16, persist=True)
    epsc = P.sb("epsc", [128, 1], F32, persist=True)
    modt = P.sb("modt", [128, 2, 48, 2], F32, persist=True)
    Atab = P.sb("Atab", [128, 2, 2, 8, 2], F32, persist=True)
    gPt = P.sb("gPt", [128, 5, 8], F32, persist=True)
    wst = [P.sb("wst%d" % i, [128, 4096], F32, persist=True) for i in range(2)]
    wbf = [P.sb("wbf%d" % i, [128, 4096], BF16, persist=True) for i in range(2)]
    wctr = [0]
    scv_p = P.sb("scv", [128, 8, 2], F32, persist=True)
    bt_p = P.sb("bada", [128, 2, 48], F32, persist=True)

    P.begin_phase()
    P.dma(cst_t[:], cst[:, :, :])
    P.dma(gPt[:], gP[:, :, :])
    P.v("tensor_copy", out=ident_bf[:], in_=ident)
    P.v("memset", ap=ones_f[:], constant=1.0)
    P.v("memset", ap=ones_bf[:], constant=1.0)
    P.v("memset", ap=epsc[:], constant=EPS)

    wplan = {"reqs": None}

    def w_plan(reqs):
        mx_e = max(r[1] * r[3] for r in reqs)
        wplan.update(reqs=reqs, cur=0, dma=0, cast=0, nslots=(2 if mx_e > 1024 else 8),
                     se=(4096 if mx_e > 1024 else 1024), base=wctr[0])
        wctr[0] += 1

    def _w_views(k):
        Wap, nk, c0, ncols = wplan["reqs"][k]
        slot = k % wplan["nslots"]
        off = slot * wplan["se"]
        ti, o = off // 4096, off % 4096
        tag = "w%d_%d" % (wplan["base"], slot)
        sv = wst[ti][:, o:o + nk * ncols].rearrange("p (k n) -> p k n", k=nk)
        bv = wbf[ti][:, o:o + nk * ncols].rearrange("p (k n) -> p k n", k=nk)
        return Wap, nk, c0, ncols, sv, bv, tag

    def _w_dma(k):
        Wap, nk, c0, ncols, sv, bv, tag = _w_views(k)
        P.dma(sv, Wap[0:nk * 128, c0:c0 + ncols].rearrange("(k p) n -> p k n", p=128), q="sp", wt=tag)

    def _w_cast(k):
        Wap, nk, c0, ncols, sv, bv, tag = _w_views(k)
        P.act(bv, sv, AF.Identity, rt=tag, wt=tag)

    def w_tick():
        if wplan["reqs"] is None:
            return
        k = wplan["cur"]
        if k < len(wplan["reqs"]) and wplan["cast"] <= k and wplan["dma"] > k:
            _w_cast(k)
            wplan["cast"] = k + 1

    def w_end():
        wplan["reqs"] = None

    def load_w(Wap, k0, nk, c0, ncols, cast=True):
        if wplan["reqs"] is None or not cast:
            i = wctr[0]
            wctr[0] += 1
            sv = wst[i % 2][:, 0:nk * ncols].rearrange("p (k n) -> p k n", k=nk)
            P.dma(sv, Wap[k0 * 128:(k0 + nk) * 128, c0:c0 + ncols].rearrange("(k p) n -> p k n", p=128), q="sp")
            if not cast:
                return sv
            bv = wbf[i % 2][:, 0:nk * ncols].rearrange("p (k n) -> p k n", k=nk)
            P.act(bv, sv, AF.Identity)
            return bv, None
        k = wplan["cur"]
        rq = wplan["reqs"][k]
        assert rq[1] == nk and rq[2] == c0 and rq[3] == ncols, (rq[1:], nk, c0, ncols)
        while wplan["dma"] <= k:
            _w_dma(wplan["dma"])
            wplan["dma"] += 1
        while wplan["cast"] <= k:
            _w_cast(wplan["cast"])
            wplan["cast"] += 1
        wplan["cur"] = k + 1
        depth = 1 if wplan["nslots"] == 2 else 3
        while wplan["dma"] < min(len(wplan["reqs"]), k + 1 + depth):
            _w_dma(wplan["dma"])
            wplan["dma"] += 1
        return _w_views(k)[5], _w_views(k)[6]

    def linear_fm(Wap, nk, c0, ncols_total, src, psums, epi, tbs=range(NTB), chunk0=0, ensure=None, fb_outer=False):
        pi = [0]
        done = 0
        pending = []
        while done < ncols_total:
            nc_ = min(512, ncols_total - done)
            bv, wtag = load_w(Wap, 0, nk, c0 + done, nc_)
            nm = nc_ // 128
            if ensure is not None and done == 0 and fb_outer:
                new_pending = []
                for tb in tbs:
                    ensure(tb)
                    for m in range(nm):
                        ps = psums[pi[0] % len(psums)]
                        pi[0] += 1
                        for kc in range(nk):
                            P.mm(ps[:, :], bv[:, kc, m * 128:(m + 1) * 128], src[:, kc, tb * 512:(tb + 1) * 512],
                                 start=(kc == 0), stop=(kc == nk - 1), lt=wtag, rtg=tb)
                        t_ = epi(chunk0 + m, tb, ps)
                        if t_ is not None:
                            new_pending.append(t_)
                    if tb == 2:
                        w_tick()
                for t_ in new_pending:
                    t_()
                pending = []
                done += nc_
                continue
            for m in range(nm):
                new_pending = []
                for tb in tbs:
                    if ensure is not None:
                        ensure(tb)
                    ps = psums[pi[0] % len(psums)]
                    pi[0] += 1
                    for kc in range(nk):
                        P.mm(ps[:, :], bv[:, kc, m * 128:(m + 1) * 128], src[:, kc, tb * 512:(tb + 1) * 512],
                             start=(kc == 0), stop=(kc == nk - 1), lt=wtag, rtg=tb)
                    t_ = epi(chunk0 + (done // 128) + m, tb, ps)
                    if t_ is not None:
                        new_pending.append(t_)
                    if m == (nm - 1) // 2 and tb == 2:
                        w_tick()
                for t_ in pending:
                    t_()
                pending = new_pending
            done += nc_
        for t_ in pending:
            t_()

    def linear_tm(Wap, nk, c0, ncols, src, psums, epi, tok_chunks, tok_off=0):
        bv, wtag = load_w(Wap, 0, nk, c0, ncols)
        tok_chunks = list(tok_chunks)
        for j, tc in enumerate(tok_chunks):
            ps = psums[j % len(psums)]
            t0 = tok_off + tc * 128
            for kc in range(nk):
                P.mm(ps[:, 0:ncols], src[:, kc, t0:t0 + 128], bv[:, kc, :], start=(kc == 0), stop=(kc == nk - 1),
                     lt=(t0 // 512 if tok_off == 0 else None), rtg=wtag)
            epi(tc, ps)
            if j == len(tok_chunks) // 2:
                w_tick()

    def conv3(acc, raw, taps):
        P.v("tensor_scalar", out=acc[:, :], in0=raw[:, :], scalar1=taps[:, 1:2], scalar2=None, op0=ALU.mult)
        for (s0, s1) in SEQS:
            P.v("scalar_tensor_tensor", out=acc[:, s0 + 1:s1], in0=raw[:, s0:s1 - 1], scalar=taps[:, 0:1],
                in1=acc[:, s0 + 1:s1], op0=ALU.mult, op1=ALU.add)
            P.v("scalar_tensor_tensor", out=acc[:, s0:s1 - 1], in0=raw[:, s0 + 1:s1], scalar=taps[:, 2:3],
                in1=acc[:, s0:s1 - 1], op0=ALU.mult, op1=ALU.add)

    def adaln_layer(l, bufs, pa, pre_hook=None):
        nb_ = len(bufs)

        def dma_(s_):
            sv_ = bufs[s_ % nb_][:, :].rearrange("p (k n) -> p k n", k=8)
            P.dma(sv_, w_ada[l][:, s_ * 512:(s_ + 1) * 512].rearrange("(k p) n -> p k n", p=128), q="sp")
        for s_ in range(min(12, nb_)):
            dma_(s_)
        if pre_hook is not None:
            pre_hook()
        for s_ in range(12):
            sv = bufs[s_ % nb_][:, :].rearrange("p (k n) -> p k n", k=8)
            if s_ >= nb_:
                dma_(s_)
            ps = pa[s_ % len(pa)]
            for m in range(4):
                for kc in range(8):
                    P.mm(ps[:, 2 * m:2 * m + 2], sv[:, kc, m * 128:(m + 1) * 128], scv_p[:, kc, :],
                         start=(kc == 0), stop=(kc == 7))
            for m in range(4):
                n = s_ * 4 + m
                P.v("tensor_scalar", out=modt[:, l, n, :], in0=ps[:, 2 * m:2 * m + 2], scalar1=bt_p[:, l, n:n + 1],
                    scalar2=None, op0=ALU.add)
        for sub in range(2):
            sc0 = 8 + 24 * sub
            P.v("tensor_scalar", out=Atab[:, l, sub, :, :], in0=modt[:, l, sc0:sc0 + 8, :], scalar1=1.0,
                scalar2=None, op0=ALU.add)
            for j in range(2):
                P.v("tensor_tensor", out=Atab[:, l, sub, :, j], in0=Atab[:, l, sub, :, j],
                    in1=gPt[:, 2 * sub + l, :], op=ALU.mult)

    def phase_adaln():
        pa = [P.ps("pa%d" % i) for i in range(2)]
        P.dma(scv_p[:], cv[:, :, :])
        P.dma(bt_p[:], b_adaP[:, :, :])
        P.act(scv_p[:], scv_p[:], AF.Silu)
        extra = [P.sb("adx%d" % i, [128, 4096], F32) for i in range(4)]
        adaln_layer(0, wst + extra, pa)

    def phase_norm(l, sub, src, final=False):
        xb = [P.sb("xb%d" % i, [128, 8, 512], F32) for i in range(2)]
        sq = [P.sb("sq%d" % i, [128, 8, 512], BF16) for i in range(2)]
        rs = [P.sb("rs%d" % i, [128, 512], F32) for i in range(2)]
        tmp = [P.sb("tmp%d" % i, [128, 8, 512], F32) for i in range(2)]
        hb = [P.sb("hb%d" % i, [128, 8, 512], (F32 if final else BF16)) for i in range(2)]
        pn = [P.ps("pn%d" % i) for i in range(2)]
        for tb in range(NTB):
            j = 0 if tb < 4 else 1
            i = tb % 2
            tsl = slice(tb * 512, (tb + 1) * 512)
            P.dma(xb[i][:], src[:, :, tsl].rearrange("c p t -> p c t"))
            P.act(sq[i][:], xb[i][:], AF.Square)
            for kc in range(8):
                P.mm(pn[i][:, :], ones_bf[:], sq[i][:, kc, :], start=(kc == 0), stop=(kc == 7))
            P.act(rs[i][:], pn[i][:, :], AF.Sqrt, scale=1.0 / D, bias=epsc[:, 0:1])
            P.v("reciprocal", out=rs[i][:], in_=rs[i][:])
            for kc in range(8):
                if final:
                    P.v("scalar_tensor_tensor", out=hb[i][:, kc, :], in0=xb[i][:, kc, :], scalar=gPt[:, 4, kc:kc + 1],
                        in1=rs[i][:], op0=ALU.mult, op1=ALU.mult, rt=kc, wt=kc)
                else:
                    P.v("scalar_tensor_tensor", out=tmp[i][:, kc, :], in0=xb[i][:, kc, :],
                        scalar=Atab[:, l, sub, kc, j:j + 1], in1=rs[i][:], op0=ALU.mult, op1=ALU.mult, rt=kc, wt=kc)
                    P.act(hb[i][:, kc, :], tmp[i][:, kc, :], AF.Identity, bias=modt[:, l, 24 * sub + kc, j:j + 1],
                          rt=kc, wt=kc)
            dst = yT if final else HN
            P.dma(dst[:, :, tsl].rearrange("c p t -> p c t"), hb[i][:], q="pool")
            if tb == 0 and l == 0 and sub == 0 and not final:
                dbgdump("rs", rs[i][:])
                dbgdump("xb", xb[i][:, 0, :])
                dbgdump("tmp", tmp[i][:, 0, :])
                dbgdump("atab", Atab[:].rearrange("p a b c d -> p (a b c d)"))
                dbgdump("modt", modt[:].rearrange("p a b c -> p (a b c)"))

    def norm_hn(l, sub, src):
        hn = P.sb("hn", [128, 8, T], BF16)
        xb = [P.sb("nxb%d" % i, [128, 8, 512], F32) for i in range(2)]
        rs = [P.sb("nrs%d" % i, [128, 512], F32) for i in range(2)]
        pn = [P.ps("npn%d" % i) for i in range(2)]
        st_ = {"dma": 0, "done": 0}

        def dma(tb):
            tsl = slice(tb * 512, (tb + 1) * 512)
            P.dma(xb[tb % 2][:], src[:, :, tsl].rearrange("c p t -> p c t"), q=("pool" if tb % 2 == 0 else "sp"))

        def ensure(tb):
            while st_["done"] <= tb:
                t = st_["done"]
                while st_["dma"] < min(NTB, t + 2):
                    dma(st_["dma"])
                    st_["dma"] += 1
                i = t % 2
                j = 0 if t < 4 else 1
                tsl = slice(t * 512, (t + 1) * 512)
                P.act(hn[:, :, tsl], xb[i][:], AF.Square, wt=t)
                for kc in range(8):
                    P.mm(pn[i][:, :], ones_bf[:], hn[:, kc, tsl], start=(kc == 0), stop=(kc == 7), rtg=t)
                P.act(rs[i][:], pn[i][:, :], AF.Sqrt, scale=1.0 / D, bias=epsc[:, 0:1])
                P.v("reciprocal", out=rs[i][:], in_=rs[i][:])
                for kc in range(8):
                    P.v("scalar_tensor_tensor", out=xb[i][:, kc, :], in0=xb[i][:, kc, :],
                        scalar=Atab[:, l, sub, kc, j:j + 1], in1=rs[i][:], op0=ALU.mult, op1=ALU.mult, rt=kc, wt=kc)
                for kc in range(8):
                    P.add("act", "activation", dict(out=hn[:, kc, tsl], in_=xb[i][:, kc, :], func=AF.Identity,
                                                    bias=modt[:, l, 24 * sub + kc, j:j + 1]),
                          r=[(xb[i], kc), modt], w=[(hn, t)])
                st_["done"] += 1
        return hn, ensure

    def load_hn():
        hn = P.sb("hn", [128, 8, T], BF16)
        for tb in range(NTB):
            tsl = slice(tb * 512, (tb + 1) * 512)
            P.dma(hn[:, :, tsl], HN[:, :, tsl].rearrange("c p t -> p c t"), q="pool", wt=tb)
        return hn

    def phase_proj_ab():
        w_plan([(w_in_ab, 8, c_, n_) for (c_, n_) in [(0, 512), (512, 512), (1024, 512), (1536, 512), (2048, 16),
                                                      (2064, 512), (2576, 512), (3088, 512)]])
        hn, ens = norm_hn(0, 0, xT)
        raw = [P.sb("raw%d" % i, [128, T], F32) for i in range(2)]
        acc = [P.sb("acc%d" % i, [128, T], F32) for i in range(2)]
        obf = [P.sb("obf%d" % i, [128, T], BF16) for i in range(2)]
        tq = P.sb("tq", [128, 8, 3], F32)
        th = P.sb("th", [128, 12, 3], F32)
        bg = P.sb("bg", [128, 16], F32)
        gt = P.sb("gt", [128, NTC, 16], F32)
        gtmp = P.sb("gtmp", [128, NTC, 8], F32)
        vb = [P.sb("vb%d" % i, [128, 512], BF16) for i in range(2)]
        sob = [P.sb("sob%d" % i, [128, 512], BF16) for i in range(2)]
        pp = [P.ps("pp%d" % i) for i in range(4)]
        P.dma(tq[:], wcqk[:, :, :])
        P.dma(th[:], wchy[:, :, :])
        P.dma(bg[:], b_gates[0:1, :].broadcast_to([128, 16]))

        def epi_qk(n, tb, ps):
            i = n % 2
            P.act(raw[i][:, tb * 512:(tb + 1) * 512], ps[:, :], AF.Identity)
            if tb == NTB - 1:
                def tail():
                    conv3(acc[i], raw[i], tq[:, n, :])
                    P.act(acc[i][:], acc[i][:], AF.Silu)
                    P.v("tensor_scalar", out=obf[i][:], in0=acc[i][:], scalar1=(1.0 if n < 4 else 128.0 ** -0.5),
                        scalar2=None, op0=ALU.mult)
                    P.dma(QK[n], obf[i][:], q="pool")
                return tail
        linear_fm(w_in_ab, 8, 0, 1024, hn, pp, epi_qk, ensure=ens)

        def epi_v(tc, ps):
            i = tc % 2
            P.act(vb[i][:], ps[:, :], AF.Identity)
            P.dma(V[tc, :, 0:512], vb[i][:], q="pool")
        linear_tm(w_in_ab, 8, 1024, 512, hn, pp, epi_v, range(NTC))

        def epi_o(n, tb, ps):
            i = (n * NTB + tb) % 2
            P.act(sob[i][:], ps[:, :], AF.Sigmoid)
            P.dma(SO[n, :, tb * 512:(tb + 1) * 512], sob[i][:], q="pool")
        linear_fm(w_in_ab, 8, 1536, 512, hn, pp, epi_o)

        def epi_g(tc, ps):
            P.v("tensor_tensor", out=gt[:, tc, :], in0=ps[:, 0:16], in1=bg[:], op=ALU.add)
        linear_tm(w_in_ab, 8, 2048, 16, hn, pp, epi_g, range(NTC))
        for d in range(2):
            fs = gt[:, :, 4 + 8 * d:8 + 8 * d]
            ts = gtmp[:, :, 4 * d:4 * d + 4]
            P.act(ts, fs, AF.Exp, scale=-1.0)
            P.act(ts, ts, AF.Ln, bias=ones_f[:, 0:1])
            P.v("tensor_scalar", out=fs, in0=ts, scalar1=-1.0, scalar2=None, op0=ALU.mult)
        P.dma(G[:, :, :], gt[:], q="pool")

        def epi_hy(n, tb, ps):
            i = n % 2
            P.act(raw[i][:, tb * 512:(tb + 1) * 512], ps[:, :], AF.Identity)
            if tb == NTB - 1:
                def tail():
                    conv3(acc[i], raw[i], th[:, n, :])
                    P.dma(HY[n], acc[i][:], q="pool")
                return tail
        linear_fm(w_in_ab, 8, 2064, 1536, hn, pp, epi_hy)
        w_end()

    def phase_mlstm():
        gt = P.sb("gt", [128, NTC, 16], F32)
        P.dma(gt[:], G[:, :, :])
        stn = P.sb("stn", [128, 8], F32)
        stm = P.sb("stm", [128, 8], F32)
        P.dma(stn[:], st_n[:, :])
        P.dma(stm[:], st_m[:, :])
        P.act(stm[:], stm[:], AF.Exp)
        Cst = P.sb("Cst", [128, 4, 128], F32)
        Cbf = P.sb("Cbf", [128, 4, 128], BF16)
        nst = P.sb("nst", [128, 4], F32)
        nrep = P.sb("nrep", [128, 4, 128], BF16)
        mrun = P.sb("mrun", [128, 4], F32)
        ktm_all = P.sb("ktmall", [128, NTC, 512], BF16)
        H4 = range(4)
        NB = 2

        def mk(name, shape, dt):
            return [[P.sb("%s%d_%d" % (name, b, h), shape, dt) for h in H4] for b in range(NB)]
        qTt = [P.sb("qTt%d" % i, [128, 4, 128], BF16) for i in range(3)]
        kTt = [P.sb("kTt%d" % i, [128, 4, 128], BF16) for i in range(3)]
        vch = [P.sb("vch%d" % i, [128, 512], BF16) for i in range(3)]
        acol = [P.sb("acol%d" % i, [128, 4], F32) for i in range(NB)]
        hfo = [P.sb("hfo%d" % i, [128, 4, 128], F32) for i in range(NB)]
        lfrep, brow, arg, eb = mk("lfrep", [128, 128], F32), mk("brow", [128, 128], F32), mk("arg", [128, 128], F32), mk("eb", [128, 128], F32)
        PT, qt, kw = mk("PT", [128, 128], BF16), mk("qt", [128, 128], BF16), mk("kw", [128, 128], BF16)
        wcol, tmx = mk("wcol", [128, 2], F32), mk("tmx", [128, 1], F32)
        irep, te = mk("irep", [128, 128], F32), mk("te", [128, 128], F32)
        dd = [P.sb("dd%d" % h, [128, 128], F32) for h in H4]
        c0t = [P.sb("c0t%d" % i, [128, 128], F32) for i in H4]
        cout = P.sb("cout", [128, 4, 130], F32)
        pA = [P.ps("pmA%d" % i) for i in H4]
        pND = [P.ps("pmN%d" % i) for i in range(2)]
        pCU = [P.ps("pmCU%d" % i) for i in range(NB)]

        for tc in range(NTC):
            i = tc % 3
            P.dma(kTt[i][:], QK[4:8, :, tc * 128:(tc + 1) * 128].rearrange("h p t -> p h t"))
            for h in H4:
                P.tr(pND[tc % 2][:, :].bitcast(BF16)[:, h * 128:(h + 1) * 128], kTt[i][:, h, :], ident_bf[:])
            P.act(ktm_all[:, tc, :], pND[tc % 2][:, :].bitcast(BF16)[:, 0:512], AF.Identity)

        steps = []
        for si, (s0, s1) in enumerate(SEQS):
            nch = (s1 - s0) // 128
            for d in range(2):
                order = list(range(nch)) if d == 0 else list(range(nch - 1, -1, -1))
                for j, c in enumerate(order):
                    steps.append(dict(si=si, d=d, c=c, first=(j == 0), last=(j == nch - 1), t0=s0 + c * 128))
        ld = [0]

        def front(k):
            st = steps[k]
            b = k % NB
            d, t0 = st["d"], st["t0"]
            prompt = st["si"] > 0
            tc = t0 // 128
            li = ld[0] % 3
            ld[0] += 1
            st["li"] = li
            bend_c = 127 if d == 0 else 0
            g = gt[:, tc, :]

            def f0():
                P.dma(qTt[li][:], QK[0:4, :, t0:t0 + 128].rearrange("h p t -> p h t"))
                P.dma(kTt[li][:], QK[4:8, :, t0:t0 + 128].rearrange("h p t -> p h t"))
                P.dma(vch[li][:], V[tc, :, 0:512])
                P.mm(pA[0][:, 392 + 4 * b:396 + 4 * b], tri[d], g[:, 4 + 8 * d:8 + 8 * d])
                P.v("tensor_tensor", out=acol[b][:], in0=g[:, 8 * d:8 * d + 4], in1=pA[0][:, 392 + 4 * b:396 + 4 * b],
                    op=ALU.subtract)

            def f1():
                for h in H4:
                    P.act(lfrep[b][h][:], ones_f[:], AF.Identity, scale=g[:, 4 + 8 * d + h:5 + 8 * d + h])
                    if prompt:
                        P.act(irep[b][h][:], ones_f[:], AF.Identity, scale=g[:, 8 * d + h:8 * d + h + 1])

            def f2():
                for h in H4:
                    P.mm(pA[h][:, 0:128], lfrep[b][h][:], tri[d])
                    P.mm(pA[h][:, 256:384], kTt[li][:, h, :], qTt[li][:, h, :])
                    if prompt:
                        P.mm(pA[h][:, 128:256], irep[b][h][:], ident)

            def f3():
                for h in H4:
                    P.act(brow[b][h][:], pA[h][:, 0:128], AF.Identity)

            def f4():
                for h in H4:
                    P.v("scalar_tensor_tensor", out=arg[b][h][:], in0=brow[b][h][:], scalar=acol[b][:, h:h + 1],
                        in1=maskb[d], op0=ALU.add, op1=ALU.add)
                    if prompt:
                        bend = brow[b][h][:, bend_c:bend_c + 1]
                        P.v("scalar_tensor_tensor", out=te[b][h][:], in0=pA[h][:, 128:256], scalar=bend,
                            in1=brow[b][h][:], op0=ALU.add, op1=ALU.subtract)
                        P.v("tensor_reduce", out=tmx[b][h][:], in_=te[b][h][:], axis=AX.X, op=ALU.max)

            def f5():
                for h in H4:
                    bend = brow[b][h][:, bend_c:bend_c + 1]
                    P.act(arg[b][h][:], arg[b][h][:], AF.Exp)
                    P.act(eb[b][h][:], brow[b][h][:], AF.Exp)
                    P.act(wcol[b][h][:, 0:1], acol[b][:, h:h + 1], AF.Exp, bias=bend)
                    P.act(wcol[b][h][:, 1:2], bend, AF.Exp)

            def f6():
                for h in H4:
                    P.v("tensor_tensor", out=PT[b][h][:], in0=pA[h][:, 256:384], in1=arg[b][h][:], op=ALU.mult)
                    P.v("tensor_tensor", out=qt[b][h][:], in0=qTt[li][:, h, :], in1=eb[b][h][:], op=ALU.mult)
                    P.v("tensor_scalar", out=kw[b][h][:], in0=ktm_all[:, tc, h * 128:(h + 1) * 128],
                        scalar1=wcol[b][h][:, 0:1], scalar2=None, op0=ALU.mult)

            def f7():
                for h in H4:
                    P.mm(pCU[b][:, h * 128:(h + 1) * 128], kw[b][h][:], vch[li][:, h * 128:(h + 1) * 128])
                    P.mm(pA[h][:, 384 + b:385 + b], kw[b][h][:], ones_bf[:, 0:1])
            return [f0, f1, f2, f3, f4, f5, f6, f7]

        def back(k):
            st = steps[k]
            b = k % NB
            d, t0, li = st["d"], st["t0"], st["li"]
            prompt = st["si"] > 0
            bend_c = 127 if d == 0 else 0
            HOUT = HF if d == 0 else HB

            def b0():
                if st["first"]:
                    for h in H4:
                        if prompt:
                            P.v("memset", ap=Cst[:, h, :], constant=0.0)
                        else:
                            P.dma(c0t[h][:], st_C[d * 4 + h], q="pool")
                            P.v("tensor_scalar", out=Cst[:, h, :], in0=c0t[h][:], scalar1=stm[:, d * 4 + h:d * 4 + h + 1],
                                scalar2=None, op0=ALU.mult)
                    if prompt:
                        P.v("memset", ap=nst[:], constant=0.0)
                        P.v("memset", ap=mrun[:], constant=0.0)
                    else:
                        P.v("tensor_tensor", out=nst[:], in0=stn[:, d * 4:d * 4 + 4], in1=stm[:, d * 4:d * 4 + 4],
                            op=ALU.mult)
                    for h in H4:
                        P.act(Cbf[:, h, :], Cst[:, h, :], AF.Identity, wt=h)
                        P.act(nrep[:, h, :], ones_f[:], AF.Identity, scale=nst[:, h:h + 1], wt=h)
                for h in H4:
                    nd = pND[h // 2]
                    o0 = (h % 2) * 256
                    P.mm(nd[:, o0:o0 + 128], vch[li][:, h * 128:(h + 1) * 128], PT[b][h][:], start=True, stop=False)
                    P.mm(nd[:, o0:o0 + 128], Cbf[:, h, :], qt[b][h][:], start=False, stop=True, lt=h)
                    P.mm(nd[:, o0 + 128:o0 + 256], ones_bf[:], PT[b][h][:], start=True, stop=False)
                    P.mm(nd[:, o0 + 128:o0 + 256], nrep[:, h, :], qt[b][h][:], start=False, stop=True, lt=h)

            def b1():
                for h in H4:
                    nd = pND[h // 2]
                    o0 = (h % 2) * 256
                    P.act(dd[h][:], nd[:, o0 + 128:o0 + 256], AF.Abs)

            def b2():
                for h in H4:
                    nd = pND[h // 2]
                    o0 = (h % 2) * 256
                    bend = brow[b][h][:, bend_c:bend_c + 1]
                    P.v("tensor_scalar", out=dd[h][:], in0=dd[h][:], scalar1=1.0, scalar2=None, op0=ALU.max)
                    P.v("reciprocal", out=dd[h][:], in_=dd[h][:])
                    P.v("tensor_tensor", out=hfo[b][:, h, :], in0=nd[:, o0:o0 + 128], in1=dd[h][:], op=ALU.mult)
                    if prompt:
                        P.v("scalar_tensor_tensor", out=mrun[:, h:h + 1], in0=mrun[:, h:h + 1], scalar=bend,
                            in1=tmx[b][h][:], op0=ALU.add, op1=ALU.max)
                    P.v("scalar_tensor_tensor", out=Cst[:, h, :], in0=Cst[:, h, :], scalar=wcol[b][h][:, 1:2],
                        in1=pCU[b][:, h * 128:(h + 1) * 128], op0=ALU.mult, op1=ALU.add)
                    P.v("scalar_tensor_tensor", out=nst[:, h:h + 1], in0=nst[:, h:h + 1], scalar=wcol[b][h][:, 1:2],
                        in1=pA[h][:, 384 + b:385 + b], op0=ALU.mult, op1=ALU.add)

            def b3():
                for h in H4:
                    P.act(Cbf[:, h, :], Cst[:, h, :], AF.Identity, wt=h)
                    P.act(nrep[:, h, :], ones_f[:], AF.Identity, scale=nst[:, h:h + 1], wt=h)
                P.dma(HOUT[:, :, t0:t0 + 128].rearrange("h p t -> p h t"), hfo[b][:], q="pool")
                if st["last"] and prompt:
                    pi = st["si"] - 1
                    P.act(cout[:, :, 129:130], mrun[:].rearrange("p (h o) -> p h o", o=1), AF.Exp, scale=-1.0)
                    for h in H4:
                        P.v("tensor_scalar", out=cout[:, h, 0:128], in0=Cst[:, h, :], scalar1=cout[:, h, 129:130],
                            scalar2=None, op0=ALU.mult)
                        P.v("tensor_scalar", out=cout[:, h, 128:129], in0=nst[:, h:h + 1], scalar1=cout[:, h, 129:130],
                            scalar2=None, op0=ALU.mult)
                    P.dma(o_C[pi, d].rearrange("h p e -> p h e"), cout[:, :, 0:128], q="pool")
                    P.dma(o_n[pi, d].rearrange("h (p o) -> p h o", o=1), cout[:, :, 128:129], q="pool", slow=True)
                    P.dma(o_m[pi, d:d + 1, :], mrun[0:1, :], q="pool")
            return [b0, b1, b2, b3]

        for f in front(0):
            f()
        for k in range(len(steps)):
            B = back(k)
            if k + 1 < len(steps):
                F = front(k + 1)
                for fn in (F[0], F[1], B[0], F[2], B[1], F[3], B[2], F[4], B[3], F[5], F[6], F[7]):
                    fn()
            else:
                for fn in B:
                    fn()

    def phase_mlstm_fin():
        gml = P.sb("gml", [128, 4], F32)
        P.dma(gml[:], g_ml[:, :])
        hf = [P.sb("fhf%d" % i, [128, T], F32) for i in range(2)]
        hb_ = [P.sb("fhb%d" % i, [128, T], F32) for i in range(2)]
        so = [P.sb("fso%d" % i, [128, T], BF16) for i in range(2)]
        sq = [P.sb("fsq%d" % i, [128, T], BF16) for i in range(2)]
        rs = [P.sb("frs%d" % i, [128, T], F32) for i in range(2)]
        ym = [P.sb("fym%d" % i, [128, T], BF16) for i in range(2)]
        pn = [P.ps("fpn%d" % i) for i in range(5)]
        for h in range(4):
            i = h % 2
            P.dma(hf[i][:], HF[h])
            P.dma(hb_[i][:], HB[h])
            P.dma(so[i][:], SO[h])
            P.v("tensor_tensor", out=hf[i][:], in0=hf[i][:], in1=hb_[i][:], op=ALU.add)
            P.act(sq[i][:], hf[i][:], AF.Square)
            for tb in range(NTB):
                P.mm(pn[tb][:, :], ones_bf[:], sq[i][:, tb * 512:(tb + 1) * 512])
                P.act(rs[i][:, tb * 512:(tb + 1) * 512], pn[tb][:, :], AF.Ln, scale=1.0 / 128, bias=epsc[:, 0:1])
                P.act(rs[i][:, tb * 512:(tb + 1) * 512], rs[i][:, tb * 512:(tb + 1) * 512], AF.Exp, scale=-0.5)
            P.v("scalar_tensor_tensor", out=hf[i][:], in0=hf[i][:], scalar=gml[:, h:h + 1], in1=rs[i][:], op0=ALU.mult,
                op1=ALU.mult)
            P.v("tensor_tensor", out=ym[i][:], in0=hf[i][:], in1=so[i][:], op=ALU.mult)
            P.dma(Y[h], ym[i][:], q="pool")

    def phase_filters():
        w1 = P.sb("w1", [33, 64], F32)
        w2 = P.sb("w2", [64, 64], F32)
        w3 = P.sb("w3", [64, 1024], F32)
        fv = P.sb("fv", [64, 4], F32)
        dl = P.sb("dl", [128, 512], F32)
        P.dma(w1[:], w_f1[:, :], q="pool")
        P.dma(w2[:], w_f2[:, :], q="pool")
        P.dma(w3[:], w_f3[:, :], q="pool")
        P.dma(fv[:, 0:3], fvec[:, :], q="pool")
        P.dma(dl[:], deltas[0:1, 0:512].broadcast_to([128, 512]), q="pool")
        pg = P.ps("pg")
        pf = [P.ps("pff%d" % i) for i in range(2)]
        z = P.sb("z", [33, LS], F32)
        h1 = P.sb("h1", [64, LS], F32)
        h2 = P.sb("h2", [64, LS], F32)
        a1 = P.sb("a1", [64, 512], F32)
        a2 = P.sb("a2", [64, 512], F32)
        dec = P.sb("dec", [128, 512], F32)
        tng = P.sb("tng", [128, 16], F32)
        fo = [P.sb("fo%d" % i, [128, 1024], BF16) for i in range(2)]
        frb2 = P.sb("frb2", [64, 1], F32)
        P.v("tensor_scalar", out=fv[:, 3:4], in0=fv[:, 0:1], scalar1=fv[:, 2:3], scalar2=None, op0=ALU.mult)
        P.v("tensor_scalar", out=frb2[:], in0=fv[:, 1:2], scalar1=fv[:, 2:3], scalar2=None, op0=ALU.mult)

        def sin_layer(dst, n, bias_ap):
            P.act(a1[:, 0:n], pg[0:64, 0:n], AF.Identity, scale=fv[:, 2:3], bias=bias_ap)
            P.v("tensor_scalar", out=a2[:, 0:n], in0=a1[:, 0:n], scalar1=1.0 / TWO_PI, scalar2=MAGIC, op0=ALU.mult,
                op1=ALU.add)
            P.v("tensor_scalar", out=a2[:, 0:n], in0=a2[:, 0:n], scalar1=MAGIC, scalar2=None, op0=ALU.subtract)
            P.v("scalar_tensor_tensor", out=a1[:, 0:n], in0=a2[:, 0:n], scalar=-TWO_PI, in1=a1[:, 0:n], op0=ALU.mult,
                op1=ALU.add)
            P.v("tensor_scalar", out=a1[:, 0:n], in0=a1[:, 0:n], scalar1=-3.141592, scalar2=3.141592, op0=ALU.max,
                op1=ALU.min)
            P.act(dst, a1[:, 0:n], AF.Sin)

        for L in (LS, LP):
            c = hc[L]
            na = c["na"]
            P.dma(z[:, 0:L], c["zT"][:, :], q="pool")
            P.dma(tng[:, 0:na], c["tneg"][:, :], q="pool")
            nblk = max(1, L // 512)
            n = min(L, 512)
            for b in range(nblk):
                sl = slice(b * n, (b + 1) * n)
                P.mm(pg[0:64, 0:n], w1[:, :], z[:, sl])
                sin_layer(h1[:, sl], n, fv[:, 3:4])
            for b in range(nblk):
                sl = slice(b * n, (b + 1) * n)
                P.mm(pg[0:64, 0:n], w2[:, :], h1[:, sl])
                sin_layer(h2[:, sl], n, frb2[:, 0:1])
            for a in range(na):
                i = a % 2
                P.act(dec[:], dl[:], AF.Exp, scale=tng[:, a:a + 1])
                for half in range(2):
                    ps = pf[half]
                    P.mm(ps[:, :], h2[:, a * 128:(a + 1) * 128], w3[:, half * 512:(half + 1) * 512])
                    dst = fo[i][:, half * 512:(half + 1) * 512]
                    P.v("tensor_tensor", out=dst, in0=ps[:, :], in1=dec[:], op=ALU.mult)
                    if half == 1 and a == 0:
                        P.v("tensor_scalar", out=dst, in0=dst, scalar1=cst_t[:, 5, 64:65], scalar2=None, op0=ALU.mult)
                P.dma(c["FL"][a], fo[i][:], q="pool")

    def phase_filters_adaln1():
        extra = [P.sb("adx%d" % i, [128, 4096], F32) for i in range(6)]
        pa = [P.ps("pa%d" % i) for i in range(2)]
        adaln_layer(1, wst + extra, pa, pre_hook=phase_filters)

    def phase_hyena():
        hb_t = P.sb("hybt", [128, 4], F32)
        P.dma(hb_t[:], hyb[:, :])
        pf = [P.ps("pf%d" % i) for i in range(6)]
        p_tr = P.ps("ptrh", [128, 1024], BF16)
        rhs_all = P.sb("rhsall", [128, 16, 1536], BF16)
        GH = P.sb("GH", [128, 17, 2, 512], BF16)
        tis = [P.sb("tis%d" % i, [128, 2, 512], F32) for i in range(3)]
        tib = [P.sb("tib%d" % i, [128, 2, 512], BF16) for i in range(3)]
        wtt = P.sb("wtt", [128, 17, 2], F32)
        ua = [P.sb("ua%d" % i, [128, 512], F32) for i in range(4)]
        ub = [P.sb("ub%d" % i, [128, 512], BF16) for i in range(2)]
        e1 = [P.sb("e1_%d" % i, [128, 512], F32) for i in range(12)]
        m1 = [P.sb("m1_%d" % i, [128, 512], F32) for i in range(2)]
        uc = [P.sb("uc%d" % i, [128, 512], F32) for i in range(2)]
        x2c = [P.sb("x2c%d" % i, [128, 512], F32) for i in range(2)]
        yh = [P.sb("yh%d" % i, [128, 512], BF16) for i in range(2)]
        tctr = [0]

        def hyena_seq(c, s0, load_filt):
            L, na, nb = c["L"], c["na"], c["nb"]
            n = min(L, 512)
            nblk = max(1, L // 512)
            if load_filt:
                P.dma(wtt[:, 0:nb, :], c["wt"][:, :, :])
                P.dma(rhs_all[:, 0:na, 512:1536], c["FL"][:, :, :].rearrange("a p n -> p a n"))
            for cc in range(4):
                for tb in range(nblk):
                    i = (cc * nblk + tb) % 2
                    tsl = slice(s0 + tb * n, s0 + (tb + 1) * n)
                    P.dma(ua[i][:, 0:n], HY[cc, :, tsl])
                    P.dma(ua[2 + i][:, 0:n], HY[4 + cc, :, tsl])
                    P.v("tensor_tensor", out=ua[i][:, 0:n], in0=ua[i][:, 0:n], in1=ua[2 + i][:, 0:n], op=ALU.mult)
                    P.dma(U[cc, :, tsl], ua[i][:, 0:n])
                    P.v("tensor_copy", out=ub[i][:, 0:n], in_=ua[i][:, 0:n])
                    nn = n // 128
                    for a in range(nn):
                        P.tr(p_tr[:, a * 128:(a + 1) * 128], ub[i][:, a * 128:(a + 1) * 128], ident_bf[:])
                    a0 = tb * 4
                    P.act(rhs_all[:, a0:a0 + nn, cc * 128:(cc + 1) * 128],
                          p_tr[:, 0:nn * 128].rearrange("p (a t) -> p a t", a=nn), AF.Identity)
            def load_tf(b):
                i = wctr[0] % 2
                wctr[0] += 1
                sv = wst[i][:, 0:na * 256].rearrange("p (a s f) -> p a s f", a=na, s=2)
                bv = wbf[i][:, 0:na * 256].rearrange("p (a s f) -> p a s f", a=na, s=2)
                P.dma(sv, c["TF"][b], q="pool")
                P.act(bv, sv, AF.Identity)
                return bv
            nxt = load_tf(0)
            for b in range(nb):
                bv = nxt
                if b + 1 < nb:
                    nxt = load_tf(b + 1)
                for a in range(na):
                    for j in range(3):
                        P.mm(pf[j][:, :], bv[:, a, 0, :], rhs_all[:, a, j * 512:(j + 1) * 512], start=(a == 0),
                             stop=(a == na - 1))
                        P.mm(pf[3 + j][:, :], bv[:, a, 1, :], rhs_all[:, a, j * 512:(j + 1) * 512], start=(a == 0),
                             stop=(a == na - 1))
                wtc = wtt[:, b, 0:1]
                nwtc = wtt[:, b, 1:2]
                ee = e1[(b % 2) * 6:(b % 2) * 6 + 6]
                au, ap_, aq, bu, bp, bq = ee
                P.act(au[:], pf[0][:, :], AF.Identity)
                P.act(ap_[:], pf[1][:, :], AF.Identity, scale=wtc)
                P.act(aq[:], pf[2][:, :], AF.Identity, scale=wtc)
                P.act(bu[:], pf[3][:, :], AF.Identity)
                P.act(bp[:], pf[4][:, :], AF.Identity, scale=wtc)
                P.act(bq[:], pf[5][:, :], AF.Identity, scale=nwtc)
                Kr, Ki = ap_, bp
                P.v("tensor_tensor", out=Kr[:], in0=ap_[:], in1=aq[:], op=ALU.add)
                P.v("tensor_tensor", out=Ki[:], in0=bp[:], in1=bq[:], op=ALU.add)
                P.v("tensor_tensor", out=m1[0][:], in0=au[:], in1=Kr[:], op=ALU.mult)
                P.v("tensor_tensor", out=m1[1][:], in0=bu[:], in1=Ki[:], op=ALU.mult)
                P.v("tensor_tensor", out=GH[:, b, 0, :], in0=m1[0][:], in1=m1[1][:], op=ALU.subtract)
                P.v("tensor_tensor", out=m1[0][:], in0=au[:], in1=Ki[:], op=ALU.mult)
                P.v("tensor_tensor", out=m1[1][:], in0=bu[:], in1=Kr[:], op=ALU.mult)
                P.v("tensor_tensor", out=GH[:, b, 1, :], in0=m1[0][:], in1=m1[1][:], op=ALU.add)
            seq = [(tb, b) for tb in range(nblk) for b in range(nb)]
            st_ = {"dma": 0, "cast": 0}

            def ti_dma(k):
                tb_k, b_k = seq[k]
                i = (tctr[0] + k) % 3
                P.dma(tis[i][:, :, 0:n], c["TI"][:, b_k, :, tb_k * n:(tb_k + 1) * n], q=("pool" if k % 2 == 0 else "sp"))

            def ti_cast(k):
                i = (tctr[0] + k) % 3
                P.act(tib[i][:, :, 0:n], tis[i][:, :, 0:n], AF.Identity)

            def ti_get(k):
                while st_["dma"] < min(len(seq), k + 3):
                    ti_dma(st_["dma"])
                    st_["dma"] += 1
                while st_["cast"] < min(len(seq), k + 2):
                    ti_cast(st_["cast"])
                    st_["cast"] += 1
                return tib[(tctr[0] + k) % 3]
            kk = 0
            for tb in range(nblk):
                for b in range(nb):
                    tb_ = ti_get(kk)
                    kk += 1
                    for cc in range(4):
                        P.mm(pf[cc][:, 0:n], GH[:, b, 0, cc * 128:(cc + 1) * 128], tb_[:, 0, 0:n], start=(b == 0),
                             stop=False)
                        P.mm(pf[cc][:, 0:n], GH[:, b, 1, cc * 128:(cc + 1) * 128], tb_[:, 1, 0:n], start=False,
                             stop=(b == nb - 1))
                for cc in range(4):
                    ps = pf[cc]
                    i = cc % 2
                    tsl = slice(s0 + tb * n, s0 + (tb + 1) * n)
                    P.dma(uc[i][:, 0:n], U[cc, :, tsl])
                    P.dma(x2c[i][:, 0:n], HY[8 + cc, :, tsl])
                    P.v("scalar_tensor_tensor", out=uc[i][:, 0:n], in0=uc[i][:, 0:n], scalar=hb_t[:, cc:cc + 1],
                        in1=ps[:, 0:n], op0=ALU.mult, op1=ALU.add)
                    P.v("tensor_tensor", out=yh[i][:, 0:n], in0=uc[i][:, 0:n], in1=x2c[i][:, 0:n], op=ALU.mult)
                    P.dma(Y[4 + cc, :, tsl], yh[i][:, 0:n])
            tctr[0] += len(seq)

        hyena_seq(hc[LS], 0, True)
        hyena_seq(hc[LP], 2048, True)
        hyena_seq(hc[LP], 2304, False)

    def phase_resproj(Wap, nk, srcD, l, sub, xsrc):
        src = P.sb("rsrc", [128, nk, T], BF16)
        for tb in range(NTB):
            tsl_ = slice(tb * 512, (tb + 1) * 512)
            P.dma(src[:, :, tsl_], srcD[:, :, tsl_].rearrange("c p t -> p c t"), q=("pool" if tb % 2 == 0 else "sp"), wt=tb)
        pp = [P.ps("pr%d" % i) for i in range(4)]
        xr = [P.sb("xr%d" % i, [128, T], F32) for i in range(2)]

        def epi(n, tb, ps):
            i = n % 2
            j = 0 if tb < 4 else 1
            tsl = slice(tb * 512, (tb + 1) * 512)
            if tb == 0:
                P.dma(xr[i][:], xsrc[n], q="pool")
            P.v("scalar_tensor_tensor", out=xr[i][:, tsl], in0=ps[:, :], scalar=modt[:, l, 16 + 24 * sub + n, j:j + 1],
                in1=xr[i][:, tsl], op0=ALU.mult, op1=ALU.add)
            if tb == NTB - 1:
                P.dma(X[n], xr[i][:], q="pool")
        ncols = max(128, (4096 // nk) // 128 * 128)
        ncols = min(ncols, 512)
        w_plan([(Wap, nk, c_, ncols) for c_ in range(0, D, ncols)])
        done = 0
        pi = [0]
        while done < D:
            nc_ = min(ncols, D - done)
            bv, wtag = load_w(Wap, 0, nk, done, nc_)
            nm = nc_ // 128
            for m in range(nm):
                for tb in range(NTB):
                    ps = pp[pi[0] % 4]
                    pi[0] += 1
                    for kc in range(nk):
                        P.mm(ps[:, :], bv[:, kc, m * 128:(m + 1) * 128], src[:, kc, tb * 512:(tb + 1) * 512],
                             start=(kc == 0), stop=(kc == nk - 1), lt=wtag, rtg=tb)
                    epi(done // 128 + m, tb, ps)
                    if m == (nm - 1) // 2 and tb == 2:
                        w_tick()
            done += nc_
        w_end()

    def phase_ffn_up(l):
        w_plan([(w_up[l], 8, br * FF + n * 128, 128) for n in range(22) for br in range(2)])
        hn, ens = norm_hn(l, 1, X)
        raw = [P.sb("raw%d" % i, [128, T], F32) for i in range(2)]
        acc = [P.sb("acc%d" % i, [128, T], F32) for i in range(2)]
        gbuf = [P.sb("gbuf%d" % i, [128, T], F32) for i in range(2)]
        obf = [P.sb("obf%d" % i, [128, T], BF16) for i in range(2)]
        tw = P.sb("tw", [128, 22, 3], F32)
        P.dma(tw[:], wcffn[:, l, :, :])
        pp = [P.ps("pu%d" % i) for i in range(4)]
        W = w_up[l]
        pend = [None]
        for n in range(22):
            i = n % 2
            for br in range(2):
                bv, wtag = load_w(W, 0, 8, br * FF + n * 128, 128)
                for tb in range(NTB):
                    ens(tb)
                    ps = pp[(br * NTB + tb) % 4]
                    for kc in range(8):
                        P.mm(ps[:, :], bv[:, kc, :], hn[:, kc, tb * 512:(tb + 1) * 512], start=(kc == 0), stop=(kc == 7),
                             lt=wtag, rtg=tb)
                    dstt = raw[i] if br == 0 else gbuf[i]
                    P.act(dstt[:, tb * 512:(tb + 1) * 512], ps[:, :], AF.Identity)
                    if tb == 2:
                        w_tick()
            def tail(i=i, n=n):
                conv3(acc[i], raw[i], tw[:, n, :])
                P.act(acc[i][:], acc[i][:], AF.Gelu_apprx_tanh)
                P.v("tensor_tensor", out=obf[i][:], in0=acc[i][:], in1=gbuf[i][:], op=ALU.mult)
                P.dma(ACTS[n], obf[i][:], q="pool")
            if pend[0] is not None:
                pend[0]()
            pend[0] = tail
        pend[0]()
        w_end()

    def phase_proj_c():
        rq = [(w_in_c, 8, c_, 512) for c_ in (0, 512, 1024, 1536)]
        for half_ in range(2):
            rq += [(w_in_c, 8, 2048 + half_ * 512, 512), (w_in_c, 8, 1024 + half_ * 512, 512)]
        w_plan(rq)
        hn, ens = norm_hn(1, 0, X)
        pp = [P.ps("pc%d" % i) for i in range(4)]
        qb = [P.sb("qb%d" % i, [128, 512], BF16) for i in range(3)]
        vb = [P.sb("vb%d" % i, [128, 512], BF16) for i in range(3)]
        kf = [P.sb("kf%d" % i, [128, 512], F32) for i in range(3)]
        ctr = [0]

        def epi_qk(n, tb, ps):
            i = ctr[0] % 3
            ctr[0] += 1
            if n < 8:
                P.act(qb[i][:], ps[:, :], AF.Identity, scale=0.125)
            else:
                P.act(qb[i][:], ps[:, :], AF.Identity)
            P.dma(QK[n, :, tb * 512:(tb + 1) * 512], qb[i][:], q="pool")
        linear_fm(w_in_c, 8, 0, 2048, hn, pp, epi_qk, ensure=ens, fb_outer=True)
        for half in range(2):
            def epi_v(tc, ps, half=half):
                i = ctr[0] % 3
                ctr[0] += 1
                P.act(vb[i][:], ps[:, :], AF.Identity)
                P.dma(V[tc, :, half * 512:(half + 1) * 512], vb[i][:], q="pool")
                if tc >= 16:
                    pi_, t0 = (tc - 16) // 2, ((tc - 16) % 2) * 128
                    P.v("tensor_copy", out=kf[i][:], in_=ps[:, :])
                    P.dma(o_v[pi_, half * 8:(half + 1) * 8, t0:t0 + 128, :].rearrange("h t d -> t h d"),
                          kf[i][:].rearrange("p (h d) -> p h d", d=64), q="pool")
            linear_tm(w_in_c, 8, 2048 + half * 512, 512, hn, pp, epi_v, range(NTC))

            def epi_k(tc, ps, half=half):
                i = ctr[0] % 3
                ctr[0] += 1
                pi_, t0 = (tc - 16) // 2, ((tc - 16) % 2) * 128
                P.act(kf[i][:], ps[:, :], AF.Identity)
                P.dma(o_k[pi_, half * 8:(half + 1) * 8, t0:t0 + 128, :].rearrange("h t d -> t h d"),
                      kf[i][:].rearrange("p (h d) -> p h d", d=64), q="pool")
            linear_tm(w_in_c, 8, 1024 + half * 512, 512, hn, pp, epi_k, range(16, 20))
        w_end()

    def phase_attn():
        def mkset(i):
            d_ = {}
            d_["qbd"] = P.sb("aqbd%d" % i, [128, 2, T], BF16)
            d_["kT"] = P.sb("akT%d" % i, [128, T], BF16)
            d_["kcx"] = P.sb("akc%d" % i, [128, 512], F32)
            d_["kcb"] = P.sb("akcb%d" % i, [128, 512], BF16)
            d_["vcx"] = P.sb("avc%d" % i, [128, 4, 2, 64], F32)
            d_["vcb"] = P.sb("avcb%d" % i, [128, 4, 128], BF16)
            d_["Vs"] = P.sb("aVs%d" % i, [128, 16, 128], BF16)
            d_["Vs64"] = P.sb("aVs64%d" % i, [128, 15, 128], BF16)
            d_["Vp"] = P.sb("aVp%d" % i, [128, 4, 128], BF16)
            d_["Tmf"] = [P.sb("aTmf%d_%d" % (i, j), [128, 15, 64], F32) for j in range(2)]
            d_["Tmb2"] = P.sb("aTmb2%d" % i, [128, 15, 2, 64], BF16)
            d_["mx"] = P.sb("amx%d" % i, [128, 32], F32)
            d_["negC"] = P.sb("anegC%d" % i, [128, 1], F32)
            d_["ysb"] = P.sb("aysb%d" % i, [128, T], BF16)
            P.v("memset", ap=d_["qbd"][:], constant=0.0)
            return d_
        sets = [mkset(0), mkset(1)]
        sqq = P.sb("asqq", [128, 2, T], BF16)
        sqk = P.sb("asqk", [128, T + 512], BF16)
        selM = P.sb("aselM", [128, 2, 128], BF16)
        P.v("memset", ap=selM[:], constant=0.0)
        P.v("memset", ap=selM[0:64, 0, :], constant=1.0)
        P.v("memset", ap=selM[64:128, 1, :], constant=1.0)
        NW = 2
        PTt = [P.sb("aPT%d" % i, [128, 1024], BF16) for i in range(NW)]
        pts = [P.sb("apts%d" % i, [128, 256], F32) for i in range(NW)]
        rd = [P.sb("ard%d" % i, [128, 256], F32) for i in range(NW)]
        pstA = [P.ps("apsa%d" % i) for i in range(NW)]
        pstB = [P.ps("apsb%d" % i) for i in range(NW)]
        po = [P.ps("apo%d" % i) for i in range(NW)]
        pn_ = P.ps("apn")

        def setup_dma(n):
            S_ = sets[n % 2]
            q_ = "pool"
            P.dma(S_["qbd"][0:64, 0, :], QK[n, 0:64, :], q=q_)
            P.dma(S_["qbd"][64:128, 1, :], QK[n, 64:128, :], q=q_)
            P.dma(S_["kT"][:], QK[8 + n], q=q_)
            P.dma(S_["kcx"][:], kcT[n], q=q_)
            for hh in range(2):
                P.dma(S_["vcx"][:, :, hh, :], vc[2 * n + hh].rearrange("(c p) d -> p c d", p=128), q=q_)
                P.dma(S_["Tmf"][hh][:], TmE[2 * n + hh], q=q_)
            P.dma(S_["Vs"][:], V[0:16, :, n * 128:(n + 1) * 128].rearrange("c p d -> p c d"), q=q_)
            P.dma(S_["Vs64"][0:64, :, :], V[0:15, 64:128, n * 128:(n + 1) * 128].rearrange("c p d -> p c d"), q=q_)
            P.dma(S_["Vs64"][64:128, :, :], V[1:16, 0:64, n * 128:(n + 1) * 128].rearrange("c p d -> p c d"), q=q_)
            P.dma(S_["Vp"][:], V[16:20, :, n * 128:(n + 1) * 128].rearrange("c p d -> p c d"), q=q_)

        def setup_compute(n):
            S_ = sets[n % 2]
            mx = S_["mx"]
            P.v("tensor_copy", eng="pool", out=S_["kcb"][:], in_=S_["kcx"][:])
            P.v("tensor_copy", eng="pool", out=S_["vcb"][:].rearrange("p c (h d) -> p c h d", h=2), in_=S_["vcx"][:])
            for hh in range(2):
                P.v("tensor_tensor", eng="pool", out=S_["Tmf"][hh][:], in0=S_["Tmf"][hh][:],
                    in1=colmask[:, None, :].broadcast_to([128, 15, 64]), op=ALU.add)
                P.v("tensor_copy", eng="pool", out=S_["Tmb2"][:, :, hh, :], in_=S_["Tmf"][hh][:])
            P.act(sqq[:], S_["qbd"][:], AF.Square)
            P.act(sqk[:, 0:T], S_["kT"][:], AF.Square)
            P.act(sqk[:, T:T + 512], S_["kcb"][:], AF.Square)
            for hh in range(2):
                for tb in range(5):
                    P.mm(pn_[:, 0:512], ones_bf[:], sqq[:, hh, tb * 512:(tb + 1) * 512])
                    P.v("tensor_reduce", out=mx[:, hh * 16 + tb:hh * 16 + tb + 1], in_=pn_[:, 0:512], axis=AX.X, op=ALU.max)
                for tb in range(6):
                    P.mm(pn_[:, 0:512], selM[:, hh, :], sqk[:, tb * 512:(tb + 1) * 512])
                    P.v("tensor_reduce", out=mx[:, hh * 16 + 5 + tb:hh * 16 + 6 + tb], in_=pn_[:, 0:512], axis=AX.X,
                        op=ALU.max)
                P.v("tensor_reduce", out=mx[:, hh * 16 + 12:hh * 16 + 13], in_=mx[:, hh * 16:hh * 16 + 5], axis=AX.X, op=ALU.max)
                P.v("tensor_reduce", out=mx[:, hh * 16 + 13:hh * 16 + 14], in_=mx[:, hh * 16 + 5:hh * 16 + 11], axis=AX.X,
                    op=ALU.max)
                P.v("tensor_tensor", out=mx[:, hh * 16 + 14:hh * 16 + 15], in0=mx[:, hh * 16 + 12:hh * 16 + 13],
                    in1=mx[:, hh * 16 + 13:hh * 16 + 14], op=ALU.mult)
            P.v("tensor_tensor", out=mx[:, 15:16], in0=mx[:, 14:15], in1=mx[:, 30:31], op=ALU.max)
            P.act(mx[:, 31:32], mx[:, 15:16], AF.Sqrt)
            P.v("tensor_scalar", out=S_["negC"][:], in0=mx[:, 31:32], scalar1=-1.0, scalar2=None, op0=ALU.mult)

        def stage_s(w, S_, qt0, nq, chunks, x0):
            n2 = 2 * nq
            qsl = S_["qbd"][:, :, qt0:qt0 + nq]
            for ch, (kap, vap) in enumerate(chunks):
                pt_, c_ = (pstA[w], ch) if ch * n2 < 512 else (pstB[w], ch - 512 // n2)
                dst = pt_[:, c_ * n2:(c_ + 1) * n2]
                local = (x0 is not None and ch < 4)
                P.mm(dst, kap, qsl, start=True, stop=(not local))
                if local:
                    P.mm(dst, ident_bf[:], S_["Tmb2"][:, x0 + 2 * ch, :, :], start=False, stop=True)
            P.act(PTt[w][:, 0:512], pstA[w][:, :], AF.Exp, bias=S_["negC"][:, 0:1])
            if len(chunks) * n2 > 512:
                P.act(PTt[w][:, 512:1024], pstB[w][:, :], AF.Exp, bias=S_["negC"][:, 0:1])

        def stage_o(w, S_, qt0, nq, chunks, x0):
            n2 = 2 * nq
            nch = len(chunks)
            for ch, (kap, vap) in enumerate(chunks):
                P.mm(po[w][:, 0:n2], vap, PTt[w][:, ch * n2:(ch + 1) * n2], start=(ch == 0), stop=(ch == nch - 1))
            for ch in range(nch):
                P.mm(po[w][:, 256:256 + n2], ones_bf[:], PTt[w][:, ch * n2:(ch + 1) * n2], start=(ch == 0),
                     stop=(ch == nch - 1))
            P.act(rd[w][:, 0:n2], po[w][:, 256:256 + n2], AF.Ln)
            P.act(rd[w][:, 0:n2], rd[w][:, 0:n2], AF.Exp, scale=-1.0)
            for hh in range(2):
                ps_ = slice(hh * 64, hh * 64 + 64)
                cs_ = slice(hh * nq, (hh + 1) * nq)
                P.v("tensor_tensor", out=S_["ysb"][ps_, qt0:qt0 + nq], in0=po[w][ps_, cs_], in1=rd[w][ps_, cs_], op=ALU.mult,
                    wt=(qt0 * 2 + hh))

        blk = [0]
        setup_dma(0)
        setup_compute(0)
        for n in range(8):
            S_ = sets[n % 2]
            if n + 1 < 8:
                setup_dma(n + 1)
            kT, kcb, vcb, Vs, Vs64, Vp = S_["kT"], S_["kcb"], S_["vcb"], S_["Vs"], S_["Vs64"], S_["Vp"]
            blocks = []
            for r in range(32):
                j0 = min(max(r - 4, 0), 24)
                k0 = j0 * 64
                chunks = []
                for i in range(4):
                    vap = Vs[:, j0 // 2 + i, :] if j0 % 2 == 0 else Vs64[:, j0 // 2 + i, :]
                    chunks.append((kT[:, k0 + i * 128:k0 + (i + 1) * 128], vap))
                for i in range(4):
                    chunks.append((kcb[:, i * 128:(i + 1) * 128], vcb[:, i, :]))
                blocks.append((S_, r * 64, 64, chunks, j0 - r + 7))
            for pi_ in range(2):
                s0 = 2048 + pi_ * 256
                chunks = [(kT[:, s0 + i * 128:s0 + (i + 1) * 128], Vp[:, pi_ * 2 + i, :]) for i in range(2)]
                for hf in range(2):
                    blocks.append((S_, s0 + hf * 128, 128, chunks, None))
            prev = None
            for bi, bk in enumerate(blocks):
                w = blk[0] % NW
                blk[0] += 1
                stage_s(w, *bk)
                if prev is not None:
                    stage_o(*prev)
                prev = (w,) + bk
                if bi == 12 and n + 1 < 8:
                    setup_compute(n + 1)
            stage_o(*prev)
            P.dma(Y[n], S_["ysb"][:])

    P.end_phase()

    stages = [
        ("adaln", phase_adaln),
        ("copyx", None),
        ("projab", phase_proj_ab),
        ("mlstm0", phase_mlstm),
        ("mlstm", phase_mlstm_fin),
        ("filters", phase_filters_adaln1),
        ("hyena", phase_hyena),
        ("outab", lambda: phase_resproj(w_out_ab, 8, Y, 0, 0, xT)),
        ("ffnup0", lambda: phase_ffn_up(0)),
        ("down0", lambda: phase_resproj(w_down[0], 22, ACTS, 0, 1, X)),
        ("projc", phase_proj_c),
        ("attn", phase_attn),
        ("outc", lambda: phase_resproj(w_out_c, 8, Y, 1, 0, X)),
        ("ffnup1", lambda: phase_ffn_up(1)),
        ("down1", lambda: phase_resproj(w_down[1], 22, ACTS, 1, 1, X)),
        ("final", lambda: phase_norm(0, 0, X, final=True)),
    ]
    dbg_outs = {}
    for name, fn in stages:
        if fn is None:
            continue
        P.begin_phase()
        fn()
        P.end_phase()
        if debug is not None and name == debug[0]:
            P.begin_phase()
            for (tn, shape, dt) in debug[1]:
                srcT = {"X": X, "HN": HN, "QK": QK, "V": V, "SO": SO, "G": G, "HY": HY, "U": U, "HF": HF, "Y": Y,
                        "ACTS": ACTS, "V64": V64}[tn]
                o = dout("dbg_" + tn, shape, dt)
                dbg_outs[tn] = o
                isc = (len(shape) == 3 and shape[1] == 128)
                buf = P.sb("dbgbuf", [128, int(np.prod(shape[2:])) if isc else int(np.prod(shape[1:]))], dt)
                if isc:
                    for ci in range(shape[0]):
                        P.dma(buf[:], srcT[ci])
                        P.dma(o[ci], buf[:])
                else:
                    P.dma(buf[:], srcT.rearrange("p a b -> p (a b)"))
                    P.dma(o.rearrange("p a b -> p (a b)"), buf[:])
            P.end_phase()
            break
    P.close()
    nc._phase_counts = P.phase_counts
    return nc


def _consts():
    ar = np.arange(128)
    ident = np.eye(128, dtype=np.float32)
    tri_f = (ar[:, None] <= ar[None, :]).astype(np.float32)
    tri_b = (ar[:, None] >= ar[None, :]).astype(np.float32)
    maskb_f = np.where(ar[:, None] <= ar[None, :], 0.0, NEG).astype(np.float32)
    maskb_b = np.where(ar[:, None] >= ar[None, :], 0.0, NEG).astype(np.float32)
    cols = np.arange(64)
    c_start = np.clip(cols - 8, 0, 48)
    valid = (cols[:, None] >= c_start[None, :]) & (cols[:, None] < c_start[None, :] + 16)
    cm = np.where(valid, 0.0, NEG).astype(np.float32)
    last = np.zeros((128, 128), np.float32)
    last[0:64, 0:64] = cm
    last[64:128, 0:64] = cm
    last[:, 64] = 1.0
    last[0, 64] = 0.0
    cst = np.stack([ident, tri_f, tri_b, maskb_f, maskb_b, last], axis=1)
    out = {"cst": np.ascontiguousarray(cst)}
    deltas = np.abs(np.linspace(math.log(1e-2) / 1.5, math.log(1e-2) / 0.3, 512, dtype=np.float32))
    out["deltas"] = np.concatenate([deltas, deltas])[None, :].astype(np.float32)
    for L in (LS, LP):
        c = hy_cfg(L)
        N, na, nb = c["N"], c["na"], c["nb"]
        t = np.arange(na * 128, dtype=np.int64)
        f = np.arange(nb * 128, dtype=np.int64)
        ang = 2.0 * np.pi * ((t[:, None] * f[None, :]) % N).astype(np.float64) / N
        Cm, Sm = np.cos(ang), np.sin(ang)
        TF = np.stack([Cm, Sm], axis=0).reshape(2, na, 128, nb, 128).transpose(3, 2, 1, 0, 4)
        out["TF%d" % L] = np.ascontiguousarray(TF).astype(np.float32)
        tt = np.arange(L, dtype=np.int64)
        ang2 = 2.0 * np.pi * ((f[:, None] * tt[None, :]) % N).astype(np.float64) / N
        TI = np.stack([np.cos(ang2), np.sin(ang2)], axis=0).reshape(2, nb, 128, L).transpose(2, 1, 0, 3)
        out["TI%d" % L] = np.ascontiguousarray(TI).astype(np.float32)
        tl = np.linspace(0.0, 1.0, L, dtype=np.float32)
        wpos = (2.0 * np.pi * np.arange(L, dtype=np.float32) / L).astype(np.float32)
        bands = np.linspace(1e-4, 15, 16, dtype=np.float32)
        z = np.concatenate([tl[:, None], np.cos(bands[None, :] * wpos[:, None]), -np.sin(bands[None, :] * wpos[:, None])],
                           axis=-1).astype(np.float32)
        out["zT%d" % L] = np.ascontiguousarray(z.T)
        out["tneg%d" % L] = np.ascontiguousarray((-tl).reshape(na, 128).T)
        wt = np.zeros(nb * 128, np.float64)
        wt[0] = 1.0 / N
        wt[1:L] = 2.0 / N
        wt[L] = 1.0 / N
        wt2 = np.stack([wt, -wt], axis=-1).reshape(nb, 128, 2).transpose(1, 0, 2)
        out["wt%d" % L] = np.ascontiguousarray(wt2).astype(np.float32)
    return out


_CACHE = {}


def _prep_inputs(inp):
    f = lambda a: np.ascontiguousarray(np.asarray(a, dtype=np.float32))
    shared = dict(_consts())
    shared["w_ada"] = f(inp["w_ada"])
    shared["b_adaP"] = f(np.asarray(inp["b_ada"]).reshape(2, 48, 128).transpose(2, 0, 1))
    gs = np.stack([inp["g_mix"][0], inp["g_mix"][1], inp["g_ffn"][0], inp["g_ffn"][1], inp["g_final"]], axis=0)
    shared["gP"] = f(gs.reshape(5, 8, 128).transpose(2, 0, 1))
    shared["w_in_ab"] = f(inp["w_in_ab"][0])
    shared["b_gates"] = f(inp["b_gates"][0][None, :])
    shared["wcqk"] = f(np.asarray(inp["w_conv_qk"][0]).reshape(3, 8, 128).transpose(2, 1, 0))
    shared["g_ml"] = f(np.asarray(inp["g_mlstm"][0]).reshape(4, 128).T)
    shared["wchy"] = f(np.asarray(inp["w_conv_hy"][0]).reshape(3, 12, 128).transpose(2, 1, 0))
    shared["w_f1"] = f(inp["w_filt1"][0])
    shared["w_f2"] = f(inp["w_filt2"][0])
    shared["w_f3"] = f(inp["w_filt3"][0])
    shared["fvec"] = f(np.stack([inp["b_filt1"][0], inp["b_filt2"][0], inp["filt_freq"][0]], axis=-1))
    shared["hyb"] = f(np.asarray(inp["hyena_bias"][0]).reshape(4, 128).T)
    shared["w_out_ab"] = f(inp["w_out_ab"][0])
    shared["w_in_c"] = f(inp["w_in_c"][0])
    rp = np.asarray(inp["rpb_c"][0], dtype=np.float32)
    pidx = np.arange(128)
    wk = (pidx % 64)[:, None, None]
    up = (pidx >= 64).astype(np.int64)[:, None, None]
    xx = np.arange(15)[None, :, None]
    wq = np.arange(64)[None, None, :]
    ci = wk - wq + 15
    ri = xx + up
    ok = (ci >= 0) & (ci <= 30) & (ri <= 14)
    gat = rp[:, np.clip(ri, 0, 14), np.clip(ci, 0, 30)]
    shared["TmE"] = np.ascontiguousarray(np.where(ok[None], gat, 0.0).astype(np.float32))
    shared["w_out_c"] = f(inp["w_out_c"][0])
    shared["w_up"] = f(inp["w_up"])
    shared["wcffn"] = f(np.asarray(inp["w_conv_ffn"]).reshape(2, 3, 22, 128).transpose(3, 0, 2, 1))
    shared["w_down"] = f(inp["w_down"])
    maps = []
    for c in range(8):
        b = c % 4
        m = dict(shared)
        toks = np.concatenate([inp["x_sample"][b], inp["x_prompt"][2 * c], inp["x_prompt"][2 * c + 1]], axis=0)
        m["xT"] = f(np.asarray(toks).T.reshape(8, 128, T))
        cvv = np.stack([inp["c"][b], inp["c_ctx"]], axis=-1)
        m["cv"] = f(cvv.reshape(8, 128, 2).transpose(1, 0, 2))
        m["st_C"] = f(np.asarray(inp["state_mlstm_C"][b, 0]).reshape(8, 128, 128))
        m["st_n"] = f(np.asarray(inp["state_mlstm_n"][b, 0]).reshape(8, 128).T)
        m["st_m"] = f(np.broadcast_to(np.asarray(inp["state_mlstm_m"][b, 0]).reshape(1, 8), (128, 8)))
        m["kcT"] = f(np.asarray(inp["cache_na_k"][b, 0]).transpose(0, 2, 1).reshape(8, 128, 512))
        m["vc"] = f(inp["cache_na_v"][b, 0])
        maps.append(m)
    return maps


def kernel(**inputs):
    inp = {k: np.asarray(v) for k, v in inputs.items()}
    if "nc" not in _CACHE:
        _CACHE["nc"] = build()
    nc = _CACHE["nc"]
    maps = _prep_inputs(inp)
    res = run_bass_kernel_spmd(nc, maps, core_ids=list(range(8)))
    R = res.results
    y_prompt = np.zeros((16, 256, D), np.float32)
    y_sample = np.zeros((4, 2048, D), np.float32)
    nC = np.zeros((16, 1, 2, 4, 128, 128), np.float32)
    nn = np.zeros((16, 1, 2, 4, 128), np.float32)
    nm = np.zeros((16, 1, 2, 4), np.float32)
    nk = np.zeros((16, 1, 16, 256, 64), np.float32)
    nv = np.zeros((16, 1, 16, 256, 64), np.float32)
    for c in range(8):
        yt = np.asarray(R[c]["yT"]).reshape(D, T).T
        if c < 4:
            y_sample[c] = yt[0:2048]
        y_prompt[2 * c] = yt[2048:2304]
        y_prompt[2 * c + 1] = yt[2304:2560]
        for pi in range(2):
            nC[2 * c + pi, 0] = np.asarray(R[c]["o_C"])[pi]
            nn[2 * c + pi, 0] = np.asarray(R[c]["o_n"])[pi]
            nm[2 * c + pi, 0] = np.asarray(R[c]["o_m"])[pi]
            nk[2 * c + pi, 0] = np.asarray(R[c]["o_k"])[pi]
            nv[2 * c + pi, 0] = np.asarray(R[c]["o_v"])[pi]
    return (y_prompt, y_sample, nC, nn, nm, nk, nv)
```

```python
import math
import os
import numpy as np
from contextlib import ExitStack
import concourse.bass as bass
import concourse.mybir as mybir
from concourse.bass_utils import run_bass_kernel_spmd

F32 = mybir.dt.float32
BF16 = mybir.dt.bfloat16
AF = mybir.ActivationFunctionType
ALU = mybir.AluOpType
AX = mybir.AxisListType

SAME_ENGINE_SYNC = True
N_DMA_SEMS = 20
_READ_KW = ("in_", "in0", "in1", "lhsT", "rhs", "scalar1", "scalar2", "scalar", "bias", "scale",
            "data0", "data1", "initial", "identity")
_WRITE_KW = ("out", "accum_out", "ap")


class Op:
    __slots__ = ("eng", "meth", "kw", "reads", "writes", "deps", "needs_inc", "sem", "val", "is_dma", "done")

    def __init__(self, eng, meth, kw, reads, writes, is_dma):
        self.eng, self.meth, self.kw = eng, meth, kw
        self.reads, self.writes = reads, writes
        self.deps = []
        self.needs_inc = False
        self.sem = None
        self.val = None
        self.is_dma = is_dma
        self.done = False


class Prog:
    def __init__(self, nc):
        self.nc = nc
        self.es = ExitStack()
        self.phase_es = None
        self.ops = []
        self.state = {}
        self.engs = {"pe": nc.tensor, "act": nc.scalar, "dve": nc.vector, "pool": nc.gpsimd, "sp": nc.sync}
        self.esem = {}
        for e in ("pe", "act", "dve", "pool"):
            self.esem[e] = self.es.enter_context(nc.semaphore("es_" + e))
        self.dsems = {}
        for q in ("sp", "pool"):
            self.dsems[q] = [self.es.enter_context(nc.semaphore("ds_%s_%d" % (q, i))) for i in range(N_DMA_SEMS)]
        self.dcount = {"sp": 0, "pool": 0}
        self.ecount = {e: 0 for e in self.esem}
        self.seen = {e: {} for e in self.engs}
        self.emitted = 0
        self.uid = 0
        self.phase_counts = []

    def sb(self, name, shape, dtype, persist=False):
        self.uid += 1
        st = self.es if (persist or self.phase_es is None) else self.phase_es
        return st.enter_context(self.nc.sbuf_tensor("%s_%d" % (name, self.uid), list(shape), dtype))

    def ps(self, name, shape=(128, 512), dtype=F32):
        self.uid += 1
        return self.phase_es.enter_context(self.nc.psum_tensor("%s_%d" % (name, self.uid), list(shape), dtype))

    def dram(self, name, shape, dtype, kind="Internal"):
        return self.nc.dram_tensor(name, list(shape), dtype, kind=kind).ap()

    @staticmethod
    def _key(x):
        if isinstance(x, tuple):
            return (x[0].name if not isinstance(x[0], str) else x[0]), x[1]
        if isinstance(x, str):
            return x, None
        return x.name, None

    def _access(self, op, key, write):
        name, tag = key
        st = self.state.setdefault(name, {})
        tags = list(st.keys()) if tag is None else [t for t in (tag, None) if t in st]
        for t in tags:
            lw, rd = st[t]
            if lw is not None:
                op.deps.append(lw)
            if write:
                op.deps.extend(rd)
        if write:
            if tag is None:
                st.clear()
            st[tag] = [op, []]
        else:
            if tag not in st:
                st[tag] = [None, []]
            st[tag][1].append(op)

    def add(self, eng, meth, kw, r=None, w=None, rt=None, wt=None, is_dma=False):
        reads = list(r) if r is not None else []
        writes = list(w) if w is not None else []
        if r is None:
            for k in _READ_KW:
                v = kw.get(k, None)
                if v is not None and hasattr(v, "name") and hasattr(v, "ap"):
                    reads.append((v, rt) if rt is not None else v)
        if w is None:
            for k in _WRITE_KW:
                v = kw.get(k, None)
                if v is not None and hasattr(v, "name"):
                    writes.append((v, wt) if wt is not None else v)
        op = Op(eng, meth, kw, [self._key(x) for x in reads], [self._key(x) for x in writes], is_dma)
        for k in op.reads:
            self._access(op, k, False)
        for k in op.writes:
            self._access(op, k, True)
        seen = set()
        dd = []
        for d in op.deps:
            if d is op or id(d) in seen or d.done:
                continue
            seen.add(id(d))
            dd.append(d)
        op.deps = dd
        for d in dd:
            if d.is_dma:
                continue
            if d.eng == op.eng and not op.is_dma and (d.eng == "pe" or not SAME_ENGINE_SYNC):
                continue
            d.needs_inc = True
        self.ops.append(op)
        return op

    def dma(self, out, in_, q="sp", slow=False, **k):
        kw = dict(out=out, in_=in_)
        if slow:
            kw["allow_slow_non_contiguous"] = True
        return self.add(q, "dma_start", kw, is_dma=True, **k)

    def mm(self, out, lhsT, rhs, start=True, stop=True, lt=None, rtg=None, **k):
        if lt is not None or rtg is not None:
            k["r"] = [(lhsT, lt) if lt is not None else lhsT, (rhs, rtg) if rtg is not None else rhs]
        return self.add("pe", "matmul", dict(out=out, lhsT=lhsT, rhs=rhs, start=start, stop=stop), **k)

    def tr(self, out, in_, identity, **k):
        return self.add("pe", "transpose", dict(out=out, in_=in_, identity=identity), **k)

    def act(self, out, in_, func, rt=None, wt=None, **kw):
        kw.update(out=out, in_=in_, func=func)
        return self.add("act", "activation", kw, rt=rt, wt=wt)

    def v(self, meth, eng="dve", rt=None, wt=None, r=None, w=None, **kw):
        return self.add(eng, meth, kw, rt=rt, wt=wt, r=r, w=w)

    def _wait(self, eng, sem, val):
        s = self.seen[eng]
        k = id(sem)
        if s.get(k, 0) >= val:
            return
        s[k] = val
        self.engs[eng].wait_ge(sem, val)

    def flush(self):
        for op in self.ops[self.emitted:]:
            e = op.eng
            for d in op.deps:
                if d.is_dma:
                    self._wait(e, d.sem, d.val)
                else:
                    if d.eng == e and not op.is_dma and (e == "pe" or not SAME_ENGINE_SYNC):
                        continue
                    self._wait(e, d.sem, d.val)
            if op.is_dma:
                q = e
                i = self.dcount[q]
                self.dcount[q] += 1
                sem = self.dsems[q][i % N_DMA_SEMS]
                val = 16 * (i // N_DMA_SEMS + 1)
                if val > 16:
                    self._wait(q, sem, val - 16)
                op.sem, op.val = sem, val
                self.engs[q].dma_start(**op.kw).then_inc(sem, 16)
            else:
                ins = getattr(self.engs[e], op.meth)(**op.kw)
                if op.needs_inc:
                    self.ecount[e] += 1
                    op.sem, op.val = self.esem[e], self.ecount[e]
                    ins.then_inc(op.sem, 1)
        self.emitted = len(self.ops)

    def barrier(self):
        for e in ("pe", "act", "dve", "pool"):
            for op in reversed(self.ops[self.emitted:]):
                if op.eng == e and not op.is_dma:
                    op.needs_inc = True
                    break
        self.flush()
        cnt = {}
        for op in self.ops:
            k = op.eng + ("_dma" if op.is_dma else "")
            cnt[k] = cnt.get(k, 0) + 1
        self.phase_counts.append(cnt)
        for eng in ("sp", "pe", "act", "dve", "pool"):
            for q in ("sp", "pool"):
                n = self.dcount[q]
                for j in range(min(n, N_DMA_SEMS)):
                    cnt = (n - 1 - j) // N_DMA_SEMS + 1
                    self._wait(eng, self.dsems[q][j], 16 * cnt)
            for e in ("pe", "act", "dve", "pool"):
                if e != eng and self.ecount[e] > 0:
                    self._wait(eng, self.esem[e], self.ecount[e])
        for op in self.ops:
            op.done = True
        self.ops = []
        self.emitted = 0
        self.state = {}

    def begin_phase(self):
        self.phase_es = ExitStack()

    def end_phase(self):
        self.barrier()
        self.phase_es.close()
        self.phase_es = None

    def close(self):
        self.es.close()


D = 1024
T = 2560
LS, LP = 2048, 256
SEQS = [(0, 2048), (2048, 2304), (2304, 2560)]
NTB = 5
NTC = 20
FF = 2816
IN_AB = 3600
EPS = 1e-6
MAGIC = 12582912.0
TWO_PI = 2.0 * math.pi
NEG = -30000.0


def hy_cfg(L):
    N = 2 * L
    na = L // 128
    nb = (L + 1 + 127) // 128
    return dict(L=L, N=N, na=na, nb=nb)


def build(debug=None):
    nc = bass.Bass("TRN2", target_bir_lowering=False)
    P = Prog(nc)

    def din(name, shape, dt=F32):
        return nc.dram_tensor(name, list(shape), dt, kind="ExternalInput").ap()

    def dout(name, shape, dt=F32):
        return nc.dram_tensor(name, list(shape), dt, kind="ExternalOutput").ap()

    def dbgdump(name, ap, dt=F32):
        if debug is None:
            return
        o = dout("dbgm_" + name, list(ap.shape), dt)
        P.dma(o, ap)

    xT = din("xT", [8, 128, T])
    cv = din("cv", [128, 8, 2])
    st_C = din("st_C", [8, 128, 128])
    st_n = din("st_n", [128, 8])
    st_m = din("st_m", [128, 8])
    kcT = din("kcT", [8, 128, 512])
    vc = din("vc", [16, 512, 64])
    w_ada = din("w_ada", [2, D, 6 * D])
    b_adaP = din("b_adaP", [128, 2, 48])
    gP = din("gP", [128, 5, 8])
    w_in_ab = din("w_in_ab", [D, IN_AB])
    b_gates = din("b_gates", [1, 16])
    wcqk = din("wcqk", [128, 8, 3])
    g_ml = din("g_ml", [128, 4])
    wchy = din("wchy", [128, 12, 3])
    w_f1 = din("w_f1", [33, 64])
    w_f2 = din("w_f2", [64, 64])
    w_f3 = din("w_f3", [64, 1024])
    fvec = din("fvec", [64, 3])
    hyb = din("hyb", [128, 4])
    w_out_ab = din("w_out_ab", [D, D])
    w_in_c = din("w_in_c", [D, 3 * D])
    TmE = din("TmE", [16, 128, 15, 64])
    w_out_c = din("w_out_c", [D, D])
    w_up = din("w_up", [2, D, 2 * FF])
    wcffn = din("wcffn", [128, 2, 22, 3])
    w_down = din("w_down", [2, FF, D])
    cst = din("cst", [128, 6, 128])
    deltas = din("deltas", [1, 1024])
    hc = {}
    for L in (LS, LP):
        c = hy_cfg(L)
        c["TF"] = din("TF%d" % L, [c["nb"], 128, c["na"], 2, 128])
        c["TI"] = din("TI%d" % L, [128, c["nb"], 2, L])
        c["zT"] = din("zT%d" % L, [33, L])
        c["tneg"] = din("tneg%d" % L, [128, c["na"]])
        c["wt"] = din("wt%d" % L, [128, c["nb"], 2])
        hc[L] = c

    yT = dout("yT", [8, 128, T])
    o_C = dout("o_C", [2, 2, 4, 128, 128])
    o_n = dout("o_n", [2, 2, 4, 128])
    o_m = dout("o_m", [2, 2, 4])
    o_k = dout("o_k", [2, 16, 256, 64])
    o_v = dout("o_v", [2, 16, 256, 64])

    X = P.dram("X", [8, 128, T], F32)
    HN = P.dram("HN", [8, 128, T], BF16)
    QK = P.dram("QK", [16, 128, T], BF16)
    V = P.dram("V", [NTC, 128, 1024], BF16)
    V64 = P.dram("V64", [16, 128, 1024], BF16)
    SO = P.dram("SO", [4, 128, T], BF16)
    G = P.dram("G", [128, NTC, 16], F32)
    HY = P.dram("HY", [12, 128, T], F32)
    U = P.dram("U", [4, 128, T], F32)
    HF = P.dram("HF", [4, 128, T], F32)
    HB = P.dram("HB", [4, 128, T], F32)
    Y = P.dram("Y", [8, 128, T], BF16)
    ACTS = P.dram("ACTS", [22, 128, T], BF16)
    for L in (LS, LP):
        hc[L]["FL"] = P.dram("FL%d" % L, [hc[L]["na"], 128, 1024], BF16)

    cst_t = P.sb("cst", [128, 6, 128], F32, persist=True)
    ident = cst_t[:, 0, :]
    tri = [cst_t[:, 1, :], cst_t[:, 2, :]]
    maskb = [cst_t[:, 3, :], cst_t[:, 4, :]]
    colmask = cst_t[:, 5, 0:64]
    ident_bf = P.sb("identbf", [128, 128], BF16, persist=True)
    ones_f = P.sb("onesf", [128, 128], F32, persist=True)
    ones_bf = P.sb("onesbf", [128, 128], BF16, persist=True)
    epsc = P.sb("epsc", [128, 1], F32, persist=True)
    modt = P.sb("modt", [128, 2, 48, 2], F32, persist=True)
    Atab = P.sb("Atab", [128, 2, 2, 8, 2], F32, persist=True)
    gPt = P.sb("gPt", [128, 5, 8], F32, persist=True)
    wst = [P.sb("wst%d" % i, [128, 4096], F32, persist=True) for i in range(2)]
    wbf = [P.sb("wbf%d" % i, [128, 4096], BF16, persist=True) for i in range(2)]
    wctr = [0]
    scv_p = P.sb("scv", [128, 8, 2], F32, persist=True)
    bt_p = P.sb("bada", [128, 2, 48], F32, persist=True)

    P.begin_phase()
    P.dma(cst_t[:], cst[:, :, :])
    P.dma(gPt[:], gP[:, :, :])
    P.v("tensor_copy", out=ident_bf[:], in_=ident)
    P.v("memset", ap=ones_f[:], constant=1.0)
    P.v("memset", ap=ones_bf[:], constant=1.0)
    P.v("memset", ap=epsc[:], constant=EPS)

    wplan = {"reqs": None}

    def w_plan(reqs):
        mx_e = max(r[1] * r[3] for r in reqs)
        wplan.update(reqs=reqs, cur=0, dma=0, cast=0, nslots=(2 if mx_e > 1024 else 8),
                     se=(4096 if mx_e > 1024 else 1024), base=wctr[0])
        wctr[0] += 1

    def _w_views(k):
        Wap, nk, c0, ncols = wplan["reqs"][k]
        slot = k % wplan["nslots"]
        off = slot * wplan["se"]
        ti, o = off // 4096, off % 4096
        tag = "w%d_%d" % (wplan["base"], slot)
        sv = wst[ti][:, o:o + nk * ncols].rearrange("p (k n) -> p k n", k=nk)
        bv = wbf[ti][:, o:o + nk * ncols].rearrange("p (k n) -> p k n", k=nk)
        return Wap, nk, c0, ncols, sv, bv, tag

    def _w_dma(k):
        Wap, nk, c0, ncols, sv, bv, tag = _w_views(k)
        P.dma(sv, Wap[0:nk * 128, c0:c0 + ncols].rearrange("(k p) n -> p k n", p=128), q="sp", wt=tag)

    def _w_cast(k):
        Wap, nk, c0, ncols, sv, bv, tag = _w_views(k)
        P.act(bv, sv, AF.Identity, rt=tag, wt=tag)

    def w_tick():
        if wplan["reqs"] is None:
            return
        k = wplan["cur"]
        if k < len(wplan["reqs"]) and wplan["cast"] <= k and wplan["dma"] > k:
            _w_cast(k)
            wplan["cast"] = k + 1

    def w_end():
        wplan["reqs"] = None

    def load_w(Wap, k0, nk, c0, ncols, cast=True):
        if wplan["reqs"] is None or not cast:
            i = wctr[0]
            wctr[0] += 1
            sv = wst[i % 2][:, 0:nk * ncols].rearrange("p (k n) -> p k n", k=nk)
            P.dma(sv, Wap[k0 * 128:(k0 + nk) * 128, c0:c0 + ncols].rearrange("(k p) n -> p k n", p=128), q="sp")
            if not cast:
                return sv
            bv = wbf[i % 2][:, 0:nk * ncols].rearrange("p (k n) -> p k n", k=nk)
            P.act(bv, sv, AF.Identity)
            return bv, None
        k = wplan["cur"]
        rq = wplan["reqs"][k]
        assert rq[1] == nk and rq[2] == c0 and rq[3] == ncols, (rq[1:], nk, c0, ncols)
        while wplan["dma"] <= k:
            _w_dma(wplan["dma"])
            wplan["dma"] += 1
        while wplan["cast"] <= k:
            _w_cast(wplan["cast"])
            wplan["cast"] += 1
        wplan["cur"] = k + 1
        depth = 1 if wplan["nslots"] == 2 else 3
        while wplan["dma"] < min(len(wplan["reqs"]), k + 1 + depth):
            _w_dma(wplan["dma"])
            wplan["dma"] += 1
        return _w_views(k)[5], _w_views(k)[6]

    def linear_fm(Wap, nk, c0, ncols_total, src, psums, epi, tbs=range(NTB), chunk0=0, ensure=None, fb_outer=False):
        pi = [0]
        done = 0
        pending = []
        while done < ncols_total:
            nc_ = min(512, ncols_total - done)
            bv, wtag = load_w(Wap, 0, nk, c0 + done, nc_)
            nm = nc_ // 128
            if ensure is not None and done == 0 and fb_outer:
                new_pending = []
                for tb in tbs:
                    ensure(tb)
                    for m in range(nm):
                        ps = psums[pi[0] % len(psums)]
                        pi[0] += 1
                        for kc in range(nk):
                            P.mm(ps[:, :], bv[:, kc, m * 128:(m + 1) * 128], src[:, kc, tb * 512:(tb + 1) * 512],
                                 start=(kc == 0), stop=(kc == nk - 1), lt=wtag, rtg=tb)
                        t_ = epi(chunk0 + m, tb, ps)
                        if t_ is not None:
                            new_pending.append(t_)
                    if tb == 2:
                        w_tick()
                for t_ in new_pending:
                    t_()
                pending = []
                done += nc_
                continue
            for m in range(nm):
                new_pending = []
                for tb in tbs:
                    if ensure is not None:
                        ensure(tb)
                    ps = psums[pi[0] % len(psums)]
                    pi[0] += 1
                    for kc in range(nk):
                        P.mm(ps[:, :], bv[:, kc, m * 128:(m + 1) * 128], src[:, kc, tb * 512:(tb + 1) * 512],
                             start=(kc == 0), stop=(kc == nk - 1), lt=wtag, rtg=tb)
                    t_ = epi(chunk0 + (done // 128) + m, tb, ps)
                    if t_ is not None:
                        new_pending.append(t_)
                    if m == (nm - 1) // 2 and tb == 2:
                        w_tick()
                for t_ in pending:
                    t_()
                pending = new_pending
            done += nc_
        for t_ in pending:
            t_()

    def linear_tm(Wap, nk, c0, ncols, src, psums, epi, tok_chunks, tok_off=0):
        bv, wtag = load_w(Wap, 0, nk, c0, ncols)
        tok_chunks = list(tok_chunks)
        for j, tc in enumerate(tok_chunks):
            ps = psums[j % len(psums)]
            t0 = tok_off + tc * 128
            for kc in range(nk):
                P.mm(ps[:, 0:ncols], src[:, kc, t0:t0 + 128], bv[:, kc, :], start=(kc == 0), stop=(kc == nk - 1),
                     lt=(t0 // 512 if tok_off == 0 else None), rtg=wtag)
            epi(tc, ps)
            if j == len(tok_chunks) // 2:
                w_tick()

    def conv3(acc, raw, taps):
        P.v("tensor_scalar", out=acc[:, :], in0=raw[:, :], scalar1=taps[:, 1:2], scalar2=None, op0=ALU.mult)
        for (s0, s1) in SEQS:
            P.v("scalar_tensor_tensor", out=acc[:, s0 + 1:s1], in0=raw[:, s0:s1 - 1], scalar=taps[:, 0:1],
                in1=acc[:, s0 + 1:s1], op0=ALU.mult, op1=ALU.add)
            P.v("scalar_tensor_tensor", out=acc[:, s0:s1 - 1], in0=raw[:, s0 + 1:s1], scalar=taps[:, 2:3],
                in1=acc[:, s0:s1 - 1], op0=ALU.mult, op1=ALU.add)

    def adaln_layer(l, bufs, pa, pre_hook=None):
        nb_ = len(bufs)

        def dma_(s_):
            sv_ = bufs[s_ % nb_][:, :].rearrange("p (k n) -> p k n", k=8)
            P.dma(sv_, w_ada[l][:, s_ * 512:(s_ + 1) * 512].rearrange("(k p) n -> p k n", p=128),
                  q=("pool" if (l == 0 and s_ % 2 == 1) else "sp"))
        for s_ in range(min(12, nb_)):
            dma_(s_)
        if pre_hook is not None:
            pre_hook()
        for s_ in range(12):
            sv = bufs[s_ % nb_][:, :].rearrange("p (k n) -> p k n", k=8)
            if s_ >= nb_:
                dma_(s_)
            ps = pa[s_ % len(pa)]
            for m in range(4):
                for kc in range(8):
                    P.mm(ps[:, 2 * m:2 * m + 2], sv[:, kc, m * 128:(m + 1) * 128], scv_p[:, kc, :],
                         start=(kc == 0), stop=(kc == 7))
            for m in range(4):
                n = s_ * 4 + m
                P.v("tensor_scalar", out=modt[:, l, n, :], in0=ps[:, 2 * m:2 * m + 2], scalar1=bt_p[:, l, n:n + 1],
                    scalar2=None, op0=ALU.add)
        for sub in range(2):
            sc0 = 8 + 24 * sub
            P.v("tensor_scalar", out=Atab[:, l, sub, :, :], in0=modt[:, l, sc0:sc0 + 8, :], scalar1=1.0,
                scalar2=None, op0=ALU.add)
            for j in range(2):
                P.v("tensor_tensor", out=Atab[:, l, sub, :, j], in0=Atab[:, l, sub, :, j],
                    in1=gPt[:, 2 * sub + l, :], op=ALU.mult)

    def phase_adaln():
        pa = [P.ps("pa%d" % i) for i in range(2)]
        P.dma(scv_p[:], cv[:, :, :])
        P.dma(bt_p[:], b_adaP[:, :, :])
        P.act(scv_p[:], scv_p[:], AF.Silu)
        extra = [P.sb("adx%d" % i, [128, 4096], F32) for i in range(4)]
        adaln_layer(0, wst + extra, pa)

    def phase_norm(l, sub, src, final=False):
        xb = [P.sb("xb%d" % i, [128, 8, 512], F32) for i in range(2)]
        sq = [P.sb("sq%d" % i, [128, 8, 512], BF16) for i in range(2)]
        rs = [P.sb("rs%d" % i, [128, 512], F32) for i in range(2)]
        tmp = [P.sb("tmp%d" % i, [128, 8, 512], F32) for i in range(2)]
        hb = [P.sb("hb%d" % i, [128, 8, 512], (F32 if final else BF16)) for i in range(2)]
        pn = [P.ps("pn%d" % i) for i in range(2)]
        for tb in range(NTB):
            j = 0 if tb < 4 else 1
            i = tb % 2
            tsl = slice(tb * 512, (tb + 1) * 512)
            P.dma(xb[i][:], src[:, :, tsl].rearrange("c p t -> p c t"))
            P.act(sq[i][:], xb[i][:], AF.Square)
            for kc in range(8):
                P.mm(pn[i][:, :], ones_bf[:], sq[i][:, kc, :], start=(kc == 0), stop=(kc == 7))
            P.act(rs[i][:], pn[i][:, :], AF.Sqrt, scale=1.0 / D, bias=epsc[:, 0:1])
            P.v("reciprocal", out=rs[i][:], in_=rs[i][:])
            for kc in range(8):
                if final:
                    P.v("scalar_tensor_tensor", out=hb[i][:, kc, :], in0=xb[i][:, kc, :], scalar=gPt[:, 4, kc:kc + 1],
                        in1=rs[i][:], op0=ALU.mult, op1=ALU.mult, rt=kc, wt=kc)
                else:
                    P.v("scalar_tensor_tensor", out=tmp[i][:, kc, :], in0=xb[i][:, kc, :],
                        scalar=Atab[:, l, sub, kc, j:j + 1], in1=rs[i][:], op0=ALU.mult, op1=ALU.mult, rt=kc, wt=kc)
                    P.act(hb[i][:, kc, :], tmp[i][:, kc, :], AF.Identity, bias=modt[:, l, 24 * sub + kc, j:j + 1],
                          rt=kc, wt=kc)
            dst = yT if final else HN
            P.dma(dst[:, :, tsl].rearrange("c p t -> p c t"), hb[i][:], q="pool")
            if tb == 0 and l == 0 and sub == 0 and not final:
                dbgdump("rs", rs[i][:])
                dbgdump("xb", xb[i][:, 0, :])
                dbgdump("tmp", tmp[i][:, 0, :])
                dbgdump("atab", Atab[:].rearrange("p a b c d -> p (a b c d)"))
                dbgdump("modt", modt[:].rearrange("p a b c -> p (a b c)"))

    def norm_hn(l, sub, src):
        hn = P.sb("hn", [128, 8, T], BF16)
        xb = [P.sb("nxb%d" % i, [128, 8, 512], F32) for i in range(2)]
        rs = [P.sb("nrs%d" % i, [128, 512], F32) for i in range(2)]
        pn = [P.ps("npn%d" % i) for i in range(2)]
        st_ = {"dma": 0, "done": 0}

        def dma(tb):
            tsl = slice(tb * 512, (tb + 1) * 512)
            P.dma(xb[tb % 2][:], src[:, :, tsl].rearrange("c p t -> p c t"), q=("pool" if tb % 2 == 0 else "sp"))

        def ensure(tb):
            while st_["done"] <= tb:
                t = st_["done"]
                while st_["dma"] < min(NTB, t + 2):
                    dma(st_["dma"])
                    st_["dma"] += 1
                i = t % 2
                j = 0 if t < 4 else 1
                tsl = slice(t * 512, (t + 1) * 512)
                P.act(hn[:, :, tsl], xb[i][:], AF.Square, wt=t)
                for kc in range(8):
                    P.mm(pn[i][:, :], ones_bf[:], hn[:, kc, tsl], start=(kc == 0), stop=(kc == 7), rtg=t)
                P.act(rs[i][:], pn[i][:, :], AF.Sqrt, scale=1.0 / D, bias=epsc[:, 0:1])
                P.v("reciprocal", out=rs[i][:], in_=rs[i][:])
                for kc in range(8):
                    P.v("scalar_tensor_tensor", out=xb[i][:, kc, :], in0=xb[i][:, kc, :],
                        scalar=Atab[:, l, sub, kc, j:j + 1], in1=rs[i][:], op0=ALU.mult, op1=ALU.mult, rt=kc, wt=kc)
                for kc in range(8):
                    P.add("act", "activation", dict(out=hn[:, kc, tsl], in_=xb[i][:, kc, :], func=AF.Identity,
                                                    bias=modt[:, l, 24 * sub + kc, j:j + 1]),
                          r=[(xb[i], kc), modt], w=[(hn, t)])
                st_["done"] += 1
        return hn, ensure

    def load_hn():
        hn = P.sb("hn", [128, 8, T], BF16)
        for tb in range(NTB):
            tsl = slice(tb * 512, (tb + 1) * 512)
            P.dma(hn[:, :, tsl], HN[:, :, tsl].rearrange("c p t -> p c t"), q="pool", wt=tb)
        return hn

    def phase_proj_ab():
        w_plan([(w_in_ab, 8, c_, n_) for (c_, n_) in [(0, 512), (512, 512), (1024, 512), (1536, 512), (2048, 16),
                                                      (2064, 512), (2576, 512), (3088, 512)]])
        hn, ens = norm_hn(0, 0, xT)
        raw = [P.sb("raw%d" % i, [128, T], F32) for i in range(2)]
        acc = [P.sb("acc%d" % i, [128, T], F32) for i in range(2)]
        obf = [P.sb("obf%d" % i, [128, T], BF16) for i in range(2)]
        tq = P.sb("tq", [128, 8, 3], F32)
        th = P.sb("th", [128, 12, 3], F32)
        bg = P.sb("bg", [128, 16], F32)
        gt = P.sb("gt", [128, NTC, 16], F32)
        gtmp = P.sb("gtmp", [128, NTC, 8], F32)
        vb = [P.sb("vb%d" % i, [128, 512], BF16) for i in range(2)]
        sob = [P.sb("sob%d" % i, [128, 512], BF16) for i in range(2)]
        pp = [P.ps("pp%d" % i) for i in range(4)]
        P.dma(tq[:], wcqk[:, :, :])
        P.dma(th[:], wchy[:, :, :])
        P.dma(bg[:], b_gates[0:1, :].broadcast_to([128, 16]))

        def epi_qk(n, tb, ps):
            i = n % 2
            P.act(raw[i][:, tb * 512:(tb + 1) * 512], ps[:, :], AF.Identity)
            if tb == NTB - 1:
                def tail():
                    conv3(acc[i], raw[i], tq[:, n, :])
                    P.act(acc[i][:], acc[i][:], AF.Silu)
                    P.v("tensor_scalar", out=obf[i][:], in0=acc[i][:], scalar1=(1.0 if n < 4 else 128.0 ** -0.5),
                        scalar2=None, op0=ALU.mult)
                    P.dma(QK[n], obf[i][:], q="pool")
                return tail
        linear_fm(w_in_ab, 8, 0, 1024, hn, pp, epi_qk, ensure=ens)

        def epi_v(tc, ps):
            i = tc % 2
            P.act(vb[i][:], ps[:, :], AF.Identity)
            P.dma(V[tc, :, 0:512], vb[i][:], q="pool")
        linear_tm(w_in_ab, 8, 1024, 512, hn, pp, epi_v, range(NTC))

        def epi_o(n, tb, ps):
            i = (n * NTB + tb) % 2
            P.act(sob[i][:], ps[:, :], AF.Sigmoid)
            P.dma(SO[n, :, tb * 512:(tb + 1) * 512], sob[i][:], q="pool")
        linear_fm(w_in_ab, 8, 1536, 512, hn, pp, epi_o)

        def epi_g(tc, ps):
            P.v("tensor_tensor", out=gt[:, tc, :], in0=ps[:, 0:16], in1=bg[:], op=ALU.add)
        linear_tm(w_in_ab, 8, 2048, 16, hn, pp, epi_g, range(NTC))
        for d in range(2):
            fs = gt[:, :, 4 + 8 * d:8 + 8 * d]
            ts = gtmp[:, :, 4 * d:4 * d + 4]
            P.act(ts, fs, AF.Exp, scale=-1.0)
            P.act(ts, ts, AF.Ln, bias=ones_f[:, 0:1])
            P.v("tensor_scalar", out=fs, in0=ts, scalar1=-1.0, scalar2=None, op0=ALU.mult)
        P.dma(G[:, :, :], gt[:], q="pool")

        def epi_hy(n, tb, ps):
            i = n % 2
            P.act(raw[i][:, tb * 512:(tb + 1) * 512], ps[:, :], AF.Identity)
            if tb == NTB - 1:
                def tail():
                    conv3(acc[i], raw[i], th[:, n, :])
                    P.dma(HY[n], acc[i][:], q="pool")
                return tail
        linear_fm(w_in_ab, 8, 2064, 1536, hn, pp, epi_hy)
        w_end()

    def phase_mlstm():
        gt = P.sb("gt", [128, NTC, 16], F32)
        P.dma(gt[:], G[:, :, :])
        stn = P.sb("stn", [128, 8], F32)
        stm = P.sb("stm", [128, 8], F32)
        P.dma(stn[:], st_n[:, :])
        P.dma(stm[:], st_m[:, :])
        P.act(stm[:], stm[:], AF.Exp)
        Cst = P.sb("Cst", [128, 4, 128], F32)
        Cbf = P.sb("Cbf", [128, 4, 128], BF16)
        nst = P.sb("nst", [128, 4], F32)
        nrep = P.sb("nrep", [128, 4, 128], BF16)
        mrun = P.sb("mrun", [128, 4], F32)
        ktm_all = P.sb("ktmall", [128, NTC, 512], BF16)
        H4 = range(4)
        NB = 2

        def mk(name, shape, dt):
            return [[P.sb("%s%d_%d" % (name, b, h), shape, dt) for h in H4] for b in range(NB)]
        qTt = [P.sb("qTt%d" % i, [128, 4, 128], BF16) for i in range(3)]
        kTt = [P.sb("kTt%d" % i, [128, 4, 128], BF16) for i in range(3)]
        vch = [P.sb("vch%d" % i, [128, 512], BF16) for i in range(3)]
        acol = [P.sb("acol%d" % i, [128, 4], F32) for i in range(NB)]
        hfo = [P.sb("hfo%d" % i, [128, 4, 128], F32) for i in range(NB)]
        lfrep, brow, arg, eb = mk("lfrep", [128, 128], F32), mk("brow", [128, 128], F32), mk("arg", [128, 128], F32), mk("eb", [128, 128], F32)
        PT, qt, kw = mk("PT", [128, 128], BF16), mk("qt", [128, 128], BF16), mk("kw", [128, 128], BF16)
        wcol, tmx = mk("wcol", [128, 2], F32), mk("tmx", [128, 1], F32)
        irep, te = mk("irep", [128, 128], F32), mk("te", [128, 128], F32)
        dd = [P.sb("dd%d" % h, [128, 128], F32) for h in H4]
        c0t = [P.sb("c0t%d" % i, [128, 128], F32) for i in H4]
        cout = P.sb("cout", [128, 4, 130], F32)
        pA = [P.ps("pmA%d" % i) for i in H4]
        pND = [P.ps("pmN%d" % i) for i in range(2)]
        pCU = [P.ps("pmCU%d" % i) for i in range(NB)]

        for tc in range(NTC):
            i = tc % 3
            P.dma(kTt[i][:], QK[4:8, :, tc * 128:(tc + 1) * 128].rearrange("h p t -> p h t"))
            for h in H4:
                P.tr(pND[tc % 2][:, :].bitcast(BF16)[:, h * 128:(h + 1) * 128], kTt[i][:, h, :], ident_bf[:])
            P.act(ktm_all[:, tc, :], pND[tc % 2][:, :].bitcast(BF16)[:, 0:512], AF.Identity)

        steps = []
        for si, (s0, s1) in enumerate(SEQS):
            nch = (s1 - s0) // 128
            for d in range(2):
                order = list(range(nch)) if d == 0 else list(range(nch - 1, -1, -1))
                for j, c in enumerate(order):
                    steps.append(dict(si=si, d=d, c=c, first=(j == 0), last=(j == nch - 1), t0=s0 + c * 128))
        ld = [0]

        def front(k):
            st = steps[k]
            b = k % NB
            d, t0 = st["d"], st["t0"]
            prompt = st["si"] > 0
            tc = t0 // 128
            li = ld[0] % 3
            ld[0] += 1
            st["li"] = li
            bend_c = 127 if d == 0 else 0
            g = gt[:, tc, :]

            def f0():
                P.dma(qTt[li][:], QK[0:4, :, t0:t0 + 128].rearrange("h p t -> p h t"))
                P.dma(kTt[li][:], QK[4:8, :, t0:t0 + 128].rearrange("h p t -> p h t"))
                P.dma(vch[li][:], V[tc, :, 0:512])
                P.mm(pA[0][:, 392 + 4 * b:396 + 4 * b], tri[d], g[:, 4 + 8 * d:8 + 8 * d])
                P.v("tensor_tensor", out=acol[b][:], in0=g[:, 8 * d:8 * d + 4], in1=pA[0][:, 392 + 4 * b:396 + 4 * b],
                    op=ALU.subtract)

            def f1():
                for h in H4:
                    P.act(lfrep[b][h][:], ones_f[:], AF.Identity, scale=g[:, 4 + 8 * d + h:5 + 8 * d + h])
                    if prompt:
                        P.act(irep[b][h][:], ones_f[:], AF.Identity, scale=g[:, 8 * d + h:8 * d + h + 1])

            def f2():
                for h in H4:
                    P.mm(pA[h][:, 0:128], lfrep[b][h][:], tri[d])
                    P.mm(pA[h][:, 256:384], kTt[li][:, h, :], qTt[li][:, h, :])
                    if prompt:
                        P.mm(pA[h][:, 128:256], irep[b][h][:], ident)

            def f3():
                for h in H4:
                    P.act(brow[b][h][:], pA[h][:, 0:128], AF.Identity)

            def f4():
                for h in H4:
                    P.v("scalar_tensor_tensor", out=arg[b][h][:], in0=brow[b][h][:], scalar=acol[b][:, h:h + 1],
                        in1=maskb[d], op0=ALU.add, op1=ALU.add)
                    if prompt:
                        bend = brow[b][h][:, bend_c:bend_c + 1]
                        P.v("scalar_tensor_tensor", out=te[b][h][:], in0=pA[h][:, 128:256], scalar=bend,
                            in1=brow[b][h][:], op0=ALU.add, op1=ALU.subtract)
                        P.v("tensor_reduce", out=tmx[b][h][:], in_=te[b][h][:], axis=AX.X, op=ALU.max)

            def f5():
                for h in H4:
                    bend = brow[b][h][:, bend_c:bend_c + 1]
                    P.act(arg[b][h][:], arg[b][h][:], AF.Exp)
                    P.act(eb[b][h][:], brow[b][h][:], AF.Exp)
                    P.act(wcol[b][h][:, 0:1], acol[b][:, h:h + 1], AF.Exp, bias=bend)
                    P.act(wcol[b][h][:, 1:2], bend, AF.Exp)

            def f6():
                for h in H4:
                    P.v("tensor_tensor", out=PT[b][h][:], in0=pA[h][:, 256:384], in1=arg[b][h][:], op=ALU.mult)
                    P.v("tensor_tensor", out=qt[b][h][:], in0=qTt[li][:, h, :], in1=eb[b][h][:], op=ALU.mult)
                    P.v("tensor_scalar", out=kw[b][h][:], in0=ktm_all[:, tc, h * 128:(h + 1) * 128],
                        scalar1=wcol[b][h][:, 0:1], scalar2=None, op0=ALU.mult)

            def f7():
                for h in H4:
                    P.mm(pCU[b][:, h * 128:(h + 1) * 128], kw[b][h][:], vch[li][:, h * 128:(h + 1) * 128])
                    P.mm(pA[h][:, 384 + b:385 + b], kw[b][h][:], ones_bf[:, 0:1])
            return [f0, f1, f2, f3, f4, f5, f6, f7]

        def back(k):
            st = steps[k]
            b = k % NB
            d, t0, li = st["d"], st["t0"], st["li"]
            prompt = st["si"] > 0
            bend_c = 127 if d == 0 else 0
            HOUT = HF if d == 0 else HB

            def b0():
                if st["first"]:
                    for h in H4:
                        if prompt:
                            P.v("memset", ap=Cst[:, h, :], constant=0.0)
                        else:
                            P.dma(c0t[h][:], st_C[d * 4 + h], q="pool")
                            P.v("tensor_scalar", out=Cst[:, h, :], in0=c0t[h][:], scalar1=stm[:, d * 4 + h:d * 4 + h + 1],
                                scalar2=None, op0=ALU.mult)
                    if prompt:
                        P.v("memset", ap=nst[:], constant=0.0)
                        P.v("memset", ap=mrun[:], constant=0.0)
                    else:
                        P.v("tensor_tensor", out=nst[:], in0=stn[:, d * 4:d * 4 + 4], in1=stm[:, d * 4:d * 4 + 4],
                            op=ALU.mult)
                    for h in H4:
                        P.act(Cbf[:, h, :], Cst[:, h, :], AF.Identity, wt=h)
                        P.act(nrep[:, h, :], ones_f[:], AF.Identity, scale=nst[:, h:h + 1], wt=h)
                for h in H4:
                    nd = pND[h // 2]
                    o0 = (h % 2) * 256
                    P.mm(nd[:, o0:o0 + 128], vch[li][:, h * 128:(h + 1) * 128], PT[b][h][:], start=True, stop=False)
                    P.mm(nd[:, o0:o0 + 128], Cbf[:, h, :], qt[b][h][:], start=False, stop=True, lt=h)
                    P.mm(nd[:, o0 + 128:o0 + 256], ones_bf[:], PT[b][h][:], start=True, stop=False)
                    P.mm(nd[:, o0 + 128:o0 + 256], nrep[:, h, :], qt[b][h][:], start=False, stop=True, lt=h)

            def b1():
                for h in H4:
                    nd = pND[h // 2]
                    o0 = (h % 2) * 256
                    P.act(dd[h][:], nd[:, o0 + 128:o0 + 256], AF.Abs)

            def b2():
                for h in H4:
                    nd = pND[h // 2]
                    o0 = (h % 2) * 256
                    bend = brow[b][h][:, bend_c:bend_c + 1]
                    P.v("tensor_scalar", out=dd[h][:], in0=dd[h][:], scalar1=1.0, scalar2=None, op0=ALU.max)
                    P.v("reciprocal", out=dd[h][:], in_=dd[h][:])
                    P.v("tensor_tensor", out=hfo[b][:, h, :], in0=nd[:, o0:o0 + 128], in1=dd[h][:], op=ALU.mult)
                    if prompt:
                        P.v("scalar_tensor_tensor", out=mrun[:, h:h + 1], in0=mrun[:, h:h + 1], scalar=bend,
                            in1=tmx[b][h][:], op0=ALU.add, op1=ALU.max)
                    P.v("scalar_tensor_tensor", out=Cst[:, h, :], in0=Cst[:, h, :], scalar=wcol[b][h][:, 1:2],
                        in1=pCU[b][:, h * 128:(h + 1) * 128], op0=ALU.mult, op1=ALU.add)
                    P.v("scalar_tensor_tensor", out=nst[:, h:h + 1], in0=nst[:, h:h + 1], scalar=wcol[b][h][:, 1:2],
                        in1=pA[h][:, 384 + b:385 + b], op0=ALU.mult, op1=ALU.add)

            def b3():
                for h in H4:
                    P.act(Cbf[:, h, :], Cst[:, h, :], AF.Identity, wt=h)
                    P.act(nrep[:, h, :], ones_f[:], AF.Identity, scale=nst[:, h:h + 1], wt=h)
                P.dma(HOUT[:, :, t0:t0 + 128].rearrange("h p t -> p h t"), hfo[b][:], q="pool")
                if st["last"] and prompt:
                    pi = st["si"] - 1
                    P.act(cout[:, :, 129:130], mrun[:].rearrange("p (h o) -> p h o", o=1), AF.Exp, scale=-1.0)
                    for h in H4:
                        P.v("tensor_scalar", out=cout[:, h, 0:128], in0=Cst[:, h, :], scalar1=cout[:, h, 129:130],
                            scalar2=None, op0=ALU.mult)
                        P.v("tensor_scalar", out=cout[:, h, 128:129], in0=nst[:, h:h + 1], scalar1=cout[:, h, 129:130],
                            scalar2=None, op0=ALU.mult)
                    P.dma(o_C[pi, d].rearrange("h p e -> p h e"), cout[:, :, 0:128], q="pool")
                    P.dma(o_n[pi, d].rearrange("h (p o) -> p h o", o=1), cout[:, :, 128:129], q="pool", slow=True)
                    P.dma(o_m[pi, d:d + 1, :], mrun[0:1, :], q="pool")
            return [b0, b1, b2, b3]

        for f in front(0):
            f()
        for k in range(len(steps)):
            B = back(k)
            if k + 1 < len(steps):
                F = front(k + 1)
                for fn in (F[0], F[1], B[0], F[2], B[1], F[3], B[2], F[4], B[3], F[5], F[6], F[7]):
                    fn()
            else:
                for fn in B:
                    fn()

    def phase_mlstm_fin():
        gml = P.sb("gml", [128, 4], F32)
        P.dma(gml[:], g_ml[:, :])
        hf = [P.sb("fhf%d" % i, [128, T], F32) for i in range(2)]
        hb_ = [P.sb("fhb%d" % i, [128, T], F32) for i in range(2)]
        so = [P.sb("fso%d" % i, [128, T], BF16) for i in range(2)]
        sq = [P.sb("fsq%d" % i, [128, T], BF16) for i in range(2)]
        rs = [P.sb("frs%d" % i, [128, T], F32) for i in range(2)]
        ym = [P.sb("fym%d" % i, [128, T], BF16) for i in range(2)]
        pn = [P.ps("fpn%d" % i) for i in range(5)]
        for h in range(4):
            i = h % 2
            P.dma(hf[i][:], HF[h])
            P.dma(hb_[i][:], HB[h])
            P.dma(so[i][:], SO[h])
            P.v("tensor_tensor", out=hf[i][:], in0=hf[i][:], in1=hb_[i][:], op=ALU.add)
            P.act(sq[i][:], hf[i][:], AF.Square)
            for tb in range(NTB):
                P.mm(pn[tb][:, :], ones_bf[:], sq[i][:, tb * 512:(tb + 1) * 512])
                P.act(rs[i][:, tb * 512:(tb + 1) * 512], pn[tb][:, :], AF.Ln, scale=1.0 / 128, bias=epsc[:, 0:1])
                P.act(rs[i][:, tb * 512:(tb + 1) * 512], rs[i][:, tb * 512:(tb + 1) * 512], AF.Exp, scale=-0.5)
            P.v("scalar_tensor_tensor", out=hf[i][:], in0=hf[i][:], scalar=gml[:, h:h + 1], in1=rs[i][:], op0=ALU.mult,
                op1=ALU.mult)
            P.v("tensor_tensor", out=ym[i][:], in0=hf[i][:], in1=so[i][:], op=ALU.mult)
            P.dma(Y[h], ym[i][:], q="pool")

    def phase_filters():
        w1 = P.sb("w1", [33, 64], F32)
        w2 = P.sb("w2", [64, 64], F32)
        w3 = P.sb("w3", [64, 1024], F32)
        fv = P.sb("fv", [64, 4], F32)
        dl = P.sb("dl", [128, 512], F32)
        P.dma(w1[:], w_f1[:, :], q="pool")
        P.dma(w2[:], w_f2[:, :], q="pool")
        P.dma(w3[:], w_f3[:, :], q="pool")
        P.dma(fv[:, 0:3], fvec[:, :], q="pool")
        P.dma(dl[:], deltas[0:1, 0:512].broadcast_to([128, 512]), q="pool")
        pg = P.ps("pg")
        pf = [P.ps("pff%d" % i) for i in range(2)]
        z = P.sb("z", [33, LS], F32)
        h1 = P.sb("h1", [64, LS], F32)
        h2 = P.sb("h2", [64, LS], F32)
        a1 = P.sb("a1", [64, 512], F32)
        a2 = P.sb("a2", [64, 512], F32)
        dec = P.sb("dec", [128, 512], F32)
        tng = P.sb("tng", [128, 16], F32)
        fo = [P.sb("fo%d" % i, [128, 1024], BF16) for i in range(2)]
        frb2 = P.sb("frb2", [64, 1], F32)
        P.v("tensor_scalar", out=fv[:, 3:4], in0=fv[:, 0:1], scalar1=fv[:, 2:3], scalar2=None, op0=ALU.mult)
        P.v("tensor_scalar", out=frb2[:], in0=fv[:, 1:2], scalar1=fv[:, 2:3], scalar2=None, op0=ALU.mult)

        def sin_layer(dst, n, bias_ap):
            P.act(a1[:, 0:n], pg[0:64, 0:n], AF.Identity, scale=fv[:, 2:3], bias=bias_ap)
            P.v("tensor_scalar", out=a2[:, 0:n], in0=a1[:, 0:n], scalar1=1.0 / TWO_PI, scalar2=MAGIC, op0=ALU.mult,
                op1=ALU.add)
            P.v("tensor_scalar", out=a2[:, 0:n], in0=a2[:, 0:n], scalar1=MAGIC, scalar2=None, op0=ALU.subtract)
            P.v("scalar_tensor_tensor", out=a1[:, 0:n], in0=a2[:, 0:n], scalar=-TWO_PI, in1=a1[:, 0:n], op0=ALU.mult,
                op1=ALU.add)
            P.v("tensor_scalar", out=a1[:, 0:n], in0=a1[:, 0:n], scalar1=-3.141592, scalar2=3.141592, op0=ALU.max,
                op1=ALU.min)
            P.act(dst, a1[:, 0:n], AF.Sin)

        for L in (LS, LP):
            c = hc[L]
            na = c["na"]
            P.dma(z[:, 0:L], c["zT"][:, :], q="pool")
            P.dma(tng[:, 0:na], c["tneg"][:, :], q="pool")
            nblk = max(1, L // 512)
            n = min(L, 512)
            for b in range(nblk):
                sl = slice(b * n, (b + 1) * n)
                P.mm(pg[0:64, 0:n], w1[:, :], z[:, sl])
                sin_layer(h1[:, sl], n, fv[:, 3:4])
            for b in range(nblk):
                sl = slice(b * n, (b + 1) * n)
                P.mm(pg[0:64, 0:n], w2[:, :], h1[:, sl])
                sin_layer(h2[:, sl], n, frb2[:, 0:1])
            for a in range(na):
                i = a % 2
                P.act(dec[:], dl[:], AF.Exp, scale=tng[:, a:a + 1])
                for half in range(2):
                    ps = pf[half]
                    P.mm(ps[:, :], h2[:, a * 128:(a + 1) * 128], w3[:, half * 512:(half + 1) * 512])
                    dst = fo[i][:, half * 512:(half + 1) * 512]
                    P.v("tensor_tensor", out=dst, in0=ps[:, :], in1=dec[:], op=ALU.mult)
                    if half == 1 and a == 0:
                        P.v("tensor_scalar", out=dst, in0=dst, scalar1=cst_t[:, 5, 64:65], scalar2=None, op0=ALU.mult)
                P.dma(c["FL"][a], fo[i][:], q="pool")

    def phase_filters_adaln1():
        extra = [P.sb("adx%d" % i, [128, 4096], F32) for i in range(6)]
        pa = [P.ps("pa%d" % i) for i in range(2)]
        adaln_layer(1, wst + extra, pa, pre_hook=phase_filters)

    def phase_hyena():
        hb_t = P.sb("hybt", [128, 4], F32)
        P.dma(hb_t[:], hyb[:, :])
        pf = [P.ps("pf%d" % i) for i in range(6)]
        p_tr = P.ps("ptrh", [128, 1024], BF16)
        rhs_all = P.sb("rhsall", [128, 16, 1536], BF16)
        GH = P.sb("GH", [128, 17, 2, 512], BF16)
        tis = [P.sb("tis%d" % i, [128, 2, 512], F32) for i in range(3)]
        tib = [P.sb("tib%d" % i, [128, 2, 512], BF16) for i in range(3)]
        wtt = P.sb("wtt", [128, 17, 2], F32)
        ua = [P.sb("ua%d" % i, [128, 512], F32) for i in range(4)]
        ub = [P.sb("ub%d" % i, [128, 512], BF16) for i in range(2)]
        e1 = [P.sb("e1_%d" % i, [128, 512], F32) for i in range(12)]
        m1 = [P.sb("m1_%d" % i, [128, 512], F32) for i in range(2)]
        uc = [P.sb("uc%d" % i, [128, 512], F32) for i in range(2)]
        x2c = [P.sb("x2c%d" % i, [128, 512], F32) for i in range(2)]
        yh = [P.sb("yh%d" % i, [128, 512], BF16) for i in range(2)]
        tctr = [0]

        def hyena_seq(c, s0, load_filt):
            L, na, nb = c["L"], c["na"], c["nb"]
            n = min(L, 512)
            nblk = max(1, L // 512)
            if load_filt:
                P.dma(wtt[:, 0:nb, :], c["wt"][:, :, :])
                P.dma(rhs_all[:, 0:na, 512:1536], c["FL"][:, :, :].rearrange("a p n -> p a n"))
            for cc in range(4):
                for tb in range(nblk):
                    i = (cc * nblk + tb) % 2
                    tsl = slice(s0 + tb * n, s0 + (tb + 1) * n)
                    P.dma(ua[i][:, 0:n], HY[cc, :, tsl])
                    P.dma(ua[2 + i][:, 0:n], HY[4 + cc, :, tsl])
                    P.v("tensor_tensor", out=ua[i][:, 0:n], in0=ua[i][:, 0:n], in1=ua[2 + i][:, 0:n], op=ALU.mult)
                    P.dma(U[cc, :, tsl], ua[i][:, 0:n], q="pool")
                    P.v("tensor_copy", out=ub[i][:, 0:n], in_=ua[i][:, 0:n])
                    nn = n // 128
                    for a in range(nn):
                        P.tr(p_tr[:, a * 128:(a + 1) * 128], ub[i][:, a * 128:(a + 1) * 128], ident_bf[:])
                    a0 = tb * 4
                    P.act(rhs_all[:, a0:a0 + nn, cc * 128:(cc + 1) * 128],
                          p_tr[:, 0:nn * 128].rearrange("p (a t) -> p a t", a=nn), AF.Identity)
            def load_tf(b):
                i = wctr[0] % 2
                wctr[0] += 1
                sv = wst[i][:, 0:na * 256].rearrange("p (a s f) -> p a s f", a=na, s=2)
                bv = wbf[i][:, 0:na * 256].rearrange("p (a s f) -> p a s f", a=na, s=2)
                P.dma(sv, c["TF"][b], q="pool")
                P.act(bv, sv, AF.Identity)
                return bv
            nxt = load_tf(0)
            for b in range(nb):
                bv = nxt
                if b + 1 < nb:
                    nxt = load_tf(b + 1)
                for a in range(na):
                    for j in range(3):
                        P.mm(pf[j][:, :], bv[:, a, 0, :], rhs_all[:, a, j * 512:(j + 1) * 512], start=(a == 0),
                             stop=(a == na - 1))
                        P.mm(pf[3 + j][:, :], bv[:, a, 1, :], rhs_all[:, a, j * 512:(j + 1) * 512], start=(a == 0),
                             stop=(a == na - 1))
                wtc = wtt[:, b, 0:1]
                nwtc = wtt[:, b, 1:2]
                ee = e1[(b % 2) * 6:(b % 2) * 6 + 6]
                au, ap_, aq, bu, bp, bq = ee
                P.act(au[:], pf[0][:, :], AF.Identity)
                P.v("tensor_scalar", out=ap_[:], in0=pf[1][:, :], scalar1=wtc, scalar2=None, op0=ALU.mult)
                P.act(aq[:], pf[2][:, :], AF.Identity, scale=wtc)
                P.v("tensor_copy", out=bu[:], in_=pf[3][:, :])
                P.act(bp[:], pf[4][:, :], AF.Identity, scale=wtc)
                P.v("tensor_scalar", out=bq[:], in0=pf[5][:, :], scalar1=nwtc, scalar2=None, op0=ALU.mult)
                Kr, Ki = ap_, bp
                P.v("tensor_tensor", out=Kr[:], in0=ap_[:], in1=aq[:], op=ALU.add)
                P.v("tensor_tensor", out=Ki[:], in0=bp[:], in1=bq[:], op=ALU.add)
                P.v("tensor_tensor", out=m1[0][:], in0=au[:], in1=Kr[:], op=ALU.mult)
                P.v("tensor_tensor", out=m1[1][:], in0=bu[:], in1=Ki[:], op=ALU.mult)
                P.v("tensor_tensor", out=GH[:, b, 0, :], in0=m1[0][:], in1=m1[1][:], op=ALU.subtract)
                P.v("tensor_tensor", out=m1[0][:], in0=au[:], in1=Ki[:], op=ALU.mult)
                P.v("tensor_tensor", out=m1[1][:], in0=bu[:], in1=Kr[:], op=ALU.mult)
                P.v("tensor_tensor", out=GH[:, b, 1, :], in0=m1[0][:], in1=m1[1][:], op=ALU.add)
            seq = [(tb, b) for tb in range(nblk) for b in range(nb)]
            st_ = {"dma": 0, "cast": 0}

            def ti_dma(k):
                tb_k, b_k = seq[k]
                i = (tctr[0] + k) % 3
                P.dma(tis[i][:, :, 0:n], c["TI"][:, b_k, :, tb_k * n:(tb_k + 1) * n], q=("pool" if k % 2 == 0 else "sp"))

            def ti_cast(k):
                i = (tctr[0] + k) % 3
                P.act(tib[i][:, :, 0:n], tis[i][:, :, 0:n], AF.Identity)

            def ti_get(k):
                while st_["dma"] < min(len(seq), k + 3):
                    ti_dma(st_["dma"])
                    st_["dma"] += 1
                while st_["cast"] < min(len(seq), k + 2):
                    ti_cast(st_["cast"])
                    st_["cast"] += 1
                return tib[(tctr[0] + k) % 3]
            kk = 0
            for tb in range(nblk):
                for b in range(nb):
                    tb_ = ti_get(kk)
                    kk += 1
                    for cc in range(4):
                        P.mm(pf[cc][:, 0:n], GH[:, b, 0, cc * 128:(cc + 1) * 128], tb_[:, 0, 0:n], start=(b == 0),
                             stop=False)
                        P.mm(pf[cc][:, 0:n], GH[:, b, 1, cc * 128:(cc + 1) * 128], tb_[:, 1, 0:n], start=False,
                             stop=(b == nb - 1))
                for cc in range(4):
                    ps = pf[cc]
                    i = cc % 2
                    tsl = slice(s0 + tb * n, s0 + (tb + 1) * n)
                    P.dma(uc[i][:, 0:n], U[cc, :, tsl])
                    P.dma(x2c[i][:, 0:n], HY[8 + cc, :, tsl])
                    P.v("scalar_tensor_tensor", out=uc[i][:, 0:n], in0=uc[i][:, 0:n], scalar=hb_t[:, cc:cc + 1],
                        in1=ps[:, 0:n], op0=ALU.mult, op1=ALU.add)
                    P.v("tensor_tensor", out=yh[i][:, 0:n], in0=uc[i][:, 0:n], in1=x2c[i][:, 0:n], op=ALU.mult)
                    P.dma(Y[4 + cc, :, tsl], yh[i][:, 0:n])
            tctr[0] += len(seq)

        hyena_seq(hc[LS], 0, True)
        hyena_seq(hc[LP], 2048, True)
        hyena_seq(hc[LP], 2304, False)

    def phase_resproj(Wap, nk, srcD, l, sub, xsrc):
        src = P.sb("rsrc", [128, nk, T], BF16)
        for tb in range(NTB):
            tsl_ = slice(tb * 512, (tb + 1) * 512)
            P.dma(src[:, :, tsl_], srcD[:, :, tsl_].rearrange("c p t -> p c t"), q=("pool" if tb % 2 == 0 else "sp"), wt=tb)
        pp = [P.ps("pr%d" % i) for i in range(4)]
        xr = [P.sb("xr%d" % i, [128, T], F32) for i in range(2)]

        def epi(n, tb, ps):
            i = n % 2
            j = 0 if tb < 4 else 1
            tsl = slice(tb * 512, (tb + 1) * 512)
            if tb == 0:
                P.dma(xr[i][:], xsrc[n], q="pool")
            P.v("scalar_tensor_tensor", out=xr[i][:, tsl], in0=ps[:, :], scalar=modt[:, l, 16 + 24 * sub + n, j:j + 1],
                in1=xr[i][:, tsl], op0=ALU.mult, op1=ALU.add)
            if tb == NTB - 1:
                P.dma(X[n], xr[i][:], q="pool")
        ncols = max(128, (4096 // nk) // 128 * 128)
        ncols = min(ncols, 512)
        w_plan([(Wap, nk, c_, ncols) for c_ in range(0, D, ncols)])
        done = 0
        pi = [0]
        while done < D:
            nc_ = min(ncols, D - done)
            bv, wtag = load_w(Wap, 0, nk, done, nc_)
            nm = nc_ // 128
            for m in range(nm):
                for tb in range(NTB):
                    ps = pp[pi[0] % 4]
                    pi[0] += 1
                    for kc in range(nk):
                        P.mm(ps[:, :], bv[:, kc, m * 128:(m + 1) * 128], src[:, kc, tb * 512:(tb + 1) * 512],
                             start=(kc == 0), stop=(kc == nk - 1), lt=wtag, rtg=tb)
                    epi(done // 128 + m, tb, ps)
                    if m == (nm - 1) // 2 and tb == 2:
                        w_tick()
            done += nc_
        w_end()

    def phase_ffn_up(l):
        w_plan([(w_up[l], 8, br * FF + n * 128, 128) for n in range(22) for br in range(2)])
        hn, ens = norm_hn(l, 1, X)
        raw = [P.sb("raw%d" % i, [128, T], F32) for i in range(2)]
        acc = [P.sb("acc%d" % i, [128, T], F32) for i in range(2)]
        gbuf = [P.sb("gbuf%d" % i, [128, T], F32) for i in range(2)]
        obf = [P.sb("obf%d" % i, [128, T], BF16) for i in range(2)]
        tw = P.sb("tw", [128, 22, 3], F32)
        P.dma(tw[:], wcffn[:, l, :, :])
        pp = [P.ps("pu%d" % i) for i in range(4)]
        W = w_up[l]
        pend = [None]
        for n in range(22):
            i = n % 2
            for br in range(2):
                bv, wtag = load_w(W, 0, 8, br * FF + n * 128, 128)
                for tb in range(NTB):
                    ens(tb)
                    ps = pp[(br * NTB + tb) % 4]
                    for kc in range(8):
                        P.mm(ps[:, :], bv[:, kc, :], hn[:, kc, tb * 512:(tb + 1) * 512], start=(kc == 0), stop=(kc == 7),
                             lt=wtag, rtg=tb)
                    dstt = raw[i] if br == 0 else gbuf[i]
                    P.act(dstt[:, tb * 512:(tb + 1) * 512], ps[:, :], AF.Identity)
                    if tb == 2:
                        w_tick()
            def tail(i=i, n=n):
                conv3(acc[i], raw[i], tw[:, n, :])
                P.act(acc[i][:], acc[i][:], AF.Gelu_apprx_tanh)
                P.v("tensor_tensor", out=obf[i][:], in0=acc[i][:], in1=gbuf[i][:], op=ALU.mult)
                P.dma(ACTS[n], obf[i][:], q="pool")
            if pend[0] is not None:
                pend[0]()
            pend[0] = tail
        pend[0]()
        w_end()

    def phase_proj_c():
        rq = [(w_in_c, 8, c_, 512) for c_ in (0, 512, 1024, 1536)]
        for half_ in range(2):
            rq += [(w_in_c, 8, 2048 + half_ * 512, 512), (w_in_c, 8, 1024 + half_ * 512, 512)]
        w_plan(rq)
        hn, ens = norm_hn(1, 0, X)
        pp = [P.ps("pc%d" % i) for i in range(4)]
        qb = [P.sb("qb%d" % i, [128, 512], BF16) for i in range(3)]
        vb = [P.sb("vb%d" % i, [128, 512], BF16) for i in range(3)]
        kf = [P.sb("kf%d" % i, [128, 512], F32) for i in range(3)]
        ctr = [0]

        def epi_qk(n, tb, ps):
            i = ctr[0] % 3
            ctr[0] += 1
            if n < 8:
                P.act(qb[i][:], ps[:, :], AF.Identity, scale=0.125)
            else:
                P.act(qb[i][:], ps[:, :], AF.Identity)
            P.dma(QK[n, :, tb * 512:(tb + 1) * 512], qb[i][:], q="pool")
        linear_fm(w_in_c, 8, 0, 2048, hn, pp, epi_qk, ensure=ens, fb_outer=True)
        for half in range(2):
            def epi_v(tc, ps, half=half):
                i = ctr[0] % 3
                ctr[0] += 1
                P.act(vb[i][:], ps[:, :], AF.Identity)
                P.dma(V[tc, :, half * 512:(half + 1) * 512], vb[i][:], q="pool")
                if tc >= 16:
                    pi_, t0 = (tc - 16) // 2, ((tc - 16) % 2) * 128
                    P.v("tensor_copy", out=kf[i][:], in_=ps[:, :])
                    P.dma(o_v[pi_, half * 8:(half + 1) * 8, t0:t0 + 128, :].rearrange("h t d -> t h d"),
                          kf[i][:].rearrange("p (h d) -> p h d", d=64), q="pool")
            linear_tm(w_in_c, 8, 2048 + half * 512, 512, hn, pp, epi_v, range(NTC))

            def epi_k(tc, ps, half=half):
                i = ctr[0] % 3
                ctr[0] += 1
                pi_, t0 = (tc - 16) // 2, ((tc - 16) % 2) * 128
                P.act(kf[i][:], ps[:, :], AF.Identity)
                P.dma(o_k[pi_, half * 8:(half + 1) * 8, t0:t0 + 128, :].rearrange("h t d -> t h d"),
                      kf[i][:].rearrange("p (h d) -> p h d", d=64), q="pool")
            linear_tm(w_in_c, 8, 1024 + half * 512, 512, hn, pp, epi_k, range(16, 20))
        w_end()

    def phase_attn():
        def mkset(i):
            d_ = {}
            d_["qbd"] = P.sb("aqbd%d" % i, [128, 2, T], BF16)
            d_["kT"] = P.sb("akT%d" % i, [128, T], BF16)
            d_["kcx"] = P.sb("akc%d" % i, [128, 512], F32)
            d_["kcb"] = P.sb("akcb%d" % i, [128, 512], BF16)
            d_["vcx"] = P.sb("avc%d" % i, [128, 4, 2, 64], F32)
            d_["vcb"] = P.sb("avcb%d" % i, [128, 4, 128], BF16)
            d_["Vs"] = P.sb("aVs%d" % i, [128, 16, 128], BF16)
            d_["Vs64"] = P.sb("aVs64%d" % i, [128, 15, 128], BF16)
            d_["Vp"] = P.sb("aVp%d" % i, [128, 4, 128], BF16)
            d_["Tmf"] = [P.sb("aTmf%d_%d" % (i, j), [128, 15, 64], F32) for j in range(2)]
            d_["Tmb2"] = P.sb("aTmb2%d" % i, [128, 15, 2, 64], BF16)
            d_["mx"] = P.sb("amx%d" % i, [128, 32], F32)
            d_["negC"] = P.sb("anegC%d" % i, [128, 1], F32)
            d_["ysb"] = P.sb("aysb%d" % i, [128, T], BF16)
            P.v("memset", ap=d_["qbd"][:], constant=0.0)
            return d_
        sets = [mkset(0), mkset(1)]
        sqq = P.sb("asqq", [128, 2, T], BF16)
        sqk = P.sb("asqk", [128, T + 512], BF16)
        selM = P.sb("aselM", [128, 2, 128], BF16)
        P.v("memset", ap=selM[:], constant=0.0)
        P.v("memset", ap=selM[0:64, 0, :], constant=1.0)
        P.v("memset", ap=selM[64:128, 1, :], constant=1.0)
        NW = 2
        PTt = [P.sb("aPT%d" % i, [128, 1024], BF16) for i in range(NW)]
        pts = [P.sb("apts%d" % i, [128, 256], F32) for i in range(NW)]
        rd = [P.sb("ard%d" % i, [128, 256], F32) for i in range(NW)]
        pstA = [P.ps("apsa%d" % i) for i in range(NW)]
        pstB = [P.ps("apsb%d" % i) for i in range(NW)]
        po = [P.ps("apo%d" % i) for i in range(NW)]
        pn_ = P.ps("apn")

        def setup_dma(n):
            S_ = sets[n % 2]
            q_ = "pool"
            P.dma(S_["qbd"][0:64, 0, :], QK[n, 0:64, :], q=q_)
            P.dma(S_["qbd"][64:128, 1, :], QK[n, 64:128, :], q=q_)
            P.dma(S_["kT"][:], QK[8 + n], q=q_)
            P.dma(S_["kcx"][:], kcT[n], q=q_)
            for hh in range(2):
                P.dma(S_["vcx"][:, :, hh, :], vc[2 * n + hh].rearrange("(c p) d -> p c d", p=128), q=q_)
                P.dma(S_["Tmf"][hh][:], TmE[2 * n + hh], q=q_)
            P.dma(S_["Vs"][:], V[0:16, :, n * 128:(n + 1) * 128].rearrange("c p d -> p c d"), q=q_)
            P.dma(S_["Vs64"][0:64, :, :], V[0:15, 64:128, n * 128:(n + 1) * 128].rearrange("c p d -> p c d"), q=q_)
            P.dma(S_["Vs64"][64:128, :, :], V[1:16, 0:64, n * 128:(n + 1) * 128].rearrange("c p d -> p c d"), q=q_)
            P.dma(S_["Vp"][:], V[16:20, :, n * 128:(n + 1) * 128].rearrange("c p d -> p c d"), q=q_)

        def setup_compute(n):
            S_ = sets[n % 2]
            mx = S_["mx"]
            P.v("tensor_copy", eng="pool", out=S_["kcb"][:], in_=S_["kcx"][:])
            P.v("tensor_copy", eng="pool", out=S_["vcb"][:].rearrange("p c (h d) -> p c h d", h=2), in_=S_["vcx"][:])
            for hh in range(2):
                P.v("tensor_tensor", eng="pool", out=S_["Tmf"][hh][:], in0=S_["Tmf"][hh][:],
                    in1=colmask[:, None, :].broadcast_to([128, 15, 64]), op=ALU.add)
                P.v("tensor_copy", eng="pool", out=S_["Tmb2"][:, :, hh, :], in_=S_["Tmf"][hh][:])
            P.act(sqq[:], S_["qbd"][:], AF.Square)
            P.act(sqk[:, 0:T], S_["kT"][:], AF.Square)
            P.act(sqk[:, T:T + 512], S_["kcb"][:], AF.Square)
            for hh in range(2):
                for tb in range(5):
                    P.mm(pn_[:, 0:512], ones_bf[:], sqq[:, hh, tb * 512:(tb + 1) * 512])
                    P.v("tensor_reduce", out=mx[:, hh * 16 + tb:hh * 16 + tb + 1], in_=pn_[:, 0:512], axis=AX.X, op=ALU.max)
                for tb in range(6):
                    P.mm(pn_[:, 0:512], selM[:, hh, :], sqk[:, tb * 512:(tb + 1) * 512])
                    P.v("tensor_reduce", out=mx[:, hh * 16 + 5 + tb:hh * 16 + 6 + tb], in_=pn_[:, 0:512], axis=AX.X,
                        op=ALU.max)
                P.v("tensor_reduce", out=mx[:, hh * 16 + 12:hh * 16 + 13], in_=mx[:, hh * 16:hh * 16 + 5], axis=AX.X, op=ALU.max)
                P.v("tensor_reduce", out=mx[:, hh * 16 + 13:hh * 16 + 14], in_=mx[:, hh * 16 + 5:hh * 16 + 11], axis=AX.X,
                    op=ALU.max)
                P.v("tensor_tensor", out=mx[:, hh * 16 + 14:hh * 16 + 15], in0=mx[:, hh * 16 + 12:hh * 16 + 13],
                    in1=mx[:, hh * 16 + 13:hh * 16 + 14], op=ALU.mult)
            P.v("tensor_tensor", out=mx[:, 15:16], in0=mx[:, 14:15], in1=mx[:, 30:31], op=ALU.max)
            P.act(mx[:, 31:32], mx[:, 15:16], AF.Sqrt)
            P.v("tensor_scalar", out=S_["negC"][:], in0=mx[:, 31:32], scalar1=-1.0, scalar2=None, op0=ALU.mult)

        def stage_s(w, S_, qt0, nq, chunks, x0):
            n2 = 2 * nq
            qsl = S_["qbd"][:, :, qt0:qt0 + nq]
            for ch, (kap, vap) in enumerate(chunks):
                pt_, c_ = (pstA[w], ch) if ch * n2 < 512 else (pstB[w], ch - 512 // n2)
                dst = pt_[:, c_ * n2:(c_ + 1) * n2]
                local = (x0 is not None and ch < 4)
                P.mm(dst, kap, qsl, start=True, stop=(not local))
                if local:
                    P.mm(dst, ident_bf[:], S_["Tmb2"][:, x0 + 2 * ch, :, :], start=False, stop=True)
            P.act(PTt[w][:, 0:512], pstA[w][:, :], AF.Exp, bias=S_["negC"][:, 0:1])
            if len(chunks) * n2 > 512:
                P.act(PTt[w][:, 512:1024], pstB[w][:, :], AF.Exp, bias=S_["negC"][:, 0:1])

        def stage_o(w, S_, qt0, nq, chunks, x0):
            n2 = 2 * nq
            nch = len(chunks)
            for ch, (kap, vap) in enumerate(chunks):
                P.mm(po[w][:, 0:n2], vap, PTt[w][:, ch * n2:(ch + 1) * n2], start=(ch == 0), stop=(ch == nch - 1))
            for ch in range(nch):
                P.mm(po[w][:, 256:256 + n2], ones_bf[:], PTt[w][:, ch * n2:(ch + 1) * n2], start=(ch == 0),
                     stop=(ch == nch - 1))
            P.act(rd[w][:, 0:n2], po[w][:, 256:256 + n2], AF.Ln)
            P.act(rd[w][:, 0:n2], rd[w][:, 0:n2], AF.Exp, scale=-1.0)
            for hh in range(2):
                ps_ = slice(hh * 64, hh * 64 + 64)
                cs_ = slice(hh * nq, (hh + 1) * nq)
                P.v("tensor_tensor", out=S_["ysb"][ps_, qt0:qt0 + nq], in0=po[w][ps_, cs_], in1=rd[w][ps_, cs_], op=ALU.mult,
                    wt=(qt0 * 2 + hh))

        blk = [0]
        setup_dma(0)
        setup_compute(0)
        for n in range(8):
            S_ = sets[n % 2]
            if n + 1 < 8:
                setup_dma(n + 1)
            kT, kcb, vcb, Vs, Vs64, Vp = S_["kT"], S_["kcb"], S_["vcb"], S_["Vs"], S_["Vs64"], S_["Vp"]
            blocks = []
            for r in range(32):
                j0 = min(max(r - 4, 0), 24)
                k0 = j0 * 64
                chunks = []
                for i in range(4):
                    vap = Vs[:, j0 // 2 + i, :] if j0 % 2 == 0 else Vs64[:, j0 // 2 + i, :]
                    chunks.append((kT[:, k0 + i * 128:k0 + (i + 1) * 128], vap))
                for i in range(4):
                    chunks.append((kcb[:, i * 128:(i + 1) * 128], vcb[:, i, :]))
                blocks.append((S_, r * 64, 64, chunks, j0 - r + 7))
            for pi_ in range(2):
                s0 = 2048 + pi_ * 256
                chunks = [(kT[:, s0 + i * 128:s0 + (i + 1) * 128], Vp[:, pi_ * 2 + i, :]) for i in range(2)]
                for hf in range(2):
                    blocks.append((S_, s0 + hf * 128, 128, chunks, None))
            prev = None
            for bi, bk in enumerate(blocks):
                w = blk[0] % NW
                blk[0] += 1
                stage_s(w, *bk)
                if prev is not None:
                    stage_o(*prev)
                prev = (w,) + bk
                if bi == 12 and n + 1 < 8:
                    setup_compute(n + 1)
            stage_o(*prev)
            P.dma(Y[n], S_["ysb"][:])

    P.end_phase()

    stages = [
        ("adaln", phase_adaln),
        ("copyx", None),
        ("projab", phase_proj_ab),
        ("mlstm0", phase_mlstm),
        ("mlstm", phase_mlstm_fin),
        ("filters", phase_filters_adaln1),
        ("hyena", phase_hyena),
        ("outab", lambda: phase_resproj(w_out_ab, 8, Y, 0, 0, xT)),
        ("ffnup0", lambda: phase_ffn_up(0)),
        ("down0", lambda: phase_resproj(w_down[0], 22, ACTS, 0, 1, X)),
        ("projc", phase_proj_c),
        ("attn", phase_attn),
        ("outc", lambda: phase_resproj(w_out_c, 8, Y, 1, 0, X)),
        ("ffnup1", lambda: phase_ffn_up(1)),
        ("down1", lambda: phase_resproj(w_down[1], 22, ACTS, 1, 1, X)),
        ("final", lambda: phase_norm(0, 0, X, final=True)),
    ]
    dbg_outs = {}
    for name, fn in stages:
        if fn is None:
            continue
        P.begin_phase()
        fn()
        P.end_phase()
        if debug is not None and name == debug[0]:
            P.begin_phase()
            for (tn, shape, dt) in debug[1]:
                srcT = {"X": X, "HN": HN, "QK": QK, "V": V, "SO": SO, "G": G, "HY": HY, "U": U, "HF": HF, "Y": Y,
                        "ACTS": ACTS, "V64": V64}[tn]
                o = dout("dbg_" + tn, shape, dt)
                dbg_outs[tn] = o
                isc = (len(shape) == 3 and shape[1] == 128)
                buf = P.sb("dbgbuf", [128, int(np.prod(shape[2:])) if isc else int(np.prod(shape[1:]))], dt)
                if isc:
                    for ci in range(shape[0]):
                        P.dma(buf[:], srcT[ci])
                        P.dma(o[ci], buf[:])
                else:
                    P.dma(buf[:], srcT.rearrange("p a b -> p (a b)"))
                    P.dma(o.rearrange("p a b -> p (a b)"), buf[:])
            P.end_phase()
            break
    P.close()
    nc._phase_counts = P.phase_counts
    return nc


def _consts():
    ar = np.arange(128)
    ident = np.eye(128, dtype=np.float32)
    tri_f = (ar[:, None] <= ar[None, :]).astype(np.float32)
    tri_b = (ar[:, None] >= ar[None, :]).astype(np.float32)
    maskb_f = np.where(ar[:, None] <= ar[None, :], 0.0, NEG).astype(np.float32)
    maskb_b = np.where(ar[:, None] >= ar[None, :], 0.0, NEG).astype(np.float32)
    cols = np.arange(64)
    c_start = np.clip(cols - 8, 0, 48)
    valid = (cols[:, None] >= c_start[None, :]) & (cols[:, None] < c_start[None, :] + 16)
    cm = np.where(valid, 0.0, NEG).astype(np.float32)
    last = np.zeros((128, 128), np.float32)
    last[0:64, 0:64] = cm
    last[64:128, 0:64] = cm
    last[:, 64] = 1.0
    last[0, 64] = 0.0
    cst = np.stack([ident, tri_f, tri_b, maskb_f, maskb_b, last], axis=1)
    out = {"cst": np.ascontiguousarray(cst)}
    deltas = np.abs(np.linspace(math.log(1e-2) / 1.5, math.log(1e-2) / 0.3, 512, dtype=np.float32))
    out["deltas"] = np.concatenate([deltas, deltas])[None, :].astype(np.float32)
    for L in (LS, LP):
        c = hy_cfg(L)
        N, na, nb = c["N"], c["na"], c["nb"]
        t = np.arange(na * 128, dtype=np.int64)
        f = np.arange(nb * 128, dtype=np.int64)
        ang = 2.0 * np.pi * ((t[:, None] * f[None, :]) % N).astype(np.float64) / N
        Cm, Sm = np.cos(ang), np.sin(ang)
        TF = np.stack([Cm, Sm], axis=0).reshape(2, na, 128, nb, 128).transpose(3, 2, 1, 0, 4)
        out["TF%d" % L] = np.ascontiguousarray(TF).astype(np.float32)
        tt = np.arange(L, dtype=np.int64)
        ang2 = 2.0 * np.pi * ((f[:, None] * tt[None, :]) % N).astype(np.float64) / N
        TI = np.stack([np.cos(ang2), np.sin(ang2)], axis=0).reshape(2, nb, 128, L).transpose(2, 1, 0, 3)
        out["TI%d" % L] = np.ascontiguousarray(TI).astype(np.float32)
        tl = np.linspace(0.0, 1.0, L, dtype=np.float32)
        wpos = (2.0 * np.pi * np.arange(L, dtype=np.float32) / L).astype(np.float32)
        bands = np.linspace(1e-4, 15, 16, dtype=np.float32)
        z = np.concatenate([tl[:, None], np.cos(bands[None, :] * wpos[:, None]), -np.sin(bands[None, :] * wpos[:, None])],
                           axis=-1).astype(np.float32)
        out["zT%d" % L] = np.ascontiguousarray(z.T)
        out["tneg%d" % L] = np.ascontiguousarray((-tl).reshape(na, 128).T)
        wt = np.zeros(nb * 128, np.float64)
        wt[0] = 1.0 / N
        wt[1:L] = 2.0 / N
        wt[L] = 1.0 / N
        wt2 = np.stack([wt, -wt], axis=-1).reshape(nb, 128, 2).transpose(1, 0, 2)
        out["wt%d" % L] = np.ascontiguousarray(wt2).astype(np.float32)
    return out


_CACHE = {}


def _prep_inputs(inp):
    f = lambda a: np.ascontiguousarray(np.asarray(a, dtype=np.float32))
    shared = dict(_consts())
    shared["w_ada"] = f(inp["w_ada"])
    shared["b_adaP"] = f(np.asarray(inp["b_ada"]).reshape(2, 48, 128).transpose(2, 0, 1))
    gs = np.stack([inp["g_mix"][0], inp["g_mix"][1], inp["g_ffn"][0], inp["g_ffn"][1], inp["g_final"]], axis=0)
    shared["gP"] = f(gs.reshape(5, 8, 128).transpose(2, 0, 1))
    shared["w_in_ab"] = f(inp["w_in_ab"][0])
    shared["b_gates"] = f(inp["b_gates"][0][None, :])
    shared["wcqk"] = f(np.asarray(inp["w_conv_qk"][0]).reshape(3, 8, 128).transpose(2, 1, 0))
    shared["g_ml"] = f(np.asarray(inp["g_mlstm"][0]).reshape(4, 128).T)
    shared["wchy"] = f(np.asarray(inp["w_conv_hy"][0]).reshape(3, 12, 128).transpose(2, 1, 0))
    shared["w_f1"] = f(inp["w_filt1"][0])
    shared["w_f2"] = f(inp["w_filt2"][0])
    shared["w_f3"] = f(inp["w_filt3"][0])
    shared["fvec"] = f(np.stack([inp["b_filt1"][0], inp["b_filt2"][0], inp["filt_freq"][0]], axis=-1))
    shared["hyb"] = f(np.asarray(inp["hyena_bias"][0]).reshape(4, 128).T)
    shared["w_out_ab"] = f(inp["w_out_ab"][0])
    shared["w_in_c"] = f(inp["w_in_c"][0])
    rp = np.asarray(inp["rpb_c"][0], dtype=np.float32)
    pidx = np.arange(128)
    wk = (pidx % 64)[:, None, None]
    up = (pidx >= 64).astype(np.int64)[:, None, None]
    xx = np.arange(15)[None, :, None]
    wq = np.arange(64)[None, None, :]
    ci = wk - wq + 15
    ri = xx + up
    ok = (ci >= 0) & (ci <= 30) & (ri <= 14)
    gat = rp[:, np.clip(ri, 0, 14), np.clip(ci, 0, 30)]
    shared["TmE"] = np.ascontiguousarray(np.where(ok[None], gat, 0.0).astype(np.float32))
    shared["w_out_c"] = f(inp["w_out_c"][0])
    shared["w_up"] = f(inp["w_up"])
    shared["wcffn"] = f(np.asarray(inp["w_conv_ffn"]).reshape(2, 3, 22, 128).transpose(3, 0, 2, 1))
    shared["w_down"] = f(inp["w_down"])
    maps = []
    for c in range(8):
        b = c % 4
        m = dict(shared)
        toks = np.concatenate([inp["x_sample"][b], inp["x_prompt"][2 * c], inp["x_prompt"][2 * c + 1]], axis=0)
        m["xT"] = f(np.asarray(toks).T.reshape(8, 128, T))
        cvv = np.stack([inp["c"][b], inp["c_ctx"]], axis=-1)
        m["cv"] = f(cvv.reshape(8, 128, 2).transpose(1, 0, 2))
        m["st_C"] = f(np.asarray(inp["state_mlstm_C"][b, 0]).reshape(8, 128, 128))
        m["st_n"] = f(np.asarray(inp["state_mlstm_n"][b, 0]).reshape(8, 128).T)
        m["st_m"] = f(np.broadcast_to(np.asarray(inp["state_mlstm_m"][b, 0]).reshape(1, 8), (128, 8)))
        m["kcT"] = f(np.asarray(inp["cache_na_k"][b, 0]).transpose(0, 2, 1).reshape(8, 128, 512))
        m["vc"] = f(inp["cache_na_v"][b, 0])
        maps.append(m)
    return maps


def kernel(**inputs):
    inp = {k: np.asarray(v) for k, v in inputs.items()}
    if "nc" not in _CACHE:
        _CACHE["nc"] = build()
    nc = _CACHE["nc"]
    maps = _prep_inputs(inp)
    res = run_bass_kernel_spmd(nc, maps, core_ids=list(range(8)))
    R = res.results
    y_prompt = np.zeros((16, 256, D), np.float32)
    y_sample = np.zeros((4, 2048, D), np.float32)
    nC = np.zeros((16, 1, 2, 4, 128, 128), np.float32)
    nn = np.zeros((16, 1, 2, 4, 128), np.float32)
    nm = np.zeros((16, 1, 2, 4), np.float32)
    nk = np.zeros((16, 1, 16, 256, 64), np.float32)
    nv = np.zeros((16, 1, 16, 256, 64), np.float32)
    for c in range(8):
        yt = np.asarray(R[c]["yT"]).reshape(D, T).T
        if c < 4:
            y_sample[c] = yt[0:2048]
        y_prompt[2 * c] = yt[2048:2304]
        y_prompt[2 * c + 1] = yt[2304:2560]
        for pi in range(2):
            nC[2 * c + pi, 0] = np.asarray(R[c]["o_C"])[pi]
            nn[2 * c + pi, 0] = np.asarray(R[c]["o_n"])[pi]
            nm[2 * c + pi, 0] = np.asarray(R[c]["o_m"])[pi]
            nk[2 * c + pi, 0] = np.asarray(R[c]["o_k"])[pi]
            nv[2 * c + pi, 0] = np.asarray(R[c]["o_v"])[pi]
    return (y_prompt, y_sample, nC, nn, nm, nk, nv)
```

```python
import math
import os
import numpy as np
from contextlib import ExitStack
import concourse.bass as bass
import concourse.mybir as mybir
from concourse.bass_utils import run_bass_kernel_spmd

F32 = mybir.dt.float32
BF16 = mybir.dt.bfloat16
AF = mybir.ActivationFunctionType
ALU = mybir.AluOpType
AX = mybir.AxisListType

SAME_ENGINE_SYNC = True
N_DMA_SEMS = 20
_READ_KW = ("in_", "in0", "in1", "lhsT", "rhs", "scalar1", "scalar2", "scalar", "bias", "scale",
            "data0", "data1", "initial", "identity")
_WRITE_KW = ("out", "accum_out", "ap")


class Op:
    __slots__ = ("eng", "meth", "kw", "reads", "writes", "deps", "needs_inc", "sem", "val", "is_dma", "done")

    def __init__(self, eng, meth, kw, reads, writes, is_dma):
        self.eng, self.meth, self.kw = eng, meth, kw
        self.reads, self.writes = reads, writes
        self.deps = []
        self.needs_inc = False
        self.sem = None
        self.val = None
        self.is_dma = is_dma
        self.done = False


class Prog:
    def __init__(self, nc):
        self.nc = nc
        self.es = ExitStack()
        self.phase_es = None
        self.ops = []
        self.state = {}
        self.engs = {"pe": nc.tensor, "act": nc.scalar, "dve": nc.vector, "pool": nc.gpsimd, "sp": nc.sync}
        self.esem = {}
        for e in ("pe", "act", "dve", "pool"):
            self.esem[e] = self.es.enter_context(nc.semaphore("es_" + e))
        self.dsems = {}
        for q in ("sp", "pool"):
            self.dsems[q] = [self.es.enter_context(nc.semaphore("ds_%s_%d" % (q, i))) for i in range(N_DMA_SEMS)]
        self.dcount = {"sp": 0, "pool": 0}
        self.ecount = {e: 0 for e in self.esem}
        self.seen = {e: {} for e in self.engs}
        self.emitted = 0
        self.uid = 0
        self.phase_counts = []

    def sb(self, name, shape, dtype, persist=False):
        self.uid += 1
        st = self.es if (persist or self.phase_es is None) else self.phase_es
        return st.enter_context(self.nc.sbuf_tensor("%s_%d" % (name, self.uid), list(shape), dtype))

    def ps(self, name, shape=(128, 512), dtype=F32):
        self.uid += 1
        return self.phase_es.enter_context(self.nc.psum_tensor("%s_%d" % (name, self.uid), list(shape), dtype))

    def dram(self, name, shape, dtype, kind="Internal"):
        return self.nc.dram_tensor(name, list(shape), dtype, kind=kind).ap()

    @staticmethod
    def _key(x):
        if isinstance(x, tuple):
            return (x[0].name if not isinstance(x[0], str) else x[0]), x[1]
        if isinstance(x, str):
            return x, None
        return x.name, None

    def _access(self, op, key, write):
        name, tag = key
        st = self.state.setdefault(name, {})
        tags = list(st.keys()) if tag is None else [t for t in (tag, None) if t in st]
        for t in tags:
            lw, rd = st[t]
            if lw is not None:
                op.deps.append(lw)
            if write:
                op.deps.extend(rd)
        if write:
            if tag is None:
                st.clear()
            st[tag] = [op, []]
        else:
            if tag not in st:
                st[tag] = [None, []]
            st[tag][1].append(op)

    def add(self, eng, meth, kw, r=None, w=None, rt=None, wt=None, is_dma=False):
        reads = list(r) if r is not None else []
        writes = list(w) if w is not None else []
        if r is None:
            for k in _READ_KW:
                v = kw.get(k, None)
                if v is not None and hasattr(v, "name") and hasattr(v, "ap"):
                    reads.append((v, rt) if rt is not None else v)
        if w is None:
            for k in _WRITE_KW:
                v = kw.get(k, None)
                if v is not None and hasattr(v, "name"):
                    writes.append((v, wt) if wt is not None else v)
        op = Op(eng, meth, kw, [self._key(x) for x in reads], [self._key(x) for x in writes], is_dma)
        for k in op.reads:
            self._access(op, k, False)
        for k in op.writes:
            self._access(op, k, True)
        seen = set()
        dd = []
        for d in op.deps:
            if d is op or id(d) in seen or d.done:
                continue
            seen.add(id(d))
            dd.append(d)
        op.deps = dd
        for d in dd:
            if d.is_dma:
                continue
            if d.eng == op.eng and not op.is_dma and (d.eng == "pe" or not SAME_ENGINE_SYNC):
                continue
            d.needs_inc = True
        self.ops.append(op)
        return op

    def dma(self, out, in_, q="sp", slow=False, **k):
        kw = dict(out=out, in_=in_)
        if slow:
            kw["allow_slow_non_contiguous"] = True
        return self.add(q, "dma_start", kw, is_dma=True, **k)

    def mm(self, out, lhsT, rhs, start=True, stop=True, lt=None, rtg=None, **k):
        if lt is not None or rtg is not None:
            k["r"] = [(lhsT, lt) if lt is not None else lhsT, (rhs, rtg) if rtg is not None else rhs]
        return self.add("pe", "matmul", dict(out=out, lhsT=lhsT, rhs=rhs, start=start, stop=stop), **k)

    def tr(self, out, in_, identity, **k):
        return self.add("pe", "transpose", dict(out=out, in_=in_, identity=identity), **k)

    def act(self, out, in_, func, rt=None, wt=None, **kw):
        kw.update(out=out, in_=in_, func=func)
        return self.add("act", "activation", kw, rt=rt, wt=wt)

    def v(self, meth, eng="dve", rt=None, wt=None, r=None, w=None, **kw):
        return self.add(eng, meth, kw, rt=rt, wt=wt, r=r, w=w)

    def _wait(self, eng, sem, val):
        s = self.seen[eng]
        k = id(sem)
        if s.get(k, 0) >= val:
            return
        s[k] = val
        self.engs[eng].wait_ge(sem, val)

    def flush(self):
        for op in self.ops[self.emitted:]:
            e = op.eng
            for d in op.deps:
                if d.is_dma:
                    self._wait(e, d.sem, d.val)
                else:
                    if d.eng == e and not op.is_dma and (e == "pe" or not SAME_ENGINE_SYNC):
                        continue
                    self._wait(e, d.sem, d.val)
            if op.is_dma:
                q = e
                i = self.dcount[q]
                self.dcount[q] += 1
                sem = self.dsems[q][i % N_DMA_SEMS]
                val = 16 * (i // N_DMA_SEMS + 1)
                if val > 16:
                    self._wait(q, sem, val - 16)
                op.sem, op.val = sem, val
                self.engs[q].dma_start(**op.kw).then_inc(sem, 16)
            else:
                ins = getattr(self.engs[e], op.meth)(**op.kw)
                if op.needs_inc:
                    self.ecount[e] += 1
                    op.sem, op.val = self.esem[e], self.ecount[e]
                    ins.then_inc(op.sem, 1)
        self.emitted = len(self.ops)

    def barrier(self):
        for e in ("pe", "act", "dve", "pool"):
            for op in reversed(self.ops[self.emitted:]):
                if op.eng == e and not op.is_dma:
                    op.needs_inc = True
                    break
        self.flush()
        cnt = {}
        for op in self.ops:
            k = op.eng + ("_dma" if op.is_dma else "")
            cnt[k] = cnt.get(k, 0) + 1
        self.phase_counts.append(cnt)
        for eng in ("sp", "pe", "act", "dve", "pool"):
            for q in ("sp", "pool"):
                n = self.dcount[q]
                for j in range(min(n, N_DMA_SEMS)):
                    cnt = (n - 1 - j) // N_DMA_SEMS + 1
                    self._wait(eng, self.dsems[q][j], 16 * cnt)
            for e in ("pe", "act", "dve", "pool"):
                if e != eng and self.ecount[e] > 0:
                    self._wait(eng, self.esem[e], self.ecount[e])
        for op in self.ops:
            op.done = True
        self.ops = []
        self.emitted = 0
        self.state = {}

    def begin_phase(self):
        self.phase_es = ExitStack()

    def end_phase(self):
        self.barrier()
        self.phase_es.close()
        self.phase_es = None

    def close(self):
        self.es.close()


D = 1024
T = 2560
LS, LP = 2048, 256
SEQS = [(0, 2048), (2048, 2304), (2304, 2560)]
NTB = 5
NTC = 20
FF = 2816
IN_AB = 3600
EPS = 1e-6
MAGIC = 12582912.0
TWO_PI = 2.0 * math.pi
NEG = -30000.0


def hy_cfg(L):
    N = 2 * L
    na = L // 128
    nb = (L + 1 + 127) // 128
    return dict(L=L, N=N, na=na, nb=nb)


def build(debug=None):
    nc = bass.Bass("TRN2", target_bir_lowering=False)
    P = Prog(nc)

    def din(name, shape, dt=F32):
        return nc.dram_tensor(name, list(shape), dt, kind="ExternalInput").ap()

    def dout(name, shape, dt=F32):
        return nc.dram_tensor(name, list(shape), dt, kind="ExternalOutput").ap()

    def dbgdump(name, ap, dt=F32):
        if debug is None:
            return
        o = dout("dbgm_" + name, list(ap.shape), dt)
        P.dma(o, ap)

    xT = din("xT", [8, 128, T])
    cv = din("cv", [128, 8, 2])
    st_C = din("st_C", [8, 128, 128])
    st_n = din("st_n", [128, 8])
    st_m = din("st_m", [128, 8])
    kcT = din("kcT", [8, 128, 512])
    vc = din("vc", [16, 512, 64])
    w_ada = din("w_ada", [2, D, 6 * D])
    b_adaP = din("b_adaP", [128, 2, 48])
    gP = din("gP", [128, 5, 8])
    w_in_ab = din("w_in_ab", [D, IN_AB])
    b_gates = din("b_gates", [1, 16])
    wcqk = din("wcqk", [128, 8, 3])
    g_ml = din("g_ml", [128, 4])
    wchy = din("wchy", [128, 12, 3])
    w_f1 = din("w_f1", [33, 64])
    w_f2 = din("w_f2", [64, 64])
    w_f3 = din("w_f3", [64, 1024])
    fvec = din("fvec", [64, 3])
    hyb = din("hyb", [128, 4])
    w_out_ab = din("w_out_ab", [D, D])
    w_in_c = din("w_in_c", [D, 3 * D])
    TmE = din("TmE", [16, 128, 15, 64])
    w_out_c = din("w_out_c", [D, D])
    w_up = din("w_up", [2, D, 2 * FF])
    wcffn = din("wcffn", [128, 2, 22, 3])
    w_down = din("w_down", [2, FF, D])
    cst = din("cst", [128, 6, 128])
    deltas = din("deltas", [1, 1024])
    hc = {}
    for L in (LS, LP):
        c = hy_cfg(L)
        c["TF"] = din("TF%d" % L, [c["nb"], 128, c["na"], 2, 128])
        c["TI"] = din("TI%d" % L, [128, c["nb"], 2, L])
        c["zT"] = din("zT%d" % L, [33, L])
        c["tneg"] = din("tneg%d" % L, [128, c["na"]])
        c["wt"] = din("wt%d" % L, [128, c["nb"], 2])
        hc[L] = c

    yT = dout("yT", [8, 128, T])
    o_C = dout("o_C", [2, 2, 4, 128, 128])
    o_n = dout("o_n", [2, 2, 4, 128])
    o_m = dout("o_m", [2, 2, 4])
    o_k = dout("o_k", [2, 16, 256, 64])
    o_v = dout("o_v", [2, 16, 256, 64])

    X = P.dram("X", [8, 128, T], F32)
    HN = P.dram("HN", [8, 128, T], BF16)
    QK = P.dram("QK", [16, 128, T], BF16)
    V = P.dram("V", [NTC, 128, 1024], BF16)
    V64 = P.dram("V64", [16, 128, 1024], BF16)
    SO = P.dram("SO", [4, 128, T], BF16)
    G = P.dram("G", [128, NTC, 16], F32)
    HY = P.dram("HY", [12, 128, T], F32)
    U = P.dram("U", [4, 128, T], F32)
    HF = P.dram("HF", [4, 128, T], F32)
    HB = P.dram("HB", [4, 128, T], F32)
    Y = P.dram("Y", [8, 128, T], BF16)
    ACTS = P.dram("ACTS", [22, 128, T], BF16)
    for L in (LS, LP):
        hc[L]["FL"] = P.dram("FL%d" % L, [hc[L]["na"], 128, 1024], BF16)

    cst_t = P.sb("cst", [128, 6, 128], F32, persist=True)
    ident = cst_t[:, 0, :]
    tri = [cst_t[:, 1, :], cst_t[:, 2, :]]
    maskb = [cst_t[:, 3, :], cst_t[:, 4, :]]
    colmask = cst_t[:, 5, 0:64]
    ident_bf = P.sb("identbf", [128, 128], BF16, persist=True)
    ones_f = P.sb("onesf", [128, 128], F32, persist=True)
    ones_bf = P.sb("onesbf", [128, 128], BF16, persist=True)
    epsc = P.sb("epsc", [128, 1], F32, persist=True)
    modt = P.sb("modt", [128, 2, 48, 2], F32, persist=True)
    Atab = P.sb("Atab", [128, 2, 2, 8, 2], F32, persist=True)
    gPt = P.sb("gPt", [128, 5, 8], F32, persist=True)
    wst = [P.sb("wst%d" % i, [128, 4096], F32, persist=True) for i in range(2)]
    wbf = [P.sb("wbf%d" % i, [128, 4096], BF16, persist=True) for i in range(2)]
    wctr = [0]
    scv_p = P.sb("scv", [128, 8, 2], F32, persist=True)
    bt_p = P.sb("bada", [128, 2, 48], F32, persist=True)

    P.begin_phase()
    P.dma(cst_t[:], cst[:, :, :])
    P.dma(gPt[:], gP[:, :, :])
    P.v("tensor_copy", out=ident_bf[:], in_=ident)
    P.v("memset", ap=ones_f[:], constant=1.0)
    P.v("memset", ap=ones_bf[:], constant=1.0)
    P.v("memset", ap=epsc[:], constant=EPS)

    wplan = {"reqs": None}

    def w_plan(reqs):
        mx_e = max(r[1] * r[3] for r in reqs)
        wplan.update(reqs=reqs, cur=0, dma=0, cast=0, nslots=(2 if mx_e > 1024 else 8),
                     se=(4096 if mx_e > 1024 else 1024), base=wctr[0])
        wctr[0] += 1

    def _w_views(k):
        Wap, nk, c0, ncols = wplan["reqs"][k]
        slot = k % wplan["nslots"]
        off = slot * wplan["se"]
        ti, o = off // 4096, off % 4096
        tag = "w%d_%d" % (wplan["base"], slot)
        sv = wst[ti][:, o:o + nk * ncols].rearrange("p (k n) -> p k n", k=nk)
        bv = wbf[ti][:, o:o + nk * ncols].rearrange("p (k n) -> p k n", k=nk)
        return Wap, nk, c0, ncols, sv, bv, tag

    def _w_dma(k):
        Wap, nk, c0, ncols, sv, bv, tag = _w_views(k)
        P.dma(sv, Wap[0:nk * 128, c0:c0 + ncols].rearrange("(k p) n -> p k n", p=128), q="sp", wt=tag)

    def _w_cast(k):
        Wap, nk, c0, ncols, sv, bv, tag = _w_views(k)
        P.act(bv, sv, AF.Identity, rt=tag, wt=tag)

    def w_tick():
        if wplan["reqs"] is None:
            return
        k = wplan["cur"]
        if k < len(wplan["reqs"]) and wplan["cast"] <= k and wplan["dma"] > k:
            _w_cast(k)
            wplan["cast"] = k + 1

    def w_end():
        wplan["reqs"] = None

    def load_w(Wap, k0, nk, c0, ncols, cast=True):
        if wplan["reqs"] is None or not cast:
            i = wctr[0]
            wctr[0] += 1
            sv = wst[i % 2][:, 0:nk * ncols].rearrange("p (k n) -> p k n", k=nk)
            P.dma(sv, Wap[k0 * 128:(k0 + nk) * 128, c0:c0 + ncols].rearrange("(k p) n -> p k n", p=128), q="sp")
            if not cast:
                return sv
            bv = wbf[i % 2][:, 0:nk * ncols].rearrange("p (k n) -> p k n", k=nk)
            P.act(bv, sv, AF.Identity)
            return bv, None
        k = wplan["cur"]
        rq = wplan["reqs"][k]
        assert rq[1] == nk and rq[2] == c0 and rq[3] == ncols, (rq[1:], nk, c0, ncols)
        while wplan["dma"] <= k:
            _w_dma(wplan["dma"])
            wplan["dma"] += 1
        while wplan["cast"] <= k:
            _w_cast(wplan["cast"])
            wplan["cast"] += 1
        wplan["cur"] = k + 1
        depth = 1 if wplan["nslots"] == 2 else 3
        while wplan["dma"] < min(len(wplan["reqs"]), k + 1 + depth):
            _w_dma(wplan["dma"])
            wplan["dma"] += 1
        return _w_views(k)[5], _w_views(k)[6]

    def linear_fm(Wap, nk, c0, ncols_total, src, psums, epi, tbs=range(NTB), chunk0=0, ensure=None, fb_outer=False):
        pi = [0]
        done = 0
        pending = []
        while done < ncols_total:
            nc_ = min(512, ncols_total - done)
            bv, wtag = load_w(Wap, 0, nk, c0 + done, nc_)
            nm = nc_ // 128
            if ensure is not None and done == 0 and fb_outer:
                new_pending = []
                for tb in tbs:
                    ensure(tb)
                    for m in range(nm):
                        ps = psums[pi[0] % len(psums)]
                        pi[0] += 1
                        for kc in range(nk):
                            P.mm(ps[:, :], bv[:, kc, m * 128:(m + 1) * 128], src[:, kc, tb * 512:(tb + 1) * 512],
                                 start=(kc == 0), stop=(kc == nk - 1), lt=wtag, rtg=tb)
                        t_ = epi(chunk0 + m, tb, ps)
                        if t_ is not None:
                            new_pending.append(t_)
                    if tb == 2:
                        w_tick()
                for t_ in new_pending:
                    t_()
                pending = []
                done += nc_
                continue
            for m in range(nm):
                new_pending = []
                for tb in tbs:
                    if ensure is not None:
                        ensure(tb)
                    ps = psums[pi[0] % len(psums)]
                    pi[0] += 1
                    for kc in range(nk):
                        P.mm(ps[:, :], bv[:, kc, m * 128:(m + 1) * 128], src[:, kc, tb * 512:(tb + 1) * 512],
                             start=(kc == 0), stop=(kc == nk - 1), lt=wtag, rtg=tb)
                    t_ = epi(chunk0 + (done // 128) + m, tb, ps)
                    if t_ is not None:
                        new_pending.append(t_)
                    if m == (nm - 1) // 2 and tb == 2:
                        w_tick()
                for t_ in pending:
                    t_()
                pending = new_pending
            done += nc_
        for t_ in pending:
            t_()

    def linear_tm(Wap, nk, c0, ncols, src, psums, epi, tok_chunks, tok_off=0):
        bv, wtag = load_w(Wap, 0, nk, c0, ncols)
        tok_chunks = list(tok_chunks)
        for j, tc in enumerate(tok_chunks):
            ps = psums[j % len(psums)]
            t0 = tok_off + tc * 128
            for kc in range(nk):
                P.mm(ps[:, 0:ncols], src[:, kc, t0:t0 + 128], bv[:, kc, :], start=(kc == 0), stop=(kc == nk - 1),
                     lt=(t0 // 512 if tok_off == 0 else None), rtg=wtag)
            epi(tc, ps)
            if j == len(tok_chunks) // 2:
                w_tick()

    def conv3(acc, raw, taps):
        P.v("tensor_scalar", out=acc[:, :], in0=raw[:, :], scalar1=taps[:, 1:2], scalar2=None, op0=ALU.mult)
        for (s0, s1) in SEQS:
            P.v("scalar_tensor_tensor", out=acc[:, s0 + 1:s1], in0=raw[:, s0:s1 - 1], scalar=taps[:, 0:1],
                in1=acc[:, s0 + 1:s1], op0=ALU.mult, op1=ALU.add)
            P.v("scalar_tensor_tensor", out=acc[:, s0:s1 - 1], in0=raw[:, s0 + 1:s1], scalar=taps[:, 2:3],
                in1=acc[:, s0:s1 - 1], op0=ALU.mult, op1=ALU.add)

    def adaln_layer(l, bufs, pa, pre_hook=None):
        nb_ = len(bufs)

        def dma_(s_):
            sv_ = bufs[s_ % nb_][:, :].rearrange("p (k n) -> p k n", k=8)
            P.dma(sv_, w_ada[l][:, s_ * 512:(s_ + 1) * 512].rearrange("(k p) n -> p k n", p=128),
                  q=("pool" if (l == 0 and s_ % 2 == 1) else "sp"))
        for s_ in range(min(12, nb_)):
            dma_(s_)
        if pre_hook is not None:
            pre_hook()
        for s_ in range(12):
            sv = bufs[s_ % nb_][:, :].rearrange("p (k n) -> p k n", k=8)
            if s_ >= nb_:
                dma_(s_)
            ps = pa[s_ % len(pa)]
            for m in range(4):
                for kc in range(8):
                    P.mm(ps[:, 2 * m:2 * m + 2], sv[:, kc, m * 128:(m + 1) * 128], scv_p[:, kc, :],
                         start=(kc == 0), stop=(kc == 7))
            for m in range(4):
                n = s_ * 4 + m
                P.v("tensor_scalar", out=modt[:, l, n, :], in0=ps[:, 2 * m:2 * m + 2], scalar1=bt_p[:, l, n:n + 1],
                    scalar2=None, op0=ALU.add)
        for sub in range(2):
            sc0 = 8 + 24 * sub
            P.v("tensor_scalar", out=Atab[:, l, sub, :, :], in0=modt[:, l, sc0:sc0 + 8, :], scalar1=1.0,
                scalar2=None, op0=ALU.add)
            for j in range(2):
                P.v("tensor_tensor", out=Atab[:, l, sub, :, j], in0=Atab[:, l, sub, :, j],
                    in1=gPt[:, 2 * sub + l, :], op=ALU.mult)

    def phase_adaln():
        pa = [P.ps("pa%d" % i) for i in range(2)]
        P.dma(scv_p[:], cv[:, :, :])
        P.dma(bt_p[:], b_adaP[:, :, :])
        P.act(scv_p[:], scv_p[:], AF.Silu)
        extra = [P.sb("adx%d" % i, [128, 4096], F32) for i in range(4)]
        adaln_layer(0, wst + extra, pa)

    def phase_norm(l, sub, src, final=False):
        xb = [P.sb("xb%d" % i, [128, 8, 512], F32) for i in range(2)]
        sq = [P.sb("sq%d" % i, [128, 8, 512], BF16) for i in range(2)]
        rs = [P.sb("rs%d" % i, [128, 512], F32) for i in range(2)]
        tmp = [P.sb("tmp%d" % i, [128, 8, 512], F32) for i in range(2)]
        hb = [P.sb("hb%d" % i, [128, 8, 512], (F32 if final else BF16)) for i in range(2)]
        pn = [P.ps("pn%d" % i) for i in range(2)]
        for tb in range(NTB):
            j = 0 if tb < 4 else 1
            i = tb % 2
            tsl = slice(tb * 512, (tb + 1) * 512)
            P.dma(xb[i][:], src[:, :, tsl].rearrange("c p t -> p c t"))
            P.act(sq[i][:], xb[i][:], AF.Square)
            for kc in range(8):
                P.mm(pn[i][:, :], ones_bf[:], sq[i][:, kc, :], start=(kc == 0), stop=(kc == 7))
            P.act(rs[i][:], pn[i][:, :], AF.Ln, scale=1.0 / D, bias=epsc[:, 0:1])
            P.act(rs[i][:], rs[i][:], AF.Exp, scale=-0.5)
            for kc in range(8):
                if final:
                    P.v("scalar_tensor_tensor", out=hb[i][:, kc, :], in0=xb[i][:, kc, :], scalar=gPt[:, 4, kc:kc + 1],
                        in1=rs[i][:], op0=ALU.mult, op1=ALU.mult, rt=kc, wt=kc)
                else:
                    P.v("scalar_tensor_tensor", out=tmp[i][:, kc, :], in0=xb[i][:, kc, :],
                        scalar=Atab[:, l, sub, kc, j:j + 1], in1=rs[i][:], op0=ALU.mult, op1=ALU.mult, rt=kc, wt=kc)
                    P.act(hb[i][:, kc, :], tmp[i][:, kc, :], AF.Identity, bias=modt[:, l, 24 * sub + kc, j:j + 1],
                          rt=kc, wt=kc)
            dst = yT if final else HN
            P.dma(dst[:, :, tsl].rearrange("c p t -> p c t"), hb[i][:], q="pool")
            if tb == 0 and l == 0 and sub == 0 and not final:
                dbgdump("rs", rs[i][:])
                dbgdump("xb", xb[i][:, 0, :])
                dbgdump("tmp", tmp[i][:, 0, :])
                dbgdump("atab", Atab[:].rearrange("p a b c d -> p (a b c d)"))
                dbgdump("modt", modt[:].rearrange("p a b c -> p (a b c)"))

    def norm_hn(l, sub, src):
        hn = P.sb("hn", [128, 8, T], BF16)
        xb = [P.sb("nxb%d" % i, [128, 8, 512], F32) for i in range(2)]
        rs = [P.sb("nrs%d" % i, [128, 512], F32) for i in range(2)]
        pn = [P.ps("npn%d" % i) for i in range(2)]
        st_ = {"dma": 0, "done": 0}

        def dma(tb):
            tsl = slice(tb * 512, (tb + 1) * 512)
            P.dma(xb[tb % 2][:], src[:, :, tsl].rearrange("c p t -> p c t"), q=("pool" if tb % 2 == 0 else "sp"))

        def ensure(tb):
            while st_["done"] <= tb:
                t = st_["done"]
                while st_["dma"] < min(NTB, t + 2):
                    dma(st_["dma"])
                    st_["dma"] += 1
                i = t % 2
                j = 0 if t < 4 else 1
                tsl = slice(t * 512, (t + 1) * 512)
                P.act(hn[:, :, tsl], xb[i][:], AF.Square, wt=t)
                for kc in range(8):
                    P.mm(pn[i][:, :], ones_bf[:], hn[:, kc, tsl], start=(kc == 0), stop=(kc == 7), rtg=t)
                P.act(rs[i][:], pn[i][:, :], AF.Ln, scale=1.0 / D, bias=epsc[:, 0:1])
                P.act(rs[i][:], rs[i][:], AF.Exp, scale=-0.5)
                for kc in range(8):
                    P.v("scalar_tensor_tensor", out=xb[i][:, kc, :], in0=xb[i][:, kc, :],
                        scalar=Atab[:, l, sub, kc, j:j + 1], in1=rs[i][:], op0=ALU.mult, op1=ALU.mult, rt=kc, wt=kc)
                for kc in range(8):
                    P.add("act", "activation", dict(out=hn[:, kc, tsl], in_=xb[i][:, kc, :], func=AF.Identity,
                                                    bias=modt[:, l, 24 * sub + kc, j:j + 1]),
                          r=[(xb[i], kc), modt], w=[(hn, t)])
                st_["done"] += 1
        return hn, ensure

    def load_hn():
        hn = P.sb("hn", [128, 8, T], BF16)
        for tb in range(NTB):
            tsl = slice(tb * 512, (tb + 1) * 512)
            P.dma(hn[:, :, tsl], HN[:, :, tsl].rearrange("c p t -> p c t"), q="pool", wt=tb)
        return hn

    def phase_proj_ab():
        w_plan([(w_in_ab, 8, c_, n_) for (c_, n_) in [(0, 512), (512, 512), (1024, 512), (1536, 512), (2048, 16),
                                                      (2064, 512), (2576, 512), (3088, 512)]])
        hn, ens = norm_hn(0, 0, xT)
        raw = [P.sb("raw%d" % i, [128, T], F32) for i in range(2)]
        acc = [P.sb("acc%d" % i, [128, T], F32) for i in range(2)]
        obf = [P.sb("obf%d" % i, [128, T], BF16) for i in range(2)]
        tq = P.sb("tq", [128, 8, 3], F32)
        th = P.sb("th", [128, 12, 3], F32)
        bg = P.sb("bg", [128, 16], F32)
        gt = P.sb("gt", [128, NTC, 16], F32)
        gtmp = P.sb("gtmp", [128, NTC, 8], F32)
        vb = [P.sb("vb%d" % i, [128, 512], BF16) for i in range(2)]
        sob = [P.sb("sob%d" % i, [128, 512], BF16) for i in range(2)]
        pp = [P.ps("pp%d" % i) for i in range(4)]
        P.dma(tq[:], wcqk[:, :, :])
        P.dma(th[:], wchy[:, :, :])
        P.dma(bg[:], b_gates[0:1, :].broadcast_to([128, 16]))

        def epi_qk(n, tb, ps):
            i = n % 2
            P.act(raw[i][:, tb * 512:(tb + 1) * 512], ps[:, :], AF.Identity)
            if tb == NTB - 1:
                def tail():
                    conv3(acc[i], raw[i], tq[:, n, :])
                    P.act(acc[i][:], acc[i][:], AF.Silu)
                    P.v("tensor_scalar", out=obf[i][:], in0=acc[i][:], scalar1=(1.0 if n < 4 else 128.0 ** -0.5),
                        scalar2=None, op0=ALU.mult)
                    P.dma(QK[n], obf[i][:], q="pool")
                return tail
        linear_fm(w_in_ab, 8, 0, 1024, hn, pp, epi_qk, ensure=ens)

        def epi_v(tc, ps):
            i = tc % 2
            P.act(vb[i][:], ps[:, :], AF.Identity)
            P.dma(V[tc, :, 0:512], vb[i][:], q="pool")
        linear_tm(w_in_ab, 8, 1024, 512, hn, pp, epi_v, range(NTC))

        def epi_o(n, tb, ps):
            i = (n * NTB + tb) % 2
            P.act(sob[i][:], ps[:, :], AF.Sigmoid)
            P.dma(SO[n, :, tb * 512:(tb + 1) * 512], sob[i][:], q="pool")
        linear_fm(w_in_ab, 8, 1536, 512, hn, pp, epi_o)

        def epi_g(tc, ps):
            P.v("tensor_tensor", out=gt[:, tc, :], in0=ps[:, 0:16], in1=bg[:], op=ALU.add)
        linear_tm(w_in_ab, 8, 2048, 16, hn, pp, epi_g, range(NTC))
        for d in range(2):
            fs = gt[:, :, 4 + 8 * d:8 + 8 * d]
            ts = gtmp[:, :, 4 * d:4 * d + 4]
            P.act(ts, fs, AF.Exp, scale=-1.0)
            P.act(ts, ts, AF.Ln, bias=ones_f[:, 0:1])
            P.v("tensor_scalar", out=fs, in0=ts, scalar1=-1.0, scalar2=None, op0=ALU.mult)
        P.dma(G[:, :, :], gt[:], q="pool")

        def epi_hy(n, tb, ps):
            i = n % 2
            P.act(raw[i][:, tb * 512:(tb + 1) * 512], ps[:, :], AF.Identity)
            if tb == NTB - 1:
                def tail():
                    conv3(acc[i], raw[i], th[:, n, :])
                    P.dma(HY[n], acc[i][:], q="pool")
                return tail
        linear_fm(w_in_ab, 8, 2064, 1536, hn, pp, epi_hy)
        w_end()

    def phase_mlstm():
        gt = P.sb("gt", [128, NTC, 16], F32)
        P.dma(gt[:], G[:, :, :])
        stn = P.sb("stn", [128, 8], F32)
        stm = P.sb("stm", [128, 8], F32)
        P.dma(stn[:], st_n[:, :])
        P.dma(stm[:], st_m[:, :])
        P.act(stm[:], stm[:], AF.Exp)
        Cst = P.sb("Cst", [128, 4, 128], F32)
        Cbf = P.sb("Cbf", [128, 4, 128], BF16)
        nst = P.sb("nst", [128, 4], F32)
        nrep = P.sb("nrep", [128, 4, 128], BF16)
        mrun = P.sb("mrun", [128, 4], F32)
        ktm_all = P.sb("ktmall", [128, NTC, 512], BF16)
        H4 = range(4)
        NB = 2

        def mk(name, shape, dt):
            return [[P.sb("%s%d_%d" % (name, b, h), shape, dt) for h in H4] for b in range(NB)]
        qTt = [P.sb("qTt%d" % i, [128, 4, 128], BF16) for i in range(3)]
        kTt = [P.sb("kTt%d" % i, [128, 4, 128], BF16) for i in range(3)]
        vch = [P.sb("vch%d" % i, [128, 512], BF16) for i in range(3)]
        acol = [P.sb("acol%d" % i, [128, 4], F32) for i in range(NB)]
        hfo = [P.sb("hfo%d" % i, [128, 4, 128], F32) for i in range(NB)]
        lfrep, brow, arg, eb = mk("lfrep", [128, 128], F32), mk("brow", [128, 128], F32), mk("arg", [128, 128], F32), mk("eb", [128, 128], F32)
        PT, qt, kw = mk("PT", [128, 128], BF16), mk("qt", [128, 128], BF16), mk("kw", [128, 128], BF16)
        wcol, tmx = mk("wcol", [128, 2], F32), mk("tmx", [128, 1], F32)
        irep, te = mk("irep", [128, 128], F32), mk("te", [128, 128], F32)
        dd = [P.sb("dd%d" % h, [128, 128], F32) for h in H4]
        c0t = [P.sb("c0t%d" % i, [128, 128], F32) for i in H4]
        cout = P.sb("cout", [128, 4, 130], F32)
        pA = [P.ps("pmA%d" % i) for i in H4]
        pND = [P.ps("pmN%d" % i) for i in range(2)]
        pCU = [P.ps("pmCU%d" % i) for i in range(NB)]

        for tc in range(NTC):
            i = tc % 3
            P.dma(kTt[i][:], QK[4:8, :, tc * 128:(tc + 1) * 128].rearrange("h p t -> p h t"))
            for h in H4:
                P.tr(pND[tc % 2][:, :].bitcast(BF16)[:, h * 128:(h + 1) * 128], kTt[i][:, h, :], ident_bf[:])
            P.act(ktm_all[:, tc, :], pND[tc % 2][:, :].bitcast(BF16)[:, 0:512], AF.Identity)

        steps = []
        for si, (s0, s1) in enumerate(SEQS):
            nch = (s1 - s0) // 128
            for d in range(2):
                order = list(range(nch)) if d == 0 else list(range(nch - 1, -1, -1))
                for j, c in enumerate(order):
                    steps.append(dict(si=si, d=d, c=c, first=(j == 0), last=(j == nch - 1), t0=s0 + c * 128))
        ld = [0]

        def front(k):
            st = steps[k]
            b = k % NB
            d, t0 = st["d"], st["t0"]
            prompt = st["si"] > 0
            tc = t0 // 128
            li = ld[0] % 3
            ld[0] += 1
            st["li"] = li
            bend_c = 127 if d == 0 else 0
            g = gt[:, tc, :]

            def f0():
                P.dma(qTt[li][:], QK[0:4, :, t0:t0 + 128].rearrange("h p t -> p h t"))
                P.dma(kTt[li][:], QK[4:8, :, t0:t0 + 128].rearrange("h p t -> p h t"))
                P.dma(vch[li][:], V[tc, :, 0:512])
                P.mm(pA[0][:, 392 + 4 * b:396 + 4 * b], tri[d], g[:, 4 + 8 * d:8 + 8 * d])
                P.v("tensor_tensor", out=acol[b][:], in0=g[:, 8 * d:8 * d + 4], in1=pA[0][:, 392 + 4 * b:396 + 4 * b],
                    op=ALU.subtract)

            def f1():
                for h in H4:
                    P.act(lfrep[b][h][:], ones_f[:], AF.Identity, scale=g[:, 4 + 8 * d + h:5 + 8 * d + h])
                    if prompt:
                        P.act(irep[b][h][:], ones_f[:], AF.Identity, scale=g[:, 8 * d + h:8 * d + h + 1])

            def f2():
                for h in H4:
                    P.mm(pA[h][:, 0:128], lfrep[b][h][:], tri[d])
                    P.mm(pA[h][:, 256:384], kTt[li][:, h, :], qTt[li][:, h, :])
                    if prompt:
                        P.mm(pA[h][:, 128:256], irep[b][h][:], ident)

            def f3():
                for h in H4:
                    P.act(brow[b][h][:], pA[h][:, 0:128], AF.Identity)

            def f4():
                for h in H4:
                    P.v("scalar_tensor_tensor", out=arg[b][h][:], in0=brow[b][h][:], scalar=acol[b][:, h:h + 1],
                        in1=maskb[d], op0=ALU.add, op1=ALU.add)
                    if prompt:
                        bend = brow[b][h][:, bend_c:bend_c + 1]
                        P.v("scalar_tensor_tensor", out=te[b][h][:], in0=pA[h][:, 128:256], scalar=bend,
                            in1=brow[b][h][:], op0=ALU.add, op1=ALU.subtract)
                        P.v("tensor_reduce", out=tmx[b][h][:], in_=te[b][h][:], axis=AX.X, op=ALU.max)

            def f5():
                for h in H4:
                    bend = brow[b][h][:, bend_c:bend_c + 1]
                    P.act(arg[b][h][:], arg[b][h][:], AF.Exp)
                    P.act(eb[b][h][:], brow[b][h][:], AF.Exp)
                    P.act(wcol[b][h][:, 0:1], acol[b][:, h:h + 1], AF.Exp, bias=bend)
                    P.act(wcol[b][h][:, 1:2], bend, AF.Exp)

            def f6():
                for h in H4:
                    P.v("tensor_tensor", out=PT[b][h][:], in0=pA[h][:, 256:384], in1=arg[b][h][:], op=ALU.mult)
                    P.v("tensor_tensor", out=qt[b][h][:], in0=qTt[li][:, h, :], in1=eb[b][h][:], op=ALU.mult)
                    P.v("tensor_scalar", out=kw[b][h][:], in0=ktm_all[:, tc, h * 128:(h + 1) * 128],
                        scalar1=wcol[b][h][:, 0:1], scalar2=None, op0=ALU.mult)

            def f7():
                for h in H4:
                    P.mm(pCU[b][:, h * 128:(h + 1) * 128], kw[b][h][:], vch[li][:, h * 128:(h + 1) * 128])
                    P.mm(pA[h][:, 384 + b:385 + b], kw[b][h][:], ones_bf[:, 0:1])
            return [f0, f1, f2, f3, f4, f5, f6, f7]

        def back(k):
            st = steps[k]
            b = k % NB
            d, t0, li = st["d"], st["t0"], st["li"]
            prompt = st["si"] > 0
            bend_c = 127 if d == 0 else 0
            HOUT = HF if d == 0 else HB

            def b0():
                if st["first"]:
                    for h in H4:
                        if prompt:
                            P.v("memset", ap=Cst[:, h, :], constant=0.0)
                        else:
                            P.dma(c0t[h][:], st_C[d * 4 + h], q="pool")
                            P.v("tensor_scalar", out=Cst[:, h, :], in0=c0t[h][:], scalar1=stm[:, d * 4 + h:d * 4 + h + 1],
                                scalar2=None, op0=ALU.mult)
                    if prompt:
                        P.v("memset", ap=nst[:], constant=0.0)
                        P.v("memset", ap=mrun[:], constant=0.0)
                    else:
                        P.v("tensor_tensor", out=nst[:], in0=stn[:, d * 4:d * 4 + 4], in1=stm[:, d * 4:d * 4 + 4],
                            op=ALU.mult)
                    for h in H4:
                        P.act(Cbf[:, h, :], Cst[:, h, :], AF.Identity, wt=h)
                        P.act(nrep[:, h, :], ones_f[:], AF.Identity, scale=nst[:, h:h + 1], wt=h)
                for h in H4:
                    nd = pND[h // 2]
                    o0 = (h % 2) * 256
                    P.mm(nd[:, o0:o0 + 128], vch[li][:, h * 128:(h + 1) * 128], PT[b][h][:], start=True, stop=False)
                    P.mm(nd[:, o0:o0 + 128], Cbf[:, h, :], qt[b][h][:], start=False, stop=True, lt=h)
                    P.mm(nd[:, o0 + 128:o0 + 256], ones_bf[:], PT[b][h][:], start=True, stop=False)
                    P.mm(nd[:, o0 + 128:o0 + 256], nrep[:, h, :], qt[b][h][:], start=False, stop=True, lt=h)

            def b1():
                for h in H4:
                    nd = pND[h // 2]
                    o0 = (h % 2) * 256
                    P.act(dd[h][:], nd[:, o0 + 128:o0 + 256], AF.Abs)

            def b2():
                for h in H4:
                    nd = pND[h // 2]
                    o0 = (h % 2) * 256
                    bend = brow[b][h][:, bend_c:bend_c + 1]
                    P.v("tensor_scalar", out=dd[h][:], in0=dd[h][:], scalar1=1.0, scalar2=None, op0=ALU.max)
                    P.v("reciprocal", out=dd[h][:], in_=dd[h][:])
                    P.v("tensor_tensor", out=hfo[b][:, h, :], in0=nd[:, o0:o0 + 128], in1=dd[h][:], op=ALU.mult)
                    if prompt:
                        P.v("scalar_tensor_tensor", out=mrun[:, h:h + 1], in0=mrun[:, h:h + 1], scalar=bend,
                            in1=tmx[b][h][:], op0=ALU.add, op1=ALU.max)
                    P.v("scalar_tensor_tensor", out=Cst[:, h, :], in0=Cst[:, h, :], scalar=wcol[b][h][:, 1:2],
                        in1=pCU[b][:, h * 128:(h + 1) * 128], op0=ALU.mult, op1=ALU.add)
                    P.v("scalar_tensor_tensor", out=nst[:, h:h + 1], in0=nst[:, h:h + 1], scalar=wcol[b][h][:, 1:2],
                        in1=pA[h][:, 384 + b:385 + b], op0=ALU.mult, op1=ALU.add)

            def b3():
                for h in H4:
                    P.act(Cbf[:, h, :], Cst[:, h, :], AF.Identity, wt=h)
                    P.act(nrep[:, h, :], ones_f[:], AF.Identity, scale=nst[:, h:h + 1], wt=h)
                P.dma(HOUT[:, :, t0:t0 + 128].rearrange("h p t -> p h t"), hfo[b][:], q="pool")
                if st["last"] and prompt:
                    pi = st["si"] - 1
                    P.act(cout[:, :, 129:130], mrun[:].rearrange("p (h o) -> p h o", o=1), AF.Exp, scale=-1.0)
                    for h in H4:
                        P.v("tensor_scalar", out=cout[:, h, 0:128], in0=Cst[:, h, :], scalar1=cout[:, h, 129:130],
                            scalar2=None, op0=ALU.mult)
                        P.v("tensor_scalar", out=cout[:, h, 128:129], in0=nst[:, h:h + 1], scalar1=cout[:, h, 129:130],
                            scalar2=None, op0=ALU.mult)
                    P.dma(o_C[pi, d].rearrange("h p e -> p h e"), cout[:, :, 0:128], q="pool")
                    P.dma(o_n[pi, d].rearrange("h (p o) -> p h o", o=1), cout[:, :, 128:129], q="pool", slow=True)
                    P.dma(o_m[pi, d:d + 1, :], mrun[0:1, :], q="pool")
            return [b0, b1, b2, b3]

        for f in front(0):
            f()
        for k in range(len(steps)):
            B = back(k)
            if k + 1 < len(steps):
                F = front(k + 1)
                for fn in (F[0], F[1], B[0], F[2], B[1], F[3], B[2], F[4], B[3], F[5], F[6], F[7]):
                    fn()
            else:
                for fn in B:
                    fn()

    def phase_mlstm_fin():
        gml = P.sb("gml", [128, 4], F32)
        P.dma(gml[:], g_ml[:, :])
        hf = [P.sb("fhf%d" % i, [128, T], F32) for i in range(2)]
        hb_ = [P.sb("fhb%d" % i, [128, T], F32) for i in range(2)]
        so = [P.sb("fso%d" % i, [128, T], BF16) for i in range(2)]
        sq = [P.sb("fsq%d" % i, [128, T], BF16) for i in range(2)]
        rs = [P.sb("frs%d" % i, [128, T], F32) for i in range(2)]
        ym = [P.sb("fym%d" % i, [128, T], BF16) for i in range(2)]
        pn = [P.ps("fpn%d" % i) for i in range(5)]
        for h in range(4):
            i = h % 2
            P.dma(hf[i][:], HF[h])
            P.dma(hb_[i][:], HB[h])
            P.dma(so[i][:], SO[h])
            P.v("tensor_tensor", out=hf[i][:], in0=hf[i][:], in1=hb_[i][:], op=ALU.add)
            P.act(sq[i][:], hf[i][:], AF.Square)
            for tb in range(NTB):
                P.mm(pn[tb][:, :], ones_bf[:], sq[i][:, tb * 512:(tb + 1) * 512])
                P.act(rs[i][:, tb * 512:(tb + 1) * 512], pn[tb][:, :], AF.Ln, scale=1.0 / 128, bias=epsc[:, 0:1])
                P.act(rs[i][:, tb * 512:(tb + 1) * 512], rs[i][:, tb * 512:(tb + 1) * 512], AF.Exp, scale=-0.5)
            P.v("scalar_tensor_tensor", out=hf[i][:], in0=hf[i][:], scalar=gml[:, h:h + 1], in1=rs[i][:], op0=ALU.mult,
                op1=ALU.mult)
            P.v("tensor_tensor", out=ym[i][:], in0=hf[i][:], in1=so[i][:], op=ALU.mult)
            P.dma(Y[h], ym[i][:], q="pool")

    def phase_filters():
        w1 = P.sb("w1", [33, 64], F32)
        w2 = P.sb("w2", [64, 64], F32)
        w3 = P.sb("w3", [64, 1024], F32)
        fv = P.sb("fv", [64, 4], F32)
        dl = P.sb("dl", [128, 512], F32)
        P.dma(w1[:], w_f1[:, :], q="pool")
        P.dma(w2[:], w_f2[:, :], q="pool")
        P.dma(w3[:], w_f3[:, :], q="pool")
        P.dma(fv[:, 0:3], fvec[:, :], q="pool")
        P.dma(dl[:], deltas[0:1, 0:512].broadcast_to([128, 512]), q="pool")
        pg = P.ps("pg")
        pf = [P.ps("pff%d" % i) for i in range(2)]
        z = P.sb("z", [33, LS], F32)
        h1 = P.sb("h1", [64, LS], F32)
        h2 = P.sb("h2", [64, LS], F32)
        a1 = P.sb("a1", [64, 512], F32)
        a2 = P.sb("a2", [64, 512], F32)
        dec = P.sb("dec", [128, 512], F32)
        tng = P.sb("tng", [128, 16], F32)
        fo = [P.sb("fo%d" % i, [128, 1024], BF16) for i in range(2)]
        frb2 = P.sb("frb2", [64, 1], F32)
        P.v("tensor_scalar", out=fv[:, 3:4], in0=fv[:, 0:1], scalar1=fv[:, 2:3], scalar2=None, op0=ALU.mult)
        P.v("tensor_scalar", out=frb2[:], in0=fv[:, 1:2], scalar1=fv[:, 2:3], scalar2=None, op0=ALU.mult)

        def sin_layer(dst, n, bias_ap):
            P.act(a1[:, 0:n], pg[0:64, 0:n], AF.Identity, scale=fv[:, 2:3], bias=bias_ap)
            P.v("tensor_scalar", out=a2[:, 0:n], in0=a1[:, 0:n], scalar1=1.0 / TWO_PI, scalar2=MAGIC, op0=ALU.mult,
                op1=ALU.add)
            P.v("tensor_scalar", out=a2[:, 0:n], in0=a2[:, 0:n], scalar1=MAGIC, scalar2=None, op0=ALU.subtract)
            P.v("scalar_tensor_tensor", out=a1[:, 0:n], in0=a2[:, 0:n], scalar=-TWO_PI, in1=a1[:, 0:n], op0=ALU.mult,
                op1=ALU.add)
            P.v("tensor_scalar", out=a1[:, 0:n], in0=a1[:, 0:n], scalar1=-3.141592, scalar2=3.141592, op0=ALU.max,
                op1=ALU.min)
            P.act(dst, a1[:, 0:n], AF.Sin)

        for L in (LS, LP):
            c = hc[L]
            na = c["na"]
            P.dma(z[:, 0:L], c["zT"][:, :], q="pool")
            P.dma(tng[:, 0:na], c["tneg"][:, :], q="pool")
            nblk = max(1, L // 512)
            n = min(L, 512)
            for b in range(nblk):
                sl = slice(b * n, (b + 1) * n)
                P.mm(pg[0:64, 0:n], w1[:, :], z[:, sl])
                sin_layer(h1[:, sl], n, fv[:, 3:4])
            for b in range(nblk):
                sl = slice(b * n, (b + 1) * n)
                P.mm(pg[0:64, 0:n], w2[:, :], h1[:, sl])
                sin_layer(h2[:, sl], n, frb2[:, 0:1])
            for a in range(na):
                i = a % 2
                P.act(dec[:], dl[:], AF.Exp, scale=tng[:, a:a + 1])
                for half in range(2):
                    ps = pf[half]
                    P.mm(ps[:, :], h2[:, a * 128:(a + 1) * 128], w3[:, half * 512:(half + 1) * 512])
                    dst = fo[i][:, half * 512:(half + 1) * 512]
                    P.v("tensor_tensor", out=dst, in0=ps[:, :], in1=dec[:], op=ALU.mult)
                    if half == 1 and a == 0:
                        P.v("tensor_scalar", out=dst, in0=dst, scalar1=cst_t[:, 5, 64:65], scalar2=None, op0=ALU.mult)
                P.dma(c["FL"][a], fo[i][:], q="pool")

    def phase_filters_adaln1():
        extra = [P.sb("adx%d" % i, [128, 4096], F32) for i in range(6)]
        pa = [P.ps("pa%d" % i) for i in range(2)]
        adaln_layer(1, wst + extra, pa, pre_hook=phase_filters)

    def phase_hyena():
        hb_t = P.sb("hybt", [128, 4], F32)
        P.dma(hb_t[:], hyb[:, :])
        pf = [P.ps("pf%d" % i) for i in range(6)]
        p_tr = P.ps("ptrh", [128, 1024], BF16)
        rhs_all = P.sb("rhsall", [128, 16, 1536], BF16)
        GH = P.sb("GH", [128, 17, 2, 512], BF16)
        tis = [P.sb("tis%d" % i, [128, 2, 512], F32) for i in range(3)]
        tib = [P.sb("tib%d" % i, [128, 2, 512], BF16) for i in range(3)]
        wtt = P.sb("wtt", [128, 17, 2], F32)
        ua = [P.sb("ua%d" % i, [128, 512], F32) for i in range(4)]
        ub = [P.sb("ub%d" % i, [128, 512], BF16) for i in range(2)]
        e1 = [P.sb("e1_%d" % i, [128, 512], F32) for i in range(12)]
        m1 = [P.sb("m1_%d" % i, [128, 512], F32) for i in range(2)]
        uc = [P.sb("uc%d" % i, [128, 512], F32) for i in range(2)]
        x2c = [P.sb("x2c%d" % i, [128, 512], F32) for i in range(2)]
        yh = [P.sb("yh%d" % i, [128, 512], BF16) for i in range(2)]
        tctr = [0]

        def hyena_seq(c, s0, load_filt):
            L, na, nb = c["L"], c["na"], c["nb"]
            n = min(L, 512)
            nblk = max(1, L // 512)
            if load_filt:
                P.dma(wtt[:, 0:nb, :], c["wt"][:, :, :])
                P.dma(rhs_all[:, 0:na, 512:1536], c["FL"][:, :, :].rearrange("a p n -> p a n"))
            for cc in range(4):
                for tb in range(nblk):
                    i = (cc * nblk + tb) % 2
                    tsl = slice(s0 + tb * n, s0 + (tb + 1) * n)
                    P.dma(ua[i][:, 0:n], HY[cc, :, tsl])
                    P.dma(ua[2 + i][:, 0:n], HY[4 + cc, :, tsl])
                    P.v("tensor_tensor", out=ua[i][:, 0:n], in0=ua[i][:, 0:n], in1=ua[2 + i][:, 0:n], op=ALU.mult)
                    P.dma(U[cc, :, tsl], ua[i][:, 0:n], q="pool")
                    P.v("tensor_copy", out=ub[i][:, 0:n], in_=ua[i][:, 0:n])
                    nn = n // 128
                    for a in range(nn):
                        P.tr(p_tr[:, a * 128:(a + 1) * 128], ub[i][:, a * 128:(a + 1) * 128], ident_bf[:])
                    a0 = tb * 4
                    P.act(rhs_all[:, a0:a0 + nn, cc * 128:(cc + 1) * 128],
                          p_tr[:, 0:nn * 128].rearrange("p (a t) -> p a t", a=nn), AF.Identity)
            def load_tf(b):
                i = wctr[0] % 2
                wctr[0] += 1
                sv = wst[i][:, 0:na * 256].rearrange("p (a s f) -> p a s f", a=na, s=2)
                bv = wbf[i][:, 0:na * 256].rearrange("p (a s f) -> p a s f", a=na, s=2)
                P.dma(sv, c["TF"][b], q="pool")
                P.act(bv, sv, AF.Identity)
                return bv
            nxt = load_tf(0)
            for b in range(nb):
                bv = nxt
                if b + 1 < nb:
                    nxt = load_tf(b + 1)
                for a in range(na):
                    for j in range(3):
                        P.mm(pf[j][:, :], bv[:, a, 0, :], rhs_all[:, a, j * 512:(j + 1) * 512], start=(a == 0),
                             stop=(a == na - 1))
                        P.mm(pf[3 + j][:, :], bv[:, a, 1, :], rhs_all[:, a, j * 512:(j + 1) * 512], start=(a == 0),
                             stop=(a == na - 1))
                wtc = wtt[:, b, 0:1]
                nwtc = wtt[:, b, 1:2]
                ee = e1[(b % 2) * 6:(b % 2) * 6 + 6]
                au, ap_, aq, bu, bp, bq = ee
                P.act(au[:], pf[0][:, :], AF.Identity)
                P.v("tensor_scalar", out=ap_[:], in0=pf[1][:, :], scalar1=wtc, scalar2=None, op0=ALU.mult)
                P.act(aq[:], pf[2][:, :], AF.Identity, scale=wtc)
                P.v("tensor_copy", out=bu[:], in_=pf[3][:, :])
                P.act(bp[:], pf[4][:, :], AF.Identity, scale=wtc)
                P.v("tensor_scalar", out=bq[:], in0=pf[5][:, :], scalar1=nwtc, scalar2=None, op0=ALU.mult)
                Kr, Ki = ap_, bp
                P.v("tensor_tensor", out=Kr[:], in0=ap_[:], in1=aq[:], op=ALU.add)
                P.v("tensor_tensor", out=Ki[:], in0=bp[:], in1=bq[:], op=ALU.add)
                P.v("tensor_tensor", out=m1[0][:], in0=au[:], in1=Kr[:], op=ALU.mult)
                P.v("tensor_tensor", out=m1[1][:], in0=bu[:], in1=Ki[:], op=ALU.mult)
                P.v("tensor_tensor", out=GH[:, b, 0, :], in0=m1[0][:], in1=m1[1][:], op=ALU.subtract)
                P.v("tensor_tensor", out=m1[0][:], in0=au[:], in1=Ki[:], op=ALU.mult)
                P.v("tensor_tensor", out=m1[1][:], in0=bu[:], in1=Kr[:], op=ALU.mult)
                P.v("tensor_tensor", out=GH[:, b, 1, :], in0=m1[0][:], in1=m1[1][:], op=ALU.add)
            seq = [(tb, b) for tb in range(nblk) for b in range(nb)]
            st_ = {"dma": 0, "cast": 0}

            def ti_dma(k):
                tb_k, b_k = seq[k]
                i = (tctr[0] + k) % 3
                P.dma(tis[i][:, :, 0:n], c["TI"][:, b_k, :, tb_k * n:(tb_k + 1) * n], q=("pool" if k % 2 == 0 else "sp"))

            def ti_cast(k):
                i = (tctr[0] + k) % 3
                P.act(tib[i][:, :, 0:n], tis[i][:, :, 0:n], AF.Identity)

            def ti_get(k):
                while st_["dma"] < min(len(seq), k + 3):
                    ti_dma(st_["dma"])
                    st_["dma"] += 1
                while st_["cast"] < min(len(seq), k + 2):
                    ti_cast(st_["cast"])
                    st_["cast"] += 1
                return tib[(tctr[0] + k) % 3]
            kk = 0
            for tb in range(nblk):
                for b in range(nb):
                    tb_ = ti_get(kk)
                    kk += 1
                    for cc in range(4):
                        P.mm(pf[cc][:, 0:n], GH[:, b, 0, cc * 128:(cc + 1) * 128], tb_[:, 0, 0:n], start=(b == 0),
                             stop=False)
                        P.mm(pf[cc][:, 0:n], GH[:, b, 1, cc * 128:(cc + 1) * 128], tb_[:, 1, 0:n], start=False,
                             stop=(b == nb - 1))
                for cc in range(4):
                    ps = pf[cc]
                    i = cc % 2
                    tsl = slice(s0 + tb * n, s0 + (tb + 1) * n)
                    P.dma(uc[i][:, 0:n], U[cc, :, tsl])
                    P.dma(x2c[i][:, 0:n], HY[8 + cc, :, tsl])
                    P.v("scalar_tensor_tensor", out=uc[i][:, 0:n], in0=uc[i][:, 0:n], scalar=hb_t[:, cc:cc + 1],
                        in1=ps[:, 0:n], op0=ALU.mult, op1=ALU.add)
                    P.v("tensor_tensor", out=yh[i][:, 0:n], in0=uc[i][:, 0:n], in1=x2c[i][:, 0:n], op=ALU.mult)
                    P.dma(Y[4 + cc, :, tsl], yh[i][:, 0:n])
            tctr[0] += len(seq)

        hyena_seq(hc[LS], 0, True)
        hyena_seq(hc[LP], 2048, True)
        hyena_seq(hc[LP], 2304, False)

    def phase_resproj(Wap, nk, srcD, l, sub, xsrc):
        src = P.sb("rsrc", [128, nk, T], BF16)
        for tb in range(NTB):
            tsl_ = slice(tb * 512, (tb + 1) * 512)
            P.dma(src[:, :, tsl_], srcD[:, :, tsl_].rearrange("c p t -> p c t"), q=("pool" if tb % 2 == 0 else "sp"), wt=tb)
        pp = [P.ps("pr%d" % i) for i in range(4)]
        xr = [P.sb("xr%d" % i, [128, T], F32) for i in range(2)]

        def epi(n, tb, ps):
            i = n % 2
            j = 0 if tb < 4 else 1
            tsl = slice(tb * 512, (tb + 1) * 512)
            if tb == 0:
                P.dma(xr[i][:], xsrc[n], q="pool")
            P.v("scalar_tensor_tensor", out=xr[i][:, tsl], in0=ps[:, :], scalar=modt[:, l, 16 + 24 * sub + n, j:j + 1],
                in1=xr[i][:, tsl], op0=ALU.mult, op1=ALU.add)
            if tb == NTB - 1:
                P.dma(X[n], xr[i][:], q="pool")
        ncols = max(128, (4096 // nk) // 128 * 128)
        ncols = min(ncols, 512)
        w_plan([(Wap, nk, c_, ncols) for c_ in range(0, D, ncols)])
        done = 0
        pi = [0]
        while done < D:
            nc_ = min(ncols, D - done)
            bv, wtag = load_w(Wap, 0, nk, done, nc_)
            nm = nc_ // 128
            for m in range(nm):
                for tb in range(NTB):
                    ps = pp[pi[0] % 4]
                    pi[0] += 1
                    for kc in range(nk):
                        P.mm(ps[:, :], bv[:, kc, m * 128:(m + 1) * 128], src[:, kc, tb * 512:(tb + 1) * 512],
                             start=(kc == 0), stop=(kc == nk - 1), lt=wtag, rtg=tb)
                    epi(done // 128 + m, tb, ps)
                    if m == (nm - 1) // 2 and tb == 2:
                        w_tick()
            done += nc_
        w_end()

    def phase_ffn_up(l):
        w_plan([(w_up[l], 8, br * FF + n * 128, 128) for n in range(22) for br in range(2)])
        hn, ens = norm_hn(l, 1, X)
        raw = [P.sb("raw%d" % i, [128, T], F32) for i in range(2)]
        acc = [P.sb("acc%d" % i, [128, T], F32) for i in range(2)]
        gbuf = [P.sb("gbuf%d" % i, [128, T], F32) for i in range(2)]
        obf = [P.sb("obf%d" % i, [128, T], BF16) for i in range(2)]
        tw = P.sb("tw", [128, 22, 3], F32)
        P.dma(tw[:], wcffn[:, l, :, :])
        pp = [P.ps("pu%d" % i) for i in range(4)]
        W = w_up[l]
        pend = [None]
        for n in range(22):
            i = n % 2
            for br in range(2):
                bv, wtag = load_w(W, 0, 8, br * FF + n * 128, 128)
                for tb in range(NTB):
                    ens(tb)
                    ps = pp[(br * NTB + tb) % 4]
                    for kc in range(8):
                        P.mm(ps[:, :], bv[:, kc, :], hn[:, kc, tb * 512:(tb + 1) * 512], start=(kc == 0), stop=(kc == 7),
                             lt=wtag, rtg=tb)
                    dstt = raw[i] if br == 0 else gbuf[i]
                    P.act(dstt[:, tb * 512:(tb + 1) * 512], ps[:, :], AF.Identity)
                    if tb == 2:
                        w_tick()
            def tail(i=i, n=n):
                conv3(acc[i], raw[i], tw[:, n, :])
                P.act(acc[i][:], acc[i][:], AF.Gelu_apprx_tanh)
                P.v("tensor_tensor", out=obf[i][:], in0=acc[i][:], in1=gbuf[i][:], op=ALU.mult)
                P.dma(ACTS[n], obf[i][:], q="pool")
            if pend[0] is not None:
                pend[0]()
            pend[0] = tail
        pend[0]()
        w_end()

    def phase_proj_c():
        rq = [(w_in_c, 8, c_, 512) for c_ in (0, 512, 1024, 1536)]
        for half_ in range(2):
            rq += [(w_in_c, 8, 2048 + half_ * 512, 512), (w_in_c, 8, 1024 + half_ * 512, 512)]
        w_plan(rq)
        hn, ens = norm_hn(1, 0, X)
        pp = [P.ps("pc%d" % i) for i in range(4)]
        qb = [P.sb("qb%d" % i, [128, 512], BF16) for i in range(3)]
        vb = [P.sb("vb%d" % i, [128, 512], BF16) for i in range(3)]
        kf = [P.sb("kf%d" % i, [128, 512], F32) for i in range(3)]
        ctr = [0]

        def epi_qk(n, tb, ps):
            i = ctr[0] % 3
            ctr[0] += 1
            if n < 8:
                P.act(qb[i][:], ps[:, :], AF.Identity, scale=0.125)
            else:
                P.act(qb[i][:], ps[:, :], AF.Identity)
            P.dma(QK[n, :, tb * 512:(tb + 1) * 512], qb[i][:], q="pool")
        linear_fm(w_in_c, 8, 0, 2048, hn, pp, epi_qk, ensure=ens, fb_outer=True)
        for half in range(2):
            def epi_v(tc, ps, half=half):
                i = ctr[0] % 3
                ctr[0] += 1
                P.act(vb[i][:], ps[:, :], AF.Identity)
                P.dma(V[tc, :, half * 512:(half + 1) * 512], vb[i][:], q="pool")
                if tc >= 16:
                    pi_, t0 = (tc - 16) // 2, ((tc - 16) % 2) * 128
                    P.v("tensor_copy", out=kf[i][:], in_=ps[:, :])
                    P.dma(o_v[pi_, half * 8:(half + 1) * 8, t0:t0 + 128, :].rearrange("h t d -> t h d"),
                          kf[i][:].rearrange("p (h d) -> p h d", d=64), q="pool")
            linear_tm(w_in_c, 8, 2048 + half * 512, 512, hn, pp, epi_v, range(NTC))

            def epi_k(tc, ps, half=half):
                i = ctr[0] % 3
                ctr[0] += 1
                pi_, t0 = (tc - 16) // 2, ((tc - 16) % 2) * 128
                P.act(kf[i][:], ps[:, :], AF.Identity)
                P.dma(o_k[pi_, half * 8:(half + 1) * 8, t0:t0 + 128, :].rearrange("h t d -> t h d"),
                      kf[i][:].rearrange("p (h d) -> p h d", d=64), q="pool")
            linear_tm(w_in_c, 8, 1024 + half * 512, 512, hn, pp, epi_k, range(16, 20))
        w_end()

    def phase_attn():
        def mkset(i):
            d_ = {}
            d_["qbd"] = P.sb("aqbd%d" % i, [128, 2, T], BF16)
            d_["kT"] = P.sb("akT%d" % i, [128, T], BF16)
            d_["kcx"] = P.sb("akc%d" % i, [128, 512], F32)
            d_["kcb"] = P.sb("akcb%d" % i, [128, 512], BF16)
            d_["vcx"] = P.sb("avc%d" % i, [128, 4, 2, 64], F32)
            d_["vcb"] = P.sb("avcb%d" % i, [128, 4, 128], BF16)
            d_["Vs"] = P.sb("aVs%d" % i, [128, 16, 128], BF16)
            d_["Vs64"] = P.sb("aVs64%d" % i, [128, 15, 128], BF16)
            d_["Vp"] = P.sb("aVp%d" % i, [128, 4, 128], BF16)
            d_["Tmf"] = [P.sb("aTmf%d_%d" % (i, j), [128, 15, 64], F32) for j in range(2)]
            d_["Tmb2"] = P.sb("aTmb2%d" % i, [128, 15, 2, 64], BF16)
            d_["mx"] = P.sb("amx%d" % i, [128, 32], F32)
            d_["negC"] = P.sb("anegC%d" % i, [128, 1], F32)
            d_["ysb"] = P.sb("aysb%d" % i, [128, T], BF16)
            P.v("memset", ap=d_["qbd"][:], constant=0.0)
            return d_
        sets = [mkset(0), mkset(1)]
        sqq = P.sb("asqq", [128, 2, T], BF16)
        sqk = P.sb("asqk", [128, T + 512], BF16)
        selM = P.sb("aselM", [128, 2, 128], BF16)
        P.v("memset", ap=selM[:], constant=0.0)
        P.v("memset", ap=selM[0:64, 0, :], constant=1.0)
        P.v("memset", ap=selM[64:128, 1, :], constant=1.0)
        NW = 2
        PTt = [P.sb("aPT%d" % i, [128, 1024], BF16) for i in range(NW)]
        pts = [P.sb("apts%d" % i, [128, 256], F32) for i in range(NW)]
        rd = [P.sb("ard%d" % i, [128, 256], F32) for i in range(NW)]
        pstA = [P.ps("apsa%d" % i) for i in range(NW)]
        pstB = [P.ps("apsb%d" % i) for i in range(NW)]
        po = [P.ps("apo%d" % i) for i in range(NW)]
        pn_ = P.ps("apn")

        def setup_dma(n):
            S_ = sets[n % 2]
            q_ = "pool"
            P.dma(S_["qbd"][0:64, 0, :], QK[n, 0:64, :], q=q_)
            P.dma(S_["qbd"][64:128, 1, :], QK[n, 64:128, :], q=q_)
            P.dma(S_["kT"][:], QK[8 + n], q=q_)
            P.dma(S_["kcx"][:], kcT[n], q=q_)
            for hh in range(2):
                P.dma(S_["vcx"][:, :, hh, :], vc[2 * n + hh].rearrange("(c p) d -> p c d", p=128), q=q_)
                P.dma(S_["Tmf"][hh][:], TmE[2 * n + hh], q=q_)
            P.dma(S_["Vs"][:], V[0:16, :, n * 128:(n + 1) * 128].rearrange("c p d -> p c d"), q=q_)
            P.dma(S_["Vs64"][0:64, :, :], V[0:15, 64:128, n * 128:(n + 1) * 128].rearrange("c p d -> p c d"), q=q_)
            P.dma(S_["Vs64"][64:128, :, :], V[1:16, 0:64, n * 128:(n + 1) * 128].rearrange("c p d -> p c d"), q=q_)
            P.dma(S_["Vp"][:], V[16:20, :, n * 128:(n + 1) * 128].rearrange("c p d -> p c d"), q=q_)

        def setup_compute(n):
            S_ = sets[n % 2]
            mx = S_["mx"]
            P.v("tensor_copy", eng="pool", out=S_["kcb"][:], in_=S_["kcx"][:])
            P.v("tensor_copy", eng="pool", out=S_["vcb"][:].rearrange("p c (h d) -> p c h d", h=2), in_=S_["vcx"][:])
            for hh in range(2):
                P.v("tensor_tensor", eng="pool", out=S_["Tmf"][hh][:], in0=S_["Tmf"][hh][:],
                    in1=colmask[:, None, :].broadcast_to([128, 15, 64]), op=ALU.add)
                P.v("tensor_copy", eng="pool", out=S_["Tmb2"][:, :, hh, :], in_=S_["Tmf"][hh][:])
            P.act(sqq[:], S_["qbd"][:], AF.Square)
            P.act(sqk[:, 0:T], S_["kT"][:], AF.Square)
            P.act(sqk[:, T:T + 512], S_["kcb"][:], AF.Square)
            for hh in range(2):
                for tb in range(5):
                    P.mm(pn_[:, 0:512], ones_bf[:], sqq[:, hh, tb * 512:(tb + 1) * 512])
                    P.v("tensor_reduce", out=mx[:, hh * 16 + tb:hh * 16 + tb + 1], in_=pn_[:, 0:512], axis=AX.X, op=ALU.max)
                for tb in range(6):
                    P.mm(pn_[:, 0:512], selM[:, hh, :], sqk[:, tb * 512:(tb + 1) * 512])
                    P.v("tensor_reduce", out=mx[:, hh * 16 + 5 + tb:hh * 16 + 6 + tb], in_=pn_[:, 0:512], axis=AX.X,
                        op=ALU.max)
                P.v("tensor_reduce", out=mx[:, hh * 16 + 12:hh * 16 + 13], in_=mx[:, hh * 16:hh * 16 + 5], axis=AX.X, op=ALU.max)
                P.v("tensor_reduce", out=mx[:, hh * 16 + 13:hh * 16 + 14], in_=mx[:, hh * 16 + 5:hh * 16 + 11], axis=AX.X,
                    op=ALU.max)
                P.v("tensor_tensor", out=mx[:, hh * 16 + 14:hh * 16 + 15], in0=mx[:, hh * 16 + 12:hh * 16 + 13],
                    in1=mx[:, hh * 16 + 13:hh * 16 + 14], op=ALU.mult)
            P.v("tensor_tensor", out=mx[:, 15:16], in0=mx[:, 14:15], in1=mx[:, 30:31], op=ALU.max)
            P.act(mx[:, 31:32], mx[:, 15:16], AF.Sqrt)
            P.v("tensor_scalar", out=S_["negC"][:], in0=mx[:, 31:32], scalar1=-1.0, scalar2=None, op0=ALU.mult)

        def stage_s(w, S_, qt0, nq, chunks, x0):
            n2 = 2 * nq
            qsl = S_["qbd"][:, :, qt0:qt0 + nq]
            for ch, (kap, vap) in enumerate(chunks):
                pt_, c_ = (pstA[w], ch) if ch * n2 < 512 else (pstB[w], ch - 512 // n2)
                dst = pt_[:, c_ * n2:(c_ + 1) * n2]
                local = (x0 is not None and ch < 4)
                P.mm(dst, kap, qsl, start=True, stop=(not local))
                if local:
                    P.mm(dst, ident_bf[:], S_["Tmb2"][:, x0 + 2 * ch, :, :], start=False, stop=True)
            P.act(PTt[w][:, 0:512], pstA[w][:, :], AF.Exp, bias=S_["negC"][:, 0:1])
            if len(chunks) * n2 > 512:
                P.act(PTt[w][:, 512:1024], pstB[w][:, :], AF.Exp, bias=S_["negC"][:, 0:1])

        def stage_o(w, S_, qt0, nq, chunks, x0):
            n2 = 2 * nq
            nch = len(chunks)
            for ch, (kap, vap) in enumerate(chunks):
                P.mm(po[w][:, 0:n2], vap, PTt[w][:, ch * n2:(ch + 1) * n2], start=(ch == 0), stop=(ch == nch - 1))
            for ch in range(nch):
                P.mm(po[w][:, 256:256 + n2], ones_bf[:], PTt[w][:, ch * n2:(ch + 1) * n2], start=(ch == 0),
                     stop=(ch == nch - 1))
            P.act(rd[w][:, 0:n2], po[w][:, 256:256 + n2], AF.Ln)
            P.act(rd[w][:, 0:n2], rd[w][:, 0:n2], AF.Exp, scale=-1.0)
            for hh in range(2):
                ps_ = slice(hh * 64, hh * 64 + 64)
                cs_ = slice(hh * nq, (hh + 1) * nq)
                P.v("tensor_tensor", out=S_["ysb"][ps_, qt0:qt0 + nq], in0=po[w][ps_, cs_], in1=rd[w][ps_, cs_], op=ALU.mult,
                    wt=(qt0 * 2 + hh))

        blk = [0]
        setup_dma(0)
        setup_compute(0)
        for n in range(8):
            S_ = sets[n % 2]
            if n + 1 < 8:
                setup_dma(n + 1)
            kT, kcb, vcb, Vs, Vs64, Vp = S_["kT"], S_["kcb"], S_["vcb"], S_["Vs"], S_["Vs64"], S_["Vp"]
            blocks = []
            for r in range(32):
                j0 = min(max(r - 4, 0), 24)
                k0 = j0 * 64
                chunks = []
                for i in range(4):
                    vap = Vs[:, j0 // 2 + i, :] if j0 % 2 == 0 else Vs64[:, j0 // 2 + i, :]
                    chunks.append((kT[:, k0 + i * 128:k0 + (i + 1) * 128], vap))
                for i in range(4):
                    chunks.append((kcb[:, i * 128:(i + 1) * 128], vcb[:, i, :]))
                blocks.append((S_, r * 64, 64, chunks, j0 - r + 7))
            for pi_ in range(2):
                s0 = 2048 + pi_ * 256
                chunks = [(kT[:, s0 + i * 128:s0 + (i + 1) * 128], Vp[:, pi_ * 2 + i, :]) for i in range(2)]
                for hf in range(2):
                    blocks.append((S_, s0 + hf * 128, 128, chunks, None))
            prev = None
            for bi, bk in enumerate(blocks):
                w = blk[0] % NW
                blk[0] += 1
                stage_s(w, *bk)
                if prev is not None:
                    stage_o(*prev)
                prev = (w,) + bk
                if bi == 12 and n + 1 < 8:
                    setup_compute(n + 1)
            stage_o(*prev)
            P.dma(Y[n], S_["ysb"][:])

    P.end_phase()

    stages = [
        ("adaln", phase_adaln),
        ("copyx", None),
        ("projab", phase_proj_ab),
        ("mlstm0", phase_mlstm),
        ("mlstm", phase_mlstm_fin),
        ("filters", phase_filters_adaln1),
        ("hyena", phase_hyena),
        ("outab", lambda: phase_resproj(w_out_ab, 8, Y, 0, 0, xT)),
        ("ffnup0", lambda: phase_ffn_up(0)),
        ("down0", lambda: phase_resproj(w_down[0], 22, ACTS, 0, 1, X)),
        ("projc", phase_proj_c),
        ("attn", phase_attn),
        ("outc", lambda: phase_resproj(w_out_c, 8, Y, 1, 0, X)),
        ("ffnup1", lambda: phase_ffn_up(1)),
        ("down1", lambda: phase_resproj(w_down[1], 22, ACTS, 1, 1, X)),
        ("final", lambda: phase_norm(0, 0, X, final=True)),
    ]
    dbg_outs = {}
    for name, fn in stages:
        if fn is None:
            continue
        P.begin_phase()
        fn()
        P.end_phase()
        if debug is not None and name == debug[0]:
            P.begin_phase()
            for (tn, shape, dt) in debug[1]:
                srcT = {"X": X, "HN": HN, "QK": QK, "V": V, "SO": SO, "G": G, "HY": HY, "U": U, "HF": HF, "Y": Y,
                        "ACTS": ACTS, "V64": V64}[tn]
                o = dout("dbg_" + tn, shape, dt)
                dbg_outs[tn] = o
                isc = (len(shape) == 3 and shape[1] == 128)
                buf = P.sb("dbgbuf", [128, int(np.prod(shape[2:])) if isc else int(np.prod(shape[1:]))], dt)
                if isc:
                    for ci in range(shape[0]):
                        P.dma(buf[:], srcT[ci])
                        P.dma(o[ci], buf[:])
                else:
                    P.dma(buf[:], srcT.rearrange("p a b -> p (a b)"))
                    P.dma(o.rearrange("p a b -> p (a b)"), buf[:])
            P.end_phase()
            break
    P.close()
    nc._phase_counts = P.phase_counts
    return nc


def _consts():
    ar = np.arange(128)
    ident = np.eye(128, dtype=np.float32)
    tri_f = (ar[:, None] <= ar[None, :]).astype(np.float32)
    tri_b = (ar[:, None] >= ar[None, :]).astype(np.float32)
    maskb_f = np.where(ar[:, None] <= ar[None, :], 0.0, NEG).astype(np.float32)
    maskb_b = np.where(ar[:, None] >= ar[None, :], 0.0, NEG).astype(np.float32)
    cols = np.arange(64)
    c_start = np.clip(cols - 8, 0, 48)
    valid = (cols[:, None] >= c_start[None, :]) & (cols[:, None] < c_start[None, :] + 16)
    cm = np.where(valid, 0.0, NEG).astype(np.float32)
    last = np.zeros((128, 128), np.float32)
    last[0:64, 0:64] = cm
    last[64:128, 0:64] = cm
    last[:, 64] = 1.0
    last[0, 64] = 0.0
    cst = np.stack([ident, tri_f, tri_b, maskb_f, maskb_b, last], axis=1)
    out = {"cst": np.ascontiguousarray(cst)}
    deltas = np.abs(np.linspace(math.log(1e-2) / 1.5, math.log(1e-2) / 0.3, 512, dtype=np.float32))
    out["deltas"] = np.concatenate([deltas, deltas])[None, :].astype(np.float32)
    for L in (LS, LP):
        c = hy_cfg(L)
        N, na, nb = c["N"], c["na"], c["nb"]
        t = np.arange(na * 128, dtype=np.int64)
        f = np.arange(nb * 128, dtype=np.int64)
        ang = 2.0 * np.pi * ((t[:, None] * f[None, :]) % N).astype(np.float64) / N
        Cm, Sm = np.cos(ang), np.sin(ang)
        TF = np.stack([Cm, Sm], axis=0).reshape(2, na, 128, nb, 128).transpose(3, 2, 1, 0, 4)
        out["TF%d" % L] = np.ascontiguousarray(TF).astype(np.float32)
        tt = np.arange(L, dtype=np.int64)
        ang2 = 2.0 * np.pi * ((f[:, None] * tt[None, :]) % N).astype(np.float64) / N
        TI = np.stack([np.cos(ang2), np.sin(ang2)], axis=0).reshape(2, nb, 128, L).transpose(2, 1, 0, 3)
        out["TI%d" % L] = np.ascontiguousarray(TI).astype(np.float32)
        tl = np.linspace(0.0, 1.0, L, dtype=np.float32)
        wpos = (2.0 * np.pi * np.arange(L, dtype=np.float32) / L).astype(np.float32)
        bands = np.linspace(1e-4, 15, 16, dtype=np.float32)
        z = np.concatenate([tl[:, None], np.cos(bands[None, :] * wpos[:, None]), -np.sin(bands[None, :] * wpos[:, None])],
                           axis=-1).astype(np.float32)
        out["zT%d" % L] = np.ascontiguousarray(z.T)
        out["tneg%d" % L] = np.ascontiguousarray((-tl).reshape(na, 128).T)
        wt = np.zeros(nb * 128, np.float64)
        wt[0] = 1.0 / N
        wt[1:L] = 2.0 / N
        wt[L] = 1.0 / N
        wt2 = np.stack([wt, -wt], axis=-1).reshape(nb, 128, 2).transpose(1, 0, 2)
        out["wt%d" % L] = np.ascontiguousarray(wt2).astype(np.float32)
    return out


_CACHE = {}


def _prep_inputs(inp):
    f = lambda a: np.ascontiguousarray(np.asarray(a, dtype=np.float32))
    shared = dict(_consts())
    shared["w_ada"] = f(inp["w_ada"])
    shared["b_adaP"] = f(np.asarray(inp["b_ada"]).reshape(2, 48, 128).transpose(2, 0, 1))
    gs = np.stack([inp["g_mix"][0], inp["g_mix"][1], inp["g_ffn"][0], inp["g_ffn"][1], inp["g_final"]], axis=0)
    shared["gP"] = f(gs.reshape(5, 8, 128).transpose(2, 0, 1))
    shared["w_in_ab"] = f(inp["w_in_ab"][0])
    shared["b_gates"] = f(inp["b_gates"][0][None, :])
    shared["wcqk"] = f(np.asarray(inp["w_conv_qk"][0]).reshape(3, 8, 128).transpose(2, 1, 0))
    shared["g_ml"] = f(np.asarray(inp["g_mlstm"][0]).reshape(4, 128).T)
    shared["wchy"] = f(np.asarray(inp["w_conv_hy"][0]).reshape(3, 12, 128).transpose(2, 1, 0))
    shared["w_f1"] = f(inp["w_filt1"][0])
    shared["w_f2"] = f(inp["w_filt2"][0])
    shared["w_f3"] = f(inp["w_filt3"][0])
    shared["fvec"] = f(np.stack([inp["b_filt1"][0], inp["b_filt2"][0], inp["filt_freq"][0]], axis=-1))
    shared["hyb"] = f(np.asarray(inp["hyena_bias"][0]).reshape(4, 128).T)
    shared["w_out_ab"] = f(inp["w_out_ab"][0])
    shared["w_in_c"] = f(inp["w_in_c"][0])
    rp = np.asarray(inp["rpb_c"][0], dtype=np.float32)
    pidx = np.arange(128)
    wk = (pidx % 64)[:, None, None]
    up = (pidx >= 64).astype(np.int64)[:, None, None]
    xx = np.arange(15)[None, :, None]
    wq = np.arange(64)[None, None, :]
    ci = wk - wq + 15
    ri = xx + up
    ok = (ci >= 0) & (ci <= 30) & (ri <= 14)
    gat = rp[:, np.clip(ri, 0, 14), np.clip(ci, 0, 30)]
    shared["TmE"] = np.ascontiguousarray(np.where(ok[None], gat, 0.0).astype(np.float32))
    shared["w_out_c"] = f(inp["w_out_c"][0])
    shared["w_up"] = f(inp["w_up"])
    shared["wcffn"] = f(np.asarray(inp["w_conv_ffn"]).reshape(2, 3, 22, 128).transpose(3, 0, 2, 1))
    shared["w_down"] = f(inp["w_down"])
    maps = []
    for c in range(8):
        b = c % 4
        m = dict(shared)
        toks = np.concatenate([inp["x_sample"][b], inp["x_prompt"][2 * c], inp["x_prompt"][2 * c + 1]], axis=0)
        m["xT"] = f(np.asarray(toks).T.reshape(8, 128, T))
        cvv = np.stack([inp["c"][b], inp["c_ctx"]], axis=-1)
        m["cv"] = f(cvv.reshape(8, 128, 2).transpose(1, 0, 2))
        m["st_C"] = f(np.asarray(inp["state_mlstm_C"][b, 0]).reshape(8, 128, 128))
        m["st_n"] = f(np.asarray(inp["state_mlstm_n"][b, 0]).reshape(8, 128).T)
        m["st_m"] = f(np.broadcast_to(np.asarray(inp["state_mlstm_m"][b, 0]).reshape(1, 8), (128, 8)))
        m["kcT"] = f(np.asarray(inp["cache_na_k"][b, 0]).transpose(0, 2, 1).reshape(8, 128, 512))
        m["vc"] = f(inp["cache_na_v"][b, 0])
        maps.append(m)
    return maps


def kernel(**inputs):
    inp = {k: np.asarray(v) for k, v in inputs.items()}
    if "nc" not in _CACHE:
        _CACHE["nc"] = build()
    nc = _CACHE["nc"]
    maps = _prep_inputs(inp)
    res = run_bass_kernel_spmd(nc, maps, core_ids=list(range(8)))
    R = res.results
    y_prompt = np.zeros((16, 256, D), np.float32)
    y_sample = np.zeros((4, 2048, D), np.float32)
    nC = np.zeros((16, 1, 2, 4, 128, 128), np.float32)
    nn = np.zeros((16, 1, 2, 4, 128), np.float32)
    nm = np.zeros((16, 1, 2, 4), np.float32)
    nk = np.zeros((16, 1, 16, 256, 64), np.float32)
    nv = np.zeros((16, 1, 16, 256, 64), np.float32)
    for c in range(8):
        yt = np.asarray(R[c]["yT"]).reshape(D, T).T
        if c < 4:
            y_sample[c] = yt[0:2048]
        y_prompt[2 * c] = yt[2048:2304]
        y_prompt[2 * c + 1] = yt[2304:2560]
        for pi in range(2):
            nC[2 * c + pi, 0] = np.asarray(R[c]["o_C"])[pi]
            nn[2 * c + pi, 0] = np.asarray(R[c]["o_n"])[pi]
            nm[2 * c + pi, 0] = np.asarray(R[c]["o_m"])[pi]
            nk[2 * c + pi, 0] = np.asarray(R[c]["o_k"])[pi]
            nv[2 * c + pi, 0] = np.asarray(R[c]["o_v"])[pi]
    return (y_prompt, y_sample, nC, nn, nm, nk, nv)
```
